# Optimizing a Trainium2 kernel written in Bass

```python
import math
import jax, jax.numpy as jnp
from jax import lax
import numpy as np

D_MODEL = 1024
BATCH = 8
SEQ = 8192
DEPTH = 2
DEC_BATCH = 16
DEC_SEQ = 32
PAST_LEN = 4096

F32 = jnp.float32
CHUNK = 64
Q_BLOCK = 128
D_MIX = D_MODEL
MLA_HEADS = 6
NOPE_DIM = 64
ROPE_DIM = 32
V_DIM = 64
Q_RANK = 256
KV_RANK = 128
ROPE_BASE = 10000.0
MLA_W = MLA_HEADS * V_DIM
MLA_SCALE = (NOPE_DIM + ROPE_DIM) ** -0.5
MLA_PROJ = Q_RANK + KV_RANK + ROPE_DIM
S5_GROUP_CH = 16
S5_W = 256
S5_GROUPS = S5_W // S5_GROUP_CH
S5_STATE = 64
RWKV_HEADS = 6
RWKV_HEAD = 64
RWKV_W = RWKV_HEADS * RWKV_HEAD
DECAY_LORA = 32
AAA_LORA = 32
GATE_LORA = 64
RWKV_PROJ = 3 * RWKV_W + DECAY_LORA + AAA_LORA + GATE_LORA
RWKV_SPLITS = (RWKV_W, 2 * RWKV_W, 3 * RWKV_W, 3 * RWKV_W + DECAY_LORA, 3 * RWKV_W + DECAY_LORA + AAA_LORA)
N_IN = MLA_PROJ + S5_W + RWKV_PROJ
IN_SPLITS = (Q_RANK, Q_RANK + KV_RANK, MLA_PROJ, MLA_PROJ + S5_W)
D_FF = 4 * D_MODEL
ALPHA = (2 * DEPTH) ** 0.25
BETA = (8 * DEPTH) ** -0.25
LN_EPS = 1e-5
RMS_EPS = 1e-6
GN_EPS = 64e-5
NEG_INF = -1e30

kernel_name = 'hybrid_mla_s5_rwkv7_streaming_step'


def layer_norm(x, g, b):
    xf = x.astype(F32)
    mu = jnp.mean(xf, -1, keepdims=True)
    var = jnp.mean(jnp.square(xf - mu), -1, keepdims=True)
    return ((xf - mu) * lax.rsqrt(var + LN_EPS) * g.astype(F32) + b.astype(F32)).astype(x.dtype)


def rms_norm(x, g):
    xf = x.astype(F32)
    inv = lax.rsqrt(jnp.mean(jnp.square(xf), -1, keepdims=True) + RMS_EPS)
    return (xf * inv * g.astype(F32)).astype(x.dtype)


def rope_tables(pos):
    inv_freq = ROPE_BASE ** (-jnp.arange(0, ROPE_DIM, 2, dtype=F32) / ROPE_DIM)
    ang = pos.astype(F32)[:, None] * inv_freq[None, :]
    ang = jnp.concatenate([ang, ang], -1)
    return jnp.cos(ang), jnp.sin(ang)


def apply_rope(x, cos, sin):
    shape = (1, cos.shape[0]) + (1,) * (x.ndim - 3) + (ROPE_DIM,)
    c, s = cos.reshape(shape), sin.reshape(shape)
    xf = x.astype(F32)
    x1, x2 = jnp.split(xf, 2, axis=-1)
    return (xf * c + jnp.concatenate([-x2, x1], -1) * s).astype(x.dtype)


def mla_scores(q_nope, q_rope, k_nope, k_rope):
    s = jnp.einsum('bqhd,bkhd->bhqk', q_nope, k_nope) + jnp.einsum('bqhr,bkr->bhqk', q_rope, k_rope)
    return s.astype(F32) * MLA_SCALE


def attend(s, v, mask=None):
    if mask is not None:
        s = jnp.where(mask, s, NEG_INF)
    p = jax.nn.softmax(s, axis=-1).astype(v.dtype)
    return jnp.einsum('bhqk,bkhd->bqhd', p, v)


def mla_prompt_attention(q_nope, q_rope, k_nope, k_rope, v):
    b, s = q_nope.shape[:2]
    key_chunk = jnp.arange(s) // CHUNK

    def query_block(i):
        start = i * Q_BLOCK
        qn = lax.dynamic_slice_in_dim(q_nope, start, Q_BLOCK, axis=1)
        qr = lax.dynamic_slice_in_dim(q_rope, start, Q_BLOCK, axis=1)
        q_chunk = (start + jnp.arange(Q_BLOCK)) // CHUNK
        mask = key_chunk[None, :] <= q_chunk[:, None]
        return attend(mla_scores(qn, qr, k_nope, k_rope), v, mask)

    out = lax.map(query_block, jnp.arange(s // Q_BLOCK))
    return jnp.swapaxes(out, 0, 1).reshape(b, s, MLA_W)


def mla_mixer(q_lat, kv_lat, k_rope_raw, pos, ckv_past, krope_past, prm):
    b, t = q_lat.shape[:2]
    cos, sin = rope_tables(pos)
    q = (rms_norm(q_lat, prm['q_norm_g']) @ prm['w_qb']).reshape(b, t, MLA_HEADS, NOPE_DIM + ROPE_DIM)
    q_nope = q[..., :NOPE_DIM]
    q_rope = apply_rope(q[..., NOPE_DIM:], cos, sin)
    ckv = rms_norm(kv_lat, prm['kv_norm_g'])
    krope = apply_rope(k_rope_raw, cos, sin)
    if ckv_past is None:
        ckv_all, krope_all = ckv, krope
    else:
        ckv_all = jnp.concatenate([ckv_past.astype(ckv.dtype), ckv], 1)
        krope_all = jnp.concatenate([krope_past.astype(krope.dtype), krope], 1)
    kv = (ckv_all @ prm['w_kvb']).reshape(b, ckv_all.shape[1], MLA_HEADS, NOPE_DIM + V_DIM)
    k_nope, v = kv[..., :NOPE_DIM], kv[..., NOPE_DIM:]
    if ckv_past is None:
        out = mla_prompt_attention(q_nope, q_rope, k_nope, krope_all, v)
    else:
        out = attend(mla_scores(q_nope, q_rope, k_nope, krope_all), v).reshape(b, t, MLA_W)
    return out, ckv, krope


def s5_discretize(prm):
    lam = lax.complex(prm['lam_re'].astype(F32), prm['lam_im'].astype(F32))
    dt = jnp.exp(prm['log_dt'].astype(F32))[:, None]
    lam_bar = jnp.exp(lam * dt)
    b = lax.complex(prm['b_re'].astype(F32), prm['b_im'].astype(F32))
    b_bar = ((lam_bar - 1.0) / lam)[..., None] * b
    c = lax.complex(prm['c_re'].astype(F32), prm['c_im'].astype(F32))
    return lam_bar, b_bar, c


def _linear_recurrence_combine(e1, e2):
    a1, b1 = e1
    a2, b2 = e2
    return a1 * a2, a2 * b1 + b2


def s5_block(u_blk, x0, lam_bar, b_bar, c):
    bu = jnp.einsum('gpc,btgc->btgp', b_bar, u_blk.astype(jnp.complex64))
    a = jnp.broadcast_to(lam_bar, bu.shape)
    a_cum, xs = lax.associative_scan(_linear_recurrence_combine, (a, bu), axis=1)
    xs = xs + a_cum * x0[:, None]
    y = jnp.einsum('gcp,btgp->btgc', c, xs).real
    return y, xs[:, -1]


def s5_mixer(u, x0, block_len, prm):
    b, t = u.shape[:2]
    lam_bar, b_bar, c = s5_discretize(prm)
    uf = u.astype(F32)
    ub = jnp.swapaxes(uf.reshape(b, t // block_len, block_len, S5_GROUPS, S5_GROUP_CH), 0, 1)

    def step(state, u_blk):
        y, state = s5_block(u_blk, state, lam_bar, b_bar, c)
        return state, y

    x_last, ys = lax.scan(step, x0, ub)
    y = jnp.swapaxes(ys, 0, 1).reshape(b, t, S5_W) + prm['s5_d'].astype(F32) * uf
    z = jax.nn.gelu(y)
    out = z * jax.nn.sigmoid(z @ prm['w_glu'].astype(F32) + prm['b_glu'].astype(F32))
    return out.astype(u.dtype), jnp.stack([x_last.real, x_last.imag], -1)


def rwkv_scan(r, w, k, v, kk, a, s0):
    xs = tuple(jnp.moveaxis(z.astype(F32), 1, 0) for z in (r, w, k, v, kk, a))

    def step(S, inp):
        r_t, w_t, k_t, v_t, kk_t, a_t = inp
        sa = jnp.einsum('bhij,bhj->bhi', S, -kk_t)
        S = (S * w_t[:, :, None, :] + sa[..., None] * (kk_t * a_t)[:, :, None, :]
             + v_t[..., None] * k_t[:, :, None, :])
        return S, jnp.einsum('bhij,bhj->bhi', S, r_t)

    s_last, ys = lax.scan(step, s0.astype(F32), xs)
    return jnp.moveaxis(ys, 0, 1), s_last


def rwkv_mixer(p, shift0, s0, prm):
    b, t = p.shape[:2]
    heads = (RWKV_HEADS, RWKV_HEAD)
    prev = jnp.concatenate([shift0.astype(p.dtype), p[:, :-1]], 1)
    ps = (p + (prev - p) * prm['mu_shift']).astype(F32)
    r, k, v, wd, ad, gd = jnp.split(ps, RWKV_SPLITS, axis=-1)
    w_log = -jax.nn.softplus(-(prm['w0'] + jnp.tanh(wd) @ prm['w_w2'])) - 0.5
    decay = jnp.exp(-jnp.exp(w_log))
    a = jax.nn.sigmoid(prm['a0'] + ad @ prm['w_a2'])
    g = jax.nn.sigmoid(gd) @ prm['w_g2']
    hs = lambda z: z.reshape(b, t, RWKV_HEADS, RWKV_HEAD)
    r, k, v, decay, a = hs(r), hs(k), hs(v), hs(decay), hs(a)
    kk = k * prm['k_k'].reshape(heads)
    kk = kk / jnp.maximum(jnp.linalg.norm(kk, axis=-1, keepdims=True), 1e-12)
    k = k * (1.0 + (a - 1.0) * prm['k_a'].reshape(heads))
    y, s_last = rwkv_scan(r, decay, k, v, kk, a, s0)
    mu = jnp.mean(y, -1, keepdims=True)
    var = jnp.mean(jnp.square(y - mu), -1, keepdims=True)
    y = ((y - mu) * lax.rsqrt(var + GN_EPS)).reshape(b, t, RWKV_W) * prm['gn_g'] + prm['gn_b']
    bonus = jnp.sum(r * k * prm['r_k'], -1, keepdims=True) * v
    y = (y + bonus.reshape(b, t, RWKV_W)) * g
    return y.astype(p.dtype), s_last, p[:, -1:]


def trunk_layer(x, pos, ckv_past, krope_past, s5_x0, rwkv_s0, shift0, s5_block_len, prm):
    proj = x @ prm['w_in']
    q_lat, kv_lat, k_rope_raw, u, p_rwkv = jnp.split(proj, IN_SPLITS, axis=-1)
    mla_out, ckv, krope = mla_mixer(q_lat, kv_lat, k_rope_raw, pos, ckv_past, krope_past, prm)
    s5_out, s5_state = s5_mixer(u, s5_x0, s5_block_len, prm)
    rwkv_out, rwkv_state, shift = rwkv_mixer(p_rwkv, shift0, rwkv_s0, prm)
    merged = jnp.concatenate([mla_out.astype(x.dtype), s5_out.astype(x.dtype), rwkv_out.astype(x.dtype)], -1)
    x = layer_norm(ALPHA * x + merged @ prm['w_out'], prm['ln1_g'], prm['ln1_b'])
    hidden = jnp.square(jax.nn.relu(x @ prm['w_up']))
    x = layer_norm(ALPHA * x + hidden @ prm['w_down'], prm['ln2_g'], prm['ln2_b'])
    return x, ckv, krope, s5_state, rwkv_state, shift


def setup_inputs(seed: int = 0) -> dict:
    key = jax.random.key(seed)
    ks = iter(jax.random.split(key, 64))

    def nrm(shape, scale):
        return scale * jax.random.normal(next(ks), shape, F32)

    def unif(shape, lo, hi):
        return jax.random.uniform(next(ks), shape, F32, lo, hi)

    L = DEPTH
    G, P, GC = S5_GROUPS, S5_STATE, S5_GROUP_CH
    H, N = RWKV_HEADS, RWKV_HEAD
    return {
        'x_prompt': nrm((BATCH, SEQ, D_MODEL), 1.0),
        'x_sample': nrm((DEC_BATCH, DEC_SEQ, D_MODEL), 1.0),
        'cache_mla_ckv': nrm((L, DEC_BATCH, PAST_LEN, KV_RANK), 1.0),
        'cache_mla_krope': nrm((L, DEC_BATCH, PAST_LEN, ROPE_DIM), 1.0),
        'state_s5': nrm((L, DEC_BATCH, G, P, 2), 0.1),
        'state_rwkv': nrm((L, DEC_BATCH, H, N, N), 0.3),
        'state_rwkv_shift': nrm((L, DEC_BATCH, 1, RWKV_PROJ), 1.0),
        'w_in': nrm((L, D_MODEL, N_IN), D_MODEL ** -0.5),
        'q_norm_g': 1.0 + nrm((L, Q_RANK), 0.02),
        'w_qb': nrm((L, Q_RANK, MLA_HEADS * (NOPE_DIM + ROPE_DIM)), Q_RANK ** -0.5),
        'kv_norm_g': 1.0 + nrm((L, KV_RANK), 0.02),
        'w_kvb': nrm((L, KV_RANK, MLA_HEADS * (NOPE_DIM + V_DIM)), KV_RANK ** -0.5),
        'lam_re': -0.5 + nrm((L, G, P), 0.01),
        'lam_im': math.pi * jnp.arange(P, dtype=F32) + nrm((L, G, P), 0.01),
        'log_dt': unif((L, G), math.log(1e-3), math.log(1e-1)),
        'b_re': nrm((L, G, P, GC), (0.5 / GC) ** 0.5),
        'b_im': nrm((L, G, P, GC), (0.5 / GC) ** 0.5),
        'c_re': nrm((L, G, GC, P), P ** -0.5),
        'c_im': nrm((L, G, GC, P), P ** -0.5),
        's5_d': nrm((L, S5_W), 1.0),
        'w_glu': nrm((L, S5_W, S5_W), S5_W ** -0.5),
        'b_glu': nrm((L, S5_W), 0.02),
        'mu_shift': unif((L, RWKV_PROJ), 0.0, 1.0),
        'w0': unif((L, RWKV_W), -6.0, 1.0),
        'w_w2': nrm((L, DECAY_LORA, RWKV_W), 0.1 * DECAY_LORA ** -0.5),
        'a0': nrm((L, RWKV_W), 0.5),
        'w_a2': nrm((L, AAA_LORA, RWKV_W), 0.1 * AAA_LORA ** -0.5),
        'w_g2': nrm((L, GATE_LORA, RWKV_W), GATE_LORA ** -0.5),
        'k_k': 0.85 + nrm((L, RWKV_W), 0.05),
        'k_a': 1.0 + nrm((L, RWKV_W), 0.05),
        'r_k': nrm((L, H, N), 0.1),
        'gn_g': 1.0 + nrm((L, RWKV_W), 0.02),
        'gn_b': nrm((L, RWKV_W), 0.02),
        'w_out': nrm((L, D_MIX, D_MODEL), BETA * D_MIX ** -0.5),
        'ln1_g': 1.0 + nrm((L, D_MODEL), 0.02),
        'ln1_b': nrm((L, D_MODEL), 0.02),
        'w_up': nrm((L, D_MODEL, D_FF), D_MODEL ** -0.5),
        'w_down': nrm((L, D_FF, D_MODEL), BETA * D_FF ** -0.5),
        'ln2_g': 1.0 + nrm((L, D_MODEL), 0.02),
        'ln2_b': nrm((L, D_MODEL), 0.02),
    }


def reference(x_prompt, x_sample, cache_mla_ckv, cache_mla_krope, state_s5, state_rwkv, state_rwkv_shift,
              w_in, q_norm_g, w_qb, kv_norm_g, w_kvb, lam_re, lam_im, log_dt, b_re, b_im, c_re, c_im,
              s5_d, w_glu, b_glu, mu_shift, w0, w_w2, a0, w_a2, w_g2, k_k, k_a, r_k, gn_g, gn_b,
              w_out, ln1_g, ln1_b, w_up, w_down, ln2_g, ln2_b):
    bp, sp = x_prompt.shape[:2]
    ts = x_sample.shape[1]
    past = cache_mla_ckv.shape[2]
    pos_p = jnp.arange(sp)
    pos_s = past + jnp.arange(ts)
    s5_zero = jnp.zeros((bp, S5_GROUPS, S5_STATE), jnp.complex64)
    rwkv_zero = jnp.zeros((bp, RWKV_HEADS, RWKV_HEAD, RWKV_HEAD), F32)
    shift_zero = jnp.zeros((bp, 1, RWKV_PROJ), x_prompt.dtype)

    xp, xs = x_prompt, x_sample
    ckv_p, krope_p, s5_p, rwkv_p, shift_p = [], [], [], [], []
    ckv_s, krope_s, s5_s, rwkv_s, shift_s = [], [], [], [], []
    for l in range(DEPTH):
        prm = dict(w_in=w_in[l], q_norm_g=q_norm_g[l], w_qb=w_qb[l], kv_norm_g=kv_norm_g[l], w_kvb=w_kvb[l],
                   lam_re=lam_re[l], lam_im=lam_im[l], log_dt=log_dt[l], b_re=b_re[l], b_im=b_im[l],
                   c_re=c_re[l], c_im=c_im[l], s5_d=s5_d[l], w_glu=w_glu[l], b_glu=b_glu[l],
                   mu_shift=mu_shift[l], w0=w0[l], w_w2=w_w2[l], a0=a0[l], w_a2=w_a2[l], w_g2=w_g2[l],
                   k_k=k_k[l], k_a=k_a[l], r_k=r_k[l], gn_g=gn_g[l], gn_b=gn_b[l], w_out=w_out[l],
                   ln1_g=ln1_g[l], ln1_b=ln1_b[l], w_up=w_up[l], w_down=w_down[l], ln2_g=ln2_g[l], ln2_b=ln2_b[l])
        xp, c1, k1, s1, r1, h1 = trunk_layer(xp, pos_p, None, None, s5_zero, rwkv_zero, shift_zero, CHUNK, prm)
        ckv_p.append(c1); krope_p.append(k1); s5_p.append(s1); rwkv_p.append(r1); shift_p.append(h1)
        s5_x0 = lax.complex(state_s5[l, ..., 0].astype(F32), state_s5[l, ..., 1].astype(F32))
        xs, c2, k2, s2, r2, h2 = trunk_layer(xs, pos_s, cache_mla_ckv[l], cache_mla_krope[l], s5_x0,
                                             state_rwkv[l], state_rwkv_shift[l], ts, prm)
        ckv_s.append(c2); krope_s.append(k2); s5_s.append(s2); rwkv_s.append(r2); shift_s.append(h2)

    return (xp, xs,
            jnp.stack(ckv_p), jnp.stack(krope_p), jnp.stack(s5_p), jnp.stack(rwkv_p), jnp.stack(shift_p),
            jnp.stack(ckv_s), jnp.stack(krope_s), jnp.stack(s5_s), jnp.stack(rwkv_s), jnp.stack(shift_s))
```

```python
import math
import numpy as np
import concourse.bass as bass
import concourse.mybir as mybir
from concourse.bass_utils import run_bass_kernel_spmd

F32 = mybir.dt.float32
BF16 = mybir.dt.bfloat16
AF = mybir.ActivationFunctionType
ALU = mybir.AluOpType
AX = mybir.AxisListType

D = 1024
L = 2
NH = 6
SCALE = 96 ** -0.5
ALPHA = (2 * L) ** 0.25
TS = 32
NSB = 2
NCOL = 2112
GROUPS = [(0, 128), (128, 128), (256, 128), (384, 96), (480, 96), (576, 128), (704, 128)] + \
         [(832 + 128 * i, 128) for i in range(10)]
TWO_PI = 2.0 * math.pi


class Trk:
    __slots__ = ("w", "r")

    def __init__(self):
        self.w = None
        self.r = {}


class Buf:
    def __init__(self, ap, trk=None):
        self.ap = ap
        self.k = trk if trk is not None else Trk()

    def __getitem__(self, key):
        return self.ap[key]


class Ring:
    def __init__(self, bufs):
        self.bufs = bufs
        self.i = 0

    def next(self):
        b = self.bufs[self.i]
        self.i = (self.i + 1) % len(self.bufs)
        return b


class Prog:
    def __init__(self, nc, ndma=8):
        self.nc = nc
        self.es = {}
        self.sems = {}
        self._ctx = []
        for nm, eng in (("pe", nc.tensor), ("act", nc.scalar), ("dve", nc.vector), ("pool", nc.gpsimd),
                        ("sp", nc.sync)):
            cm = nc.semaphore("s_" + nm)
            self.sems[nm] = cm.__enter__()
            self._ctx.append(cm)
            self.es[nm] = dict(eng=eng, cnt=0, known={})
        self.snap = {}
        self.rings = {}
        for q in ("sp", "pool", "act"):
            ring = []
            for i in range(ndma):
                key = "d_%s%d" % (q, i)
                cm = nc.semaphore(key)
                self.sems[key] = cm.__enter__()
                self._ctx.append(cm)
                ring.append([key, 0])
            self.rings[q] = dict(ring=ring, nxt=0)

    def close(self):
        for cm in reversed(self._ctx):
            cm.__exit__(None, None, None)

    def _needs(self, r, w):
        needs = {}
        for b in r:
            t = b.k
            if t.w is not None:
                k, v = t.w
                if needs.get(k, 0) < v:
                    needs[k] = v
        for b in w:
            t = b.k
            if t.w is not None:
                k, v = t.w
                if needs.get(k, 0) < v:
                    needs[k] = v
            for k, v in t.r.items():
                if needs.get(k, 0) < v:
                    needs[k] = v
        return needs

    def _waits(self, en, needs):
        E = self.es[en]
        out = []
        for k, v in needs.items():
            if k == en and v > E["cnt"]:
                continue
            if E["known"].get(k, 0) < v:
                E["known"][k] = v
                out.append((k, v))
        for k, v in out:
            sn = self.snap.get(k, {}).get(v)
            if sn:
                for k2, v2 in sn.items():
                    if k2 != en and E["known"].get(k2, 0) < v2:
                        E["known"][k2] = v2
        return out

    def _mark(self, tok, r, w):
        for b in w:
            b.k.w = tok
            b.k.r = {}
        for b in r:
            if b.k.r.get(tok[0], 0) < tok[1]:
                b.k.r[tok[0]] = tok[1]

    def op(self, en, fn, r=(), w=(), inc=True):
        E = self.es[en]
        waits = self._waits(en, self._needs(r, w))
        for k, v in waits[:-1]:
            E["eng"].wait_ge(self.sems[k], v)
        ins = fn()
        if waits:
            k, v = waits[-1]
            ins._wait_ge(self.sems[k], v)
        if inc:
            E["cnt"] += 1
            ins.then_inc(self.sems[en], 1)
            tok = (en, E["cnt"])
            self.snap.setdefault(en, {})[E["cnt"]] = dict(E["known"])
        else:
            tok = (en, E["cnt"] + 1)
        self._mark(tok, r, w)
        return ins

    def dma(self, q, out, in_, r=(), w=(), **kw):
        E = self.es[q]
        R = self.rings[q]
        slot = R["ring"][R["nxt"]]
        R["nxt"] = (R["nxt"] + 1) % len(R["ring"])
        needs = self._needs(r, w)
        if slot[1] > 0:
            needs[slot[0]] = max(needs.get(slot[0], 0), slot[1])
        for k, v in self._waits(q, needs):
            E["eng"].wait_ge(self.sems[k], v)
        slot[1] += 16
        ins = E["eng"].dma_start(out=out, in_=in_, **kw)
        ins.then_inc(self.sems[slot[0]], 16)
        self.snap.setdefault(slot[0], {})[slot[1]] = dict(E["known"])
        self._mark((slot[0], slot[1]), r, w)
        return ins

    def barrier(self):
        targets = {}
        for en, E in self.es.items():
            if E["cnt"] > 0:
                targets[en] = E["cnt"]
        for q, R in self.rings.items():
            for k, v in R["ring"]:
                if v > 0:
                    targets[k] = v
        for en, E in self.es.items():
            for k, v in targets.items():
                if k != en and E["known"].get(k, 0) < v:
                    E["known"][k] = v
                    E["eng"].wait_ge(self.sems[k], v)


class Builder:
    def __init__(self, T, PAST):
        self.T = T
        self.PAST = PAST
        self.TT = T + NSB * TS
        self.KS = PAST + TS
        self.KTOT = T + NSB * self.KS
        self.seqs = [(0, T, 0, T, 0)]
        for s in range(NSB):
            self.seqs.append((T + s * TS, TS, T + s * self.KS, self.KS, PAST))
        self.tiles = [(i * 512, 512) for i in range(T // 512)] + [(T, NSB * TS)]
        nc = bass.Bass("TRN2", target_bir_lowering=False)
        self.nc = nc
        self.P = Prog(nc)
        self._cms = []
        self.evi = 0

    def sb(self, name, shape, dt=F32):
        self.uid = getattr(self, "uid", 0) + 1
        name = "%s_u%d" % (name, self.uid)
        cm = self.nc.sbuf_tensor(name, list(shape), dt)
        t = cm.__enter__()
        self._cms.append(cm)
        return Buf(t)

    def ps(self, name, shape=(128, 512), dt=F32):
        self.uid = getattr(self, "uid", 0) + 1
        name = "%s_u%d" % (name, self.uid)
        cm = self.nc.psum_tensor(name, list(shape), dt)
        t = cm.__enter__()
        self._cms.append(cm)
        return Buf(t)

    def mark(self):
        return len(self._cms)

    def release(self, m):
        while len(self._cms) > m:
            self._cms.pop().__exit__(None, None, None)

    def dram(self, name, shape, dt=F32, kind="Internal"):
        return Buf(self.nc.dram_tensor(name, list(shape), dt, kind=kind).ap())

    def mm(self, out, lhsT, rhs, start, stop, r, w, inc=None):
        nc = self.nc
        if inc is None:
            inc = stop
        return self.P.op("pe", lambda: nc.tensor.matmul(out, lhsT, rhs, start=start, stop=stop), r, w, inc)

    def tr(self, out, in_, ident, r, w, inc=True):
        nc = self.nc
        return self.P.op("pe", lambda: nc.tensor.transpose(out, in_, ident), r, w, inc)

    def act(self, out, in_, func, r, w, bias=None, scale=None, accum_out=None):
        nc = self.nc
        kw = {}
        if bias is not None:
            kw["bias"] = bias
        if scale is not None:
            kw["scale"] = scale
        if accum_out is not None:
            kw["accum_out"] = accum_out
        return self.P.op("act", lambda: nc.scalar.activation(out=out, in_=in_, func=func, **kw), r, w)

    def _ve(self, en):
        return self.nc.vector if en == "dve" else self.nc.gpsimd

    def cp(self, en, out, in_, r, w):
        if en == "act":
            return self.act(out, in_, AF.Copy, r, w)
        e = self._ve(en)
        return self.P.op(en, lambda: e.tensor_copy(out, in_), r, w)

    def tt(self, en, out, a, b, op, r, w):
        e = self._ve(en)
        return self.P.op(en, lambda: e.tensor_tensor(out, a, b, op), r, w)

    def tsc(self, en, out, a, s1, s2, op0, op1, r, w):
        e = self._ve(en)
        if op1 is None:
            return self.P.op(en, lambda: e.tensor_scalar(out, a, s1, None, op0), r, w)
        return self.P.op(en, lambda: e.tensor_scalar(out, a, s1, s2, op0, op1), r, w)

    def stt(self, en, out, a, s, b, op0, op1, r, w):
        en = "dve"
        e = self._ve(en)
        return self.P.op(en, lambda: e.scalar_tensor_tensor(out, a, s, b, op0, op1), r, w)

    def mset(self, en, out, val, w):
        e = self._ve(en)
        return self.P.op(en, lambda: e.memset(out, val), (), w)

    def ev(self):
        self.evi ^= 1
        return "act" if self.evi else "dve"

    def dma(self, q, out, in_, r, w, **kw):
        return self.P.dma(q, out, in_, r, w, **kw)

    def load_bf16(self, dst_ap, dst_buf, src_ap, shape, q="sp"):
        p, f = shape
        st = self.stage.next()
        self.dma(q, st[:p, :f], src_ap, r=[], w=[st])
        self.cp("pool", dst_ap, st[:p, :f], r=[st], w=[dst_buf])


def make_consts(PAST):
    c = {}
    i = np.arange(128)
    c["ident"] = np.eye(128, dtype=np.float32)
    c["ones"] = np.ones((128, 128), np.float32)
    c["bones"] = (i[:, None] // 64 == i[None, :] // 64).astype(np.float32)
    c["maskS"] = (i[None, :] > i[:, None]).astype(np.float32)
    c["maskI"] = (i[None, :] >= i[:, None]).astype(np.float32)
    c["maskL"] = -(i[None, :] < i[:, None]).astype(np.float32)
    c["m4"] = np.concatenate([-c["maskS"], c["maskS"], c["maskI"], c["maskI"]], 1)
    sel = np.zeros((128, 64), np.float32)
    sel[64 + np.arange(64), np.arange(64)] = 1.0
    c["sel"] = sel
    c["swap"] = (i[None, :] == (i[:, None] + 64) % 128).astype(np.float32)
    c["iota"] = np.tile(np.arange(512, dtype=np.float32)[None, :], (128, 1))
    c["spos"] = np.tile((PAST + np.arange(NSB * TS) % TS).astype(np.float32)[None, :], (128, 1))
    c["gm"] = (i[:, None] // 16 == np.arange(8)[None, :]).astype(np.float32)
    E = np.zeros((128, 2, 128), np.float32)
    for g in range(16):
        E[g, g // 8, (g % 8) * 16:(g % 8) * 16 + 16] = 1.0
    c["E"] = E.reshape(128, 256)
    invf = np.zeros((128, 1), np.float32)
    f = (10000.0 ** (-np.arange(0, 32, 2, dtype=np.float32) / 32)).astype(np.float32)
    invf[64:96, 0] = np.concatenate([f, f])
    c["invf"] = invf
    sgn = np.zeros((128, 1), np.float32)
    sgn[64:80] = -1.0
    sgn[80:96] = 1.0
    c["sgn"] = sgn
    sg2 = np.ones((128, 2), np.float32)
    sg2[64:, 0] = -1.0
    sg2[:, 1] = -sg2[:, 0]
    c["sg2"] = sg2
    offs = {}
    o = 0
    parts = []
    for k, v in c.items():
        offs[k] = (o, v.shape[1])
        o += v.shape[1]
        parts.append(v)
    return np.ascontiguousarray(np.concatenate(parts, 1)), offs


def build_program(T, PAST, cst_w, offs, dbg=None):
    B = Builder(T, PAST)
    nc, P = B.nc, B.P
    TT, KTOT, KS = B.TT, B.KTOT, B.KS
    _ncd = nc.allow_non_contiguous_dma("small strided state / layout transfers")
    _ncd.__enter__()
    ext = lambda name, shape: Buf(nc.dram_tensor(name, list(shape), F32, kind="ExternalInput").ap())
    out_ = lambda name, shape: Buf(nc.dram_tensor(name, list(shape), F32, kind="ExternalOutput").ap())
    xin = ext("xin", [TT, D])
    ckvc = ext("ckvc", [L, NSB, PAST, 128])
    krc = ext("krc", [L, NSB, PAST, 32])
    s5st = ext("s5st", [L, NSB, 128, 16])
    rwst = ext("rwst", [L, NSB, 128, 3, 64])
    shst = ext("shst", [L, NSB, 128, 10])
    cst = ext("cst", [128, cst_w])
    win = ext("win", [L, 128, 8, NCOL])
    wqb = ext("wqb", [L, 128, 2, 576])
    wqbs = ext("wqbs", [L, 128, 2, 576])
    wkvk = ext("wkvk", [L, 128, 384])
    wkvv = ext("wkvv", [L, 128, 384])
    wout = ext("wout", [L, 128, 8, 1024])
    wup = ext("wup", [L, 32, 128, 1024])
    wdn = ext("wdn", [L, 128, 32, 1024])
    vec = ext("vec", [L, 128, 64])
    lnp = ext("lnp", [L, 4, 128, 1024])
    lora = ext("lora", [L, 128, 384])
    s5v = ext("s5v", [L, 16, 192])
    s5b = ext("s5b", [L, 2, 128, 2, 64])
    s5c = ext("s5c", [L, 2, 64, 256])
    wglu = ext("wglu", [L, 128, 2, 256])
    y = out_("y", [TT, D])
    o_ckv = out_("o_ckv", [L, TT, 128])
    o_kr = out_("o_kr", [L, TT, 32])
    o_s5 = out_("o_s5", [L, 3, 128, 16])
    o_rw = out_("o_rw", [L, 3, 128, 3, 64])
    o_sh = out_("o_sh", [L, 3, 128, 10])
    ROPE = B.dram("ROPE", [4, 128, TT])
    QT = [B.dram("QT%d" % l, [97, NH, TT], BF16) for l in range(L)]
    KT = [B.dram("KT%d" % l, [96, NH, KTOT], BF16) for l in range(L)]
    NKT = (KTOT + 127) // 128 + 4
    VA = [B.dram("VA%d" % l, [NKT, 128, 384], BF16) for l in range(L)]
    UT = [B.dram("UT%d" % l, [256, TT]) for l in range(L)]
    PT = [B.dram("PT%d" % l, [1280, TT]) for l in range(L)]
    MT = [B.dram("MT%d" % l, [1024, TT], BF16) for l in range(L)]
    X1 = [B.dram("X1_%d" % l, [TT, D]) for l in range(L)]
    X1T = [B.dram("X1T%d" % l, [1024, TT], BF16) for l in range(L)]
    XN = [B.dram("XN%d" % l, [TT, D]) for l in range(L - 1)] + [y]
    WUPB = B.dram("WUPB", [L, 32, 128, 1024], BF16)
    KMX = B.dram("KMX", [L, 128, NH])

    ktiles = []
    vt = 0
    for (r0, n, k0, nk, p0) in B.seqs:
        lst = []
        c = 0
        while c < nk:
            m = min(128, nk - c)
            lst.append((k0 + c, m, vt))
            vt += 1
            c += m
        ktiles.append(lst)

    C = B.sb("cst", [128, cst_w])
    P.dma("sp", C[:, :], cst[:, :], r=[], w=[C])
    cs = lambda k: C[:, offs[k][0]:offs[k][0] + offs[k][1]]
    Cb = B.sb("cstb", [128, 640], BF16)
    B.cp("dve", Cb[:, 0:128], cs("ident"), r=[C], w=[Cb])
    B.cp("dve", Cb[:, 128:256], cs("ones"), r=[C], w=[Cb])
    B.cp("dve", Cb[:, 256:384], cs("bones"), r=[C], w=[Cb])
    identb, onesb, bonesb = Cb[:, 0:128], Cb[:, 128:256], Cb[:, 256:384]
    ident = cs("ident")
    B.stage = Ring([B.sb("stage%d" % i, [128, 2112]) for i in range(2)])

    EPS = {256 * 1e-6: 0, 128 * 1e-6: 1, 1e-24: 2, 64e-5: 3, 1e-5: 4, 0.0: 5}
    epsb = B.sb("epsb", [128, 8])
    for v_, i_ in EPS.items():
        B.mset("dve", epsb[:, i_:i_ + 1], float(v_), w=[epsb])
    I32 = mybir.dt.int32

    def sqrt_pow(en, out, in_, add, expo, r, w, p0=0):
        i_ = EPS[add]
        B.act(out, in_, AF.Sqrt, r=list(r) + [epsb], w=w, bias=epsb[p0:p0 + out.shape[0], i_:i_ + 1])
        if expo < 0:
            P.op("dve", lambda: nc.vector.reciprocal(out, out), r=w, w=w)

    def _p0(ap):
        return ap.base_partition()

    def sincos(en, out_s, out_c, ang, wk, r, w):
        it_ap, ft_ap, wb = wk
        for o, sh in ((out_s, 0.0), (out_c, 0.25)):
            B.tsc(en, o, ang, 1.0 / TWO_PI, sh, ALU.mult, ALU.add, r=r, w=w)
            B.cp(en, it_ap, o, r=w, w=wb)
            B.cp(en, ft_ap, it_ap, r=wb, w=wb)
            B.tt(en, o, o, ft_ap, ALU.subtract, r=w + wb, w=w)
            B.act(o, o, AF.Sin, r=w, w=w, scale=TWO_PI * (1.0 - 1e-6))

    m0 = B.mark()
    wk = Ring([B.sb("r0_%d" % i, [128, 512]) for i in range(6)])
    r0i = B.sb("r0i", [128, 512], mybir.dt.int32)
    r0f = B.sb("r0f", [128, 512])
    r0w = Buf(None)
    for (t0, n) in B.tiles:
        pos = wk.next()
        if n == 512:
            B.tsc("pool", pos[64:96, :n], cs("iota")[64:96, :n], float(t0), None, ALU.add, None, r=[C], w=[pos])
        else:
            B.cp("pool", pos[64:96, :n], cs("spos")[64:96, :n], r=[C], w=[pos])
        B.tsc("dve", pos[64:96, :n], pos[64:96, :n], cs("invf")[64:96, :], None, ALU.mult, None, r=[pos, C], w=[pos])
        res = {"sin": wk.next(), "cos": wk.next()}
        sincos("dve", res["sin"][64:96, :n], res["cos"][64:96, :n], pos[64:96, :n],
               (r0i[64:96, :n], r0f[64:96, :n], [r0w]), r=[pos], w=[res["sin"], res["cos"]])
        B.tsc("dve", res["sin"][64:96, :n], res["sin"][64:96, :n], cs("sgn")[64:96, :], None, ALU.mult, None,
              r=[res["sin"], C], w=[res["sin"]])
        for i, nm in enumerate(("cos", "sin")):
            a = res[nm]
            P.dma("sp", ROPE[2 + i, 64:96, t0:t0 + n], a[64:96, :n], r=[a], w=[ROPE])
            q = wk.next()
            B.tsc("pool", q[64:96, :n], a[64:96, :n], SCALE, None, ALU.mult, None, r=[a], w=[q])
            P.dma("sp", ROPE[i, 64:96, t0:t0 + n], q[64:96, :n], r=[q], w=[ROPE])
    P.barrier()
    B.release(m0)

    def phase1(l):
        m1 = B.mark()
        psum = Ring([B.ps("ps%d" % i) for i in range(8)])
        Wi = B.sb("Wi", [128, 8, NCOL], BF16)
        for kc in range(8):
            B.load_bf16(Wi[:, kc, :], Wi, win[l, :, kc, :], [128, NCOL], q="sp" if kc % 2 else "act")
        Wq = B.sb("Wq", [128, 2, 576], BF16)
        Wqs = B.sb("Wqs", [128, 2, 576], BF16)
        B.load_bf16(Wq[:, :, :].rearrange("p a b -> p (a b)"), Wq, wqb[l].rearrange("p a b -> p (a b)"), [128, 1152])
        B.load_bf16(Wqs[:, :, :].rearrange("p a b -> p (a b)"), Wqs, wqbs[l].rearrange("p a b -> p (a b)"), [128, 1152])
        Wk = B.sb("Wk", [128, 384], BF16)
        Wv = B.sb("Wv", [128, 384], BF16)
        B.load_bf16(Wk[:, :], Wk, wkvk[l], [128, 384])
        B.load_bf16(Wv[:, :], Wv, wkvv[l], [128, 384])
        V = B.sb("vec", [128, 64])
        P.dma("sp", V[:, :], vec[l], r=[], w=[V])
        g16 = B.sb("g16", [128, 4])
        B.tsc("dve", g16[:, 0:2], V[:, 0:2], 16.0, None, ALU.mult, None, r=[V], w=[g16])
        B.tsc("dve", g16[:, 2:3], V[:, 2:3], math.sqrt(128.0), None, ALU.mult, None, r=[V], w=[g16])
        kmx = B.sb("kmx", [128, NH, 512])
        B.mset("pool", kmx[:, :, :], 0.0, w=[kmx])
        xts = Ring([B.sb("xt%d" % i, [128, 4, D]) for i in range(2)])
        xTs = Ring([B.sb("xT%d" % i, [128, 8, 512], BF16) for i in range(2)])
        ql = B.sb("ql", [128, 2, 512])
        sq = Ring([B.sb("sq%d" % i, [128, 512], BF16) for i in range(3)])
        rin = Ring([B.sb("rin%d" % i, [128, 512]) for i in range(2)])
        qn = B.sb("qn", [128, 2, 512], BF16)
        qT = Ring([B.sb("qT%d" % i, [128, NH, 512], BF16) for i in range(1)])
        kT = Ring([B.sb("kT%d" % i, [128, NH, 512], BF16) for i in range(1)])
        va = Ring([B.sb("va%d" % i, [128, 4, 384], BF16) for i in range(2)])
        rt = Ring([B.sb("rt%d" % i, [128, 4, 512]) for i in range(1)])
        tmp = Ring([B.sb("tmp%d" % i, [128, 512]) for i in range(4)])
        kvl = B.sb("kvl", [128, 512])
        ckvT = B.sb("ckvT", [128, 512])
        ckvTb = Ring([B.sb("ckvTb%d" % i, [128, 512], BF16) for i in range(2)])
        krT = B.sb("krT", [128, 512])
        ot = Ring([B.sb("ot%d" % i, [128, 4, 128]) for i in range(2)])
        ot2 = Ring([B.sb("ot2%d" % i, [128, 4, 32]) for i in range(2)])
        ut = Ring([B.sb("ut%d" % i, [128, 2, 512]) for i in range(1)])
        pt = Ring([B.sb("pt%d" % i, [128, 5, 512]) for i in range(1)])
        Xsrc = xin if l == 0 else XN[l - 1]

        def kv_expand(cb, kr, n, seq_parts):
            k_ = kT.next()
            v_ = va.next()
            for h in range(NH):
                ps = psum.next()
                B.mm(ps[0:64, :n], Wk[:, h * 64:(h + 1) * 64], cb[:, :n], True, True, r=[Wk, cb], w=[ps])
                B.cp(B.ev(), k_[0:64, h, :n], ps[0:64, :n], r=[ps], w=[k_])
                B.cp("pool", k_[64:96, h, :n], kr[64:96, :n], r=[kr, k_], w=[k_])
            for h in range(NH):
                s_ = sq.next()
                B.tt("pool", s_[0:96, :n], k_[0:96, h, :n], k_[0:96, h, :n], ALU.mult, r=[k_], w=[s_])
                ps = psum.next()
                B.mm(ps[0:97, :n], onesb[0:96, 0:97], s_[0:96, :n], True, True, r=[Cb, s_], w=[ps])
                B.tt("dve", kmx[0:97, h, :n], kmx[0:97, h, :n], ps[0:97, :n], ALU.max, r=[ps, kmx], w=[kmx])
            ss = min(128, n)
            for s in range((n + 127) // 128):
                ps = psum.next()
                B.mm(ps[:ss, 0:384], cb[:, s * ss:(s + 1) * ss], Wv[:, :], True, True, r=[cb, Wv], w=[ps])
                B.cp(B.ev(), v_[:ss, s, :], ps[:ss, 0:384], r=[ps], w=[v_])
            for (c0, ncol, kcol, vparts) in seq_parts:
                P.dma("pool", KT[l][:, :, kcol:kcol + ncol], k_[0:96, :, c0:c0 + ncol], r=[k_], w=[KT[l]])
                for (vti, s, row0, rows) in vparts:
                    P.dma("pool", VA[l][vti, 0:rows, :], v_[row0:row0 + rows, s, :], r=[v_], w=[VA[l]])

        pre1 = {}

        def prefetch1(ti):
            t0, n = B.tiles[ti]
            ss = min(128, n)
            nsub = n // ss
            xt = xts.next()
            P.dma("sp", xt[:ss, :nsub, :], Xsrc[t0:t0 + n, :].rearrange("(s p) d -> p s d", p=ss), r=[Xsrc], w=[xt])
            pre1[ti] = xt

        prefetch1(0)
        for ti, (t0, n) in enumerate(B.tiles):
            ss = min(128, n)
            nsub = n // ss
            if ti + 1 < len(B.tiles):
                prefetch1(ti + 1)
            xt = pre1.pop(ti)
            r_ = rt.next()
            P.dma("sp", r_[64:96, :, :n], ROPE[:, 64:96, t0:t0 + n].rearrange("a p n -> p a n"), r=[ROPE], w=[r_])
            xT = xTs.next()
            for kc in range(8):
                ps = psum.next()
                for s in range(nsub):
                    B.tr(ps[:, s * ss:(s + 1) * ss], xt[:ss, s, kc * 128:(kc + 1) * 128], ident[:ss, :ss],
                         r=[xt, C], w=[ps], inc=(s == nsub - 1))
                B.cp(B.ev(), xT[:, kc, :n], ps[:, :n], r=[ps], w=[xT])
            grp = []
            for gi, (c0, M) in enumerate(GROUPS):
                ps = psum.next()
                for kc in range(8):
                    B.mm(ps[:M, :n], Wi[:, kc, c0:c0 + M], xT[:, kc, :n], kc == 0, kc == 7, r=[Wi, xT], w=[ps])
                if gi < 2:
                    B.cp("act", ql[:, gi, :n], ps[:, :n], r=[ps], w=[ql])
                    if gi == 1:
                        ps2 = psum.next()
                        for j in range(2):
                            s_ = sq.next()
                            B.tt("pool", s_[:, :n], ql[:, j, :n], ql[:, j, :n], ALU.mult, r=[ql], w=[s_])
                            B.mm(ps2[:, :n], onesb, s_[:, :n], j == 0, j == 1, r=[Cb, s_], w=[ps2])
                        ri = rin.next()
                        sqrt_pow("dve", ri[:, :n], ps2[:, :n], 256 * 1e-6, -0.5, r=[ps2], w=[ri])
                        for j in range(2):
                            B.stt("dve", qn[:, j, :n], ql[:, j, :n], g16[:, j:j + 1], ri[:, :n], ALU.mult, ALU.mult,
                                  r=[ql, g16, ri], w=[qn])
                        q_ = qT.next()
                        for h in range(NH):
                            pa, pb = psum.next(), psum.next()
                            for j in range(2):
                                B.mm(pa[0:96, :n], Wq[:, j, h * 96:(h + 1) * 96], qn[:, j, :n], j == 0, j == 1,
                                     r=[Wq, qn], w=[pa])
                            for j in range(2):
                                B.mm(pb[0:96, :n], Wqs[:, j, h * 96:(h + 1) * 96], qn[:, j, :n], j == 0, j == 1,
                                     r=[Wqs, qn], w=[pb])
                            B.act(q_[0:64, h, :n], pa[0:64, :n], AF.Copy, r=[pa], w=[q_], scale=SCALE)
                            t1, t2 = tmp.next(), tmp.next()
                            B.tt("dve", t1[64:96, :n], pa[64:96, :n], r_[64:96, 0, :n], ALU.mult, r=[pa, r_], w=[t1])
                            B.tt("dve", t2[64:96, :n], pb[64:96, :n], r_[64:96, 1, :n], ALU.mult, r=[pb, r_], w=[t2])
                            B.tt("pool", q_[64:96, h, :n], t1[64:96, :n], t2[64:96, :n], ALU.add, r=[t1, t2, q_], w=[q_])
                            s_ = sq.next()
                            B.tt("pool", s_[0:96, :n], q_[0:96, h, :n], q_[0:96, h, :n], ALU.mult, r=[q_], w=[s_])
                            ps3 = psum.next()
                            B.mm(ps3[0:97, :n], onesb[0:96, 0:97], s_[0:96, :n], True, True, r=[Cb, s_], w=[ps3])
                            t3 = tmp.next()
                            sqrt_pow("dve", t3[96:97, :n], ps3[96:97, :n], 0.0, 0.5, r=[ps3], w=[t3], p0=96)
                            B.tsc("dve", q_[96:97, h, :n], t3[96:97, :n], -1.0, None, ALU.mult, None, r=[t3, q_], w=[q_])
                        P.dma("pool", QT[l][:, :, t0:t0 + n], q_[0:97, :, :n], r=[q_], w=[QT[l]])
                elif gi == 2:
                    B.cp("act", kvl[:, :n], ps[:, :n], r=[ps], w=[kvl])
                    s_ = sq.next()
                    B.tt("pool", s_[:, :n], kvl[:, :n], kvl[:, :n], ALU.mult, r=[kvl], w=[s_])
                    ps2 = psum.next()
                    B.mm(ps2[:, :n], onesb, s_[:, :n], True, True, r=[Cb, s_], w=[ps2])
                    ri = rin.next()
                    sqrt_pow("dve", ri[:, :n], ps2[:, :n], 128 * 1e-6, -0.5, r=[ps2], w=[ri])
                    B.stt("dve", ckvT[:, :n], kvl[:, :n], g16[:, 2:3], ri[:, :n], ALU.mult, ALU.mult,
                          r=[kvl, g16, ri], w=[ckvT])
                    cb = ckvTb.next()
                    B.cp("pool", cb[:, :n], ckvT[:, :n], r=[ckvT], w=[cb])
                    ps2 = psum.next()
                    for s in range(nsub):
                        B.tr(ps2[:ss, s * 128:(s + 1) * 128], ckvT[:, s * ss:(s + 1) * ss], ident, r=[ckvT, C],
                             w=[ps2], inc=(s == nsub - 1))
                    o_ = ot.next()
                    B.cp("act", o_[:ss, :nsub, :], ps2[:ss, 0:nsub * 128].rearrange("p (s d) -> p s d", d=128),
                         r=[ps2], w=[o_])
                    P.dma("pool", o_ckv[l, t0:t0 + n, :].rearrange("(s p) d -> p s d", p=ss), o_[:ss, :nsub, :],
                          r=[o_], w=[o_ckv])
                elif gi == 3:
                    pkr = ps
                elif gi == 4:
                    t1, t2 = tmp.next(), tmp.next()
                    B.tt("dve", t1[64:96, :n], pkr[64:96, :n], r_[64:96, 2, :n], ALU.mult, r=[pkr, r_], w=[t1])
                    B.tt("dve", t2[64:96, :n], ps[64:96, :n], r_[64:96, 3, :n], ALU.mult, r=[ps, r_], w=[t2])
                    B.tt("pool", krT[64:96, :n], t1[64:96, :n], t2[64:96, :n], ALU.add, r=[t1, t2], w=[krT])
                    ps2 = psum.next()
                    for s in range(nsub):
                        B.tr(ps2[:ss, s * 32:(s + 1) * 32], krT[64:96, s * ss:(s + 1) * ss], ident[64:96, 64:96],
                             r=[krT, C], w=[ps2], inc=(s == nsub - 1))
                    o_ = ot2.next()
                    B.cp("act", o_[:ss, :nsub, :], ps2[:ss, 0:nsub * 32].rearrange("p (s d) -> p s d", d=32),
                         r=[ps2], w=[o_])
                    P.dma("pool", o_kr[l, t0:t0 + n, :].rearrange("(s p) d -> p s d", p=ss), o_[:ss, :nsub, :],
                          r=[o_], w=[o_kr])
                    if n == 512:
                        vparts = [(ktiles[0][t0 // 128 + s][2], s, 0, 128) for s in range(4)]
                        parts = [(0, 512, t0, vparts)]
                    else:
                        parts = []
                        for si in range(NSB):
                            kc_, m_, vti = ktiles[1 + si][-1]
                            parts.append((si * TS, TS, kc_, [(vti, 0, si * TS, TS)]))
                    kv_expand(cb, krT, n, parts)
                elif gi < 7:
                    u_ = ut.next() if gi == 5 else u_
                    B.cp(B.ev(), u_[:, gi - 5, :n], ps[:, :n], r=[ps], w=[u_])
                    if gi == 6:
                        P.dma("pool", UT[l][:, t0:t0 + n].rearrange("(j p) n -> p j n", p=128), u_[:, :, :n],
                              r=[u_], w=[UT[l]])
                else:
                    hf_ = (gi - 7) // 5
                    p_ = pt.next() if (gi - 7) % 5 == 0 else p_
                    B.cp(B.ev(), p_[:, (gi - 7) % 5, :n], ps[:, :n], r=[ps], w=[p_])
                    if (gi - 7) % 5 == 4:
                        P.dma("pool", PT[l][hf_ * 640:(hf_ + 1) * 640, t0:t0 + n].rearrange("(j p) n -> p j n", p=128),
                              p_[:, :, :n], r=[p_], w=[PT[l]])
                        for si, (r0, nn, k0, nk, p0) in enumerate(B.seqs):
                            last = r0 + nn - 1
                            if t0 <= last < t0 + n:
                                P.dma("pool", o_sh[l, si, :, hf_ * 5:(hf_ + 1) * 5], p_[:, :, last - t0], r=[p_], w=[o_sh])
        cin = Ring([B.sb("cin%d" % i, [128, 4, 128]) for i in range(2)])
        kin = Ring([B.sb("kin%d" % i, [128, 4, 96]) for i in range(2)])
        for b in kin.bufs:
            B.mset("pool", b[:, :, :], 0.0, w=[b])
        for si in range(NSB):
            for c0 in range(0, PAST, 512):
                n = min(512, PAST - c0)
                nsub = n // 128
                ci, ki = cin.next(), kin.next()
                P.dma("sp", ci[:, :nsub, :], ckvc[l, si, c0:c0 + n, :].rearrange("(s p) d -> p s d", p=128), r=[], w=[ci])
                P.dma("act", ki[:, :nsub, 64:96], krc[l, si, c0:c0 + n, :].rearrange("(s p) d -> p s d", p=128), r=[], w=[ki])
                ps = psum.next()
                for s in range(nsub):
                    B.tr(ps[:, s * 128:(s + 1) * 128], ci[:, s, :], ident, r=[ci, C], w=[ps], inc=(s == nsub - 1))
                cb = ckvTb.next()
                B.cp(B.ev(), cb[:, :n], ps[:, :n], r=[ps], w=[cb])
                ps = psum.next()
                for s in range(nsub):
                    B.tr(ps[0:96, s * 128:(s + 1) * 128], ki[:, s, :], ident, r=[ki, C], w=[ps], inc=(s == nsub - 1))
                B.cp(B.ev(), krT[64:96, :n], ps[64:96, :n], r=[ps], w=[krT])
                kbase = B.seqs[1 + si][2]
                vparts = [(ktiles[1 + si][c0 // 128 + s][2], s, 0, 128) for s in range(nsub)]
                kv_expand(cb, krT, n, [(0, n, kbase + c0, vparts)])
        km = B.sb("km", [128, NH])
        P.op("dve", lambda: nc.vector.tensor_reduce(km[0:97, :], kmx[0:97, :, :], AX.X, ALU.max), r=[kmx], w=[km])
        sqrt_pow("dve", km[0:97, :], km[0:97, :], 0.0, 0.5, r=[km], w=[km])
        P.dma("sp", KMX[l, 0:97, :], km[0:97, :], r=[km], w=[KMX])
        P.barrier()
        B.release(m1)

    def phase2(l):
        m2 = B.mark()
        psS = Ring([B.ps("aS%d" % i, (128, 1024)) for i in range(3)])
        psO = Ring([B.ps("aO%d" % i) for i in range(1)])
        psL = Ring([B.ps("aL%d" % i) for i in range(1)])
        kmr = B.sb("kmr", [128, NH])
        P.dma("sp", kmr[0:97, :], KMX[l, 0:97, :], r=[KMX], w=[kmr])
        maxk = max(T, KS)
        maxt = (maxk + 127) // 128
        Kb = Ring([B.sb("Kb%d" % i, [128, maxk], BF16) for i in range(2)])
        for b in Kb.bufs:
            B.mset("pool", b[96:97, :], 1.0, w=[b])
        Vb = Ring([B.sb("Vb%d" % i, [128, maxt, 64], BF16) for i in range(2)])
        Qb = Ring([B.sb("Qb%d" % i, [128, T], BF16) for i in range(2)])
        ptr = Ring([B.sb("pt%d" % i, [128, 1024], BF16) for i in range(4)])
        rl = Ring([B.sb("rl%d" % i, [64, 512]) for i in range(2)])
        o32 = Ring([B.sb("o32_%d" % i, [64, 512]) for i in range(2)])
        mo = Ring([B.sb("mo%d" % i, [64, 512], BF16) for i in range(2)])
        its = [(h, si) for h in range(NH) for si in range(len(B.seqs))]
        NWARM = 12
        LOOK = 2
        loaded = {}

        def load(i):
            h, si = its[i]
            r0, nt, k0, nk, p0 = B.seqs[si]
            kb, vb, qb = Kb.next(), Vb.next(), Qb.next()
            kts = ktiles[si]
            P.dma("pool", kb[0:96, :nk], KT[l][:, h, k0:k0 + nk], r=[KT[l]], w=[kb])
            nfull = nk // 128
            P.dma("act", vb[:, :nfull, :],
                  VA[l][kts[0][2]:kts[0][2] + nfull, :, h * 64:(h + 1) * 64].rearrange("t p e -> p t e"),
                  r=[VA[l]], w=[vb])
            if nk % 128:
                P.dma("act", vb[:nk % 128, nfull, :], VA[l][kts[-1][2], 0:nk % 128, h * 64:(h + 1) * 64],
                      r=[VA[l]], w=[vb])
            P.dma("pool", qb[0:97, :nt], QT[l][:, h, r0:r0 + nt], r=[QT[l]], w=[qb])
            B.tsc("dve", qb[96:97, :nt], qb[96:97, :nt], kmr[96:97, h:h + 1], None, ALU.mult, None,
                  r=[qb, kmr], w=[qb])
            loaded[i] = (kb, vb, qb)

        load(0)
        for it, (h, si) in enumerate(its):
            r0, nt, k0, nk, p0 = B.seqs[si]
            kts = ktiles[si]
            if it + 1 < len(its):
                load(it + 1)
            kb, vb, qb = loaded.pop(it)
            pW = psS.next()
            for wi in range(NWARM):
                B.mm(pW[:, (wi % 2) * 512:(wi % 2) * 512 + 512], kb[0:97, 0:128], qb[0:97, 0:512] if nt >= 512 else
                     kb[0:97, 0:512], True, True, r=[kb, qb], w=[pW], inc=(wi == NWARM - 1))
            for q0 in range(0, nt, 512):
                nq = min(512, nt - q0)
                W = 512 if nq == 512 else nq
                G = 2 if nq == 512 else max(1, min(8, 1024 // nq))
                pO, pL = psO.next(), psL.next()
                groups = []
                for kt, (kcol, m, vti) in enumerate(kts):
                    if si == 0:
                        if kt * 128 >= q0 + nq:
                            break
                        off = max(0, kt * 128 - q0)
                        diag = (kt * 128 >= q0)
                    else:
                        off, diag = 0, False
                    plain = (not diag) and m == 128
                    if plain and groups and groups[-1][0] and len(groups[-1][1]) < G:
                        groups[-1][1].append((kt, m, off, diag))
                    else:
                        groups.append((plain, [(kt, m, off, diag)]))
                nblk = sum(len(g[1]) for g in groups)
                pend = {}

                def score(gi):
                    plain, tl = groups[gi]
                    pS, p_ = psS.next(), ptr.next()
                    for j, (kt, m, off, diag) in enumerate(tl):
                        B.mm(pS[:m, j * W + off:j * W + nq], kb[0:97, kt * 128:kt * 128 + m],
                             qb[0:97, q0 + off:q0 + nq], True, True, r=[kb, qb], w=[pS])
                    if plain:
                        B.act(p_[:, 0:len(tl) * W], pS[:, 0:len(tl) * W], AF.Exp, r=[pS], w=[p_])
                    else:
                        kt, m, off, diag = tl[0]
                        B.act(p_[:m, off:nq], pS[:m, off:nq], AF.Exp, r=[pS], w=[p_])
                        if diag:
                            B.mset("pool", p_[64:128, off:off + 64], 0.0, w=[p_])
                    sm_ = None
                    pend[gi] = (p_, sm_)

                for gi in range(min(LOOK, len(groups))):
                    score(gi)
                done = 0
                for gi, (plain, tl) in enumerate(groups):
                    if gi + LOOK < len(groups):
                        score(gi + LOOK)
                    p_, sm_ = pend.pop(gi)
                    for j, (kt, m, off, diag) in enumerate(tl):
                        first, last = done == 0, done == nblk - 1
                        B.mm(pO[0:64, off:nq], vb[:m, kt, :], p_[:m, j * W + off:j * W + nq], first, last,
                             r=[vb, p_], w=[pO])
                        if sm_ is None:
                            B.mm(pL[0:64, off:nq], onesb[:m, 0:64], p_[:m, j * W + off:j * W + nq], first, last,
                                 r=[Cb, p_], w=[pL])
                        elif j == 1:
                            B.mm(pL[0:64, 0:nq], onesb[:, 0:64], sm_[:, 0:nq], done == 1, last, r=[Cb, sm_], w=[pL])
                        done += 1
                r_, o3 = rl.next(), o32.next()
                B.cp("dve", r_[:, :nq], pL[0:64, :nq], r=[pL], w=[r_])
                B.cp("dve", o3[:, :nq], pO[0:64, :nq], r=[pO], w=[o3])
                P.op("dve", lambda: nc.vector.reciprocal(r_[:, :nq], r_[:, :nq]), r=[r_], w=[r_])
                o_ = mo.next()
                B.tt("pool", o_[:, :nq], o3[:, :nq], r_[:, :nq], ALU.mult, r=[o3, r_], w=[o_])
                P.dma("sp", MT[l][h * 64:(h + 1) * 64, r0 + q0:r0 + q0 + nq], o_[:, :nq], r=[o_], w=[MT[l]])
        P.barrier()
        B.release(m2)

    LT = 256

    def phase3(l):
        m3 = B.mark()
        pAB = Ring([B.ps("sA%d" % i) for i in range(4)])
        pY = [B.ps("sY%d" % i) for i in range(2)]
        pG = Ring([B.ps("sG%d" % i) for i in range(2)])
        V = B.sb("vec3", [128, 64])
        P.dma("sp", V[:, :], vec[l], r=[], w=[V])
        s5d, bgl = V[:, 3:5], V[:, 5:7]
        sv = B.sb("sv", [16, 192])
        P.dma("sp", sv[:, :], s5v[l], r=[], w=[sv])
        w16 = B.sb("w16", [16, 16, 64])
        W = lambda i: w16[:, i, :]
        K16 = [w16]
        lre, lim = sv[:, 0:64], sv[:, 64:128]
        B.act(W(0), sv[:, 128:192], AF.Exp, r=[sv], w=K16)
        B.tt("dve", W(1), lre, W(0), ALU.mult, r=[sv] + K16, w=K16)
        B.tt("dve", W(2), lim, W(0), ALU.mult, r=[sv] + K16, w=K16)
        B.act(W(3), W(1), AF.Exp, r=K16, w=K16)
        w16i = B.sb("w16i", [16, 64], mybir.dt.int32)
        sincos("dve", W(4), W(5), W(2), (w16i[:, :], W(13), K16), r=K16, w=K16)
        B.tt("dve", W(6), W(3), W(5), ALU.mult, r=K16, w=K16)
        B.tsc("dve", W(6), W(6), -1.0, None, ALU.add, None, r=K16, w=K16)
        B.tt("dve", W(7), W(3), W(4), ALU.mult, r=K16, w=K16)
        B.tt("dve", W(8), lre, lre, ALU.mult, r=[sv] + K16, w=K16)
        B.tt("dve", W(9), lim, lim, ALU.mult, r=[sv] + K16, w=K16)
        B.tt("dve", W(8), W(8), W(9), ALU.add, r=K16, w=K16)
        P.op("dve", lambda: nc.vector.reciprocal(W(8), W(8)), r=K16, w=K16)
        B.tt("dve", W(9), W(6), lre, ALU.mult, r=[sv] + K16, w=K16)
        B.tt("dve", W(10), W(7), lim, ALU.mult, r=[sv] + K16, w=K16)
        B.tt("dve", W(9), W(9), W(10), ALU.add, r=K16, w=K16)
        B.tt("dve", W(11), W(9), W(8), ALU.mult, r=K16, w=K16)
        B.tt("dve", W(9), W(7), lre, ALU.mult, r=[sv] + K16, w=K16)
        B.tt("dve", W(10), W(6), lim, ALU.mult, r=[sv] + K16, w=K16)
        B.tt("dve", W(9), W(9), W(10), ALU.subtract, r=K16, w=K16)
        B.tt("dve", W(12), W(9), W(8), ALU.mult, r=K16, w=K16)
        cat = B.sb("cat", [16, 2, 128])
        for i, src in enumerate((W(2), W(3))):
            B.cp("dve", cat[:, i, 0:64], src, r=K16, w=[cat])
            B.cp("dve", cat[:, i, 64:128], src, r=K16, w=[cat])
        thr = B.sb("thr", [128, 2, 16])
        for i in range(2):
            ps = pG.next()
            B.tr(ps[:, 0:16], cat[:, i, :], ident[0:16, 0:16], r=[cat, C], w=[ps])
            B.cp("dve", thr[:, i, :], ps[:, 0:16], r=[ps], w=[thr])
        thS, rS = thr[:, 0, :], thr[:, 1, :]
        ctab = B.sb("ctab", [128, 16, LT])
        stab = B.sb("stab", [128, 16, LT])
        rmat = B.sb("rmat", [128, 16, LT])
        stabS = B.sb("stabS", [128, 16, LT])
        io1 = B.sb("io1", [128, LT])
        B.tsc("dve", io1[:, :], cs("iota")[:, 0:LT], 1.0, None, ALU.add, None, r=[C], w=[io1])
        ang = B.sb("ang", [128, LT])
        angi = B.sb("angi", [128, LT], mybir.dt.int32)
        angf = B.sb("angf", [128, LT])
        for g in range(16):
            B.tsc("dve", ang[:, :], io1[:, :], thS[:, g:g + 1], None, ALU.mult, None, r=[io1, thr], w=[ang])
            sincos("dve", stab[:, g, :], ctab[:, g, :], ang[:, :], (angi[:, :], angf[:, :], [angf]), r=[ang],
                   w=[stab, ctab])
            B.tsc("pool", rmat[:, g, :], io1[:, :], 0.0, rS[:, g:g + 1], ALU.mult, ALU.add, r=[io1, thr], w=[rmat])
            B.tsc("pool", stabS[:, g, :], stab[:, g, :], cs("sg2")[:, 0:1], None, ALU.mult, None, r=[stab, C], w=[stabS])
        bb = B.sb("bb", [128, 2, 2, 64])
        P.dma("sp", bb[:, 0, :, :], s5b[l, 0], r=[], w=[bb])
        P.dma("sp", bb[:, 1, :, :], s5b[l, 1], r=[], w=[bb])
        Ff = B.sb("Ff", [128, 2, 2, 64])
        for i, fsrc in enumerate((W(11), W(12))):
            for gh in range(2):
                ps = pG.next()
                B.mm(ps[:, 0:64], cs("E")[0:16, gh * 128:(gh + 1) * 128], fsrc, True, True, r=[C] + K16, w=[ps])
                B.cp("dve", Ff[:, i, gh, :], ps[:, 0:64], r=[ps], w=[Ff])
        bbar = B.sb("bbar", [128, 2, 2, 64])
        t5 = B.sb("t5", [128, 2, 64])
        B.tt("dve", bbar[:, 0, :, :], Ff[:, 0, :, :], bb[:, 0, :, :], ALU.mult, r=[Ff, bb], w=[bbar])
        B.tt("dve", t5[:, :, :], Ff[:, 1, :, :], bb[:, 1, :, :], ALU.mult, r=[Ff, bb], w=[t5])
        B.tt("dve", bbar[:, 0, :, :], bbar[:, 0, :, :], t5[:, :, :], ALU.subtract, r=[bbar, t5], w=[bbar])
        B.tt("dve", bbar[:, 1, :, :], Ff[:, 0, :, :], bb[:, 1, :, :], ALU.mult, r=[Ff, bb], w=[bbar])
        B.tt("dve", t5[:, :, :], Ff[:, 1, :, :], bb[:, 0, :, :], ALU.mult, r=[Ff, bb, bbar], w=[t5])
        B.tt("dve", bbar[:, 1, :, :], bbar[:, 1, :, :], t5[:, :, :], ALU.add, r=[bbar, t5], w=[bbar])
        LB = B.sb("LB", [128, 2, 16, 128], BF16)
        for g in range(16):
            gh, g8 = g // 8, g % 8
            for sw in range(2):
                for half in range(2):
                    src = bbar[:, half ^ sw, gh, :]
                    B.tsc("pool" if half else "dve", LB[:, sw, g, half * 64:(half + 1) * 64], src,
                          cs("gm")[:, g8:g8 + 1], None, ALU.mult, None, r=[bbar, C], w=[LB])
        cc = B.sb("cc", [128, 2, 256])
        P.dma("sp", cc[0:64, 0, :], s5c[l, 0], r=[], w=[cc])
        P.dma("sp", cc[64:128, 0, :], s5c[l, 1], r=[], w=[cc])
        P.dma("sp", cc[0:64, 1, :], s5c[l, 1], r=[], w=[cc])
        P.dma("sp", cc[64:128, 1, :], s5c[l, 0], r=[], w=[cc])
        B.tsc("dve", cc[64:128, 0, :], cc[64:128, 0, :], -1.0, None, ALU.mult, None, r=[cc], w=[cc])
        B.tsc("dve", cc[:, 1, :], cc[:, 1, :], -1.0, None, ALU.mult, None, r=[cc], w=[cc])
        CP = B.sb("CP", [128, 2, 16, 128], BF16)
        B.mset("pool", CP[:, :, :, :], 0.0, w=[CP])
        for g in range(16):
            g8 = g % 8
            for i in range(2):
                B.cp("dve", CP[:, i, g, g8 * 16:(g8 + 1) * 16], cc[:, i, g * 16:(g + 1) * 16], r=[cc], w=[CP])
        Wg = B.sb("Wg", [128, 2, 256], BF16)
        B.load_bf16(Wg[:, :, :].rearrange("p a b -> p (a b)"), Wg, wglu[l].rearrange("p a b -> p (a b)"), [128, 512])
        uts = Ring([B.sb("u3_%d" % i, [128, 2, LT]) for i in range(2)])
        ubs = Ring([B.sb("ub3_%d" % i, [128, 2, LT], BF16) for i in range(2)])
        w1 = Ring([B.sb("w1_%d" % i, [128, LT]) for i in range(12)])
        zr = Ring([B.sb("z_%d" % i, [128, LT]) for i in range(4)])
        zb = Ring([B.sb("zb_%d" % i, [128, LT], BF16) for i in range(6)])
        zend = B.sb("zend", [128, 16])
        xst = B.sb("xst", [128, 16])
        yv = Ring([B.sb("yv_%d" % i, [128, LT]) for i in range(4)])
        zz = B.sb("zz", [128, 2, LT])
        zzb = B.sb("zzb", [128, 2, LT], BF16)
        mo = Ring([B.sb("mo3_%d" % i, [128, LT], BF16) for i in range(2)])
        for si, (r0, nt, k0, nk, p0) in enumerate(B.seqs):
            if si == 0:
                B.mset("pool", xst[:, :], 0.0, w=[xst])
            else:
                P.dma("sp", xst[:, :], s5st[l, si - 1], r=[], w=[xst])
            for c0 in range(0, nt, LT):
                n = min(LT, nt - c0)
                u_, ub = uts.next(), ubs.next()
                P.dma("sp", u_[:, :, :n], UT[l][:, r0 + c0:r0 + c0 + n].rearrange("(j p) n -> p j n", p=128),
                      r=[UT[l]], w=[u_])
                B.cp("pool", ub[:, :, :n], u_[:, :, :n], r=[u_], w=[ub])
                pab = {}

                def stA(g):
                    gh = g // 8
                    pa, pb = pAB.next(), pAB.next()
                    B.mm(pa[:, :n], LB[:, 0, g, :], ub[:, gh, :n], True, True, r=[LB, ub], w=[pa])
                    B.mm(pb[:, :n], LB[:, 1, g, :], ub[:, gh, :n], True, True, r=[LB, ub], w=[pb])
                    pab[g] = (pa, pb)

                wvs = {}

                def stB1(g):
                    pa, pb = pab.pop(g)
                    t1, t2, wv = w1.next(), w1.next(), w1.next()
                    B.tt("dve", t1[:, :n], pa[:, :n], ctab[:, g, :n], ALU.mult, r=[pa, ctab], w=[t1])
                    B.tt("dve", t2[:, :n], pb[:, :n], stabS[:, g, :n], ALU.mult, r=[pb, stabS], w=[t2])
                    B.tt("pool", wv[:, :n], t2[:, :n], t1[:, :n], ALU.add, r=[t1, t2], w=[wv])
                    wvs[g] = wv

                stA(0)
                stA(1)
                stB1(0)
                for g in range(16):
                    gh, g8 = g // 8, g % 8
                    if g + 1 < 16:
                        stB1(g + 1)
                    if g + 2 < 16:
                        stA(g + 2)
                    wv = wvs.pop(g)
                    z = zr.next()
                    P.op("dve", lambda: nc.vector.tensor_tensor_scan(z[:, :n], rmat[:, g, :n], wv[:, :n],
                                                                     xst[:, g:g + 1], ALU.mult, ALU.add),
                         r=[rmat, wv, xst], w=[z])
                    zc, zs = zb.next(), zb.next()
                    B.tt("dve", zc[:, :n], z[:, :n], ctab[:, g, :n], ALU.mult, r=[z, ctab], w=[zc])
                    B.tt("pool", zs[:, :n], z[:, :n], stab[:, g, :n], ALU.mult, r=[z, stab], w=[zs])
                    B.cp("act", zend[:, g:g + 1], z[:, n - 1:n], r=[z], w=[zend])
                    B.mm(pY[gh][:, :n], CP[:, 0, g, :], zc[:, :n], g8 == 0, False, r=[CP, zc], w=[pY[gh]], inc=True)
                    B.mm(pY[gh][:, :n], CP[:, 1, g, :], zs[:, :n], False, g8 == 7, r=[CP, zs], w=[pY[gh]], inc=True)
                ps = pG.next()
                B.mm(ps[:, 0:16], cs("swap"), zend[:, :], True, True, r=[C, zend], w=[ps])
                t1, t2 = w1.next(), w1.next()
                B.tt("dve", t1[:, 0:16], zend[:, :], ctab[:, :, n - 1], ALU.mult, r=[zend, ctab], w=[t1])
                B.tt("dve", t2[:, 0:16], ps[:, 0:16], stabS[:, :, n - 1], ALU.mult, r=[ps, stabS], w=[t2])
                B.stt("dve", xst[:, :], t2[:, 0:16], -1.0, t1[:, 0:16], ALU.mult, ALU.add,
                      r=[t1, t2], w=[xst])
                for j in range(2):
                    y_, x2 = yv.next(), yv.next()
                    B.stt("dve", y_[:, :n], u_[:, j, :n], s5d[:, j:j + 1], pY[j][:, :n], ALU.mult, ALU.add,
                          r=[u_, V, pY[j]], w=[y_])
                    B.tt("pool", x2[:, :n], y_[:, :n], y_[:, :n], ALU.mult, r=[y_], w=[x2])
                    B.tsc("pool", x2[:, :n], x2[:, :n], 0.044715, 1.0, ALU.mult, ALU.add, r=[x2], w=[x2])
                    B.tt("pool", x2[:, :n], x2[:, :n], y_[:, :n], ALU.mult, r=[x2, y_], w=[x2])
                    B.act(x2[:, :n], x2[:, :n], AF.Sigmoid, r=[x2], w=[x2], scale=1.5957691216057308)
                    B.tt("pool", zz[:, j, :n], y_[:, :n], x2[:, :n], ALU.mult, r=[y_, x2], w=[zz])
                    B.cp("pool", zzb[:, j, :n], zz[:, j, :n], r=[zz], w=[zzb])
                for jo in range(2):
                    ps = pG.next()
                    for j in range(2):
                        B.mm(ps[:, :n], Wg[:, j, jo * 128:(jo + 1) * 128], zzb[:, j, :n], j == 0, j == 1,
                             r=[Wg, zzb], w=[ps])
                    gt = yv.next()
                    B.act(gt[:, :n], ps[:, :n], AF.Sigmoid, r=[ps, V], w=[gt], bias=bgl[:, jo:jo + 1])
                    o_ = mo.next()
                    B.tt("dve", o_[:, :n], zz[:, jo, :n], gt[:, :n], ALU.mult, r=[zz, gt], w=[o_])
                    P.dma("sp", MT[l][384 + jo * 128:384 + (jo + 1) * 128, r0 + c0:r0 + c0 + n], o_[:, :n],
                          r=[o_], w=[MT[l]])
            P.dma("sp", o_s5[l, si], xst[:, :], r=[xst], w=[o_s5])
        P.barrier()
        B.release(m3)

    C0 = math.exp(-0.5)

    def phase4(l):
        m4 = B.mark()
        ring7 = Ring([B.ps("rB%d" % i) for i in range(7)])
        pSc = pM_ = slots = ring7
        pYb = B.ps("rY")
        V = B.sb("vec4", [128, 64])
        P.dma("sp", V[:, :], vec[l], r=[], w=[V])
        mu, w0, a0, k_k, k_a, r_k, gng, gnb = (V[:, 7:17], V[:, 17:20], V[:, 20:23], V[:, 23:26], V[:, 26:29],
                                               V[:, 29:32], V[:, 32:35], V[:, 35:38])
        omka = B.sb("omka", [128, 3])
        B.tsc("dve", omka[:, :], k_a, -1.0, 1.0, ALU.mult, ALU.add, r=[V], w=[omka])
        lo = B.sb("lo", [128, 384], BF16)
        B.load_bf16(lo[:, :], lo, lora[l], [128, 384])
        cmask = B.sb("cmask", [128, 640], BF16)
        B.cp("dve", cmask[:, 0:512], cs("m4"), r=[C], w=[cmask])
        B.cp("dve", cmask[:, 512:640], cs("maskL"), r=[C], w=[cmask])
        mk = lambda nm, shp, dt=F32, k=2: Ring([B.sb("%s%d" % (nm, i), shp, dt) for i in range(k)])
        cur, prv, dd = mk("cur", [128, 10, 128]), mk("prv", [128, 10, 128]), mk("dd", [128, 10, 128], F32, 1)
        psx = mk("psx", [128, 10, 128])
        sm = mk("sm", [128, 128], BF16, 4)
        lr3 = mk("lr3", [128, 128], BF16, 6)
        f3 = lambda nm, k=2: mk(nm, [128, 3, 128], F32, k)
        sig, aa, gg, kk_, kap, kti, bb_, bon, css, dm, pin, pinv, pex = (f3("sig"), f3("aa"), f3("gg", 3), f3("kk"),
            f3("kap"), f3("kti"), f3("bb"), f3("bon", 3), f3("css", 1), f3("dm", 1), f3("pin", 3), f3("pinv", 1), f3("pex", 1))
        b3 = lambda nm, k=2: mk(nm, [128, 3, 128], BF16, k)
        rh, kph, bhb, khb = b3("rh", 3), b3("kph", 3), b3("bhb"), b3("khb")
        bhf, khf = f3("bhf", 1), f3("khf", 1)
        kTt, bTt, vtt = b3("kTt", 3), b3("bTt", 3), b3("vtt", 3)
        scb = [mk("scb%d" % h, [128, 512], BF16, 3) for h in range(NH)]
        nbN = [mk("nbN%d" % h, [128, 128], BF16, 3) for h in range(NH)]
        nbB = [mk("nbB%d" % h, [128, 128], BF16, 3) for h in range(NH)]
        Mb = [mk("Mb%d" % h, [128, 128], BF16, 9) for h in range(NH)]
        zn = [mk("zn%d" % h, [128, 64], BF16, 2) for h in range(NH)]
        u2 = mk("u2", [128, 128], BF16, 6)
        Hf = B.sb("Hf", [128, 3, 64])
        Hb = mk("Hb", [128, 3, 64], BF16, 2)
        pmid, ppr = mk("pmid", [128, 3], F32, 3), mk("ppr", [128, 3], F32, 3)
        st6 = mk("st6", [128, 6, 6], F32, 2)
        mv6 = mk("mv6", [128, 6, 2], F32, 2)
        yn = mk("yn", [128, 384], F32, 2)
        fo = mk("fo", [128, 128], F32, 3)
        mo = mk("mo4", [128, 128], BF16, 3)
        chunks = [(si, c0) for si, sq in enumerate(B.seqs) for c0 in range(0, sq[1], 128)]

        def gen_prep(si, c0, X):
            r0, nt, k0, nk, p0 = B.seqs[si]
            n = min(128, nt - c0)
            a_, b_ = r0 + c0, r0 + c0 + n
            cu, pv = cur.next(), prv.next()
            P.dma("sp", cu[:, :, :n], PT[l][:, a_:b_].rearrange("(j p) n -> p j n", p=128), r=[PT[l]], w=[cu])
            if c0 == 0:
                if si == 0:
                    B.mset("pool", pv[:, :, 0:1], 0.0, w=[pv])
                else:
                    P.dma("act", pv[:, :, 0], shst[l, si - 1], r=[], w=[pv])
                if n > 1:
                    P.dma("act", pv[:, :, 1:n], PT[l][:, a_:b_ - 1].rearrange("(j p) n -> p j n", p=128),
                          r=[PT[l]], w=[pv])
            else:
                P.dma("act", pv[:, :, :n], PT[l][:, a_ - 1:b_ - 1].rearrange("(j p) n -> p j n", p=128),
                      r=[PT[l]], w=[pv])
            d_ = dd.next()
            B.tt("pool", d_[:, :, :n], pv[:, :, :n], cu[:, :, :n], ALU.subtract, r=[pv, cu], w=[d_])
            yield
            px = psx.next()
            for j in range(10):
                B.stt("dve" if j % 2 else "pool", px[:, j, :n], d_[:, j, :n], mu[:, j:j + 1], cu[:, j, :n],
                      ALU.mult, ALU.add, r=[d_, cu, V], w=[px])
            R_, K_, V_ = (lambda c: px[:, c, :n]), (lambda c: px[:, 3 + c, :n]), (lambda c: px[:, 6 + c, :n])
            th, adb, sgd = lr3.next(), lr3.next(), lr3.next()
            B.act(th[0:32, :n], px[0:32, 9, :n], AF.Tanh, r=[px], w=[th])
            B.cp("dve", adb[32:64, :n], px[32:64, 9, :n], r=[px], w=[adb])
            B.act(sgd[64:128, :n], px[64:128, 9, :n], AF.Sigmoid, r=[px], w=[sgd])
            yield
            sg, a3, g3, k3, kp3, kt3, b3_, bn3 = (sig.next(), aa.next(), gg.next(), kk_.next(), kap.next(),
                                                  kti.next(), bb_.next(), bon.next())
            for c in range(3):
                cc_ = slice(c * 128, (c + 1) * 128)
                p1 = pM_.next()
                B.mm(p1[:, 0:n], lo[0:32, cc_], th[0:32, :n], True, True, r=[lo, th], w=[p1])
                B.mm(p1[:, 128:128 + n], lo[32:64, cc_], adb[32:64, :n], True, True, r=[lo, adb], w=[p1])
                B.mm(p1[:, 256:256 + n], lo[64:128, cc_], sgd[64:128, :n], True, True, r=[lo, sgd], w=[p1])
                B.act(sg[:, c, :n], p1[:, 0:n], AF.Sigmoid, r=[p1, V], w=[sg], bias=w0[:, c:c + 1])
                B.act(a3[:, c, :n], p1[:, 128:128 + n], AF.Sigmoid, r=[p1, V], w=[a3], bias=a0[:, c:c + 1])
                B.cp("act", g3[:, c, :n], p1[:, 256:256 + n], r=[p1], w=[g3])
                B.tsc("pool", k3[:, c, :n], K_(c), k_k[:, c:c + 1], None, ALU.mult, None, r=[px, V], w=[k3])
                s_ = sm.next()
                B.tt("pool", s_[:, :n], k3[:, c, :n], k3[:, c, :n], ALU.mult, r=[k3], w=[s_])
                p2 = pM_.next()
                B.mm(p2[:, 0:n], bonesb, s_[:, :n], True, True, r=[Cb, s_], w=[p2])
                t_ = fo.next()
                sqrt_pow("dve", t_[:, :n], p2[:, 0:n], 1e-24, -0.5, r=[p2], w=[t_])
                B.tt("pool", kp3[:, c, :n], k3[:, c, :n], t_[:, :n], ALU.mult, r=[k3, t_], w=[kp3])
                t2_ = fo.next()
                B.tsc("dve", t2_[:, :n], a3[:, c, :n], k_a[:, c:c + 1], omka[:, c:c + 1], ALU.mult, ALU.add,
                      r=[a3, V, omka], w=[t2_])
                B.tt("pool", kt3[:, c, :n], K_(c), t2_[:, :n], ALU.mult, r=[px, t2_], w=[kt3])
                B.tt("pool", b3_[:, c, :n], kp3[:, c, :n], a3[:, c, :n], ALU.mult, r=[kp3, a3], w=[b3_])
                s2_ = sm.next()
                B.stt("dve", s2_[:, :n], R_(c), r_k[:, c:c + 1], kt3[:, c, :n], ALU.mult, ALU.mult,
                      r=[px, V, kt3], w=[s2_])
                B.mm(p2[:, 128:128 + n], bonesb, s2_[:, :n], True, True, r=[Cb, s2_], w=[p2])
                B.tt("dve", bn3[:, c, :n], p2[:, 128:128 + n], V_(c), ALU.mult, r=[p2, px], w=[bn3])
            yield
            cs_, dm_, pi_, piv, pe_ = css.next(), dm.next(), pin.next(), pinv.next(), pex.next()
            mid = min(63, n - 1)
            pm_, pr_ = pmid.next(), ppr.next()
            for c in range(3):
                P.op("dve", lambda: nc.vector.tensor_tensor_scan(cs_[:, c, :n], cs("ones")[:, 0:n], sg[:, c, :n],
                                                                 0.0, ALU.mult, ALU.add), r=[C, sg], w=[cs_])
                B.tsc("dve", dm_[:, c, :n], cs_[:, c, :n], cs_[:, c, mid:mid + 1], None, ALU.subtract, None,
                      r=[cs_], w=[dm_])
            B.act(pi_[:, :, :n], dm_[:, :, :n], AF.Exp, r=[dm_], w=[pi_], scale=-C0)
            B.act(piv[:, :, :n], dm_[:, :, :n], AF.Exp, r=[dm_], w=[piv], scale=C0)
            B.tt("pool", dm_[:, :, :n], dm_[:, :, :n], sg[:, :, :n], ALU.subtract, r=[dm_, sg], w=[dm_])
            B.act(pe_[:, :, :n], dm_[:, :, :n], AF.Exp, r=[dm_], w=[pe_], scale=-C0)
            B.act(pm_[:, :], cs_[:, :, mid], AF.Exp, r=[cs_], w=[pm_], scale=-C0)
            B.tt("dve", pr_[:, :], pm_[:, :], pi_[:, :, n - 1], ALU.mult, r=[pm_, pi_], w=[pr_])
            yield
            rh_, kph_, bhb_, khb_, bhf_, khf_ = rh.next(), kph.next(), bhb.next(), khb.next(), bhf.next(), khf.next()
            B.tt("pool", rh_[:, :, :n], px[:, 0:3, :n], pi_[:, :, :n], ALU.mult, r=[px, pi_], w=[rh_])
            B.tt("pool", kph_[:, :, :n], kp3[:, :, :n], pe_[:, :, :n], ALU.mult, r=[kp3, pe_], w=[kph_])
            B.tt("dve", bhf_[:, :, :n], b3_[:, :, :n], piv[:, :, :n], ALU.mult, r=[b3_, piv], w=[bhf_])
            B.tt("dve", khf_[:, :, :n], kt3[:, :, :n], piv[:, :, :n], ALU.mult, r=[kt3, piv], w=[khf_])
            B.cp("pool", bhb_[:, :, :n], bhf_[:, :, :n], r=[bhf_], w=[bhb_])
            B.cp("pool", khb_[:, :, :n], khf_[:, :, :n], r=[khf_], w=[khb_])
            yield
            kT_, bT_, vt_ = kTt.next(), bTt.next(), vtt.next()
            for c in range(3):
                for src, dst in ((khf_[:, c, :n], kT_), (bhf_[:, c, :n], bT_), (V_(c), vt_)):
                    p1 = pM_.next()
                    B.tr(p1[:n, 0:128], src, ident, r=[khf_, bhf_, px, C], w=[p1])
                    B.cp(B.ev(), dst[:n, c, :], p1[:n, 0:128], r=[p1], w=[dst])
            X.update(n=n, a_=a_, b_=b_, rh_=rh_, kph_=kph_, vt_=vt_, kT_=kT_, bT_=bT_, pi_=pi_, pm_=pm_,
                     pr_=pr_, bn3=bn3, g3=g3, bhb_=bhb_, khb_=khb_)
            yield

        def gen_ab(si, c0, X):
            n, rh_, kph_, bhb_, khb_ = (X[k] for k in ('n', 'rh_', 'kph_', 'bhb_', 'khb_'))
            nlev = 6 if n > 64 else (5 if n > 32 else 4)
            heads = [(c, hh) for c in range(3) for hh in range(2)]
            Rs = lambda hh: slice(hh * 64, hh * 64 + 64)
            st = {}
            for (c, hh) in heads:
                R = Rs(hh)
                sc = pSc.next()
                B.mm(sc[:n, 0:n], bhb_[R, c, :n], kph_[R, c, :n], True, True, r=[bhb_, kph_], w=[sc])
                B.mm(sc[:n, n:2 * n], khb_[R, c, :n], kph_[R, c, :n], True, True, r=[khb_, kph_], w=[sc])
                B.mm(sc[:n, 2 * n:3 * n], bhb_[R, c, :n], rh_[R, c, :n], True, True, r=[bhb_, rh_], w=[sc])
                B.mm(sc[:n, 3 * n:4 * n], khb_[R, c, :n], rh_[R, c, :n], True, True, r=[khb_, rh_], w=[sc])
                sb_ = scb[2 * c + hh].next()
                if n == 128:
                    B.tt("dve", sb_[:n, :], sc[:n, :], cmask[:n, 0:512], ALU.mult, r=[sc, cmask], w=[sb_])
                else:
                    for q in range(4):
                        B.tt("dve", sb_[:n, q * n:(q + 1) * n], sc[:n, q * n:(q + 1) * n],
                             cmask[:n, q * 128:q * 128 + n], ALU.mult, r=[sc, cmask], w=[sb_])
                p1 = slots.next()
                B.mm(p1[:n, 0:n], kph_[R, c, :n], bhb_[R, c, :n], True, True, r=[kph_, bhb_], w=[p1])
                Nk_ = nbN[2 * c + hh].next()
                B.tt("dve", Nk_[:n, :n], p1[:n, 0:n], cmask[:n, 512:512 + n], ALU.mult, r=[p1, cmask], w=[Nk_])
                M_ = Mb[2 * c + hh].next()
                B.tt("pool", M_[:n, :n], sb_[:n, 0:n], identb[:n, :n], ALU.add, r=[sb_, Cb], w=[M_])
                st[(c, hh)] = dict(sb=sb_, Bk=sb_[:n, 0:n], BkB=sb_, Nk=Nk_[:n, :n], NkB=Nk_, M=M_)
            yield
            for lev in range(nlev):
                lastl = lev == nlev - 1
                for (c, hh) in heads:
                    S_ = st[(c, hh)]
                    pa = slots.next()
                    B.mm(pa[:n, 0:n], S_["Bk"], S_["Nk"], True, True, r=[S_["BkB"], S_["NkB"]], w=[pa])
                    if not lastl:
                        pb = slots.next()
                        B.mm(pb[:n, 0:n], S_["Nk"], S_["Bk"], True, True, r=[S_["BkB"], S_["NkB"]], w=[pb])
                    nb1 = nbN[2 * c + hh].next()
                    B.cp("act", nb1[:n, :n], pa[:n, 0:n], r=[pa], w=[nb1])
                    if not lastl:
                        nb2 = nbB[2 * c + hh].next()
                        B.cp("act", nb2[:n, :n], pb[:n, 0:n], r=[pb], w=[nb2])
                        S_["Bk"], S_["BkB"] = nb2[:n, :n], nb2
                    S_["Nk"], S_["NkB"] = nb1[:n, :n], nb1
                for (c, hh) in heads:
                    S_ = st[(c, hh)]
                    pm2 = slots.next()
                    B.mm(pm2[:n, 0:n], S_["Nk"], S_["M"][:n, :n], True, True, r=[S_["NkB"], S_["M"]], w=[pm2])
                    M2 = Mb[2 * c + hh].next()
                    B.tt("dve", M2[:n, :n], pm2[:n, 0:n], S_["M"][:n, :n], ALU.add, r=[pm2, S_["M"]], w=[M2])
                    S_["M"] = M2
                yield
            X.update(st=st, heads=heads, Rs=Rs)
            yield

        def gen_c(si, c0, X):
            r0, nt, k0, nk, p0 = B.seqs[si]
            n, a_, b_, rh_, kph_, vt_, kT_, bT_, st, pi_, pm_, pr_, bn3, g3, heads, Rs = (X[k] for k in (
                'n', 'a_', 'b_', 'rh_', 'kph_', 'vt_', 'kT_', 'bT_', 'st', 'pi_', 'pm_', 'pr_', 'bn3', 'g3', 'heads', 'Rs'))
            if c0 == 0:
                if si == 0:
                    B.mset("pool", Hf[:, :, :], 0.0, w=[Hf])
                else:
                    P.dma("sp", Hf[:, :, :], rwst[l, si - 1], r=[], w=[Hf])
            hb = Hb.next()
            for c in range(3):
                B.tsc("dve", hb[:, c, :], Hf[:, c, :], pm_[:, c:c + 1], None, ALU.mult, None, r=[Hf, pm_], w=[hb])
            u_s = [u2.next() for c in range(3)]
            yield
            for (c, hh) in heads:
                S_ = st[(c, hh)]
                R = Rs(hh)
                pz = slots.next()
                AKm = S_["sb"][:n, n:2 * n]
                B.mm(pz[:n, 0:64], kph_[R, c, :n], hb[R, c, :], True, False, r=[kph_, hb], w=[pz], inc=True)
                B.mm(pz[:n, 0:64], AKm, vt_[:n, c, R], False, True, r=[S_["sb"], vt_], w=[pz])
                z_ = zn[2 * c + hh].next()
                B.act(z_[:n, :], pz[:n, 0:64], AF.Copy, r=[pz], w=[z_], scale=-1.0)
                S_["z"] = z_
            yield
            for (c, hh) in heads:
                S_ = st[(c, hh)]
                R = Rs(hh)
                pu = slots.next()
                B.mm(pu[:n, 0:64], S_["M"][:n, :n], S_["z"][:n, :], True, True, r=[S_["M"], S_["z"]], w=[pu])
                B.cp("act", u_s[c][:n, R], pu[:n, 0:64], r=[pu], w=[u_s[c]])
            yield
            for (c, hh) in heads:
                S_ = st[(c, hh)]
                R = Rs(hh)
                h = 2 * c + hh
                RBm, RKm = S_["sb"][:n, 2 * n:3 * n], S_["sb"][:n, 3 * n:4 * n]
                yr = pYb[:n, h * 64:(h + 1) * 64]
                B.mm(yr, rh_[R, c, :n], hb[R, c, :], True, False, r=[rh_, hb], w=[pYb], inc=True)
                B.mm(yr, RBm, u_s[c][:n, R], False, False, r=[S_["sb"], u_s[c]], w=[pYb], inc=True)
                B.mm(yr, RKm, vt_[:n, c, R], False, True, r=[S_["sb"], vt_], w=[pYb])
            yield
            for c in range(3):
                pH_ = ring7.next()
                B.mm(pH_[:, 0:128], kT_[:n, c, :], vt_[:n, c, :], True, False, r=[kT_, vt_], w=[pH_], inc=True)
                B.mm(pH_[:, 0:128], bT_[:n, c, :], u_s[c][:n, :], False, True, r=[bT_, u_s[c]], w=[pH_])
                for hh in range(2):
                    R = Rs(hh)
                    t_ = fo.next()
                    B.tsc("dve", t_[R, 0:64], pH_[R, R], pi_[R, c, n - 1:n], None, ALU.mult, None,
                          r=[pH_, pi_], w=[t_])
                    B.stt("dve", Hf[R, c, :], Hf[R, c, :], pr_[R, c:c + 1], t_[R, 0:64], ALU.mult, ALU.add,
                          r=[Hf, pr_, t_], w=[Hf])
            yield
            s6, m6, y_ = st6.next(), mv6.next(), yn.next()
            for h in range(NH):
                P.op("dve", lambda: nc.vector.bn_stats(s6[:n, h, :], pYb[:n, h * 64:(h + 1) * 64]), r=[pYb], w=[s6])
                P.op("dve", lambda: nc.vector.bn_aggr(m6[:n, h, :], s6[:n, h, :]), r=[s6], w=[m6])
            sqrt_pow("dve", m6[:n, :, 1], m6[:n, :, 1], 64e-5, -0.5, r=[m6], w=[m6])
            for h in range(NH):
                B.tsc("dve", y_[:n, h * 64:(h + 1) * 64], pYb[:n, h * 64:(h + 1) * 64], m6[:n, h, 0:1],
                      m6[:n, h, 1:2], ALU.subtract, ALU.mult, r=[pYb, m6], w=[y_])
            yield
            for c in range(3):
                p1 = pM_.next()
                B.tr(p1[:, 0:n], y_[:n, c * 128:(c + 1) * 128], ident[:n, :n], r=[y_, C], w=[p1])
                t_ = fo.next()
                B.tsc("dve", t_[:, :n], p1[:, 0:n], gng[:, c:c + 1], gnb[:, c:c + 1], ALU.mult, ALU.add,
                      r=[p1, V], w=[t_])
                B.tt("pool", t_[:, :n], t_[:, :n], bn3[:, c, :n], ALU.add, r=[t_, bn3], w=[t_])
                o_ = mo.next()
                B.tt("pool", o_[:, :n], t_[:, :n], g3[:, c, :n], ALU.mult, r=[t_, g3], w=[o_])
                P.dma("sp", MT[l][640 + c * 128:640 + (c + 1) * 128, a_:b_], o_[:, :n], r=[o_], w=[MT[l]])
            if c0 + 128 >= nt:
                P.dma("sp", o_rw[l, si], Hf[:, :, :], r=[Hf], w=[o_rw])
            yield

        nck = len(chunks)
        Xs = {}
        for k in range(nck + 2):
            gens = []
            if k < nck:
                Xs[k] = {}
                gens.append(gen_prep(chunks[k][0], chunks[k][1], Xs[k]))
            if 0 <= k - 1 < nck:
                gens.append(gen_ab(chunks[k - 1][0], chunks[k - 1][1], Xs[k - 1]))
            if 0 <= k - 2 < nck:
                gens.append(gen_c(chunks[k - 2][0], chunks[k - 2][1], Xs[k - 2]))
            alive = list(gens)
            while alive:
                for g_ in list(alive):
                    try:
                        next(g_)
                    except StopIteration:
                        alive.remove(g_)
            Xs.pop(k - 2, None)
        P.barrier()
        B.release(m4)

    def layer_norm(z, ss, gB, bB, outb, st, mv):
        for hf in range(2):
            P.op("dve", lambda: nc.vector.bn_stats(st[:ss, hf, :], z[:ss, hf * 512:(hf + 1) * 512]), r=[z], w=[st])
        P.op("dve", lambda: nc.vector.bn_aggr(mv[:ss, :], st[:ss, :, :].rearrange("p a b -> p (a b)")), r=[st], w=[mv])
        sqrt_pow("dve", mv[:ss, 1:2], mv[:ss, 1:2], 1e-5, -0.5, r=[mv], w=[mv])
        B.tsc("dve", outb[:ss, :], z[:ss, :], mv[:ss, 0:1], mv[:ss, 1:2], ALU.subtract, ALU.mult, r=[z, mv], w=[outb])
        B.tt("pool", outb[:ss, :], outb[:ss, :], gB[:ss, :], ALU.mult, r=[outb, gB], w=[outb])
        B.tt("pool", outb[:ss, :], outb[:ss, :], bB[:ss, :], ALU.add, r=[outb, bB], w=[outb])

    def phase5a(l):
        m5 = B.mark()
        pso = Ring([B.ps("oP%d" % i) for i in range(6)])
        Wo = B.sb("Wo", [128, 8, 1024], BF16)
        for kc in range(8):
            B.load_bf16(Wo[:, kc, :], Wo, wout[l, :, kc, :], [128, 1024], q="sp" if kc % 2 else "act")
        gB, bB = B.sb("g1", [128, 1024]), B.sb("b1", [128, 1024])
        P.dma("sp", gB[:, :], lnp[l, 0], r=[], w=[gB])
        P.dma("sp", bB[:, :], lnp[l, 1], r=[], w=[bB])
        Xsrc = xin if l == 0 else XN[l - 1]
        mts = Ring([B.sb("mt%d" % i, [128, 8, 512], BF16) for i in range(2)])
        xts = Ring([B.sb("x5_%d" % i, [128, 4, D]) for i in range(2)])
        zs = Ring([B.sb("z5_%d" % i, [128, D]) for i in range(2)])
        os_ = Ring([B.sb("o5_%d" % i, [128, D]) for i in range(3)])
        xT = Ring([B.sb("xT5_%d" % i, [128, 8, 512], BF16) for i in range(2)])
        st, mv = B.sb("st5", [128, 2, 6]), B.sb("mv5", [128, 2])
        wcv = Ring([B.sb("wcv%d" % i, [128, 1024], BF16) for i in range(2)])
        for j in range(32):
            stg = B.stage.next()
            P.dma("pool", stg[:, 0:1024], wup[l, j], r=[], w=[stg])
            wb = wcv.next()
            B.cp("pool", wb[:, :], stg[:, 0:1024], r=[stg], w=[wb])
            P.dma("pool", WUPB[l, j], wb[:, :], r=[wb], w=[WUPB])
        pre5 = {}

        def prefetch5(ti):
            t0, n = B.tiles[ti]
            ss = min(128, n)
            nsub = n // ss
            mt, xt = mts.next(), xts.next()
            P.dma("sp", mt[:, :, :n], MT[l][:, t0:t0 + n].rearrange("(k p) n -> p k n", p=128), r=[MT[l]], w=[mt])
            P.dma("sp", xt[:ss, :nsub, :], Xsrc[t0:t0 + n, :].rearrange("(s p) d -> p s d", p=ss), r=[Xsrc], w=[xt])
            pre5[ti] = (mt, xt)

        prefetch5(0)
        for ti, (t0, n) in enumerate(B.tiles):
            ss = min(128, n)
            nsub = n // ss
            if ti + 1 < len(B.tiles):
                prefetch5(ti + 1)
            mt, xt = pre5.pop(ti)
            x_T = xT.next()
            def mm_part(s):
                z = zs.next()
                for hf in range(2):
                    ps = pso.next()
                    for kc in range(8):
                        B.mm(ps[:ss, :], mt[:, kc, s * ss:(s + 1) * ss], Wo[:, kc, hf * 512:(hf + 1) * 512],
                             kc == 0, kc == 7, r=[mt, Wo], w=[ps])
                    B.stt("dve", z[:ss, hf * 512:(hf + 1) * 512], xt[:ss, s, hf * 512:(hf + 1) * 512], ALPHA,
                          ps[:ss, :], ALU.mult, ALU.add, r=[xt, ps], w=[z])
                o_ = os_.next()
                layer_norm(z, ss, gB, bB, o_, st, mv)
                P.dma("pool", X1[l][t0 + s * ss:t0 + (s + 1) * ss, :], o_[:ss, :], r=[o_], w=[X1[l]])
                return o_

            def tr_part(s, o_):
                for kc in range(0, 8, 4):
                    ps = pso.next()
                    for k2 in range(4):
                        B.tr(ps[:, k2 * 128:k2 * 128 + ss], o_[:ss, (kc + k2) * 128:(kc + k2 + 1) * 128], ident[:ss, :ss],
                             r=[o_, C], w=[ps], inc=(k2 == 3))
                    B.cp(B.ev(), x_T[:, kc:kc + 4, s * ss:(s + 1) * ss],
                         ps[:, :].rearrange("p (a b) -> p a b", a=4)[:, :, 0:ss], r=[ps], w=[x_T])

            prev = None
            for s in range(nsub):
                o_ = mm_part(s)
                if prev is not None:
                    tr_part(*prev)
                prev = (s, o_)
            tr_part(*prev)
            P.dma("pool", X1T[l][:, t0:t0 + n].rearrange("(k p) n -> p k n", p=128), x_T[:, :, :n], r=[x_T], w=[X1T[l]])
        P.barrier()
        B.release(m5)

    def phase5b(l):
        m5 = B.mark()
        psu = Ring([B.ps("uP%d" % i) for i in range(4)])
        psd = Ring([B.ps("dP%d" % i) for i in range(4)])
        Wd = B.sb("Wd", [128, 32, 1024], BF16)
        for j in range(32):
            B.load_bf16(Wd[:, j, :], Wd, wdn[l, :, j, :], [128, 1024], q="sp" if j % 2 else "act")
        gB, bB = B.sb("g2", [128, 1024]), B.sb("b2", [128, 1024])
        P.dma("sp", gB[:, :], lnp[l, 2], r=[], w=[gB])
        P.dma("sp", bB[:, :], lnp[l, 3], r=[], w=[bB])
        xTs = Ring([B.sb("xT6_%d" % i, [128, 8, 512], BF16) for i in range(2)])
        slab = Ring([B.sb("sl%d" % i, [128, 1024], BF16) for i in range(6)])
        hT = B.sb("hT", [128, 32, 512], BF16)
        rl_ = Ring([B.sb("rl6_%d" % i, [128, 512]) for i in range(4)])
        x1s = Ring([B.sb("x6_%d" % i, [128, D]) for i in range(2)])
        zs = Ring([B.sb("z6_%d" % i, [128, D]) for i in range(2)])
        os_ = Ring([B.sb("o6_%d" % i, [128, D]) for i in range(2)])
        st, mv = B.sb("st6b", [128, 2, 6]), B.sb("mv6b", [128, 2])
        pre6 = {}

        def prefetch6(ti):
            t0, n = B.tiles[ti]
            x_T = xTs.next()
            P.dma("sp", x_T[:, :, :n], X1T[l][:, t0:t0 + n].rearrange("(k p) n -> p k n", p=128), r=[X1T[l]], w=[x_T])
            pre6[ti] = x_T

        prefetch6(0)
        for ti, (t0, n) in enumerate(B.tiles):
            ss = min(128, n)
            nsub = n // ss
            if ti + 1 < len(B.tiles):
                prefetch6(ti + 1)
            x_T = pre6.pop(ti)
            slq = {}

            def slab_load(j):
                sl = slab.next()
                P.dma("sp", sl[:, :], WUPB[l, j], r=[WUPB], w=[sl])
                slq[j] = sl

            for j in range(3):
                slab_load(j)
            for j in range(32):
                if j + 3 < 32:
                    slab_load(j + 3)
                sl = slq.pop(j)
                ps = psu.next()
                for kc in range(8):
                    B.mm(ps[:, :n], sl[:, kc * 128:(kc + 1) * 128], x_T[:, kc, :n], kc == 0, kc == 7, r=[sl, x_T], w=[ps])
                if j % 2:
                    t_ = rl_.next()
                    B.tsc("dve", t_[:, :n], ps[:, :n], 0.0, None, ALU.max, None, r=[ps], w=[t_])
                    B.tt("dve", hT[:, j, :n], t_[:, :n], t_[:, :n], ALU.mult, r=[t_], w=[hT])
                else:
                    t_ = rl_.next()
                    B.act(t_[:, :n], ps[:, :n], AF.Relu, r=[ps], w=[t_])
                    B.tt("pool", hT[:, j, :n], t_[:, :n], t_[:, :n], ALU.mult, r=[t_], w=[hT])
            for s in range(nsub):
                x1 = x1s.next()
                P.dma("sp", x1[:ss, :], X1[l][t0 + s * ss:t0 + (s + 1) * ss, :], r=[X1[l]], w=[x1])
                z = zs.next()
                for hf in range(2):
                    ps = psd.next()
                    for j in range(32):
                        B.mm(ps[:ss, :], hT[:, j, s * ss:(s + 1) * ss], Wd[:, j, hf * 512:(hf + 1) * 512],
                             j == 0, j == 31, r=[hT, Wd], w=[ps])
                    B.stt("dve", z[:ss, hf * 512:(hf + 1) * 512], x1[:ss, hf * 512:(hf + 1) * 512], ALPHA,
                          ps[:ss, :], ALU.mult, ALU.add, r=[x1, ps], w=[z])
                o_ = os_.next()
                layer_norm(z, ss, gB, bB, o_, st, mv)
                P.dma("pool", XN[l][t0 + s * ss:t0 + (s + 1) * ss, :], o_[:ss, :], r=[o_], w=[XN[l]])
        P.barrier()
        B.release(m5)

    phases = dbg if dbg is not None else ["1", "2", "3", "4", "5a", "5b"]
    for l in range(L):
        for ph, fn in (("1", phase1), ("2", phase2), ("3", phase3), ("4", phase4), ("5a", phase5a), ("5b", phase5b)):
            if ph in phases:
                fn(l)
    P.barrier()
    return nc


def _layout_weights(w, PAST):
    f = lambda a: np.ascontiguousarray(a, dtype=np.float32)
    cm = lambda v: np.ascontiguousarray(v.reshape(L, -1, 128).transpose(0, 2, 1))
    w_in = w["w_in"]
    Wp = np.zeros((L, D, NCOL), np.float32)
    Wp[:, :, 0:384] = w_in[:, :, 0:384]
    kr = w_in[:, :, 384:416]
    Wp[:, :, 384 + 64:480] = kr
    Wp[:, :, 480 + 64:480 + 80] = kr[:, :, 16:32]
    Wp[:, :, 480 + 80:576] = kr[:, :, 0:16]
    Wp[:, :, 576:832] = w_in[:, :, 416:672]
    Wp[:, :, 832:2112] = w_in[:, :, 672:1952]
    o = {}
    o["win"] = f(Wp.reshape(L, 8, 128, NCOL).transpose(0, 2, 1, 3))
    wq = w["w_qb"]
    o["wqb"] = f(wq.reshape(L, 2, 128, 576).transpose(0, 2, 1, 3))
    wqs = np.zeros_like(wq)
    for h in range(NH):
        b = h * 96
        wqs[:, :, b + 64:b + 80] = wq[:, :, b + 80:b + 96]
        wqs[:, :, b + 80:b + 96] = wq[:, :, b + 64:b + 80]
    o["wqbs"] = f(wqs.reshape(L, 2, 128, 576).transpose(0, 2, 1, 3))
    wkv = w["w_kvb"].reshape(L, 128, NH, 128)
    o["wkvk"] = f(wkv[:, :, :, 0:64].reshape(L, 128, 384))
    o["wkvv"] = f(wkv[:, :, :, 64:128].reshape(L, 128, 384))
    o["wout"] = f(w["w_out"].reshape(L, 8, 128, 1024).transpose(0, 2, 1, 3))
    o["wup"] = f(w["w_up"].reshape(L, 8, 128, 32, 128).transpose(0, 3, 2, 1, 4).reshape(L, 32, 128, 1024))
    o["wdn"] = f(w["w_down"].reshape(L, 32, 128, 1024).transpose(0, 2, 1, 3))
    vec = np.zeros((L, 128, 64), np.float32)
    vec[:, :, 0:2] = cm(w["q_norm_g"])
    vec[:, :, 2:3] = cm(w["kv_norm_g"])
    vec[:, :, 3:5] = cm(w["s5_d"])
    vec[:, :, 5:7] = cm(w["b_glu"])
    vec[:, :, 7:17] = cm(w["mu_shift"])
    vec[:, :, 17:20] = cm(w["w0"])
    vec[:, :, 20:23] = cm(w["a0"])
    vec[:, :, 23:26] = cm(w["k_k"])
    vec[:, :, 26:29] = cm(w["k_a"])
    vec[:, :, 29:32] = cm(w["r_k"].reshape(L, 384))
    vec[:, :, 32:35] = cm(w["gn_g"])
    vec[:, :, 35:38] = cm(w["gn_b"])
    o["vec"] = vec
    ln = np.stack([w["ln1_g"], w["ln1_b"], w["ln2_g"], w["ln2_b"]], 1)
    o["lnp"] = f(np.broadcast_to(ln[:, :, None, :], (L, 4, 128, 1024)))
    o["lora"] = f(np.concatenate([w["w_w2"], w["w_a2"], w["w_g2"]], 1))
    o["s5v"] = f(np.concatenate([w["lam_re"], w["lam_im"], np.repeat(w["log_dt"][:, :, None], 64, 2)], 2))
    bt = lambda b: b.reshape(L, 2, 8, 64, 16).transpose(0, 2, 4, 1, 3).reshape(L, 128, 2, 64)
    o["s5b"] = f(np.stack([bt(w["b_re"]), bt(w["b_im"])], 1))
    ct = lambda c: c.transpose(0, 3, 1, 2).reshape(L, 64, 256)
    o["s5c"] = f(np.stack([ct(w["c_re"]), ct(w["c_im"])], 1))
    o["wglu"] = f(w["w_glu"].reshape(L, 2, 128, 256).transpose(0, 2, 1, 3))
    return o


_CACHE = {}


def run_cores(inp, T, PAST, n_cores, dbg=None, extra_out=()):
    cstv, offs = make_consts(PAST)
    key = (T, PAST, tuple(dbg) if dbg else None)
    if key not in _CACHE:
        _CACHE[key] = build_program(T, PAST, cstv.shape[1], offs, dbg)
    nc = _CACHE[key]
    wl = _layout_weights(inp, PAST)
    wl["cst"] = cstv
    in_maps = []
    for c in range(n_cores):
        m = dict(wl)
        sb = slice(NSB * c, NSB * c + NSB)
        m["xin"] = np.ascontiguousarray(np.concatenate(
            [inp["x_prompt"][c], inp["x_sample"][sb].reshape(NSB * TS, D)], 0), dtype=np.float32)
        m["ckvc"] = np.ascontiguousarray(inp["cache_mla_ckv"][:, sb])
        m["krc"] = np.ascontiguousarray(inp["cache_mla_krope"][:, sb])
        s5 = inp["state_s5"][:, sb]
        m["s5st"] = np.ascontiguousarray(s5.transpose(0, 1, 4, 3, 2).reshape(L, NSB, 128, 16))
        rw = inp["state_rwkv"][:, sb].reshape(L, NSB, 3, 2, 64, 64)
        m["rwst"] = np.ascontiguousarray(rw.transpose(0, 1, 3, 5, 2, 4).reshape(L, NSB, 128, 3, 64))
        sh = inp["state_rwkv_shift"][:, sb].reshape(L, NSB, 10, 128)
        m["shst"] = np.ascontiguousarray(sh.transpose(0, 1, 3, 2))
        in_maps.append(m)
    res = run_bass_kernel_spmd(nc, in_maps, core_ids=list(range(n_cores)))
    return res.results


def assemble(rs, T):
    nb = len(rs)
    cat = lambda f: np.stack([f(r) for r in rs], 0)
    y = cat(lambda r: r["y"])
    y_p = y[:, :T]
    y_s = y[:, T:].reshape(nb * NSB, TS, D)

    def tok(name, w):
        a = cat(lambda r: r[name])
        p = a[:, :, :T].transpose(1, 0, 2, 3)
        s = a[:, :, T:].reshape(nb, L, NSB, TS, w).transpose(1, 0, 2, 3, 4).reshape(L, nb * NSB, TS, w)
        return np.ascontiguousarray(p), np.ascontiguousarray(s)

    ckv_p, ckv_s = tok("o_ckv", 128)
    kr_p, kr_s = tok("o_kr", 32)

    def st(name, conv):
        a = cat(lambda r: r[name])
        a = conv(a)
        p = a[:, :, 0].transpose(1, 0, *range(2, a.ndim - 1))
        s = a[:, :, 1:].transpose(1, 0, *range(2, a.ndim))
        s = s.reshape((L, nb * NSB) + s.shape[3:])
        return np.ascontiguousarray(p), np.ascontiguousarray(s)

    s5_p, s5_s = st("o_s5", lambda a: a.reshape(nb, L, 3, 2, 64, 16).transpose(0, 1, 2, 5, 4, 3))
    rw_p, rw_s = st("o_rw", lambda a: a.reshape(nb, L, 3, 2, 64, 3, 64).transpose(0, 1, 2, 5, 3, 6, 4)
                    .reshape(nb, L, 3, 6, 64, 64))
    sh_p, sh_s = st("o_sh", lambda a: a.transpose(0, 1, 2, 4, 3).reshape(nb, L, 3, 1, 1280))
    return (np.ascontiguousarray(y_p), np.ascontiguousarray(y_s), ckv_p, kr_p, s5_p, rw_p, sh_p,
            ckv_s, kr_s, s5_s, rw_s, sh_s)


def kernel(**inputs):
    inp = {k: np.asarray(v) for k, v in inputs.items()}
    T = inp["x_prompt"].shape[1]
    PAST = inp["cache_mla_ckv"].shape[2]
    nb = inp["x_prompt"].shape[0]
    rs = run_cores(inp, T, PAST, nb)
    return assemble(rs, T)
```

```python
import math
import numpy as np
import concourse.bass as bass
import concourse.mybir as mybir
from concourse.bass_utils import run_bass_kernel_spmd

F32 = mybir.dt.float32
BF16 = mybir.dt.bfloat16
AF = mybir.ActivationFunctionType
ALU = mybir.AluOpType
AX = mybir.AxisListType

D = 1024
L = 2
NH = 6
SCALE = 96 ** -0.5
ALPHA = (2 * L) ** 0.25
TS = 32
NSB = 2
NCOL = 2112
GROUPS = [(0, 128), (128, 128), (256, 128), (384, 96), (480, 96), (576, 128), (704, 128)] + \
         [(832 + 128 * i, 128) for i in range(10)]
TWO_PI = 2.0 * math.pi


class Trk:
    __slots__ = ("w", "r")

    def __init__(self):
        self.w = None
        self.r = {}


class Buf:
    def __init__(self, ap, trk=None):
        self.ap = ap
        self.k = trk if trk is not None else Trk()

    def __getitem__(self, key):
        return self.ap[key]


class Ring:
    def __init__(self, bufs):
        self.bufs = bufs
        self.i = 0

    def next(self):
        b = self.bufs[self.i]
        self.i = (self.i + 1) % len(self.bufs)
        return b


class Prog:
    def __init__(self, nc, ndma=8):
        self.nc = nc
        self.es = {}
        self.sems = {}
        self._ctx = []
        for nm, eng in (("pe", nc.tensor), ("act", nc.scalar), ("dve", nc.vector), ("pool", nc.gpsimd),
                        ("sp", nc.sync)):
            cm = nc.semaphore("s_" + nm)
            self.sems[nm] = cm.__enter__()
            self._ctx.append(cm)
            self.es[nm] = dict(eng=eng, cnt=0, known={})
        self.snap = {}
        self.rings = {}
        for q in ("sp", "pool", "act"):
            ring = []
            for i in range(ndma):
                key = "d_%s%d" % (q, i)
                cm = nc.semaphore(key)
                self.sems[key] = cm.__enter__()
                self._ctx.append(cm)
                ring.append([key, 0])
            self.rings[q] = dict(ring=ring, nxt=0)

    def close(self):
        for cm in reversed(self._ctx):
            cm.__exit__(None, None, None)

    def _needs(self, r, w):
        needs = {}
        for b in r:
            t = b.k
            if t.w is not None:
                k, v = t.w
                if needs.get(k, 0) < v:
                    needs[k] = v
        for b in w:
            t = b.k
            if t.w is not None:
                k, v = t.w
                if needs.get(k, 0) < v:
                    needs[k] = v
            for k, v in t.r.items():
                if needs.get(k, 0) < v:
                    needs[k] = v
        return needs

    def _waits(self, en, needs):
        E = self.es[en]
        out = []
        for k, v in needs.items():
            if k == en and v > E["cnt"]:
                continue
            if E["known"].get(k, 0) < v:
                E["known"][k] = v
                out.append((k, v))
        for k, v in out:
            sn = self.snap.get(k, {}).get(v)
            if sn:
                for k2, v2 in sn.items():
                    if k2 != en and E["known"].get(k2, 0) < v2:
                        E["known"][k2] = v2
        return out

    def _mark(self, tok, r, w):
        for b in w:
            b.k.w = tok
            b.k.r = {}
        for b in r:
            if b.k.r.get(tok[0], 0) < tok[1]:
                b.k.r[tok[0]] = tok[1]

    def op(self, en, fn, r=(), w=(), inc=True):
        E = self.es[en]
        waits = self._waits(en, self._needs(r, w))
        for k, v in waits[:-1]:
            E["eng"].wait_ge(self.sems[k], v)
        ins = fn()
        if waits:
            k, v = waits[-1]
            ins._wait_ge(self.sems[k], v)
        if inc:
            E["cnt"] += 1
            ins.then_inc(self.sems[en], 1)
            tok = (en, E["cnt"])
            self.snap.setdefault(en, {})[E["cnt"]] = dict(E["known"])
        else:
            tok = (en, E["cnt"] + 1)
        self._mark(tok, r, w)
        return ins

    def dma(self, q, out, in_, r=(), w=(), **kw):
        E = self.es[q]
        R = self.rings[q]
        slot = R["ring"][R["nxt"]]
        R["nxt"] = (R["nxt"] + 1) % len(R["ring"])
        needs = self._needs(r, w)
        if slot[1] > 0:
            needs[slot[0]] = max(needs.get(slot[0], 0), slot[1])
        for k, v in self._waits(q, needs):
            E["eng"].wait_ge(self.sems[k], v)
        slot[1] += 16
        ins = E["eng"].dma_start(out=out, in_=in_, **kw)
        ins.then_inc(self.sems[slot[0]], 16)
        self.snap.setdefault(slot[0], {})[slot[1]] = dict(E["known"])
        self._mark((slot[0], slot[1]), r, w)
        return ins

    def barrier(self):
        targets = {}
        for en, E in self.es.items():
            if E["cnt"] > 0:
                targets[en] = E["cnt"]
        for q, R in self.rings.items():
            for k, v in R["ring"]:
                if v > 0:
                    targets[k] = v
        for en, E in self.es.items():
            for k, v in targets.items():
                if k != en and E["known"].get(k, 0) < v:
                    E["known"][k] = v
                    E["eng"].wait_ge(self.sems[k], v)


class Builder:
    def __init__(self, T, PAST):
        self.T = T
        self.PAST = PAST
        self.TT = T + NSB * TS
        self.KS = PAST + TS
        self.KTOT = T + NSB * self.KS
        self.seqs = [(0, T, 0, T, 0)]
        for s in range(NSB):
            self.seqs.append((T + s * TS, TS, T + s * self.KS, self.KS, PAST))
        self.tiles = [(i * 512, 512) for i in range(T // 512)] + [(T, NSB * TS)]
        nc = bass.Bass("TRN2", target_bir_lowering=False)
        self.nc = nc
        self.P = Prog(nc)
        self._cms = []
        self.evi = 0

    def sb(self, name, shape, dt=F32):
        self.uid = getattr(self, "uid", 0) + 1
        name = "%s_u%d" % (name, self.uid)
        cm = self.nc.sbuf_tensor(name, list(shape), dt)
        t = cm.__enter__()
        self._cms.append(cm)
        return Buf(t)

    def ps(self, name, shape=(128, 512), dt=F32):
        self.uid = getattr(self, "uid", 0) + 1
        name = "%s_u%d" % (name, self.uid)
        cm = self.nc.psum_tensor(name, list(shape), dt)
        t = cm.__enter__()
        self._cms.append(cm)
        return Buf(t)

    def mark(self):
        return len(self._cms)

    def release(self, m):
        while len(self._cms) > m:
            self._cms.pop().__exit__(None, None, None)

    def dram(self, name, shape, dt=F32, kind="Internal"):
        return Buf(self.nc.dram_tensor(name, list(shape), dt, kind=kind).ap())

    def mm(self, out, lhsT, rhs, start, stop, r, w, inc=None):
        nc = self.nc
        if inc is None:
            inc = stop
        return self.P.op("pe", lambda: nc.tensor.matmul(out, lhsT, rhs, start=start, stop=stop), r, w, inc)

    def tr(self, out, in_, ident, r, w, inc=True):
        nc = self.nc
        return self.P.op("pe", lambda: nc.tensor.transpose(out, in_, ident), r, w, inc)

    def act(self, out, in_, func, r, w, bias=None, scale=None, accum_out=None):
        nc = self.nc
        kw = {}
        if bias is not None:
            kw["bias"] = bias
        if scale is not None:
            kw["scale"] = scale
        if accum_out is not None:
            kw["accum_out"] = accum_out
        return self.P.op("act", lambda: nc.scalar.activation(out=out, in_=in_, func=func, **kw), r, w)

    def _ve(self, en):
        return self.nc.vector if en == "dve" else self.nc.gpsimd

    def cp(self, en, out, in_, r, w):
        if en == "act":
            return self.act(out, in_, AF.Copy, r, w)
        e = self._ve(en)
        return self.P.op(en, lambda: e.tensor_copy(out, in_), r, w)

    def tt(self, en, out, a, b, op, r, w):
        e = self._ve(en)
        return self.P.op(en, lambda: e.tensor_tensor(out, a, b, op), r, w)

    def tsc(self, en, out, a, s1, s2, op0, op1, r, w):
        e = self._ve(en)
        if op1 is None:
            return self.P.op(en, lambda: e.tensor_scalar(out, a, s1, None, op0), r, w)
        return self.P.op(en, lambda: e.tensor_scalar(out, a, s1, s2, op0, op1), r, w)

    def stt(self, en, out, a, s, b, op0, op1, r, w):
        en = "dve"
        e = self._ve(en)
        return self.P.op(en, lambda: e.scalar_tensor_tensor(out, a, s, b, op0, op1), r, w)

    def mset(self, en, out, val, w):
        e = self._ve(en)
        return self.P.op(en, lambda: e.memset(out, val), (), w)

    def ev(self):
        self.evi ^= 1
        return "act" if self.evi else "dve"

    def dma(self, q, out, in_, r, w, **kw):
        return self.P.dma(q, out, in_, r, w, **kw)

    def load_bf16(self, dst_ap, dst_buf, src_ap, shape, q="sp"):
        p, f = shape
        st = self.stage.next()
        self.dma(q, st[:p, :f], src_ap, r=[], w=[st])
        self.cvi = getattr(self, "cvi", 0) ^ 1
        self.cp("dve" if self.cvi else "act", dst_ap, st[:p, :f], r=[st], w=[dst_buf])


def make_consts(PAST):
    c = {}
    i = np.arange(128)
    c["ident"] = np.eye(128, dtype=np.float32)
    c["ones"] = np.ones((128, 128), np.float32)
    c["bones"] = (i[:, None] // 64 == i[None, :] // 64).astype(np.float32)
    c["maskS"] = (i[None, :] > i[:, None]).astype(np.float32)
    c["maskI"] = (i[None, :] >= i[:, None]).astype(np.float32)
    c["maskL"] = -(i[None, :] < i[:, None]).astype(np.float32)
    c["m4"] = np.concatenate([-c["maskS"], c["maskS"], c["maskI"], c["maskI"]], 1)
    sel = np.zeros((128, 64), np.float32)
    sel[64 + np.arange(64), np.arange(64)] = 1.0
    c["sel"] = sel
    c["swap"] = (i[None, :] == (i[:, None] + 64) % 128).astype(np.float32)
    c["iota"] = np.tile(np.arange(512, dtype=np.float32)[None, :], (128, 1))
    c["spos"] = np.tile((PAST + np.arange(NSB * TS) % TS).astype(np.float32)[None, :], (128, 1))
    c["gm"] = (i[:, None] // 16 == np.arange(8)[None, :]).astype(np.float32)
    E = np.zeros((128, 2, 128), np.float32)
    for g in range(16):
        E[g, g // 8, (g % 8) * 16:(g % 8) * 16 + 16] = 1.0
    c["E"] = E.reshape(128, 256)
    invf = np.zeros((128, 1), np.float32)
    f = (10000.0 ** (-np.arange(0, 32, 2, dtype=np.float32) / 32)).astype(np.float32)
    invf[64:96, 0] = np.concatenate([f, f])
    c["invf"] = invf
    sgn = np.zeros((128, 1), np.float32)
    sgn[64:80] = -1.0
    sgn[80:96] = 1.0
    c["sgn"] = sgn
    sg2 = np.ones((128, 2), np.float32)
    sg2[64:, 0] = -1.0
    sg2[:, 1] = -sg2[:, 0]
    c["sg2"] = sg2
    offs = {}
    o = 0
    parts = []
    for k, v in c.items():
        offs[k] = (o, v.shape[1])
        o += v.shape[1]
        parts.append(v)
    return np.ascontiguousarray(np.concatenate(parts, 1)), offs


def build_program(T, PAST, cst_w, offs, dbg=None):
    B = Builder(T, PAST)
    nc, P = B.nc, B.P
    TT, KTOT, KS = B.TT, B.KTOT, B.KS
    _ncd = nc.allow_non_contiguous_dma("small strided state / layout transfers")
    _ncd.__enter__()
    ext = lambda name, shape: Buf(nc.dram_tensor(name, list(shape), F32, kind="ExternalInput").ap())
    out_ = lambda name, shape: Buf(nc.dram_tensor(name, list(shape), F32, kind="ExternalOutput").ap())
    xin = ext("xin", [TT, D])
    ckvc = ext("ckvc", [L, NSB, PAST, 128])
    krc = ext("krc", [L, NSB, PAST, 32])
    s5st = ext("s5st", [L, NSB, 128, 16])
    rwst = ext("rwst", [L, NSB, 128, 3, 64])
    shst = ext("shst", [L, NSB, 128, 10])
    cst = ext("cst", [128, cst_w])
    win = ext("win", [L, 128, 8, NCOL])
    wqb = ext("wqb", [L, 128, 2, 576])
    wqbs = ext("wqbs", [L, 128, 2, 576])
    wkvk = ext("wkvk", [L, 128, 384])
    wkvv = ext("wkvv", [L, 128, 384])
    wout = ext("wout", [L, 128, 8, 1024])
    wup = ext("wup", [L, 32, 128, 1024])
    wdn = ext("wdn", [L, 128, 32, 1024])
    vec = ext("vec", [L, 128, 64])
    lnp = ext("lnp", [L, 4, 128, 1024])
    lora = ext("lora", [L, 128, 384])
    s5v = ext("s5v", [L, 16, 192])
    s5b = ext("s5b", [L, 2, 128, 2, 64])
    s5c = ext("s5c", [L, 2, 64, 256])
    wglu = ext("wglu", [L, 128, 2, 256])
    y = out_("y", [TT, D])
    o_ckv = out_("o_ckv", [L, TT, 128])
    o_kr = out_("o_kr", [L, TT, 32])
    o_s5 = out_("o_s5", [L, 3, 128, 16])
    o_rw = out_("o_rw", [L, 3, 128, 3, 64])
    o_sh = out_("o_sh", [L, 3, 128, 10])
    ROPE = B.dram("ROPE", [4, 128, TT])
    QT = [B.dram("QT%d" % l, [97, NH, TT], BF16) for l in range(L)]
    KT = [B.dram("KT%d" % l, [96, NH, KTOT], BF16) for l in range(L)]
    NKT = (KTOT + 127) // 128 + 4
    VA = [B.dram("VA%d" % l, [NKT, 128, 384], BF16) for l in range(L)]
    UT = [B.dram("UT%d" % l, [256, TT]) for l in range(L)]
    PT = [B.dram("PT%d" % l, [1280, TT]) for l in range(L)]
    MT = [B.dram("MT%d" % l, [1024, TT], BF16) for l in range(L)]
    X1 = [B.dram("X1_%d" % l, [TT, D]) for l in range(L)]
    X1T = [B.dram("X1T%d" % l, [1024, TT], BF16) for l in range(L)]
    XN = [B.dram("XN%d" % l, [TT, D]) for l in range(L - 1)] + [y]
    WUPB = B.dram("WUPB", [L, 32, 128, 1024], BF16)
    KMX = B.dram("KMX", [L, 128, NH])

    ktiles = []
    vt = 0
    for (r0, n, k0, nk, p0) in B.seqs:
        lst = []
        c = 0
        while c < nk:
            m = min(128, nk - c)
            lst.append((k0 + c, m, vt))
            vt += 1
            c += m
        ktiles.append(lst)

    C = B.sb("cst", [128, cst_w])
    P.dma("sp", C[:, :], cst[:, :], r=[], w=[C])
    cs = lambda k: C[:, offs[k][0]:offs[k][0] + offs[k][1]]
    Cb = B.sb("cstb", [128, 640], BF16)
    B.cp("dve", Cb[:, 0:128], cs("ident"), r=[C], w=[Cb])
    B.cp("dve", Cb[:, 128:256], cs("ones"), r=[C], w=[Cb])
    B.cp("dve", Cb[:, 256:384], cs("bones"), r=[C], w=[Cb])
    identb, onesb, bonesb = Cb[:, 0:128], Cb[:, 128:256], Cb[:, 256:384]
    ident = cs("ident")
    B.stage = Ring([B.sb("stage%d" % i, [128, 2112]) for i in range(2)])

    EPS = {256 * 1e-6: 0, 128 * 1e-6: 1, 1e-24: 2, 64e-5: 3, 1e-5: 4, 0.0: 5}
    epsb = B.sb("epsb", [128, 8])
    for v_, i_ in EPS.items():
        B.mset("dve", epsb[:, i_:i_ + 1], float(v_), w=[epsb])
    I32 = mybir.dt.int32

    def sqrt_pow(en, out, in_, add, expo, r, w, p0=0):
        i_ = EPS[add]
        B.act(out, in_, AF.Sqrt, r=list(r) + [epsb], w=w, bias=epsb[p0:p0 + out.shape[0], i_:i_ + 1])
        if expo < 0:
            P.op("dve", lambda: nc.vector.reciprocal(out, out), r=w, w=w)

    def _p0(ap):
        return ap.base_partition()

    def sincos(en, out_s, out_c, ang, wk, r, w):
        it_ap, ft_ap, wb = wk
        for o, sh in ((out_s, 0.0), (out_c, 0.25)):
            B.tsc(en, o, ang, 1.0 / TWO_PI, sh, ALU.mult, ALU.add, r=r, w=w)
            B.cp(en, it_ap, o, r=w, w=wb)
            B.cp(en, ft_ap, it_ap, r=wb, w=wb)
            B.tt(en, o, o, ft_ap, ALU.subtract, r=w + wb, w=w)
            B.act(o, o, AF.Sin, r=w, w=w, scale=TWO_PI * (1.0 - 1e-6))

    m0 = B.mark()
    wk = Ring([B.sb("r0_%d" % i, [128, 512]) for i in range(6)])
    r0i = B.sb("r0i", [128, 512], mybir.dt.int32)
    r0f = B.sb("r0f", [128, 512])
    r0w = Buf(None)
    for (t0, n) in B.tiles:
        pos = wk.next()
        if n == 512:
            B.tsc("pool", pos[64:96, :n], cs("iota")[64:96, :n], float(t0), None, ALU.add, None, r=[C], w=[pos])
        else:
            B.cp("pool", pos[64:96, :n], cs("spos")[64:96, :n], r=[C], w=[pos])
        B.tsc("dve", pos[64:96, :n], pos[64:96, :n], cs("invf")[64:96, :], None, ALU.mult, None, r=[pos, C], w=[pos])
        res = {"sin": wk.next(), "cos": wk.next()}
        sincos("dve", res["sin"][64:96, :n], res["cos"][64:96, :n], pos[64:96, :n],
               (r0i[64:96, :n], r0f[64:96, :n], [r0w]), r=[pos], w=[res["sin"], res["cos"]])
        B.tsc("dve", res["sin"][64:96, :n], res["sin"][64:96, :n], cs("sgn")[64:96, :], None, ALU.mult, None,
              r=[res["sin"], C], w=[res["sin"]])
        for i, nm in enumerate(("cos", "sin")):
            a = res[nm]
            P.dma("sp", ROPE[2 + i, 64:96, t0:t0 + n], a[64:96, :n], r=[a], w=[ROPE])
            q = wk.next()
            B.tsc("pool", q[64:96, :n], a[64:96, :n], SCALE, None, ALU.mult, None, r=[a], w=[q])
            P.dma("sp", ROPE[i, 64:96, t0:t0 + n], q[64:96, :n], r=[q], w=[ROPE])
    P.barrier()
    B.release(m0)

    def phase1(l):
        m1 = B.mark()
        psum = Ring([B.ps("ps%d" % i) for i in range(8)])
        Wi = B.sb("Wi", [128, 8, NCOL], BF16)
        for kc in range(8):
            B.load_bf16(Wi[:, kc, :], Wi, win[l, :, kc, :], [128, NCOL], q="sp" if kc % 2 else "act")
        Wq = B.sb("Wq", [128, 2, 576], BF16)
        Wqs = B.sb("Wqs", [128, 2, 576], BF16)
        B.load_bf16(Wq[:, :, :].rearrange("p a b -> p (a b)"), Wq, wqb[l].rearrange("p a b -> p (a b)"), [128, 1152])
        B.load_bf16(Wqs[:, :, :].rearrange("p a b -> p (a b)"), Wqs, wqbs[l].rearrange("p a b -> p (a b)"), [128, 1152])
        Wk = B.sb("Wk", [128, 384], BF16)
        Wv = B.sb("Wv", [128, 384], BF16)
        B.load_bf16(Wk[:, :], Wk, wkvk[l], [128, 384])
        B.load_bf16(Wv[:, :], Wv, wkvv[l], [128, 384])
        V = B.sb("vec", [128, 64])
        P.dma("sp", V[:, :], vec[l], r=[], w=[V])
        g16 = B.sb("g16", [128, 4])
        B.tsc("dve", g16[:, 0:2], V[:, 0:2], 16.0, None, ALU.mult, None, r=[V], w=[g16])
        B.tsc("dve", g16[:, 2:3], V[:, 2:3], math.sqrt(128.0), None, ALU.mult, None, r=[V], w=[g16])
        kmx = B.sb("kmx", [128, NH, 512])
        B.mset("pool", kmx[:, :, :], 0.0, w=[kmx])
        xts = Ring([B.sb("xt%d" % i, [128, 4, D]) for i in range(2)])
        xTs = Ring([B.sb("xT%d" % i, [128, 8, 512], BF16) for i in range(2)])
        ql = B.sb("ql", [128, 2, 512])
        sq = Ring([B.sb("sq%d" % i, [128, 512], BF16) for i in range(3)])
        rin = Ring([B.sb("rin%d" % i, [128, 512]) for i in range(2)])
        qn = B.sb("qn", [128, 2, 512], BF16)
        qT = Ring([B.sb("qT%d" % i, [128, NH, 512], BF16) for i in range(1)])
        kT = Ring([B.sb("kT%d" % i, [128, NH, 512], BF16) for i in range(1)])
        va = Ring([B.sb("va%d" % i, [128, 4, 384], BF16) for i in range(2)])
        rt = Ring([B.sb("rt%d" % i, [128, 4, 512]) for i in range(1)])
        tmp = Ring([B.sb("tmp%d" % i, [128, 512]) for i in range(4)])
        kvl = B.sb("kvl", [128, 512])
        ckvT = B.sb("ckvT", [128, 512])
        ckvTb = Ring([B.sb("ckvTb%d" % i, [128, 512], BF16) for i in range(2)])
        krT = B.sb("krT", [128, 512])
        ot = Ring([B.sb("ot%d" % i, [128, 4, 128]) for i in range(2)])
        ot2 = Ring([B.sb("ot2%d" % i, [128, 4, 32]) for i in range(2)])
        ut = Ring([B.sb("ut%d" % i, [128, 2, 512]) for i in range(1)])
        pt = Ring([B.sb("pt%d" % i, [128, 5, 512]) for i in range(1)])
        Xsrc = xin if l == 0 else XN[l - 1]

        def kv_expand(cb, kr, n, seq_parts):
            k_ = kT.next()
            v_ = va.next()
            for h in range(NH):
                ps = psum.next()
                B.mm(ps[0:64, :n], Wk[:, h * 64:(h + 1) * 64], cb[:, :n], True, True, r=[Wk, cb], w=[ps])
                B.cp(B.ev(), k_[0:64, h, :n], ps[0:64, :n], r=[ps], w=[k_])
                B.cp("pool", k_[64:96, h, :n], kr[64:96, :n], r=[kr, k_], w=[k_])
            for h in range(NH):
                s_ = sq.next()
                B.tt("pool", s_[0:96, :n], k_[0:96, h, :n], k_[0:96, h, :n], ALU.mult, r=[k_], w=[s_])
                ps = psum.next()
                B.mm(ps[0:97, :n], onesb[0:96, 0:97], s_[0:96, :n], True, True, r=[Cb, s_], w=[ps])
                B.tt("dve", kmx[0:97, h, :n], kmx[0:97, h, :n], ps[0:97, :n], ALU.max, r=[ps, kmx], w=[kmx])
            ss = min(128, n)
            for s in range((n + 127) // 128):
                ps = psum.next()
                B.mm(ps[:ss, 0:384], cb[:, s * ss:(s + 1) * ss], Wv[:, :], True, True, r=[cb, Wv], w=[ps])
                B.cp(B.ev(), v_[:ss, s, :], ps[:ss, 0:384], r=[ps], w=[v_])
            for (c0, ncol, kcol, vparts) in seq_parts:
                P.dma("pool", KT[l][:, :, kcol:kcol + ncol], k_[0:96, :, c0:c0 + ncol], r=[k_], w=[KT[l]])
                for (vti, s, row0, rows) in vparts:
                    P.dma("pool", VA[l][vti, 0:rows, :], v_[row0:row0 + rows, s, :], r=[v_], w=[VA[l]])

        pre1 = {}

        def prefetch1(ti):
            t0, n = B.tiles[ti]
            ss = min(128, n)
            nsub = n // ss
            xt = xts.next()
            P.dma("sp", xt[:ss, :nsub, :], Xsrc[t0:t0 + n, :].rearrange("(s p) d -> p s d", p=ss), r=[Xsrc], w=[xt])
            pre1[ti] = xt

        prefetch1(0)
        for ti, (t0, n) in enumerate(B.tiles):
            ss = min(128, n)
            nsub = n // ss
            if ti + 1 < len(B.tiles):
                prefetch1(ti + 1)
            xt = pre1.pop(ti)
            r_ = rt.next()
            P.dma("sp", r_[64:96, :, :n], ROPE[:, 64:96, t0:t0 + n].rearrange("a p n -> p a n"), r=[ROPE], w=[r_])
            xT = xTs.next()
            for kc in range(8):
                ps = psum.next()
                for s in range(nsub):
                    B.tr(ps[:, s * ss:(s + 1) * ss], xt[:ss, s, kc * 128:(kc + 1) * 128], ident[:ss, :ss],
                         r=[xt, C], w=[ps], inc=(s == nsub - 1))
                B.cp(B.ev(), xT[:, kc, :n], ps[:, :n], r=[ps], w=[xT])
            grp = []
            for gi, (c0, M) in enumerate(GROUPS):
                ps = psum.next()
                for kc in range(8):
                    B.mm(ps[:M, :n], Wi[:, kc, c0:c0 + M], xT[:, kc, :n], kc == 0, kc == 7, r=[Wi, xT], w=[ps])
                if gi < 2:
                    B.cp("act", ql[:, gi, :n], ps[:, :n], r=[ps], w=[ql])
                    if gi == 1:
                        ps2 = psum.next()
                        for j in range(2):
                            s_ = sq.next()
                            B.tt("pool", s_[:, :n], ql[:, j, :n], ql[:, j, :n], ALU.mult, r=[ql], w=[s_])
                            B.mm(ps2[:, :n], onesb, s_[:, :n], j == 0, j == 1, r=[Cb, s_], w=[ps2])
                        ri = rin.next()
                        sqrt_pow("dve", ri[:, :n], ps2[:, :n], 256 * 1e-6, -0.5, r=[ps2], w=[ri])
                        for j in range(2):
                            B.stt("dve", qn[:, j, :n], ql[:, j, :n], g16[:, j:j + 1], ri[:, :n], ALU.mult, ALU.mult,
                                  r=[ql, g16, ri], w=[qn])
                        q_ = qT.next()
                        for h in range(NH):
                            pa, pb = psum.next(), psum.next()
                            for j in range(2):
                                B.mm(pa[0:96, :n], Wq[:, j, h * 96:(h + 1) * 96], qn[:, j, :n], j == 0, j == 1,
                                     r=[Wq, qn], w=[pa])
                            for j in range(2):
                                B.mm(pb[0:96, :n], Wqs[:, j, h * 96:(h + 1) * 96], qn[:, j, :n], j == 0, j == 1,
                                     r=[Wqs, qn], w=[pb])
                            B.act(q_[0:64, h, :n], pa[0:64, :n], AF.Copy, r=[pa], w=[q_], scale=SCALE)
                            t1, t2 = tmp.next(), tmp.next()
                            B.tt("dve", t1[64:96, :n], pa[64:96, :n], r_[64:96, 0, :n], ALU.mult, r=[pa, r_], w=[t1])
                            B.tt("dve", t2[64:96, :n], pb[64:96, :n], r_[64:96, 1, :n], ALU.mult, r=[pb, r_], w=[t2])
                            B.tt("pool", q_[64:96, h, :n], t1[64:96, :n], t2[64:96, :n], ALU.add, r=[t1, t2, q_], w=[q_])
                            s_ = sq.next()
                            B.tt("pool", s_[0:96, :n], q_[0:96, h, :n], q_[0:96, h, :n], ALU.mult, r=[q_], w=[s_])
                            ps3 = psum.next()
                            B.mm(ps3[0:97, :n], onesb[0:96, 0:97], s_[0:96, :n], True, True, r=[Cb, s_], w=[ps3])
                            t3 = tmp.next()
                            sqrt_pow("dve", t3[96:97, :n], ps3[96:97, :n], 0.0, 0.5, r=[ps3], w=[t3], p0=96)
                            B.tsc("dve", q_[96:97, h, :n], t3[96:97, :n], -1.0, None, ALU.mult, None, r=[t3, q_], w=[q_])
                        P.dma("pool", QT[l][:, :, t0:t0 + n], q_[0:97, :, :n], r=[q_], w=[QT[l]])
                elif gi == 2:
                    B.cp("act", kvl[:, :n], ps[:, :n], r=[ps], w=[kvl])
                    s_ = sq.next()
                    B.tt("pool", s_[:, :n], kvl[:, :n], kvl[:, :n], ALU.mult, r=[kvl], w=[s_])
                    ps2 = psum.next()
                    B.mm(ps2[:, :n], onesb, s_[:, :n], True, True, r=[Cb, s_], w=[ps2])
                    ri = rin.next()
                    sqrt_pow("dve", ri[:, :n], ps2[:, :n], 128 * 1e-6, -0.5, r=[ps2], w=[ri])
                    B.stt("dve", ckvT[:, :n], kvl[:, :n], g16[:, 2:3], ri[:, :n], ALU.mult, ALU.mult,
                          r=[kvl, g16, ri], w=[ckvT])
                    cb = ckvTb.next()
                    B.cp("pool", cb[:, :n], ckvT[:, :n], r=[ckvT], w=[cb])
                    ps2 = psum.next()
                    for s in range(nsub):
                        B.tr(ps2[:ss, s * 128:(s + 1) * 128], ckvT[:, s * ss:(s + 1) * ss], ident, r=[ckvT, C],
                             w=[ps2], inc=(s == nsub - 1))
                    o_ = ot.next()
                    B.cp("act", o_[:ss, :nsub, :], ps2[:ss, 0:nsub * 128].rearrange("p (s d) -> p s d", d=128),
                         r=[ps2], w=[o_])
                    P.dma("pool", o_ckv[l, t0:t0 + n, :].rearrange("(s p) d -> p s d", p=ss), o_[:ss, :nsub, :],
                          r=[o_], w=[o_ckv])
                elif gi == 3:
                    pkr = ps
                elif gi == 4:
                    t1, t2 = tmp.next(), tmp.next()
                    B.tt("dve", t1[64:96, :n], pkr[64:96, :n], r_[64:96, 2, :n], ALU.mult, r=[pkr, r_], w=[t1])
                    B.tt("dve", t2[64:96, :n], ps[64:96, :n], r_[64:96, 3, :n], ALU.mult, r=[ps, r_], w=[t2])
                    B.tt("pool", krT[64:96, :n], t1[64:96, :n], t2[64:96, :n], ALU.add, r=[t1, t2], w=[krT])
                    ps2 = psum.next()
                    for s in range(nsub):
                        B.tr(ps2[:ss, s * 32:(s + 1) * 32], krT[64:96, s * ss:(s + 1) * ss], ident[64:96, 64:96],
                             r=[krT, C], w=[ps2], inc=(s == nsub - 1))
                    o_ = ot2.next()
                    B.cp("act", o_[:ss, :nsub, :], ps2[:ss, 0:nsub * 32].rearrange("p (s d) -> p s d", d=32),
                         r=[ps2], w=[o_])
                    P.dma("pool", o_kr[l, t0:t0 + n, :].rearrange("(s p) d -> p s d", p=ss), o_[:ss, :nsub, :],
                          r=[o_], w=[o_kr])
                    if n == 512:
                        vparts = [(ktiles[0][t0 // 128 + s][2], s, 0, 128) for s in range(4)]
                        parts = [(0, 512, t0, vparts)]
                    else:
                        parts = []
                        for si in range(NSB):
                            kc_, m_, vti = ktiles[1 + si][-1]
                            parts.append((si * TS, TS, kc_, [(vti, 0, si * TS, TS)]))
                    kv_expand(cb, krT, n, parts)
                elif gi < 7:
                    u_ = ut.next() if gi == 5 else u_
                    B.cp(B.ev(), u_[:, gi - 5, :n], ps[:, :n], r=[ps], w=[u_])
                    if gi == 6:
                        P.dma("pool", UT[l][:, t0:t0 + n].rearrange("(j p) n -> p j n", p=128), u_[:, :, :n],
                              r=[u_], w=[UT[l]])
                else:
                    hf_ = (gi - 7) // 5
                    p_ = pt.next() if (gi - 7) % 5 == 0 else p_
                    B.cp(B.ev(), p_[:, (gi - 7) % 5, :n], ps[:, :n], r=[ps], w=[p_])
                    if (gi - 7) % 5 == 4:
                        P.dma("pool", PT[l][hf_ * 640:(hf_ + 1) * 640, t0:t0 + n].rearrange("(j p) n -> p j n", p=128),
                              p_[:, :, :n], r=[p_], w=[PT[l]])
                        for si, (r0, nn, k0, nk, p0) in enumerate(B.seqs):
                            last = r0 + nn - 1
                            if t0 <= last < t0 + n:
                                P.dma("pool", o_sh[l, si, :, hf_ * 5:(hf_ + 1) * 5], p_[:, :, last - t0], r=[p_], w=[o_sh])
        cin = Ring([B.sb("cin%d" % i, [128, 4, 128]) for i in range(2)])
        kin = Ring([B.sb("kin%d" % i, [128, 4, 96]) for i in range(2)])
        for b in kin.bufs:
            B.mset("pool", b[:, :, :], 0.0, w=[b])
        for si in range(NSB):
            for c0 in range(0, PAST, 512):
                n = min(512, PAST - c0)
                nsub = n // 128
                ci, ki = cin.next(), kin.next()
                P.dma("sp", ci[:, :nsub, :], ckvc[l, si, c0:c0 + n, :].rearrange("(s p) d -> p s d", p=128), r=[], w=[ci])
                P.dma("act", ki[:, :nsub, 64:96], krc[l, si, c0:c0 + n, :].rearrange("(s p) d -> p s d", p=128), r=[], w=[ki])
                ps = psum.next()
                for s in range(nsub):
                    B.tr(ps[:, s * 128:(s + 1) * 128], ci[:, s, :], ident, r=[ci, C], w=[ps], inc=(s == nsub - 1))
                cb = ckvTb.next()
                B.cp(B.ev(), cb[:, :n], ps[:, :n], r=[ps], w=[cb])
                ps = psum.next()
                for s in range(nsub):
                    B.tr(ps[0:96, s * 128:(s + 1) * 128], ki[:, s, :], ident, r=[ki, C], w=[ps], inc=(s == nsub - 1))
                B.cp(B.ev(), krT[64:96, :n], ps[64:96, :n], r=[ps], w=[krT])
                kbase = B.seqs[1 + si][2]
                vparts = [(ktiles[1 + si][c0 // 128 + s][2], s, 0, 128) for s in range(nsub)]
                kv_expand(cb, krT, n, [(0, n, kbase + c0, vparts)])
        km = B.sb("km", [128, NH])
        P.op("dve", lambda: nc.vector.tensor_reduce(km[0:97, :], kmx[0:97, :, :], AX.X, ALU.max), r=[kmx], w=[km])
        sqrt_pow("dve", km[0:97, :], km[0:97, :], 0.0, 0.5, r=[km], w=[km])
        P.dma("sp", KMX[l, 0:97, :], km[0:97, :], r=[km], w=[KMX])
        P.barrier()
        B.release(m1)

    def phase2(l):
        m2 = B.mark()
        psS = Ring([B.ps("aS%d" % i, (128, 1024)) for i in range(3)])
        psO = Ring([B.ps("aO%d" % i) for i in range(1)])
        psL = Ring([B.ps("aL%d" % i) for i in range(1)])
        kmr = B.sb("kmr", [128, NH])
        P.dma("sp", kmr[0:97, :], KMX[l, 0:97, :], r=[KMX], w=[kmr])
        maxk = max(T, KS)
        maxt = (maxk + 127) // 128
        Kb = Ring([B.sb("Kb%d" % i, [128, maxk], BF16) for i in range(2)])
        for b in Kb.bufs:
            B.mset("pool", b[96:97, :], 1.0, w=[b])
        Vb = Ring([B.sb("Vb%d" % i, [128, maxt, 64], BF16) for i in range(2)])
        Qb = Ring([B.sb("Qb%d" % i, [128, T], BF16) for i in range(2)])
        ptr = Ring([B.sb("pt%d" % i, [128, 1024], BF16) for i in range(4)])
        rl = Ring([B.sb("rl%d" % i, [64, 512]) for i in range(2)])
        o32 = Ring([B.sb("o32_%d" % i, [64, 512]) for i in range(2)])
        mo = Ring([B.sb("mo%d" % i, [64, 512], BF16) for i in range(2)])
        its = [(h, si) for h in range(NH) for si in range(len(B.seqs))]
        NWARM = 12
        LOOK = 2
        loaded = {}

        def load(i):
            h, si = its[i]
            r0, nt, k0, nk, p0 = B.seqs[si]
            kb, vb, qb = Kb.next(), Vb.next(), Qb.next()
            kts = ktiles[si]
            P.dma("pool", kb[0:96, :nk], KT[l][:, h, k0:k0 + nk], r=[KT[l]], w=[kb])
            nfull = nk // 128
            P.dma("act", vb[:, :nfull, :],
                  VA[l][kts[0][2]:kts[0][2] + nfull, :, h * 64:(h + 1) * 64].rearrange("t p e -> p t e"),
                  r=[VA[l]], w=[vb])
            if nk % 128:
                P.dma("act", vb[:nk % 128, nfull, :], VA[l][kts[-1][2], 0:nk % 128, h * 64:(h + 1) * 64],
                      r=[VA[l]], w=[vb])
            P.dma("pool", qb[0:97, :nt], QT[l][:, h, r0:r0 + nt], r=[QT[l]], w=[qb])
            B.tsc("dve", qb[96:97, :nt], qb[96:97, :nt], kmr[96:97, h:h + 1], None, ALU.mult, None,
                  r=[qb, kmr], w=[qb])
            loaded[i] = (kb, vb, qb)

        load(0)
        for it, (h, si) in enumerate(its):
            r0, nt, k0, nk, p0 = B.seqs[si]
            kts = ktiles[si]
            if it + 1 < len(its):
                load(it + 1)
            kb, vb, qb = loaded.pop(it)
            pW = psS.next()
            for wi in range(NWARM):
                B.mm(pW[:, (wi % 2) * 512:(wi % 2) * 512 + 512], kb[0:97, 0:128], qb[0:97, 0:512] if nt >= 512 else
                     kb[0:97, 0:512], True, True, r=[kb, qb], w=[pW], inc=(wi == NWARM - 1))
            for q0 in range(0, nt, 512):
                nq = min(512, nt - q0)
                W = 512 if nq == 512 else nq
                G = 2 if nq == 512 else max(1, min(8, 1024 // nq))
                pO, pL = psO.next(), psL.next()
                groups = []
                for kt, (kcol, m, vti) in enumerate(kts):
                    if si == 0:
                        if kt * 128 >= q0 + nq:
                            break
                        off = max(0, kt * 128 - q0)
                        diag = (kt * 128 >= q0)
                    else:
                        off, diag = 0, False
                    plain = (not diag) and m == 128
                    if plain and groups and groups[-1][0] and len(groups[-1][1]) < G:
                        groups[-1][1].append((kt, m, off, diag))
                    else:
                        groups.append((plain, [(kt, m, off, diag)]))
                nblk = sum(len(g[1]) for g in groups)
                pend = {}

                def score(gi):
                    plain, tl = groups[gi]
                    pS, p_ = psS.next(), ptr.next()
                    for j, (kt, m, off, diag) in enumerate(tl):
                        B.mm(pS[:m, j * W + off:j * W + nq], kb[0:97, kt * 128:kt * 128 + m],
                             qb[0:97, q0 + off:q0 + nq], True, True, r=[kb, qb], w=[pS])
                    if plain:
                        B.act(p_[:, 0:len(tl) * W], pS[:, 0:len(tl) * W], AF.Exp, r=[pS], w=[p_])
                    else:
                        kt, m, off, diag = tl[0]
                        B.act(p_[:m, off:nq], pS[:m, off:nq], AF.Exp, r=[pS], w=[p_])
                        if diag:
                            B.mset("pool", p_[64:128, off:off + 64], 0.0, w=[p_])
                    sm_ = None
                    pend[gi] = (p_, sm_)

                for gi in range(min(LOOK, len(groups))):
                    score(gi)
                done = 0
                for gi, (plain, tl) in enumerate(groups):
                    if gi + LOOK < len(groups):
                        score(gi + LOOK)
                    p_, sm_ = pend.pop(gi)
                    for j, (kt, m, off, diag) in enumerate(tl):
                        first, last = done == 0, done == nblk - 1
                        B.mm(pO[0:64, off:nq], vb[:m, kt, :], p_[:m, j * W + off:j * W + nq], first, last,
                             r=[vb, p_], w=[pO])
                        if sm_ is None:
                            B.mm(pL[0:64, off:nq], onesb[:m, 0:64], p_[:m, j * W + off:j * W + nq], first, last,
                                 r=[Cb, p_], w=[pL])
                        elif j == 1:
                            B.mm(pL[0:64, 0:nq], onesb[:, 0:64], sm_[:, 0:nq], done == 1, last, r=[Cb, sm_], w=[pL])
                        done += 1
                r_, o3 = rl.next(), o32.next()
                B.cp("dve", r_[:, :nq], pL[0:64, :nq], r=[pL], w=[r_])
                B.cp("dve", o3[:, :nq], pO[0:64, :nq], r=[pO], w=[o3])
                P.op("dve", lambda: nc.vector.reciprocal(r_[:, :nq], r_[:, :nq]), r=[r_], w=[r_])
                o_ = mo.next()
                B.tt("pool", o_[:, :nq], o3[:, :nq], r_[:, :nq], ALU.mult, r=[o3, r_], w=[o_])
                P.dma("sp", MT[l][h * 64:(h + 1) * 64, r0 + q0:r0 + q0 + nq], o_[:, :nq], r=[o_], w=[MT[l]])
        P.barrier()
        B.release(m2)

    LT = 256

    def phase3(l):
        m3 = B.mark()
        pAB = Ring([B.ps("sA%d" % i) for i in range(4)])
        pY = [B.ps("sY%d" % i) for i in range(2)]
        pG = Ring([B.ps("sG%d" % i) for i in range(2)])
        V = B.sb("vec3", [128, 64])
        P.dma("sp", V[:, :], vec[l], r=[], w=[V])
        s5d, bgl = V[:, 3:5], V[:, 5:7]
        sv = B.sb("sv", [16, 192])
        P.dma("sp", sv[:, :], s5v[l], r=[], w=[sv])
        w16 = B.sb("w16", [16, 16, 64])
        W = lambda i: w16[:, i, :]
        K16 = [w16]
        lre, lim = sv[:, 0:64], sv[:, 64:128]
        B.act(W(0), sv[:, 128:192], AF.Exp, r=[sv], w=K16)
        B.tt("dve", W(1), lre, W(0), ALU.mult, r=[sv] + K16, w=K16)
        B.tt("dve", W(2), lim, W(0), ALU.mult, r=[sv] + K16, w=K16)
        B.act(W(3), W(1), AF.Exp, r=K16, w=K16)
        w16i = B.sb("w16i", [16, 64], mybir.dt.int32)
        sincos("dve", W(4), W(5), W(2), (w16i[:, :], W(13), K16), r=K16, w=K16)
        B.tt("dve", W(6), W(3), W(5), ALU.mult, r=K16, w=K16)
        B.tsc("dve", W(6), W(6), -1.0, None, ALU.add, None, r=K16, w=K16)
        B.tt("dve", W(7), W(3), W(4), ALU.mult, r=K16, w=K16)
        B.tt("dve", W(8), lre, lre, ALU.mult, r=[sv] + K16, w=K16)
        B.tt("dve", W(9), lim, lim, ALU.mult, r=[sv] + K16, w=K16)
        B.tt("dve", W(8), W(8), W(9), ALU.add, r=K16, w=K16)
        P.op("dve", lambda: nc.vector.reciprocal(W(8), W(8)), r=K16, w=K16)
        B.tt("dve", W(9), W(6), lre, ALU.mult, r=[sv] + K16, w=K16)
        B.tt("dve", W(10), W(7), lim, ALU.mult, r=[sv] + K16, w=K16)
        B.tt("dve", W(9), W(9), W(10), ALU.add, r=K16, w=K16)
        B.tt("dve", W(11), W(9), W(8), ALU.mult, r=K16, w=K16)
        B.tt("dve", W(9), W(7), lre, ALU.mult, r=[sv] + K16, w=K16)
        B.tt("dve", W(10), W(6), lim, ALU.mult, r=[sv] + K16, w=K16)
        B.tt("dve", W(9), W(9), W(10), ALU.subtract, r=K16, w=K16)
        B.tt("dve", W(12), W(9), W(8), ALU.mult, r=K16, w=K16)
        cat = B.sb("cat", [16, 2, 128])
        for i, src in enumerate((W(2), W(3))):
            B.cp("dve", cat[:, i, 0:64], src, r=K16, w=[cat])
            B.cp("dve", cat[:, i, 64:128], src, r=K16, w=[cat])
        thr = B.sb("thr", [128, 2, 16])
        for i in range(2):
            ps = pG.next()
            B.tr(ps[:, 0:16], cat[:, i, :], ident[0:16, 0:16], r=[cat, C], w=[ps])
            B.cp("dve", thr[:, i, :], ps[:, 0:16], r=[ps], w=[thr])
        thS, rS = thr[:, 0, :], thr[:, 1, :]
        ctab = B.sb("ctab", [128, 16, LT])
        stab = B.sb("stab", [128, 16, LT])
        rmat = B.sb("rmat", [128, 16, LT])
        stabS = B.sb("stabS", [128, 16, LT])
        io1 = B.sb("io1", [128, LT])
        B.tsc("dve", io1[:, :], cs("iota")[:, 0:LT], 1.0, None, ALU.add, None, r=[C], w=[io1])
        ang = B.sb("ang", [128, LT])
        angi = B.sb("angi", [128, LT], mybir.dt.int32)
        angf = B.sb("angf", [128, LT])
        for g in range(16):
            B.tsc("dve", ang[:, :], io1[:, :], thS[:, g:g + 1], None, ALU.mult, None, r=[io1, thr], w=[ang])
            sincos("dve", stab[:, g, :], ctab[:, g, :], ang[:, :], (angi[:, :], angf[:, :], [angf]), r=[ang],
                   w=[stab, ctab])
            B.tsc("pool", rmat[:, g, :], io1[:, :], 0.0, rS[:, g:g + 1], ALU.mult, ALU.add, r=[io1, thr], w=[rmat])
            B.tsc("pool", stabS[:, g, :], stab[:, g, :], cs("sg2")[:, 0:1], None, ALU.mult, None, r=[stab, C], w=[stabS])
        bb = B.sb("bb", [128, 2, 2, 64])
        P.dma("sp", bb[:, 0, :, :], s5b[l, 0], r=[], w=[bb])
        P.dma("sp", bb[:, 1, :, :], s5b[l, 1], r=[], w=[bb])
        Ff = B.sb("Ff", [128, 2, 2, 64])
        for i, fsrc in enumerate((W(11), W(12))):
            for gh in range(2):
                ps = pG.next()
                B.mm(ps[:, 0:64], cs("E")[0:16, gh * 128:(gh + 1) * 128], fsrc, True, True, r=[C] + K16, w=[ps])
                B.cp("dve", Ff[:, i, gh, :], ps[:, 0:64], r=[ps], w=[Ff])
        bbar = B.sb("bbar", [128, 2, 2, 64])
        t5 = B.sb("t5", [128, 2, 64])
        B.tt("dve", bbar[:, 0, :, :], Ff[:, 0, :, :], bb[:, 0, :, :], ALU.mult, r=[Ff, bb], w=[bbar])
        B.tt("dve", t5[:, :, :], Ff[:, 1, :, :], bb[:, 1, :, :], ALU.mult, r=[Ff, bb], w=[t5])
        B.tt("dve", bbar[:, 0, :, :], bbar[:, 0, :, :], t5[:, :, :], ALU.subtract, r=[bbar, t5], w=[bbar])
        B.tt("dve", bbar[:, 1, :, :], Ff[:, 0, :, :], bb[:, 1, :, :], ALU.mult, r=[Ff, bb], w=[bbar])
        B.tt("dve", t5[:, :, :], Ff[:, 1, :, :], bb[:, 0, :, :], ALU.mult, r=[Ff, bb, bbar], w=[t5])
        B.tt("dve", bbar[:, 1, :, :], bbar[:, 1, :, :], t5[:, :, :], ALU.add, r=[bbar, t5], w=[bbar])
        LB = B.sb("LB", [128, 2, 16, 128], BF16)
        for g in range(16):
            gh, g8 = g // 8, g % 8
            for sw in range(2):
                for half in range(2):
                    src = bbar[:, half ^ sw, gh, :]
                    B.tsc("pool" if half else "dve", LB[:, sw, g, half * 64:(half + 1) * 64], src,
                          cs("gm")[:, g8:g8 + 1], None, ALU.mult, None, r=[bbar, C], w=[LB])
        cc = B.sb("cc", [128, 2, 256])
        P.dma("sp", cc[0:64, 0, :], s5c[l, 0], r=[], w=[cc])
        P.dma("sp", cc[64:128, 0, :], s5c[l, 1], r=[], w=[cc])
        P.dma("sp", cc[0:64, 1, :], s5c[l, 1], r=[], w=[cc])
        P.dma("sp", cc[64:128, 1, :], s5c[l, 0], r=[], w=[cc])
        B.tsc("dve", cc[64:128, 0, :], cc[64:128, 0, :], -1.0, None, ALU.mult, None, r=[cc], w=[cc])
        B.tsc("dve", cc[:, 1, :], cc[:, 1, :], -1.0, None, ALU.mult, None, r=[cc], w=[cc])
        CP = B.sb("CP", [128, 2, 16, 128], BF16)
        B.mset("pool", CP[:, :, :, :], 0.0, w=[CP])
        for g in range(16):
            g8 = g % 8
            for i in range(2):
                B.cp("dve", CP[:, i, g, g8 * 16:(g8 + 1) * 16], cc[:, i, g * 16:(g + 1) * 16], r=[cc], w=[CP])
        Wg = B.sb("Wg", [128, 2, 256], BF16)
        B.load_bf16(Wg[:, :, :].rearrange("p a b -> p (a b)"), Wg, wglu[l].rearrange("p a b -> p (a b)"), [128, 512])
        uts = Ring([B.sb("u3_%d" % i, [128, 2, LT]) for i in range(2)])
        ubs = Ring([B.sb("ub3_%d" % i, [128, 2, LT], BF16) for i in range(2)])
        w1 = Ring([B.sb("w1_%d" % i, [128, LT]) for i in range(12)])
        zr = Ring([B.sb("z_%d" % i, [128, LT]) for i in range(4)])
        zb = Ring([B.sb("zb_%d" % i, [128, LT], BF16) for i in range(6)])
        zend = B.sb("zend", [128, 16])
        xst = B.sb("xst", [128, 16])
        yv = Ring([B.sb("yv_%d" % i, [128, LT]) for i in range(4)])
        zz = B.sb("zz", [128, 2, LT])
        zzb = B.sb("zzb", [128, 2, LT], BF16)
        mo = Ring([B.sb("mo3_%d" % i, [128, LT], BF16) for i in range(2)])
        for si, (r0, nt, k0, nk, p0) in enumerate(B.seqs):
            if si == 0:
                B.mset("pool", xst[:, :], 0.0, w=[xst])
            else:
                P.dma("sp", xst[:, :], s5st[l, si - 1], r=[], w=[xst])
            for c0 in range(0, nt, LT):
                n = min(LT, nt - c0)
                u_, ub = uts.next(), ubs.next()
                P.dma("sp", u_[:, :, :n], UT[l][:, r0 + c0:r0 + c0 + n].rearrange("(j p) n -> p j n", p=128),
                      r=[UT[l]], w=[u_])
                B.cp("pool", ub[:, :, :n], u_[:, :, :n], r=[u_], w=[ub])
                pab = {}

                def stA(g):
                    gh = g // 8
                    pa, pb = pAB.next(), pAB.next()
                    B.mm(pa[:, :n], LB[:, 0, g, :], ub[:, gh, :n], True, True, r=[LB, ub], w=[pa])
                    B.mm(pb[:, :n], LB[:, 1, g, :], ub[:, gh, :n], True, True, r=[LB, ub], w=[pb])
                    pab[g] = (pa, pb)

                wvs = {}

                def stB1(g):
                    pa, pb = pab.pop(g)
                    t1, t2, wv = w1.next(), w1.next(), w1.next()
                    B.tt("dve", t1[:, :n], pa[:, :n], ctab[:, g, :n], ALU.mult, r=[pa, ctab], w=[t1])
                    B.tt("dve", t2[:, :n], pb[:, :n], stabS[:, g, :n], ALU.mult, r=[pb, stabS], w=[t2])
                    B.tt("pool", wv[:, :n], t2[:, :n], t1[:, :n], ALU.add, r=[t1, t2], w=[wv])
                    wvs[g] = wv

                stA(0)
                stA(1)
                stB1(0)
                for g in range(16):
                    gh, g8 = g // 8, g % 8
                    if g + 1 < 16:
                        stB1(g + 1)
                    if g + 2 < 16:
                        stA(g + 2)
                    wv = wvs.pop(g)
                    z = zr.next()
                    P.op("dve", lambda: nc.vector.tensor_tensor_scan(z[:, :n], rmat[:, g, :n], wv[:, :n],
                                                                     xst[:, g:g + 1], ALU.mult, ALU.add),
                         r=[rmat, wv, xst], w=[z])
                    zc, zs = zb.next(), zb.next()
                    B.tt("dve", zc[:, :n], z[:, :n], ctab[:, g, :n], ALU.mult, r=[z, ctab], w=[zc])
                    B.tt("pool", zs[:, :n], z[:, :n], stab[:, g, :n], ALU.mult, r=[z, stab], w=[zs])
                    B.cp("act", zend[:, g:g + 1], z[:, n - 1:n], r=[z], w=[zend])
                    B.mm(pY[gh][:, :n], CP[:, 0, g, :], zc[:, :n], g8 == 0, False, r=[CP, zc], w=[pY[gh]], inc=True)
                    B.mm(pY[gh][:, :n], CP[:, 1, g, :], zs[:, :n], False, g8 == 7, r=[CP, zs], w=[pY[gh]], inc=True)
                ps = pG.next()
                B.mm(ps[:, 0:16], cs("swap"), zend[:, :], True, True, r=[C, zend], w=[ps])
                t1, t2 = w1.next(), w1.next()
                B.tt("dve", t1[:, 0:16], zend[:, :], ctab[:, :, n - 1], ALU.mult, r=[zend, ctab], w=[t1])
                B.tt("dve", t2[:, 0:16], ps[:, 0:16], stabS[:, :, n - 1], ALU.mult, r=[ps, stabS], w=[t2])
                B.stt("dve", xst[:, :], t2[:, 0:16], -1.0, t1[:, 0:16], ALU.mult, ALU.add,
                      r=[t1, t2], w=[xst])
                for j in range(2):
                    y_, x2 = yv.next(), yv.next()
                    B.stt("dve", y_[:, :n], u_[:, j, :n], s5d[:, j:j + 1], pY[j][:, :n], ALU.mult, ALU.add,
                          r=[u_, V, pY[j]], w=[y_])
                    B.tt("pool", x2[:, :n], y_[:, :n], y_[:, :n], ALU.mult, r=[y_], w=[x2])
                    B.tsc("pool", x2[:, :n], x2[:, :n], 0.044715, 1.0, ALU.mult, ALU.add, r=[x2], w=[x2])
                    B.tt("pool", x2[:, :n], x2[:, :n], y_[:, :n], ALU.mult, r=[x2, y_], w=[x2])
                    B.act(x2[:, :n], x2[:, :n], AF.Sigmoid, r=[x2], w=[x2], scale=1.5957691216057308)
                    B.tt("pool", zz[:, j, :n], y_[:, :n], x2[:, :n], ALU.mult, r=[y_, x2], w=[zz])
                    B.cp("pool", zzb[:, j, :n], zz[:, j, :n], r=[zz], w=[zzb])
                for jo in range(2):
                    ps = pG.next()
                    for j in range(2):
                        B.mm(ps[:, :n], Wg[:, j, jo * 128:(jo + 1) * 128], zzb[:, j, :n], j == 0, j == 1,
                             r=[Wg, zzb], w=[ps])
                    gt = yv.next()
                    B.act(gt[:, :n], ps[:, :n], AF.Sigmoid, r=[ps, V], w=[gt], bias=bgl[:, jo:jo + 1])
                    o_ = mo.next()
                    B.tt("dve", o_[:, :n], zz[:, jo, :n], gt[:, :n], ALU.mult, r=[zz, gt], w=[o_])
                    P.dma("sp", MT[l][384 + jo * 128:384 + (jo + 1) * 128, r0 + c0:r0 + c0 + n], o_[:, :n],
                          r=[o_], w=[MT[l]])
            P.dma("sp", o_s5[l, si], xst[:, :], r=[xst], w=[o_s5])
        P.barrier()
        B.release(m3)

    C0 = math.exp(-0.5)

    def phase4(l):
        m4 = B.mark()
        ring7 = Ring([B.ps("rB%d" % i) for i in range(7)])
        pSc = pM_ = slots = ring7
        pYb = B.ps("rY")
        V = B.sb("vec4", [128, 64])
        P.dma("sp", V[:, :], vec[l], r=[], w=[V])
        mu, w0, a0, k_k, k_a, r_k, gng, gnb = (V[:, 7:17], V[:, 17:20], V[:, 20:23], V[:, 23:26], V[:, 26:29],
                                               V[:, 29:32], V[:, 32:35], V[:, 35:38])
        omka = B.sb("omka", [128, 3])
        B.tsc("dve", omka[:, :], k_a, -1.0, 1.0, ALU.mult, ALU.add, r=[V], w=[omka])
        lo = B.sb("lo", [128, 384], BF16)
        B.load_bf16(lo[:, :], lo, lora[l], [128, 384])
        cmask = B.sb("cmask", [128, 640], BF16)
        B.cp("dve", cmask[:, 0:512], cs("m4"), r=[C], w=[cmask])
        B.cp("dve", cmask[:, 512:640], cs("maskL"), r=[C], w=[cmask])
        mk = lambda nm, shp, dt=F32, k=2: Ring([B.sb("%s%d" % (nm, i), shp, dt) for i in range(k)])
        cur, prv, dd = mk("cur", [128, 10, 128]), mk("prv", [128, 10, 128]), mk("dd", [128, 10, 128], F32, 1)
        psx = mk("psx", [128, 10, 128])
        sm = mk("sm", [128, 128], BF16, 4)
        lr3 = mk("lr3", [128, 128], BF16, 6)
        f3 = lambda nm, k=2: mk(nm, [128, 3, 128], F32, k)
        sig, aa, gg, kk_, kap, kti, bb_, bon, css, dm, pin, pinv, pex = (f3("sig"), f3("aa"), f3("gg", 3), f3("kk"),
            f3("kap"), f3("kti"), f3("bb"), f3("bon", 3), f3("css", 1), f3("dm", 1), f3("pin", 3), f3("pinv", 1), f3("pex", 1))
        b3 = lambda nm, k=2: mk(nm, [128, 3, 128], BF16, k)
        rh, kph, bhb, khb = b3("rh", 3), b3("kph", 3), b3("bhb"), b3("khb")
        bhf, khf = f3("bhf", 1), f3("khf", 1)
        kTt, bTt, vtt = b3("kTt", 3), b3("bTt", 3), b3("vtt", 3)
        scb = [mk("scb%d" % h, [128, 512], BF16, 3) for h in range(NH)]
        nbN = [mk("nbN%d" % h, [128, 128], BF16, 3) for h in range(NH)]
        nbB = [mk("nbB%d" % h, [128, 128], BF16, 3) for h in range(NH)]
        Mb = [mk("Mb%d" % h, [128, 128], BF16, 9) for h in range(NH)]
        zn = [mk("zn%d" % h, [128, 64], BF16, 2) for h in range(NH)]
        u2 = mk("u2", [128, 128], BF16, 6)
        Hf = B.sb("Hf", [128, 3, 64])
        Hb = mk("Hb", [128, 3, 64], BF16, 2)
        pmid, ppr = mk("pmid", [128, 3], F32, 3), mk("ppr", [128, 3], F32, 3)
        st6 = mk("st6", [128, 6, 6], F32, 2)
        mv6 = mk("mv6", [128, 6, 2], F32, 2)
        yn = mk("yn", [128, 384], F32, 2)
        fo = mk("fo", [128, 128], F32, 3)
        mo = mk("mo4", [128, 128], BF16, 3)
        chunks = [(si, c0) for si, sq in enumerate(B.seqs) for c0 in range(0, sq[1], 128)]

        def gen_prep(si, c0, X):
            r0, nt, k0, nk, p0 = B.seqs[si]
            n = min(128, nt - c0)
            a_, b_ = r0 + c0, r0 + c0 + n
            cu, pv = cur.next(), prv.next()
            P.dma("sp", cu[:, :, :n], PT[l][:, a_:b_].rearrange("(j p) n -> p j n", p=128), r=[PT[l]], w=[cu])
            if c0 == 0:
                if si == 0:
                    B.mset("pool", pv[:, :, 0:1], 0.0, w=[pv])
                else:
                    P.dma("act", pv[:, :, 0], shst[l, si - 1], r=[], w=[pv])
                if n > 1:
                    P.dma("act", pv[:, :, 1:n], PT[l][:, a_:b_ - 1].rearrange("(j p) n -> p j n", p=128),
                          r=[PT[l]], w=[pv])
            else:
                P.dma("act", pv[:, :, :n], PT[l][:, a_ - 1:b_ - 1].rearrange("(j p) n -> p j n", p=128),
                      r=[PT[l]], w=[pv])
            d_ = dd.next()
            B.tt("pool", d_[:, :, :n], pv[:, :, :n], cu[:, :, :n], ALU.subtract, r=[pv, cu], w=[d_])
            yield
            px = psx.next()
            for j in range(10):
                B.stt("dve" if j % 2 else "pool", px[:, j, :n], d_[:, j, :n], mu[:, j:j + 1], cu[:, j, :n],
                      ALU.mult, ALU.add, r=[d_, cu, V], w=[px])
            R_, K_, V_ = (lambda c: px[:, c, :n]), (lambda c: px[:, 3 + c, :n]), (lambda c: px[:, 6 + c, :n])
            th, adb, sgd = lr3.next(), lr3.next(), lr3.next()
            B.act(th[0:32, :n], px[0:32, 9, :n], AF.Tanh, r=[px], w=[th])
            B.cp("dve", adb[32:64, :n], px[32:64, 9, :n], r=[px], w=[adb])
            B.act(sgd[64:128, :n], px[64:128, 9, :n], AF.Sigmoid, r=[px], w=[sgd])
            yield
            sg, a3, g3, k3, kp3, kt3, b3_, bn3 = (sig.next(), aa.next(), gg.next(), kk_.next(), kap.next(),
                                                  kti.next(), bb_.next(), bon.next())
            for c in range(3):
                cc_ = slice(c * 128, (c + 1) * 128)
                p1 = pM_.next()
                B.mm(p1[:, 0:n], lo[0:32, cc_], th[0:32, :n], True, True, r=[lo, th], w=[p1])
                B.mm(p1[:, 128:128 + n], lo[32:64, cc_], adb[32:64, :n], True, True, r=[lo, adb], w=[p1])
                B.mm(p1[:, 256:256 + n], lo[64:128, cc_], sgd[64:128, :n], True, True, r=[lo, sgd], w=[p1])
                B.act(sg[:, c, :n], p1[:, 0:n], AF.Sigmoid, r=[p1, V], w=[sg], bias=w0[:, c:c + 1])
                B.act(a3[:, c, :n], p1[:, 128:128 + n], AF.Sigmoid, r=[p1, V], w=[a3], bias=a0[:, c:c + 1])
                B.cp("act", g3[:, c, :n], p1[:, 256:256 + n], r=[p1], w=[g3])
                B.tsc("pool", k3[:, c, :n], K_(c), k_k[:, c:c + 1], None, ALU.mult, None, r=[px, V], w=[k3])
                s_ = sm.next()
                B.tt("pool", s_[:, :n], k3[:, c, :n], k3[:, c, :n], ALU.mult, r=[k3], w=[s_])
                p2 = pM_.next()
                B.mm(p2[:, 0:n], bonesb, s_[:, :n], True, True, r=[Cb, s_], w=[p2])
                t_ = fo.next()
                sqrt_pow("dve", t_[:, :n], p2[:, 0:n], 1e-24, -0.5, r=[p2], w=[t_])
                B.tt("pool", kp3[:, c, :n], k3[:, c, :n], t_[:, :n], ALU.mult, r=[k3, t_], w=[kp3])
                t2_ = fo.next()
                B.tsc("dve", t2_[:, :n], a3[:, c, :n], k_a[:, c:c + 1], omka[:, c:c + 1], ALU.mult, ALU.add,
                      r=[a3, V, omka], w=[t2_])
                B.tt("pool", kt3[:, c, :n], K_(c), t2_[:, :n], ALU.mult, r=[px, t2_], w=[kt3])
                B.tt("pool", b3_[:, c, :n], kp3[:, c, :n], a3[:, c, :n], ALU.mult, r=[kp3, a3], w=[b3_])
                s2_ = sm.next()
                B.stt("dve", s2_[:, :n], R_(c), r_k[:, c:c + 1], kt3[:, c, :n], ALU.mult, ALU.mult,
                      r=[px, V, kt3], w=[s2_])
                B.mm(p2[:, 128:128 + n], bonesb, s2_[:, :n], True, True, r=[Cb, s2_], w=[p2])
                B.tt("dve", bn3[:, c, :n], p2[:, 128:128 + n], V_(c), ALU.mult, r=[p2, px], w=[bn3])
            yield
            cs_, dm_, pi_, piv, pe_ = css.next(), dm.next(), pin.next(), pinv.next(), pex.next()
            mid = min(63, n - 1)
            pm_, pr_ = pmid.next(), ppr.next()
            for c in range(3):
                P.op("dve", lambda: nc.vector.tensor_tensor_scan(cs_[:, c, :n], cs("ones")[:, 0:n], sg[:, c, :n],
                                                                 0.0, ALU.mult, ALU.add), r=[C, sg], w=[cs_])
                B.tsc("dve", dm_[:, c, :n], cs_[:, c, :n], cs_[:, c, mid:mid + 1], None, ALU.subtract, None,
                      r=[cs_], w=[dm_])
            B.act(pi_[:, :, :n], dm_[:, :, :n], AF.Exp, r=[dm_], w=[pi_], scale=-C0)
            B.act(piv[:, :, :n], dm_[:, :, :n], AF.Exp, r=[dm_], w=[piv], scale=C0)
            B.tt("pool", dm_[:, :, :n], dm_[:, :, :n], sg[:, :, :n], ALU.subtract, r=[dm_, sg], w=[dm_])
            B.act(pe_[:, :, :n], dm_[:, :, :n], AF.Exp, r=[dm_], w=[pe_], scale=-C0)
            B.act(pm_[:, :], cs_[:, :, mid], AF.Exp, r=[cs_], w=[pm_], scale=-C0)
            B.tt("dve", pr_[:, :], pm_[:, :], pi_[:, :, n - 1], ALU.mult, r=[pm_, pi_], w=[pr_])
            yield
            rh_, kph_, bhb_, khb_, bhf_, khf_ = rh.next(), kph.next(), bhb.next(), khb.next(), bhf.next(), khf.next()
            B.tt("pool", rh_[:, :, :n], px[:, 0:3, :n], pi_[:, :, :n], ALU.mult, r=[px, pi_], w=[rh_])
            B.tt("pool", kph_[:, :, :n], kp3[:, :, :n], pe_[:, :, :n], ALU.mult, r=[kp3, pe_], w=[kph_])
            B.tt("dve", bhf_[:, :, :n], b3_[:, :, :n], piv[:, :, :n], ALU.mult, r=[b3_, piv], w=[bhf_])
            B.tt("dve", khf_[:, :, :n], kt3[:, :, :n], piv[:, :, :n], ALU.mult, r=[kt3, piv], w=[khf_])
            B.cp("pool", bhb_[:, :, :n], bhf_[:, :, :n], r=[bhf_], w=[bhb_])
            B.cp("pool", khb_[:, :, :n], khf_[:, :, :n], r=[khf_], w=[khb_])
            yield
            kT_, bT_, vt_ = kTt.next(), bTt.next(), vtt.next()
            for c in range(3):
                for src, dst in ((khf_[:, c, :n], kT_), (bhf_[:, c, :n], bT_), (V_(c), vt_)):
                    p1 = pM_.next()
                    B.tr(p1[:n, 0:128], src, ident, r=[khf_, bhf_, px, C], w=[p1])
                    B.cp(B.ev(), dst[:n, c, :], p1[:n, 0:128], r=[p1], w=[dst])
            X.update(n=n, a_=a_, b_=b_, rh_=rh_, kph_=kph_, vt_=vt_, kT_=kT_, bT_=bT_, pi_=pi_, pm_=pm_,
                     pr_=pr_, bn3=bn3, g3=g3, bhb_=bhb_, khb_=khb_)
            yield

        def gen_ab(si, c0, X):
            n, rh_, kph_, bhb_, khb_ = (X[k] for k in ('n', 'rh_', 'kph_', 'bhb_', 'khb_'))
            nlev = 6 if n > 64 else (5 if n > 32 else 4)
            heads = [(c, hh) for c in range(3) for hh in range(2)]
            Rs = lambda hh: slice(hh * 64, hh * 64 + 64)
            st = {}
            for (c, hh) in heads:
                R = Rs(hh)
                sc = pSc.next()
                B.mm(sc[:n, 0:n], bhb_[R, c, :n], kph_[R, c, :n], True, True, r=[bhb_, kph_], w=[sc])
                B.mm(sc[:n, n:2 * n], khb_[R, c, :n], kph_[R, c, :n], True, True, r=[khb_, kph_], w=[sc])
                B.mm(sc[:n, 2 * n:3 * n], bhb_[R, c, :n], rh_[R, c, :n], True, True, r=[bhb_, rh_], w=[sc])
                B.mm(sc[:n, 3 * n:4 * n], khb_[R, c, :n], rh_[R, c, :n], True, True, r=[khb_, rh_], w=[sc])
                sb_ = scb[2 * c + hh].next()
                if n == 128:
                    B.tt("dve", sb_[:n, :], sc[:n, :], cmask[:n, 0:512], ALU.mult, r=[sc, cmask], w=[sb_])
                else:
                    for q in range(4):
                        B.tt("dve", sb_[:n, q * n:(q + 1) * n], sc[:n, q * n:(q + 1) * n],
                             cmask[:n, q * 128:q * 128 + n], ALU.mult, r=[sc, cmask], w=[sb_])
                p1 = slots.next()
                B.mm(p1[:n, 0:n], kph_[R, c, :n], bhb_[R, c, :n], True, True, r=[kph_, bhb_], w=[p1])
                Nk_ = nbN[2 * c + hh].next()
                B.tt("dve", Nk_[:n, :n], p1[:n, 0:n], cmask[:n, 512:512 + n], ALU.mult, r=[p1, cmask], w=[Nk_])
                M_ = Mb[2 * c + hh].next()
                B.tt("pool", M_[:n, :n], sb_[:n, 0:n], identb[:n, :n], ALU.add, r=[sb_, Cb], w=[M_])
                st[(c, hh)] = dict(sb=sb_, Bk=sb_[:n, 0:n], BkB=sb_, Nk=Nk_[:n, :n], NkB=Nk_, M=M_)
            yield
            for lev in range(nlev):
                lastl = lev == nlev - 1
                for (c, hh) in heads:
                    S_ = st[(c, hh)]
                    pa = slots.next()
                    B.mm(pa[:n, 0:n], S_["Bk"], S_["Nk"], True, True, r=[S_["BkB"], S_["NkB"]], w=[pa])
                    if not lastl:
                        pb = slots.next()
                        B.mm(pb[:n, 0:n], S_["Nk"], S_["Bk"], True, True, r=[S_["BkB"], S_["NkB"]], w=[pb])
                    nb1 = nbN[2 * c + hh].next()
                    B.cp("act", nb1[:n, :n], pa[:n, 0:n], r=[pa], w=[nb1])
                    if not lastl:
                        nb2 = nbB[2 * c + hh].next()
                        B.cp("act", nb2[:n, :n], pb[:n, 0:n], r=[pb], w=[nb2])
                        S_["Bk"], S_["BkB"] = nb2[:n, :n], nb2
                    S_["Nk"], S_["NkB"] = nb1[:n, :n], nb1
                for (c, hh) in heads:
                    S_ = st[(c, hh)]
                    pm2 = slots.next()
                    B.mm(pm2[:n, 0:n], S_["Nk"], S_["M"][:n, :n], True, True, r=[S_["NkB"], S_["M"]], w=[pm2])
                    M2 = Mb[2 * c + hh].next()
                    B.tt("dve", M2[:n, :n], pm2[:n, 0:n], S_["M"][:n, :n], ALU.add, r=[pm2, S_["M"]], w=[M2])
                    S_["M"] = M2
                yield
            X.update(st=st, heads=heads, Rs=Rs)
            yield

        def gen_c(si, c0, X):
            r0, nt, k0, nk, p0 = B.seqs[si]
            n, a_, b_, rh_, kph_, vt_, kT_, bT_, st, pi_, pm_, pr_, bn3, g3, heads, Rs = (X[k] for k in (
                'n', 'a_', 'b_', 'rh_', 'kph_', 'vt_', 'kT_', 'bT_', 'st', 'pi_', 'pm_', 'pr_', 'bn3', 'g3', 'heads', 'Rs'))
            if c0 == 0:
                if si == 0:
                    B.mset("pool", Hf[:, :, :], 0.0, w=[Hf])
                else:
                    P.dma("sp", Hf[:, :, :], rwst[l, si - 1], r=[], w=[Hf])
            hb = Hb.next()
            for c in range(3):
                B.tsc("dve", hb[:, c, :], Hf[:, c, :], pm_[:, c:c + 1], None, ALU.mult, None, r=[Hf, pm_], w=[hb])
            u_s = [u2.next() for c in range(3)]
            yield
            for (c, hh) in heads:
                S_ = st[(c, hh)]
                R = Rs(hh)
                pz = slots.next()
                AKm = S_["sb"][:n, n:2 * n]
                B.mm(pz[:n, 0:64], kph_[R, c, :n], hb[R, c, :], True, False, r=[kph_, hb], w=[pz], inc=True)
                B.mm(pz[:n, 0:64], AKm, vt_[:n, c, R], False, True, r=[S_["sb"], vt_], w=[pz])
                z_ = zn[2 * c + hh].next()
                B.act(z_[:n, :], pz[:n, 0:64], AF.Copy, r=[pz], w=[z_], scale=-1.0)
                S_["z"] = z_
            yield
            for (c, hh) in heads:
                S_ = st[(c, hh)]
                R = Rs(hh)
                pu = slots.next()
                B.mm(pu[:n, 0:64], S_["M"][:n, :n], S_["z"][:n, :], True, True, r=[S_["M"], S_["z"]], w=[pu])
                B.cp("act", u_s[c][:n, R], pu[:n, 0:64], r=[pu], w=[u_s[c]])
            yield
            for (c, hh) in heads:
                S_ = st[(c, hh)]
                R = Rs(hh)
                h = 2 * c + hh
                RBm, RKm = S_["sb"][:n, 2 * n:3 * n], S_["sb"][:n, 3 * n:4 * n]
                yr = pYb[:n, h * 64:(h + 1) * 64]
                B.mm(yr, rh_[R, c, :n], hb[R, c, :], True, False, r=[rh_, hb], w=[pYb], inc=True)
                B.mm(yr, RBm, u_s[c][:n, R], False, False, r=[S_["sb"], u_s[c]], w=[pYb], inc=True)
                B.mm(yr, RKm, vt_[:n, c, R], False, True, r=[S_["sb"], vt_], w=[pYb])
            yield
            for c in range(3):
                pH_ = ring7.next()
                B.mm(pH_[:, 0:128], kT_[:n, c, :], vt_[:n, c, :], True, False, r=[kT_, vt_], w=[pH_], inc=True)
                B.mm(pH_[:, 0:128], bT_[:n, c, :], u_s[c][:n, :], False, True, r=[bT_, u_s[c]], w=[pH_])
                for hh in range(2):
                    R = Rs(hh)
                    t_ = fo.next()
                    B.tsc("dve", t_[R, 0:64], pH_[R, R], pi_[R, c, n - 1:n], None, ALU.mult, None,
                          r=[pH_, pi_], w=[t_])
                    B.stt("dve", Hf[R, c, :], Hf[R, c, :], pr_[R, c:c + 1], t_[R, 0:64], ALU.mult, ALU.add,
                          r=[Hf, pr_, t_], w=[Hf])
            yield
            s6, m6, y_ = st6.next(), mv6.next(), yn.next()
            for h in range(NH):
                P.op("dve", lambda: nc.vector.bn_stats(s6[:n, h, :], pYb[:n, h * 64:(h + 1) * 64]), r=[pYb], w=[s6])
                P.op("dve", lambda: nc.vector.bn_aggr(m6[:n, h, :], s6[:n, h, :]), r=[s6], w=[m6])
            sqrt_pow("dve", m6[:n, :, 1], m6[:n, :, 1], 64e-5, -0.5, r=[m6], w=[m6])
            for h in range(NH):
                B.tsc("dve", y_[:n, h * 64:(h + 1) * 64], pYb[:n, h * 64:(h + 1) * 64], m6[:n, h, 0:1],
                      m6[:n, h, 1:2], ALU.subtract, ALU.mult, r=[pYb, m6], w=[y_])
            yield
            for c in range(3):
                p1 = pM_.next()
                B.tr(p1[:, 0:n], y_[:n, c * 128:(c + 1) * 128], ident[:n, :n], r=[y_, C], w=[p1])
                t_ = fo.next()
                B.tsc("dve", t_[:, :n], p1[:, 0:n], gng[:, c:c + 1], gnb[:, c:c + 1], ALU.mult, ALU.add,
                      r=[p1, V], w=[t_])
                B.tt("pool", t_[:, :n], t_[:, :n], bn3[:, c, :n], ALU.add, r=[t_, bn3], w=[t_])
                o_ = mo.next()
                B.tt("pool", o_[:, :n], t_[:, :n], g3[:, c, :n], ALU.mult, r=[t_, g3], w=[o_])
                P.dma("sp", MT[l][640 + c * 128:640 + (c + 1) * 128, a_:b_], o_[:, :n], r=[o_], w=[MT[l]])
            if c0 + 128 >= nt:
                P.dma("sp", o_rw[l, si], Hf[:, :, :], r=[Hf], w=[o_rw])
            yield

        nck = len(chunks)
        Xs = {}
        for k in range(nck + 2):
            gens = []
            if k < nck:
                Xs[k] = {}
                gens.append(gen_prep(chunks[k][0], chunks[k][1], Xs[k]))
            if 0 <= k - 1 < nck:
                gens.append(gen_ab(chunks[k - 1][0], chunks[k - 1][1], Xs[k - 1]))
            if 0 <= k - 2 < nck:
                gens.append(gen_c(chunks[k - 2][0], chunks[k - 2][1], Xs[k - 2]))
            alive = list(gens)
            while alive:
                for g_ in list(alive):
                    try:
                        next(g_)
                    except StopIteration:
                        alive.remove(g_)
            Xs.pop(k - 2, None)
        P.barrier()
        B.release(m4)

    def layer_norm(z, ss, gB, bB, outb, st, mv):
        for hf in range(2):
            P.op("dve", lambda: nc.vector.bn_stats(st[:ss, hf, :], z[:ss, hf * 512:(hf + 1) * 512]), r=[z], w=[st])
        P.op("dve", lambda: nc.vector.bn_aggr(mv[:ss, :], st[:ss, :, :].rearrange("p a b -> p (a b)")), r=[st], w=[mv])
        sqrt_pow("dve", mv[:ss, 1:2], mv[:ss, 1:2], 1e-5, -0.5, r=[mv], w=[mv])
        B.tsc("dve", outb[:ss, :], z[:ss, :], mv[:ss, 0:1], mv[:ss, 1:2], ALU.subtract, ALU.mult, r=[z, mv], w=[outb])
        B.tt("pool", outb[:ss, :], outb[:ss, :], gB[:ss, :], ALU.mult, r=[outb, gB], w=[outb])
        B.tt("pool", outb[:ss, :], outb[:ss, :], bB[:ss, :], ALU.add, r=[outb, bB], w=[outb])

    def phase5a(l):
        m5 = B.mark()
        pso = Ring([B.ps("oP%d" % i) for i in range(6)])
        Wo = B.sb("Wo", [128, 8, 1024], BF16)
        for kc in range(8):
            B.load_bf16(Wo[:, kc, :], Wo, wout[l, :, kc, :], [128, 1024], q="sp" if kc % 2 else "act")
        gB, bB = B.sb("g1", [128, 1024]), B.sb("b1", [128, 1024])
        P.dma("sp", gB[:, :], lnp[l, 0], r=[], w=[gB])
        P.dma("sp", bB[:, :], lnp[l, 1], r=[], w=[bB])
        Xsrc = xin if l == 0 else XN[l - 1]
        mts = Ring([B.sb("mt%d" % i, [128, 8, 512], BF16) for i in range(2)])
        xts = Ring([B.sb("x5_%d" % i, [128, 4, D]) for i in range(2)])
        zs = Ring([B.sb("z5_%d" % i, [128, D]) for i in range(2)])
        os_ = Ring([B.sb("o5_%d" % i, [128, D]) for i in range(3)])
        xT = Ring([B.sb("xT5_%d" % i, [128, 8, 512], BF16) for i in range(2)])
        st, mv = B.sb("st5", [128, 2, 6]), B.sb("mv5", [128, 2])
        wcv = Ring([B.sb("wcv%d" % i, [128, 1024], BF16) for i in range(2)])
        for j in range(32):
            stg = B.stage.next()
            P.dma("pool", stg[:, 0:1024], wup[l, j], r=[], w=[stg])
            wb = wcv.next()
            B.cp("act", wb[:, :], stg[:, 0:1024], r=[stg], w=[wb])
            P.dma("pool", WUPB[l, j], wb[:, :], r=[wb], w=[WUPB])
        pre5 = {}

        def prefetch5(ti):
            t0, n = B.tiles[ti]
            ss = min(128, n)
            nsub = n // ss
            mt, xt = mts.next(), xts.next()
            P.dma("sp", mt[:, :, :n], MT[l][:, t0:t0 + n].rearrange("(k p) n -> p k n", p=128), r=[MT[l]], w=[mt])
            P.dma("sp", xt[:ss, :nsub, :], Xsrc[t0:t0 + n, :].rearrange("(s p) d -> p s d", p=ss), r=[Xsrc], w=[xt])
            pre5[ti] = (mt, xt)

        prefetch5(0)
        for ti, (t0, n) in enumerate(B.tiles):
            ss = min(128, n)
            nsub = n // ss
            if ti + 1 < len(B.tiles):
                prefetch5(ti + 1)
            mt, xt = pre5.pop(ti)
            x_T = xT.next()
            def mm_part(s):
                z = zs.next()
                for hf in range(2):
                    ps = pso.next()
                    for kc in range(8):
                        B.mm(ps[:ss, :], mt[:, kc, s * ss:(s + 1) * ss], Wo[:, kc, hf * 512:(hf + 1) * 512],
                             kc == 0, kc == 7, r=[mt, Wo], w=[ps])
                    B.stt("dve", z[:ss, hf * 512:(hf + 1) * 512], xt[:ss, s, hf * 512:(hf + 1) * 512], ALPHA,
                          ps[:ss, :], ALU.mult, ALU.add, r=[xt, ps], w=[z])
                o_ = os_.next()
                layer_norm(z, ss, gB, bB, o_, st, mv)
                P.dma("pool", X1[l][t0 + s * ss:t0 + (s + 1) * ss, :], o_[:ss, :], r=[o_], w=[X1[l]])
                return o_

            def tr_part(s, o_):
                for kc in range(0, 8, 4):
                    ps = pso.next()
                    for k2 in range(4):
                        B.tr(ps[:, k2 * 128:k2 * 128 + ss], o_[:ss, (kc + k2) * 128:(kc + k2 + 1) * 128], ident[:ss, :ss],
                             r=[o_, C], w=[ps], inc=(k2 == 3))
                    B.cp(B.ev(), x_T[:, kc:kc + 4, s * ss:(s + 1) * ss],
                         ps[:, :].rearrange("p (a b) -> p a b", a=4)[:, :, 0:ss], r=[ps], w=[x_T])

            prev = None
            for s in range(nsub):
                o_ = mm_part(s)
                if prev is not None:
                    tr_part(*prev)
                prev = (s, o_)
            tr_part(*prev)
            P.dma("pool", X1T[l][:, t0:t0 + n].rearrange("(k p) n -> p k n", p=128), x_T[:, :, :n], r=[x_T], w=[X1T[l]])
        P.barrier()
        B.release(m5)

    def phase5b(l):
        m5 = B.mark()
        psu = Ring([B.ps("uP%d" % i) for i in range(4)])
        psd = Ring([B.ps("dP%d" % i) for i in range(4)])
        Wd = B.sb("Wd", [128, 32, 1024], BF16)
        for j in range(32):
            B.load_bf16(Wd[:, j, :], Wd, wdn[l, :, j, :], [128, 1024], q="sp" if j % 2 else "act")
        gB, bB = B.sb("g2", [128, 1024]), B.sb("b2", [128, 1024])
        P.dma("sp", gB[:, :], lnp[l, 2], r=[], w=[gB])
        P.dma("sp", bB[:, :], lnp[l, 3], r=[], w=[bB])
        xTs = Ring([B.sb("xT6_%d" % i, [128, 8, 512], BF16) for i in range(2)])
        slab = Ring([B.sb("sl%d" % i, [128, 1024], BF16) for i in range(6)])
        hT = B.sb("hT", [128, 32, 512], BF16)
        rl_ = Ring([B.sb("rl6_%d" % i, [128, 512]) for i in range(4)])
        x1s = Ring([B.sb("x6_%d" % i, [128, D]) for i in range(2)])
        zs = Ring([B.sb("z6_%d" % i, [128, D]) for i in range(2)])
        os_ = Ring([B.sb("o6_%d" % i, [128, D]) for i in range(2)])
        st, mv = B.sb("st6b", [128, 2, 6]), B.sb("mv6b", [128, 2])
        pre6 = {}

        def prefetch6(ti):
            t0, n = B.tiles[ti]
            x_T = xTs.next()
            P.dma("sp", x_T[:, :, :n], X1T[l][:, t0:t0 + n].rearrange("(k p) n -> p k n", p=128), r=[X1T[l]], w=[x_T])
            pre6[ti] = x_T

        prefetch6(0)
        for ti, (t0, n) in enumerate(B.tiles):
            ss = min(128, n)
            nsub = n // ss
            if ti + 1 < len(B.tiles):
                prefetch6(ti + 1)
            x_T = pre6.pop(ti)
            slq = {}

            def slab_load(j):
                sl = slab.next()
                P.dma("sp", sl[:, :], WUPB[l, j], r=[WUPB], w=[sl])
                slq[j] = sl

            for j in range(3):
                slab_load(j)
            for j in range(32):
                if j + 3 < 32:
                    slab_load(j + 3)
                sl = slq.pop(j)
                ps = psu.next()
                for kc in range(8):
                    B.mm(ps[:, :n], sl[:, kc * 128:(kc + 1) * 128], x_T[:, kc, :n], kc == 0, kc == 7, r=[sl, x_T], w=[ps])
                if j % 2:
                    t_ = rl_.next()
                    B.tsc("dve", t_[:, :n], ps[:, :n], 0.0, None, ALU.max, None, r=[ps], w=[t_])
                    B.tt("dve", hT[:, j, :n], t_[:, :n], t_[:, :n], ALU.mult, r=[t_], w=[hT])
                else:
                    t_ = rl_.next()
                    B.act(t_[:, :n], ps[:, :n], AF.Relu, r=[ps], w=[t_])
                    B.tt("pool", hT[:, j, :n], t_[:, :n], t_[:, :n], ALU.mult, r=[t_], w=[hT])
            for s in range(nsub):
                x1 = x1s.next()
                P.dma("sp", x1[:ss, :], X1[l][t0 + s * ss:t0 + (s + 1) * ss, :], r=[X1[l]], w=[x1])
                z = zs.next()
                for hf in range(2):
                    ps = psd.next()
                    for j in range(32):
                        B.mm(ps[:ss, :], hT[:, j, s * ss:(s + 1) * ss], Wd[:, j, hf * 512:(hf + 1) * 512],
                             j == 0, j == 31, r=[hT, Wd], w=[ps])
                    B.stt("dve", z[:ss, hf * 512:(hf + 1) * 512], x1[:ss, hf * 512:(hf + 1) * 512], ALPHA,
                          ps[:ss, :], ALU.mult, ALU.add, r=[x1, ps], w=[z])
                o_ = os_.next()
                layer_norm(z, ss, gB, bB, o_, st, mv)
                P.dma("pool", XN[l][t0 + s * ss:t0 + (s + 1) * ss, :], o_[:ss, :], r=[o_], w=[XN[l]])
        P.barrier()
        B.release(m5)

    phases = dbg if dbg is not None else ["1", "2", "3", "4", "5a", "5b"]
    for l in range(L):
        for ph, fn in (("1", phase1), ("2", phase2), ("3", phase3), ("4", phase4), ("5a", phase5a), ("5b", phase5b)):
            if ph in phases:
                fn(l)
    P.barrier()
    return nc


def _layout_weights(w, PAST):
    f = lambda a: np.ascontiguousarray(a, dtype=np.float32)
    cm = lambda v: np.ascontiguousarray(v.reshape(L, -1, 128).transpose(0, 2, 1))
    w_in = w["w_in"]
    Wp = np.zeros((L, D, NCOL), np.float32)
    Wp[:, :, 0:384] = w_in[:, :, 0:384]
    kr = w_in[:, :, 384:416]
    Wp[:, :, 384 + 64:480] = kr
    Wp[:, :, 480 + 64:480 + 80] = kr[:, :, 16:32]
    Wp[:, :, 480 + 80:576] = kr[:, :, 0:16]
    Wp[:, :, 576:832] = w_in[:, :, 416:672]
    Wp[:, :, 832:2112] = w_in[:, :, 672:1952]
    o = {}
    o["win"] = f(Wp.reshape(L, 8, 128, NCOL).transpose(0, 2, 1, 3))
    wq = w["w_qb"]
    o["wqb"] = f(wq.reshape(L, 2, 128, 576).transpose(0, 2, 1, 3))
    wqs = np.zeros_like(wq)
    for h in range(NH):
        b = h * 96
        wqs[:, :, b + 64:b + 80] = wq[:, :, b + 80:b + 96]
        wqs[:, :, b + 80:b + 96] = wq[:, :, b + 64:b + 80]
    o["wqbs"] = f(wqs.reshape(L, 2, 128, 576).transpose(0, 2, 1, 3))
    wkv = w["w_kvb"].reshape(L, 128, NH, 128)
    o["wkvk"] = f(wkv[:, :, :, 0:64].reshape(L, 128, 384))
    o["wkvv"] = f(wkv[:, :, :, 64:128].reshape(L, 128, 384))
    o["wout"] = f(w["w_out"].reshape(L, 8, 128, 1024).transpose(0, 2, 1, 3))
    o["wup"] = f(w["w_up"].reshape(L, 8, 128, 32, 128).transpose(0, 3, 2, 1, 4).reshape(L, 32, 128, 1024))
    o["wdn"] = f(w["w_down"].reshape(L, 32, 128, 1024).transpose(0, 2, 1, 3))
    vec = np.zeros((L, 128, 64), np.float32)
    vec[:, :, 0:2] = cm(w["q_norm_g"])
    vec[:, :, 2:3] = cm(w["kv_norm_g"])
    vec[:, :, 3:5] = cm(w["s5_d"])
    vec[:, :, 5:7] = cm(w["b_glu"])
    vec[:, :, 7:17] = cm(w["mu_shift"])
    vec[:, :, 17:20] = cm(w["w0"])
    vec[:, :, 20:23] = cm(w["a0"])
    vec[:, :, 23:26] = cm(w["k_k"])
    vec[:, :, 26:29] = cm(w["k_a"])
    vec[:, :, 29:32] = cm(w["r_k"].reshape(L, 384))
    vec[:, :, 32:35] = cm(w["gn_g"])
    vec[:, :, 35:38] = cm(w["gn_b"])
    o["vec"] = vec
    ln = np.stack([w["ln1_g"], w["ln1_b"], w["ln2_g"], w["ln2_b"]], 1)
    o["lnp"] = f(np.broadcast_to(ln[:, :, None, :], (L, 4, 128, 1024)))
    o["lora"] = f(np.concatenate([w["w_w2"], w["w_a2"], w["w_g2"]], 1))
    o["s5v"] = f(np.concatenate([w["lam_re"], w["lam_im"], np.repeat(w["log_dt"][:, :, None], 64, 2)], 2))
    bt = lambda b: b.reshape(L, 2, 8, 64, 16).transpose(0, 2, 4, 1, 3).reshape(L, 128, 2, 64)
    o["s5b"] = f(np.stack([bt(w["b_re"]), bt(w["b_im"])], 1))
    ct = lambda c: c.transpose(0, 3, 1, 2).reshape(L, 64, 256)
    o["s5c"] = f(np.stack([ct(w["c_re"]), ct(w["c_im"])], 1))
    o["wglu"] = f(w["w_glu"].reshape(L, 2, 128, 256).transpose(0, 2, 1, 3))
    return o


_CACHE = {}


def run_cores(inp, T, PAST, n_cores, dbg=None, extra_out=()):
    cstv, offs = make_consts(PAST)
    key = (T, PAST, tuple(dbg) if dbg else None)
    if key not in _CACHE:
        _CACHE[key] = build_program(T, PAST, cstv.shape[1], offs, dbg)
    nc = _CACHE[key]
    wl = _layout_weights(inp, PAST)
    wl["cst"] = cstv
    in_maps = []
    for c in range(n_cores):
        m = dict(wl)
        sb = slice(NSB * c, NSB * c + NSB)
        m["xin"] = np.ascontiguousarray(np.concatenate(
            [inp["x_prompt"][c], inp["x_sample"][sb].reshape(NSB * TS, D)], 0), dtype=np.float32)
        m["ckvc"] = np.ascontiguousarray(inp["cache_mla_ckv"][:, sb])
        m["krc"] = np.ascontiguousarray(inp["cache_mla_krope"][:, sb])
        s5 = inp["state_s5"][:, sb]
        m["s5st"] = np.ascontiguousarray(s5.transpose(0, 1, 4, 3, 2).reshape(L, NSB, 128, 16))
        rw = inp["state_rwkv"][:, sb].reshape(L, NSB, 3, 2, 64, 64)
        m["rwst"] = np.ascontiguousarray(rw.transpose(0, 1, 3, 5, 2, 4).reshape(L, NSB, 128, 3, 64))
        sh = inp["state_rwkv_shift"][:, sb].reshape(L, NSB, 10, 128)
        m["shst"] = np.ascontiguousarray(sh.transpose(0, 1, 3, 2))
        in_maps.append(m)
    res = run_bass_kernel_spmd(nc, in_maps, core_ids=list(range(n_cores)))
    return res.results


def assemble(rs, T):
    nb = len(rs)
    cat = lambda f: np.stack([f(r) for r in rs], 0)
    y = cat(lambda r: r["y"])
    y_p = y[:, :T]
    y_s = y[:, T:].reshape(nb * NSB, TS, D)

    def tok(name, w):
        a = cat(lambda r: r[name])
        p = a[:, :, :T].transpose(1, 0, 2, 3)
        s = a[:, :, T:].reshape(nb, L, NSB, TS, w).transpose(1, 0, 2, 3, 4).reshape(L, nb * NSB, TS, w)
        return np.ascontiguousarray(p), np.ascontiguousarray(s)

    ckv_p, ckv_s = tok("o_ckv", 128)
    kr_p, kr_s = tok("o_kr", 32)

    def st(name, conv):
        a = cat(lambda r: r[name])
        a = conv(a)
        p = a[:, :, 0].transpose(1, 0, *range(2, a.ndim - 1))
        s = a[:, :, 1:].transpose(1, 0, *range(2, a.ndim))
        s = s.reshape((L, nb * NSB) + s.shape[3:])
        return np.ascontiguousarray(p), np.ascontiguousarray(s)

    s5_p, s5_s = st("o_s5", lambda a: a.reshape(nb, L, 3, 2, 64, 16).transpose(0, 1, 2, 5, 4, 3))
    rw_p, rw_s = st("o_rw", lambda a: a.reshape(nb, L, 3, 2, 64, 3, 64).transpose(0, 1, 2, 5, 3, 6, 4)
                    .reshape(nb, L, 3, 6, 64, 64))
    sh_p, sh_s = st("o_sh", lambda a: a.transpose(0, 1, 2, 4, 3).reshape(nb, L, 3, 1, 1280))
    return (np.ascontiguousarray(y_p), np.ascontiguousarray(y_s), ckv_p, kr_p, s5_p, rw_p, sh_p,
            ckv_s, kr_s, s5_s, rw_s, sh_s)


def kernel(**inputs):
    inp = {k: np.asarray(v) for k, v in inputs.items()}
    T = inp["x_prompt"].shape[1]
    PAST = inp["cache_mla_ckv"].shape[2]
    nb = inp["x_prompt"].shape[0]
    rs = run_cores(inp, T, PAST, nb)
    return assemble(rs, T)
```

```python
import math
import numpy as np
import concourse.bass as bass
import concourse.mybir as mybir
from concourse.bass_utils import run_bass_kernel_spmd

F32 = mybir.dt.float32
BF16 = mybir.dt.bfloat16
AF = mybir.ActivationFunctionType
ALU = mybir.AluOpType
AX = mybir.AxisListType

D = 1024
L = 2
NH = 6
SCALE = 96 ** -0.5
ALPHA = (2 * L) ** 0.25
TS = 32
NSB = 2
NCOL = 2112
GROUPS = [(0, 128), (128, 128), (256, 128), (384, 96), (480, 96), (576, 128), (704, 128)] + \
         [(832 + 128 * i, 128) for i in range(10)]
TWO_PI = 2.0 * math.pi


class Trk:
    __slots__ = ("w", "r")

    def __init__(self):
        self.w = None
        self.r = {}


class Buf:
    def __init__(self, ap, trk=None):
        self.ap = ap
        self.k = trk if trk is not None else Trk()

    def __getitem__(self, key):
        return self.ap[key]


class Ring:
    def __init__(self, bufs):
        self.bufs = bufs
        self.i = 0

    def next(self):
        b = self.bufs[self.i]
        self.i = (self.i + 1) % len(self.bufs)
        return b


class Prog:
    def __init__(self, nc, ndma=8):
        self.nc = nc
        self.es = {}
        self.sems = {}
        self._ctx = []
        for nm, eng in (("pe", nc.tensor), ("act", nc.scalar), ("dve", nc.vector), ("pool", nc.gpsimd),
                        ("sp", nc.sync)):
            cm = nc.semaphore("s_" + nm)
            self.sems[nm] = cm.__enter__()
            self._ctx.append(cm)
            self.es[nm] = dict(eng=eng, cnt=0, known={})
        self.snap = {}
        self.rings = {}
        for q in ("sp", "pool", "act"):
            ring = []
            for i in range(ndma):
                key = "d_%s%d" % (q, i)
                cm = nc.semaphore(key)
                self.sems[key] = cm.__enter__()
                self._ctx.append(cm)
                ring.append([key, 0])
            self.rings[q] = dict(ring=ring, nxt=0)

    def close(self):
        for cm in reversed(self._ctx):
            cm.__exit__(None, None, None)

    def _needs(self, r, w):
        needs = {}
        for b in r:
            t = b.k
            if t.w is not None:
                k, v = t.w
                if needs.get(k, 0) < v:
                    needs[k] = v
        for b in w:
            t = b.k
            if t.w is not None:
                k, v = t.w
                if needs.get(k, 0) < v:
                    needs[k] = v
            for k, v in t.r.items():
                if needs.get(k, 0) < v:
                    needs[k] = v
        return needs

    def _waits(self, en, needs):
        E = self.es[en]
        out = []
        for k, v in needs.items():
            if k == en and v > E["cnt"]:
                continue
            if E["known"].get(k, 0) < v:
                E["known"][k] = v
                out.append((k, v))
        for k, v in out:
            sn = self.snap.get(k, {}).get(v)
            if sn:
                for k2, v2 in sn.items():
                    if k2 != en and E["known"].get(k2, 0) < v2:
                        E["known"][k2] = v2
        return out

    def _mark(self, tok, r, w):
        for b in w:
            b.k.w = tok
            b.k.r = {}
        for b in r:
            if b.k.r.get(tok[0], 0) < tok[1]:
                b.k.r[tok[0]] = tok[1]

    def op(self, en, fn, r=(), w=(), inc=True):
        E = self.es[en]
        waits = self._waits(en, self._needs(r, w))
        for k, v in waits[:-1]:
            E["eng"].wait_ge(self.sems[k], v)
        ins = fn()
        if waits:
            k, v = waits[-1]
            ins._wait_ge(self.sems[k], v)
        if inc:
            E["cnt"] += 1
            ins.then_inc(self.sems[en], 1)
            tok = (en, E["cnt"])
            self.snap.setdefault(en, {})[E["cnt"]] = dict(E["known"])
        else:
            tok = (en, E["cnt"] + 1)
        self._mark(tok, r, w)
        return ins

    def dma(self, q, out, in_, r=(), w=(), **kw):
        E = self.es[q]
        R = self.rings[q]
        slot = R["ring"][R["nxt"]]
        R["nxt"] = (R["nxt"] + 1) % len(R["ring"])
        needs = self._needs(r, w)
        if slot[1] > 0:
            needs[slot[0]] = max(needs.get(slot[0], 0), slot[1])
        for k, v in self._waits(q, needs):
            E["eng"].wait_ge(self.sems[k], v)
        slot[1] += 16
        ins = E["eng"].dma_start(out=out, in_=in_, **kw)
        ins.then_inc(self.sems[slot[0]], 16)
        self.snap.setdefault(slot[0], {})[slot[1]] = dict(E["known"])
        self._mark((slot[0], slot[1]), r, w)
        return ins

    def barrier(self):
        targets = {}
        for en, E in self.es.items():
            if E["cnt"] > 0:
                targets[en] = E["cnt"]
        for q, R in self.rings.items():
            for k, v in R["ring"]:
                if v > 0:
                    targets[k] = v
        for en, E in self.es.items():
            for k, v in targets.items():
                if k != en and E["known"].get(k, 0) < v:
                    E["known"][k] = v
                    E["eng"].wait_ge(self.sems[k], v)


class Builder:
    def __init__(self, T, PAST):
        self.T = T
        self.PAST = PAST
        self.TT = T + NSB * TS
        self.KS = PAST + TS
        self.KTOT = T + NSB * self.KS
        self.seqs = [(0, T, 0, T, 0)]
        for s in range(NSB):
            self.seqs.append((T + s * TS, TS, T + s * self.KS, self.KS, PAST))
        self.tiles = [(i * 512, 512) for i in range(T // 512)] + [(T, NSB * TS)]
        nc = bass.Bass("TRN2", target_bir_lowering=False)
        self.nc = nc
        self.P = Prog(nc)
        self._cms = []
        self.evi = 0

    def sb(self, name, shape, dt=F32):
        self.uid = getattr(self, "uid", 0) + 1
        name = "%s_u%d" % (name, self.uid)
        cm = self.nc.sbuf_tensor(name, list(shape), dt)
        t = cm.__enter__()
        self._cms.append(cm)
        return Buf(t)

    def ps(self, name, shape=(128, 512), dt=F32):
        self.uid = getattr(self, "uid", 0) + 1
        name = "%s_u%d" % (name, self.uid)
        cm = self.nc.psum_tensor(name, list(shape), dt)
        t = cm.__enter__()
        self._cms.append(cm)
        return Buf(t)

    def mark(self):
        return len(self._cms)

    def release(self, m):
        while len(self._cms) > m:
            self._cms.pop().__exit__(None, None, None)

    def dram(self, name, shape, dt=F32, kind="Internal"):
        return Buf(self.nc.dram_tensor(name, list(shape), dt, kind=kind).ap())

    def mm(self, out, lhsT, rhs, start, stop, r, w, inc=None):
        nc = self.nc
        if inc is None:
            inc = stop
        return self.P.op("pe", lambda: nc.tensor.matmul(out, lhsT, rhs, start=start, stop=stop), r, w, inc)

    def tr(self, out, in_, ident, r, w, inc=True):
        nc = self.nc
        return self.P.op("pe", lambda: nc.tensor.transpose(out, in_, ident), r, w, inc)

    def act(self, out, in_, func, r, w, bias=None, scale=None, accum_out=None):
        nc = self.nc
        kw = {}
        if bias is not None:
            kw["bias"] = bias
        if scale is not None:
            kw["scale"] = scale
        if accum_out is not None:
            kw["accum_out"] = accum_out
        return self.P.op("act", lambda: nc.scalar.activation(out=out, in_=in_, func=func, **kw), r, w)

    def _ve(self, en):
        return self.nc.vector if en == "dve" else self.nc.gpsimd

    def cp(self, en, out, in_, r, w):
        if en == "act":
            return self.act(out, in_, AF.Copy, r, w)
        e = self._ve(en)
        return self.P.op(en, lambda: e.tensor_copy(out, in_), r, w)

    def tt(self, en, out, a, b, op, r, w):
        e = self._ve(en)
        return self.P.op(en, lambda: e.tensor_tensor(out, a, b, op), r, w)

    def tsc(self, en, out, a, s1, s2, op0, op1, r, w):
        e = self._ve(en)
        if op1 is None:
            return self.P.op(en, lambda: e.tensor_scalar(out, a, s1, None, op0), r, w)
        return self.P.op(en, lambda: e.tensor_scalar(out, a, s1, s2, op0, op1), r, w)

    def stt(self, en, out, a, s, b, op0, op1, r, w):
        en = "dve"
        e = self._ve(en)
        return self.P.op(en, lambda: e.scalar_tensor_tensor(out, a, s, b, op0, op1), r, w)

    def mset(self, en, out, val, w):
        e = self._ve(en)
        return self.P.op(en, lambda: e.memset(out, val), (), w)

    def ev(self):
        self.evi ^= 1
        return "act" if self.evi else "dve"

    def dma(self, q, out, in_, r, w, **kw):
        return self.P.dma(q, out, in_, r, w, **kw)

    def load_bf16(self, dst_ap, dst_buf, src_ap, shape, q="sp"):
        p, f = shape
        st = self.stage.next()
        self.dma(q, st[:p, :f], src_ap, r=[], w=[st])
        self.cvi = getattr(self, "cvi", 0) ^ 1
        self.cp("dve" if self.cvi else "act", dst_ap, st[:p, :f], r=[st], w=[dst_buf])


def make_consts(PAST):
    c = {}
    i = np.arange(128)
    c["ident"] = np.eye(128, dtype=np.float32)
    c["ones"] = np.ones((128, 128), np.float32)
    c["bones"] = (i[:, None] // 64 == i[None, :] // 64).astype(np.float32)
    c["maskS"] = (i[None, :] > i[:, None]).astype(np.float32)
    c["maskI"] = (i[None, :] >= i[:, None]).astype(np.float32)
    c["maskL"] = -(i[None, :] < i[:, None]).astype(np.float32)
    c["m4"] = np.concatenate([-c["maskS"], c["maskS"], c["maskI"], c["maskI"]], 1)
    sel = np.zeros((128, 64), np.float32)
    sel[64 + np.arange(64), np.arange(64)] = 1.0
    c["sel"] = sel
    c["swap"] = (i[None, :] == (i[:, None] + 64) % 128).astype(np.float32)
    c["iota"] = np.tile(np.arange(512, dtype=np.float32)[None, :], (128, 1))
    c["spos"] = np.tile((PAST + np.arange(NSB * TS) % TS).astype(np.float32)[None, :], (128, 1))
    c["gm"] = (i[:, None] // 16 == np.arange(8)[None, :]).astype(np.float32)
    E = np.zeros((128, 2, 128), np.float32)
    for g in range(16):
        E[g, g // 8, (g % 8) * 16:(g % 8) * 16 + 16] = 1.0
    c["E"] = E.reshape(128, 256)
    invf = np.zeros((128, 1), np.float32)
    f = (10000.0 ** (-np.arange(0, 32, 2, dtype=np.float32) / 32)).astype(np.float32)
    invf[64:96, 0] = np.concatenate([f, f])
    c["invf"] = invf
    sgn = np.zeros((128, 1), np.float32)
    sgn[64:80] = -1.0
    sgn[80:96] = 1.0
    c["sgn"] = sgn
    sg2 = np.ones((128, 2), np.float32)
    sg2[64:, 0] = -1.0
    sg2[:, 1] = -sg2[:, 0]
    c["sg2"] = sg2
    offs = {}
    o = 0
    parts = []
    for k, v in c.items():
        offs[k] = (o, v.shape[1])
        o += v.shape[1]
        parts.append(v)
    return np.ascontiguousarray(np.concatenate(parts, 1)), offs


def build_program(T, PAST, cst_w, offs, dbg=None):
    B = Builder(T, PAST)
    nc, P = B.nc, B.P
    TT, KTOT, KS = B.TT, B.KTOT, B.KS
    _ncd = nc.allow_non_contiguous_dma("small strided state / layout transfers")
    _ncd.__enter__()
    ext = lambda name, shape: Buf(nc.dram_tensor(name, list(shape), F32, kind="ExternalInput").ap())
    out_ = lambda name, shape: Buf(nc.dram_tensor(name, list(shape), F32, kind="ExternalOutput").ap())
    xin = ext("xin", [TT, D])
    ckvc = ext("ckvc", [L, NSB, PAST, 128])
    krc = ext("krc", [L, NSB, PAST, 32])
    s5st = ext("s5st", [L, NSB, 128, 16])
    rwst = ext("rwst", [L, NSB, 128, 3, 64])
    shst = ext("shst", [L, NSB, 128, 10])
    cst = ext("cst", [128, cst_w])
    win = ext("win", [L, 128, 8, NCOL])
    wqb = ext("wqb", [L, 128, 2, 576])
    wqbs = ext("wqbs", [L, 128, 2, 576])
    wkvk = ext("wkvk", [L, 128, 384])
    wkvv = ext("wkvv", [L, 128, 384])
    wout = ext("wout", [L, 128, 8, 1024])
    wup = ext("wup", [L, 32, 128, 1024])
    wdn = ext("wdn", [L, 128, 32, 1024])
    vec = ext("vec", [L, 128, 64])
    lnp = ext("lnp", [L, 4, 128, 1024])
    lora = ext("lora", [L, 128, 384])
    s5v = ext("s5v", [L, 16, 192])
    s5b = ext("s5b", [L, 2, 128, 2, 64])
    s5c = ext("s5c", [L, 2, 64, 256])
    wglu = ext("wglu", [L, 128, 2, 256])
    y = out_("y", [TT, D])
    o_ckv = out_("o_ckv", [L, TT, 128])
    o_kr = out_("o_kr", [L, TT, 32])
    o_s5 = out_("o_s5", [L, 3, 128, 16])
    o_rw = out_("o_rw", [L, 3, 128, 3, 64])
    o_sh = out_("o_sh", [L, 3, 128, 10])
    ROPE = B.dram("ROPE", [4, 128, TT])
    QT = [B.dram("QT%d" % l, [97, NH, TT], BF16) for l in range(L)]
    KT = [B.dram("KT%d" % l, [96, NH, KTOT], BF16) for l in range(L)]
    NKT = (KTOT + 127) // 128 + 4
    VA = [B.dram("VA%d" % l, [NKT, 128, 384], BF16) for l in range(L)]
    UT = [B.dram("UT%d" % l, [256, TT]) for l in range(L)]
    PT = [B.dram("PT%d" % l, [1280, TT]) for l in range(L)]
    MT = [B.dram("MT%d" % l, [1024, TT], BF16) for l in range(L)]
    X1 = [B.dram("X1_%d" % l, [TT, D]) for l in range(L)]
    X1T = [B.dram("X1T%d" % l, [1024, TT], BF16) for l in range(L)]
    XN = [B.dram("XN%d" % l, [TT, D]) for l in range(L - 1)] + [y]
    WUPB = B.dram("WUPB", [L, 32, 128, 1024], BF16)
    KMX = B.dram("KMX", [L, 128, NH])

    ktiles = []
    vt = 0
    for (r0, n, k0, nk, p0) in B.seqs:
        lst = []
        c = 0
        while c < nk:
            m = min(128, nk - c)
            lst.append((k0 + c, m, vt))
            vt += 1
            c += m
        ktiles.append(lst)

    C = B.sb("cst", [128, cst_w])
    P.dma("sp", C[:, :], cst[:, :], r=[], w=[C])
    cs = lambda k: C[:, offs[k][0]:offs[k][0] + offs[k][1]]
    Cb = B.sb("cstb", [128, 640], BF16)
    B.cp("dve", Cb[:, 0:128], cs("ident"), r=[C], w=[Cb])
    B.cp("dve", Cb[:, 128:256], cs("ones"), r=[C], w=[Cb])
    B.cp("dve", Cb[:, 256:384], cs("bones"), r=[C], w=[Cb])
    identb, onesb, bonesb = Cb[:, 0:128], Cb[:, 128:256], Cb[:, 256:384]
    ident = cs("ident")
    B.stage = Ring([B.sb("stage%d" % i, [128, 2112]) for i in range(2)])

    EPS = {256 * 1e-6: 0, 128 * 1e-6: 1, 1e-24: 2, 64e-5: 3, 1e-5: 4, 0.0: 5}
    epsb = B.sb("epsb", [128, 8])
    for v_, i_ in EPS.items():
        B.mset("dve", epsb[:, i_:i_ + 1], float(v_), w=[epsb])
    I32 = mybir.dt.int32

    def sqrt_pow(en, out, in_, add, expo, r, w, p0=0):
        i_ = EPS[add]
        B.act(out, in_, AF.Sqrt, r=list(r) + [epsb], w=w, bias=epsb[p0:p0 + out.shape[0], i_:i_ + 1])
        if expo < 0:
            P.op("dve", lambda: nc.vector.reciprocal(out, out), r=w, w=w)

    def _p0(ap):
        return ap.base_partition()

    def sincos(en, out_s, out_c, ang, wk, r, w):
        it_ap, ft_ap, wb = wk
        for o, sh in ((out_s, 0.0), (out_c, 0.25)):
            B.tsc(en, o, ang, 1.0 / TWO_PI, sh, ALU.mult, ALU.add, r=r, w=w)
            B.cp(en, it_ap, o, r=w, w=wb)
            B.cp(en, ft_ap, it_ap, r=wb, w=wb)
            B.tt(en, o, o, ft_ap, ALU.subtract, r=w + wb, w=w)
            B.act(o, o, AF.Sin, r=w, w=w, scale=TWO_PI * (1.0 - 1e-6))

    m0 = B.mark()
    wk = Ring([B.sb("r0_%d" % i, [128, 512]) for i in range(6)])
    r0i = B.sb("r0i", [128, 512], mybir.dt.int32)
    r0f = B.sb("r0f", [128, 512])
    r0w = Buf(None)
    for (t0, n) in B.tiles:
        pos = wk.next()
        if n == 512:
            B.tsc("pool", pos[64:96, :n], cs("iota")[64:96, :n], float(t0), None, ALU.add, None, r=[C], w=[pos])
        else:
            B.cp("pool", pos[64:96, :n], cs("spos")[64:96, :n], r=[C], w=[pos])
        B.tsc("dve", pos[64:96, :n], pos[64:96, :n], cs("invf")[64:96, :], None, ALU.mult, None, r=[pos, C], w=[pos])
        res = {"sin": wk.next(), "cos": wk.next()}
        sincos("dve", res["sin"][64:96, :n], res["cos"][64:96, :n], pos[64:96, :n],
               (r0i[64:96, :n], r0f[64:96, :n], [r0w]), r=[pos], w=[res["sin"], res["cos"]])
        B.tsc("dve", res["sin"][64:96, :n], res["sin"][64:96, :n], cs("sgn")[64:96, :], None, ALU.mult, None,
              r=[res["sin"], C], w=[res["sin"]])
        for i, nm in enumerate(("cos", "sin")):
            a = res[nm]
            P.dma("sp", ROPE[2 + i, 64:96, t0:t0 + n], a[64:96, :n], r=[a], w=[ROPE])
            q = wk.next()
            B.tsc("pool", q[64:96, :n], a[64:96, :n], SCALE, None, ALU.mult, None, r=[a], w=[q])
            P.dma("sp", ROPE[i, 64:96, t0:t0 + n], q[64:96, :n], r=[q], w=[ROPE])
    P.barrier()
    B.release(m0)

    def phase1(l):
        m1 = B.mark()
        psum = Ring([B.ps("ps%d" % i) for i in range(8)])
        Wi = B.sb("Wi", [128, 8, NCOL], BF16)
        for kc in range(8):
            B.load_bf16(Wi[:, kc, :], Wi, win[l, :, kc, :], [128, NCOL], q="sp" if kc % 2 else "act")
        Wq = B.sb("Wq", [128, 2, 576], BF16)
        Wqs = B.sb("Wqs", [128, 2, 576], BF16)
        B.load_bf16(Wq[:, :, :].rearrange("p a b -> p (a b)"), Wq, wqb[l].rearrange("p a b -> p (a b)"), [128, 1152])
        B.load_bf16(Wqs[:, :, :].rearrange("p a b -> p (a b)"), Wqs, wqbs[l].rearrange("p a b -> p (a b)"), [128, 1152])
        Wk = B.sb("Wk", [128, 384], BF16)
        Wv = B.sb("Wv", [128, 384], BF16)
        B.load_bf16(Wk[:, :], Wk, wkvk[l], [128, 384])
        B.load_bf16(Wv[:, :], Wv, wkvv[l], [128, 384])
        V = B.sb("vec", [128, 64])
        P.dma("sp", V[:, :], vec[l], r=[], w=[V])
        g16 = B.sb("g16", [128, 4])
        B.tsc("dve", g16[:, 0:2], V[:, 0:2], 16.0, None, ALU.mult, None, r=[V], w=[g16])
        B.tsc("dve", g16[:, 2:3], V[:, 2:3], math.sqrt(128.0), None, ALU.mult, None, r=[V], w=[g16])
        kmx = B.sb("kmx", [128, NH, 512])
        B.mset("pool", kmx[:, :, :], 0.0, w=[kmx])
        xts = Ring([B.sb("xt%d" % i, [128, 4, D]) for i in range(2)])
        xTs = Ring([B.sb("xT%d" % i, [128, 8, 512], BF16) for i in range(2)])
        ql = B.sb("ql", [128, 2, 512])
        sq = Ring([B.sb("sq%d" % i, [128, 512], BF16) for i in range(3)])
        rin = Ring([B.sb("rin%d" % i, [128, 512]) for i in range(2)])
        qn = B.sb("qn", [128, 2, 512], BF16)
        qT = Ring([B.sb("qT%d" % i, [128, NH, 512], BF16) for i in range(1)])
        kT = Ring([B.sb("kT%d" % i, [128, NH, 512], BF16) for i in range(1)])
        va = Ring([B.sb("va%d" % i, [128, 4, 384], BF16) for i in range(2)])
        rt = Ring([B.sb("rt%d" % i, [128, 4, 512]) for i in range(1)])
        tmp = Ring([B.sb("tmp%d" % i, [128, 512]) for i in range(4)])
        kvl = B.sb("kvl", [128, 512])
        ckvT = B.sb("ckvT", [128, 512])
        ckvTb = Ring([B.sb("ckvTb%d" % i, [128, 512], BF16) for i in range(2)])
        krT = B.sb("krT", [128, 512])
        ot = Ring([B.sb("ot%d" % i, [128, 4, 128]) for i in range(2)])
        ot2 = Ring([B.sb("ot2%d" % i, [128, 4, 32]) for i in range(2)])
        ut = Ring([B.sb("ut%d" % i, [128, 2, 512]) for i in range(1)])
        pt = Ring([B.sb("pt%d" % i, [128, 5, 512]) for i in range(1)])
        Xsrc = xin if l == 0 else XN[l - 1]

        def kv_expand(cb, kr, n, seq_parts):
            k_ = kT.next()
            v_ = va.next()
            for h in range(NH):
                ps = psum.next()
                B.mm(ps[0:64, :n], Wk[:, h * 64:(h + 1) * 64], cb[:, :n], True, True, r=[Wk, cb], w=[ps])
                B.cp(B.ev(), k_[0:64, h, :n], ps[0:64, :n], r=[ps], w=[k_])
                B.cp("pool", k_[64:96, h, :n], kr[64:96, :n], r=[kr, k_], w=[k_])
            for h in range(NH):
                s_ = sq.next()
                B.tt("pool", s_[0:96, :n], k_[0:96, h, :n], k_[0:96, h, :n], ALU.mult, r=[k_], w=[s_])
                ps = psum.next()
                B.mm(ps[0:97, :n], onesb[0:96, 0:97], s_[0:96, :n], True, True, r=[Cb, s_], w=[ps])
                B.tt("dve", kmx[0:97, h, :n], kmx[0:97, h, :n], ps[0:97, :n], ALU.max, r=[ps, kmx], w=[kmx])
            ss = min(128, n)
            for s in range((n + 127) // 128):
                ps = psum.next()
                B.mm(ps[:ss, 0:384], cb[:, s * ss:(s + 1) * ss], Wv[:, :], True, True, r=[cb, Wv], w=[ps])
                B.cp(B.ev(), v_[:ss, s, :], ps[:ss, 0:384], r=[ps], w=[v_])
            for (c0, ncol, kcol, vparts) in seq_parts:
                P.dma("pool", KT[l][:, :, kcol:kcol + ncol], k_[0:96, :, c0:c0 + ncol], r=[k_], w=[KT[l]])
                for (vti, s, row0, rows) in vparts:
                    P.dma("pool", VA[l][vti, 0:rows, :], v_[row0:row0 + rows, s, :], r=[v_], w=[VA[l]])

        pre1 = {}

        def prefetch1(ti):
            t0, n = B.tiles[ti]
            ss = min(128, n)
            nsub = n // ss
            xt = xts.next()
            P.dma("sp", xt[:ss, :nsub, :], Xsrc[t0:t0 + n, :].rearrange("(s p) d -> p s d", p=ss), r=[Xsrc], w=[xt])
            pre1[ti] = xt

        prefetch1(0)
        for ti, (t0, n) in enumerate(B.tiles):
            ss = min(128, n)
            nsub = n // ss
            if ti + 1 < len(B.tiles):
                prefetch1(ti + 1)
            xt = pre1.pop(ti)
            r_ = rt.next()
            P.dma("sp", r_[64:96, :, :n], ROPE[:, 64:96, t0:t0 + n].rearrange("a p n -> p a n"), r=[ROPE], w=[r_])
            xT = xTs.next()
            for kc in range(8):
                ps = psum.next()
                for s in range(nsub):
                    B.tr(ps[:, s * ss:(s + 1) * ss], xt[:ss, s, kc * 128:(kc + 1) * 128], ident[:ss, :ss],
                         r=[xt, C], w=[ps], inc=(s == nsub - 1))
                B.cp(B.ev(), xT[:, kc, :n], ps[:, :n], r=[ps], w=[xT])
            grp = []
            for gi, (c0, M) in enumerate(GROUPS):
                ps = psum.next()
                for kc in range(8):
                    B.mm(ps[:M, :n], Wi[:, kc, c0:c0 + M], xT[:, kc, :n], kc == 0, kc == 7, r=[Wi, xT], w=[ps])
                if gi < 2:
                    B.cp("act", ql[:, gi, :n], ps[:, :n], r=[ps], w=[ql])
                    if gi == 1:
                        ps2 = psum.next()
                        for j in range(2):
                            s_ = sq.next()
                            B.tt("pool", s_[:, :n], ql[:, j, :n], ql[:, j, :n], ALU.mult, r=[ql], w=[s_])
                            B.mm(ps2[:, :n], onesb, s_[:, :n], j == 0, j == 1, r=[Cb, s_], w=[ps2])
                        ri = rin.next()
                        sqrt_pow("dve", ri[:, :n], ps2[:, :n], 256 * 1e-6, -0.5, r=[ps2], w=[ri])
                        for j in range(2):
                            B.stt("dve", qn[:, j, :n], ql[:, j, :n], g16[:, j:j + 1], ri[:, :n], ALU.mult, ALU.mult,
                                  r=[ql, g16, ri], w=[qn])
                        q_ = qT.next()
                        for h in range(NH):
                            pa, pb = psum.next(), psum.next()
                            for j in range(2):
                                B.mm(pa[0:96, :n], Wq[:, j, h * 96:(h + 1) * 96], qn[:, j, :n], j == 0, j == 1,
                                     r=[Wq, qn], w=[pa])
                            for j in range(2):
                                B.mm(pb[0:96, :n], Wqs[:, j, h * 96:(h + 1) * 96], qn[:, j, :n], j == 0, j == 1,
                                     r=[Wqs, qn], w=[pb])
                            B.act(q_[0:64, h, :n], pa[0:64, :n], AF.Copy, r=[pa], w=[q_], scale=SCALE)
                            t1, t2 = tmp.next(), tmp.next()
                            B.tt("dve", t1[64:96, :n], pa[64:96, :n], r_[64:96, 0, :n], ALU.mult, r=[pa, r_], w=[t1])
                            B.tt("dve", t2[64:96, :n], pb[64:96, :n], r_[64:96, 1, :n], ALU.mult, r=[pb, r_], w=[t2])
                            B.tt("pool", q_[64:96, h, :n], t1[64:96, :n], t2[64:96, :n], ALU.add, r=[t1, t2, q_], w=[q_])
                            s_ = sq.next()
                            B.tt("pool", s_[0:96, :n], q_[0:96, h, :n], q_[0:96, h, :n], ALU.mult, r=[q_], w=[s_])
                            ps3 = psum.next()
                            B.mm(ps3[0:97, :n], onesb[0:96, 0:97], s_[0:96, :n], True, True, r=[Cb, s_], w=[ps3])
                            t3 = tmp.next()
                            sqrt_pow("dve", t3[96:97, :n], ps3[96:97, :n], 0.0, 0.5, r=[ps3], w=[t3], p0=96)
                            B.tsc("dve", q_[96:97, h, :n], t3[96:97, :n], -1.0, None, ALU.mult, None, r=[t3, q_], w=[q_])
                        P.dma("pool", QT[l][:, :, t0:t0 + n], q_[0:97, :, :n], r=[q_], w=[QT[l]])
                elif gi == 2:
                    B.cp("act", kvl[:, :n], ps[:, :n], r=[ps], w=[kvl])
                    s_ = sq.next()
                    B.tt("pool", s_[:, :n], kvl[:, :n], kvl[:, :n], ALU.mult, r=[kvl], w=[s_])
                    ps2 = psum.next()
                    B.mm(ps2[:, :n], onesb, s_[:, :n], True, True, r=[Cb, s_], w=[ps2])
                    ri = rin.next()
                    sqrt_pow("dve", ri[:, :n], ps2[:, :n], 128 * 1e-6, -0.5, r=[ps2], w=[ri])
                    B.stt("dve", ckvT[:, :n], kvl[:, :n], g16[:, 2:3], ri[:, :n], ALU.mult, ALU.mult,
                          r=[kvl, g16, ri], w=[ckvT])
                    cb = ckvTb.next()
                    B.cp("pool", cb[:, :n], ckvT[:, :n], r=[ckvT], w=[cb])
                    ps2 = psum.next()
                    for s in range(nsub):
                        B.tr(ps2[:ss, s * 128:(s + 1) * 128], ckvT[:, s * ss:(s + 1) * ss], ident, r=[ckvT, C],
                             w=[ps2], inc=(s == nsub - 1))
                    o_ = ot.next()
                    B.cp("act", o_[:ss, :nsub, :], ps2[:ss, 0:nsub * 128].rearrange("p (s d) -> p s d", d=128),
                         r=[ps2], w=[o_])
                    P.dma("pool", o_ckv[l, t0:t0 + n, :].rearrange("(s p) d -> p s d", p=ss), o_[:ss, :nsub, :],
                          r=[o_], w=[o_ckv])
                elif gi == 3:
                    pkr = ps
                elif gi == 4:
                    t1, t2 = tmp.next(), tmp.next()
                    B.tt("dve", t1[64:96, :n], pkr[64:96, :n], r_[64:96, 2, :n], ALU.mult, r=[pkr, r_], w=[t1])
                    B.tt("dve", t2[64:96, :n], ps[64:96, :n], r_[64:96, 3, :n], ALU.mult, r=[ps, r_], w=[t2])
                    B.tt("pool", krT[64:96, :n], t1[64:96, :n], t2[64:96, :n], ALU.add, r=[t1, t2], w=[krT])
                    ps2 = psum.next()
                    for s in range(nsub):
                        B.tr(ps2[:ss, s * 32:(s + 1) * 32], krT[64:96, s * ss:(s + 1) * ss], ident[64:96, 64:96],
                             r=[krT, C], w=[ps2], inc=(s == nsub - 1))
                    o_ = ot2.next()
                    B.cp("act", o_[:ss, :nsub, :], ps2[:ss, 0:nsub * 32].rearrange("p (s d) -> p s d", d=32),
                         r=[ps2], w=[o_])
                    P.dma("pool", o_kr[l, t0:t0 + n, :].rearrange("(s p) d -> p s d", p=ss), o_[:ss, :nsub, :],
                          r=[o_], w=[o_kr])
                    if n == 512:
                        vparts = [(ktiles[0][t0 // 128 + s][2], s, 0, 128) for s in range(4)]
                        parts = [(0, 512, t0, vparts)]
                    else:
                        parts = []
                        for si in range(NSB):
                            kc_, m_, vti = ktiles[1 + si][-1]
                            parts.append((si * TS, TS, kc_, [(vti, 0, si * TS, TS)]))
                    kv_expand(cb, krT, n, parts)
                elif gi < 7:
                    u_ = ut.next() if gi == 5 else u_
                    B.cp(B.ev(), u_[:, gi - 5, :n], ps[:, :n], r=[ps], w=[u_])
                    if gi == 6:
                        P.dma("pool", UT[l][:, t0:t0 + n].rearrange("(j p) n -> p j n", p=128), u_[:, :, :n],
                              r=[u_], w=[UT[l]])
                else:
                    hf_ = (gi - 7) // 5
                    p_ = pt.next() if (gi - 7) % 5 == 0 else p_
                    B.cp(B.ev(), p_[:, (gi - 7) % 5, :n], ps[:, :n], r=[ps], w=[p_])
                    if (gi - 7) % 5 == 4:
                        P.dma("pool", PT[l][hf_ * 640:(hf_ + 1) * 640, t0:t0 + n].rearrange("(j p) n -> p j n", p=128),
                              p_[:, :, :n], r=[p_], w=[PT[l]])
                        for si, (r0, nn, k0, nk, p0) in enumerate(B.seqs):
                            last = r0 + nn - 1
                            if t0 <= last < t0 + n:
                                P.dma("pool", o_sh[l, si, :, hf_ * 5:(hf_ + 1) * 5], p_[:, :, last - t0], r=[p_], w=[o_sh])
        cin = Ring([B.sb("cin%d" % i, [128, 4, 128]) for i in range(2)])
        kin = Ring([B.sb("kin%d" % i, [128, 4, 96]) for i in range(2)])
        for b in kin.bufs:
            B.mset("pool", b[:, :, :], 0.0, w=[b])
        for si in range(NSB):
            for c0 in range(0, PAST, 512):
                n = min(512, PAST - c0)
                nsub = n // 128
                ci, ki = cin.next(), kin.next()
                P.dma("sp", ci[:, :nsub, :], ckvc[l, si, c0:c0 + n, :].rearrange("(s p) d -> p s d", p=128), r=[], w=[ci])
                P.dma("act", ki[:, :nsub, 64:96], krc[l, si, c0:c0 + n, :].rearrange("(s p) d -> p s d", p=128), r=[], w=[ki])
                ps = psum.next()
                for s in range(nsub):
                    B.tr(ps[:, s * 128:(s + 1) * 128], ci[:, s, :], ident, r=[ci, C], w=[ps], inc=(s == nsub - 1))
                cb = ckvTb.next()
                B.cp(B.ev(), cb[:, :n], ps[:, :n], r=[ps], w=[cb])
                ps = psum.next()
                for s in range(nsub):
                    B.tr(ps[0:96, s * 128:(s + 1) * 128], ki[:, s, :], ident, r=[ki, C], w=[ps], inc=(s == nsub - 1))
                B.cp(B.ev(), krT[64:96, :n], ps[64:96, :n], r=[ps], w=[krT])
                kbase = B.seqs[1 + si][2]
                vparts = [(ktiles[1 + si][c0 // 128 + s][2], s, 0, 128) for s in range(nsub)]
                kv_expand(cb, krT, n, [(0, n, kbase + c0, vparts)])
        km = B.sb("km", [128, NH])
        P.op("dve", lambda: nc.vector.tensor_reduce(km[0:97, :], kmx[0:97, :, :], AX.X, ALU.max), r=[kmx], w=[km])
        sqrt_pow("dve", km[0:97, :], km[0:97, :], 0.0, 0.5, r=[km], w=[km])
        P.dma("sp", KMX[l, 0:97, :], km[0:97, :], r=[km], w=[KMX])
        P.barrier()
        B.release(m1)

    def phase2(l):
        m2 = B.mark()
        psS = Ring([B.ps("aS%d" % i, (128, 1024)) for i in range(3)])
        psO = Ring([B.ps("aO%d" % i) for i in range(1)])
        psL = Ring([B.ps("aL%d" % i) for i in range(1)])
        kmr = B.sb("kmr", [128, NH])
        P.dma("sp", kmr[0:97, :], KMX[l, 0:97, :], r=[KMX], w=[kmr])
        maxk = max(T, KS)
        maxt = (maxk + 127) // 128
        Kb = Ring([B.sb("Kb%d" % i, [128, maxk], BF16) for i in range(2)])
        for b in Kb.bufs:
            B.mset("pool", b[96:97, :], 1.0, w=[b])
        Vb = Ring([B.sb("Vb%d" % i, [128, maxt, 64], BF16) for i in range(2)])
        Qb = Ring([B.sb("Qb%d" % i, [128, T], BF16) for i in range(2)])
        ptr = Ring([B.sb("pt%d" % i, [128, 1024], BF16) for i in range(4)])
        rl = Ring([B.sb("rl%d" % i, [64, 512]) for i in range(2)])
        o32 = Ring([B.sb("o32_%d" % i, [64, 512]) for i in range(2)])
        mo = Ring([B.sb("mo%d" % i, [64, 512], BF16) for i in range(2)])
        its = [(h, si) for h in range(NH) for si in range(len(B.seqs))]
        NWARM = 0
        LOOK = 2
        loaded = {}

        def load(i):
            h, si = its[i]
            r0, nt, k0, nk, p0 = B.seqs[si]
            kb, vb, qb = Kb.next(), Vb.next(), Qb.next()
            kts = ktiles[si]
            P.dma("pool", kb[0:96, :nk], KT[l][:, h, k0:k0 + nk], r=[KT[l]], w=[kb])
            nfull = nk // 128
            P.dma("act", vb[:, :nfull, :],
                  VA[l][kts[0][2]:kts[0][2] + nfull, :, h * 64:(h + 1) * 64].rearrange("t p e -> p t e"),
                  r=[VA[l]], w=[vb])
            if nk % 128:
                P.dma("act", vb[:nk % 128, nfull, :], VA[l][kts[-1][2], 0:nk % 128, h * 64:(h + 1) * 64],
                      r=[VA[l]], w=[vb])
            P.dma("pool", qb[0:97, :nt], QT[l][:, h, r0:r0 + nt], r=[QT[l]], w=[qb])
            B.tsc("dve", qb[96:97, :nt], qb[96:97, :nt], kmr[96:97, h:h + 1], None, ALU.mult, None,
                  r=[qb, kmr], w=[qb])
            loaded[i] = (kb, vb, qb)

        load(0)
        for it, (h, si) in enumerate(its):
            r0, nt, k0, nk, p0 = B.seqs[si]
            kts = ktiles[si]
            if it + 1 < len(its):
                load(it + 1)
            kb, vb, qb = loaded.pop(it)
            pW = psS.next()
            for wi in range(NWARM):
                B.mm(pW[:, (wi % 2) * 512:(wi % 2) * 512 + 512], kb[0:97, 0:128], qb[0:97, 0:512] if nt >= 512 else
                     kb[0:97, 0:512], True, True, r=[kb, qb], w=[pW], inc=(wi == NWARM - 1))
            for q0 in range(0, nt, 512):
                nq = min(512, nt - q0)
                W = 512 if nq == 512 else nq
                G = 2 if nq == 512 else max(1, min(8, 1024 // nq))
                pO, pL = psO.next(), psL.next()
                groups = []
                for kt, (kcol, m, vti) in enumerate(kts):
                    if si == 0:
                        if kt * 128 >= q0 + nq:
                            break
                        off = max(0, kt * 128 - q0)
                        diag = (kt * 128 >= q0)
                    else:
                        off, diag = 0, False
                    plain = (not diag) and m == 128
                    if plain and groups and groups[-1][0] and len(groups[-1][1]) < G:
                        groups[-1][1].append((kt, m, off, diag))
                    else:
                        groups.append((plain, [(kt, m, off, diag)]))
                nblk = sum(len(g[1]) for g in groups)
                pend = {}

                def score(gi):
                    plain, tl = groups[gi]
                    pS, p_ = psS.next(), ptr.next()
                    for j, (kt, m, off, diag) in enumerate(tl):
                        B.mm(pS[:m, j * W + off:j * W + nq], kb[0:97, kt * 128:kt * 128 + m],
                             qb[0:97, q0 + off:q0 + nq], True, True, r=[kb, qb], w=[pS])
                    if plain:
                        B.act(p_[:, 0:len(tl) * W], pS[:, 0:len(tl) * W], AF.Exp, r=[pS], w=[p_])
                    else:
                        kt, m, off, diag = tl[0]
                        B.act(p_[:m, off:nq], pS[:m, off:nq], AF.Exp, r=[pS], w=[p_])
                        if diag:
                            B.mset("pool", p_[64:128, off:off + 64], 0.0, w=[p_])
                    sm_ = None
                    pend[gi] = (p_, sm_)

                for gi in range(min(LOOK, len(groups))):
                    score(gi)
                done = 0
                for gi, (plain, tl) in enumerate(groups):
                    if gi + LOOK < len(groups):
                        score(gi + LOOK)
                    p_, sm_ = pend.pop(gi)
                    for j, (kt, m, off, diag) in enumerate(tl):
                        first, last = done == 0, done == nblk - 1
                        B.mm(pO[0:64, off:nq], vb[:m, kt, :], p_[:m, j * W + off:j * W + nq], first, last,
                             r=[vb, p_], w=[pO])
                        if sm_ is None:
                            B.mm(pL[0:64, off:nq], onesb[:m, 0:64], p_[:m, j * W + off:j * W + nq], first, last,
                                 r=[Cb, p_], w=[pL])
                        elif j == 1:
                            B.mm(pL[0:64, 0:nq], onesb[:, 0:64], sm_[:, 0:nq], done == 1, last, r=[Cb, sm_], w=[pL])
                        done += 1
                r_, o3 = rl.next(), o32.next()
                B.cp("dve", r_[:, :nq], pL[0:64, :nq], r=[pL], w=[r_])
                B.cp("dve", o3[:, :nq], pO[0:64, :nq], r=[pO], w=[o3])
                P.op("dve", lambda: nc.vector.reciprocal(r_[:, :nq], r_[:, :nq]), r=[r_], w=[r_])
                o_ = mo.next()
                B.tt("pool", o_[:, :nq], o3[:, :nq], r_[:, :nq], ALU.mult, r=[o3, r_], w=[o_])
                P.dma("sp", MT[l][h * 64:(h + 1) * 64, r0 + q0:r0 + q0 + nq], o_[:, :nq], r=[o_], w=[MT[l]])
        P.barrier()
        B.release(m2)

    LT = 256

    def phase3(l):
        m3 = B.mark()
        pAB = Ring([B.ps("sA%d" % i) for i in range(4)])
        pY = [B.ps("sY%d" % i) for i in range(2)]
        pG = Ring([B.ps("sG%d" % i) for i in range(2)])
        V = B.sb("vec3", [128, 64])
        P.dma("sp", V[:, :], vec[l], r=[], w=[V])
        s5d, bgl = V[:, 3:5], V[:, 5:7]
        sv = B.sb("sv", [16, 192])
        P.dma("sp", sv[:, :], s5v[l], r=[], w=[sv])
        w16 = B.sb("w16", [16, 16, 64])
        W = lambda i: w16[:, i, :]
        K16 = [w16]
        lre, lim = sv[:, 0:64], sv[:, 64:128]
        B.act(W(0), sv[:, 128:192], AF.Exp, r=[sv], w=K16)
        B.tt("dve", W(1), lre, W(0), ALU.mult, r=[sv] + K16, w=K16)
        B.tt("dve", W(2), lim, W(0), ALU.mult, r=[sv] + K16, w=K16)
        B.act(W(3), W(1), AF.Exp, r=K16, w=K16)
        w16i = B.sb("w16i", [16, 64], mybir.dt.int32)
        sincos("dve", W(4), W(5), W(2), (w16i[:, :], W(13), K16), r=K16, w=K16)
        B.tt("dve", W(6), W(3), W(5), ALU.mult, r=K16, w=K16)
        B.tsc("dve", W(6), W(6), -1.0, None, ALU.add, None, r=K16, w=K16)
        B.tt("dve", W(7), W(3), W(4), ALU.mult, r=K16, w=K16)
        B.tt("dve", W(8), lre, lre, ALU.mult, r=[sv] + K16, w=K16)
        B.tt("dve", W(9), lim, lim, ALU.mult, r=[sv] + K16, w=K16)
        B.tt("dve", W(8), W(8), W(9), ALU.add, r=K16, w=K16)
        P.op("dve", lambda: nc.vector.reciprocal(W(8), W(8)), r=K16, w=K16)
        B.tt("dve", W(9), W(6), lre, ALU.mult, r=[sv] + K16, w=K16)
        B.tt("dve", W(10), W(7), lim, ALU.mult, r=[sv] + K16, w=K16)
        B.tt("dve", W(9), W(9), W(10), ALU.add, r=K16, w=K16)
        B.tt("dve", W(11), W(9), W(8), ALU.mult, r=K16, w=K16)
        B.tt("dve", W(9), W(7), lre, ALU.mult, r=[sv] + K16, w=K16)
        B.tt("dve", W(10), W(6), lim, ALU.mult, r=[sv] + K16, w=K16)
        B.tt("dve", W(9), W(9), W(10), ALU.subtract, r=K16, w=K16)
        B.tt("dve", W(12), W(9), W(8), ALU.mult, r=K16, w=K16)
        cat = B.sb("cat", [16, 2, 128])
        for i, src in enumerate((W(2), W(3))):
            B.cp("dve", cat[:, i, 0:64], src, r=K16, w=[cat])
            B.cp("dve", cat[:, i, 64:128], src, r=K16, w=[cat])
        thr = B.sb("thr", [128, 2, 16])
        for i in range(2):
            ps = pG.next()
            B.tr(ps[:, 0:16], cat[:, i, :], ident[0:16, 0:16], r=[cat, C], w=[ps])
            B.cp("dve", thr[:, i, :], ps[:, 0:16], r=[ps], w=[thr])
        thS, rS = thr[:, 0, :], thr[:, 1, :]
        ctab = B.sb("ctab", [128, 16, LT])
        stab = B.sb("stab", [128, 16, LT])
        rmat = B.sb("rmat", [128, 16, LT])
        stabS = B.sb("stabS", [128, 16, LT])
        io1 = B.sb("io1", [128, LT])
        B.tsc("dve", io1[:, :], cs("iota")[:, 0:LT], 1.0, None, ALU.add, None, r=[C], w=[io1])
        ang = B.sb("ang", [128, LT])
        angi = B.sb("angi", [128, LT], mybir.dt.int32)
        angf = B.sb("angf", [128, LT])
        for g in range(16):
            B.tsc("dve", ang[:, :], io1[:, :], thS[:, g:g + 1], None, ALU.mult, None, r=[io1, thr], w=[ang])
            sincos("dve", stab[:, g, :], ctab[:, g, :], ang[:, :], (angi[:, :], angf[:, :], [angf]), r=[ang],
                   w=[stab, ctab])
            B.tsc("pool", rmat[:, g, :], io1[:, :], 0.0, rS[:, g:g + 1], ALU.mult, ALU.add, r=[io1, thr], w=[rmat])
            B.tsc("pool", stabS[:, g, :], stab[:, g, :], cs("sg2")[:, 0:1], None, ALU.mult, None, r=[stab, C], w=[stabS])
        bb = B.sb("bb", [128, 2, 2, 64])
        P.dma("sp", bb[:, 0, :, :], s5b[l, 0], r=[], w=[bb])
        P.dma("sp", bb[:, 1, :, :], s5b[l, 1], r=[], w=[bb])
        Ff = B.sb("Ff", [128, 2, 2, 64])
        for i, fsrc in enumerate((W(11), W(12))):
            for gh in range(2):
                ps = pG.next()
                B.mm(ps[:, 0:64], cs("E")[0:16, gh * 128:(gh + 1) * 128], fsrc, True, True, r=[C] + K16, w=[ps])
                B.cp("dve", Ff[:, i, gh, :], ps[:, 0:64], r=[ps], w=[Ff])
        bbar = B.sb("bbar", [128, 2, 2, 64])
        t5 = B.sb("t5", [128, 2, 64])
        B.tt("dve", bbar[:, 0, :, :], Ff[:, 0, :, :], bb[:, 0, :, :], ALU.mult, r=[Ff, bb], w=[bbar])
        B.tt("dve", t5[:, :, :], Ff[:, 1, :, :], bb[:, 1, :, :], ALU.mult, r=[Ff, bb], w=[t5])
        B.tt("dve", bbar[:, 0, :, :], bbar[:, 0, :, :], t5[:, :, :], ALU.subtract, r=[bbar, t5], w=[bbar])
        B.tt("dve", bbar[:, 1, :, :], Ff[:, 0, :, :], bb[:, 1, :, :], ALU.mult, r=[Ff, bb], w=[bbar])
        B.tt("dve", t5[:, :, :], Ff[:, 1, :, :], bb[:, 0, :, :], ALU.mult, r=[Ff, bb, bbar], w=[t5])
        B.tt("dve", bbar[:, 1, :, :], bbar[:, 1, :, :], t5[:, :, :], ALU.add, r=[bbar, t5], w=[bbar])
        LB = B.sb("LB", [128, 2, 16, 128], BF16)
        for g in range(16):
            gh, g8 = g // 8, g % 8
            for sw in range(2):
                for half in range(2):
                    src = bbar[:, half ^ sw, gh, :]
                    B.tsc("pool" if half else "dve", LB[:, sw, g, half * 64:(half + 1) * 64], src,
                          cs("gm")[:, g8:g8 + 1], None, ALU.mult, None, r=[bbar, C], w=[LB])
        cc = B.sb("cc", [128, 2, 256])
        P.dma("sp", cc[0:64, 0, :], s5c[l, 0], r=[], w=[cc])
        P.dma("sp", cc[64:128, 0, :], s5c[l, 1], r=[], w=[cc])
        P.dma("sp", cc[0:64, 1, :], s5c[l, 1], r=[], w=[cc])
        P.dma("sp", cc[64:128, 1, :], s5c[l, 0], r=[], w=[cc])
        B.tsc("dve", cc[64:128, 0, :], cc[64:128, 0, :], -1.0, None, ALU.mult, None, r=[cc], w=[cc])
        B.tsc("dve", cc[:, 1, :], cc[:, 1, :], -1.0, None, ALU.mult, None, r=[cc], w=[cc])
        CP = B.sb("CP", [128, 2, 16, 128], BF16)
        B.mset("pool", CP[:, :, :, :], 0.0, w=[CP])
        for g in range(16):
            g8 = g % 8
            for i in range(2):
                B.cp("dve", CP[:, i, g, g8 * 16:(g8 + 1) * 16], cc[:, i, g * 16:(g + 1) * 16], r=[cc], w=[CP])
        Wg = B.sb("Wg", [128, 2, 256], BF16)
        B.load_bf16(Wg[:, :, :].rearrange("p a b -> p (a b)"), Wg, wglu[l].rearrange("p a b -> p (a b)"), [128, 512])
        uts = Ring([B.sb("u3_%d" % i, [128, 2, LT]) for i in range(2)])
        ubs = Ring([B.sb("ub3_%d" % i, [128, 2, LT], BF16) for i in range(2)])
        w1 = Ring([B.sb("w1_%d" % i, [128, LT]) for i in range(12)])
        zr = Ring([B.sb("z_%d" % i, [128, LT]) for i in range(4)])
        zb = Ring([B.sb("zb_%d" % i, [128, LT], BF16) for i in range(6)])
        zend = B.sb("zend", [128, 16])
        xst = B.sb("xst", [128, 16])
        yv = Ring([B.sb("yv_%d" % i, [128, LT]) for i in range(4)])
        zz = B.sb("zz", [128, 2, LT])
        zzb = B.sb("zzb", [128, 2, LT], BF16)
        mo = Ring([B.sb("mo3_%d" % i, [128, LT], BF16) for i in range(2)])
        for si, (r0, nt, k0, nk, p0) in enumerate(B.seqs):
            if si == 0:
                B.mset("pool", xst[:, :], 0.0, w=[xst])
            else:
                P.dma("sp", xst[:, :], s5st[l, si - 1], r=[], w=[xst])
            for c0 in range(0, nt, LT):
                n = min(LT, nt - c0)
                u_, ub = uts.next(), ubs.next()
                P.dma("sp", u_[:, :, :n], UT[l][:, r0 + c0:r0 + c0 + n].rearrange("(j p) n -> p j n", p=128),
                      r=[UT[l]], w=[u_])
                B.cp("pool", ub[:, :, :n], u_[:, :, :n], r=[u_], w=[ub])
                pab = {}

                def stA(g):
                    gh = g // 8
                    pa, pb = pAB.next(), pAB.next()
                    B.mm(pa[:, :n], LB[:, 0, g, :], ub[:, gh, :n], True, True, r=[LB, ub], w=[pa])
                    B.mm(pb[:, :n], LB[:, 1, g, :], ub[:, gh, :n], True, True, r=[LB, ub], w=[pb])
                    pab[g] = (pa, pb)

                wvs = {}

                def stB1(g):
                    pa, pb = pab.pop(g)
                    t1, t2, wv = w1.next(), w1.next(), w1.next()
                    B.tt("dve", t1[:, :n], pa[:, :n], ctab[:, g, :n], ALU.mult, r=[pa, ctab], w=[t1])
                    B.tt("dve", t2[:, :n], pb[:, :n], stabS[:, g, :n], ALU.mult, r=[pb, stabS], w=[t2])
                    B.tt("pool", wv[:, :n], t2[:, :n], t1[:, :n], ALU.add, r=[t1, t2], w=[wv])
                    wvs[g] = wv

                stA(0)
                stA(1)
                stB1(0)
                for g in range(16):
                    gh, g8 = g // 8, g % 8
                    if g + 1 < 16:
                        stB1(g + 1)
                    if g + 2 < 16:
                        stA(g + 2)
                    wv = wvs.pop(g)
                    z = zr.next()
                    P.op("dve", lambda: nc.vector.tensor_tensor_scan(z[:, :n], rmat[:, g, :n], wv[:, :n],
                                                                     xst[:, g:g + 1], ALU.mult, ALU.add),
                         r=[rmat, wv, xst], w=[z])
                    zc, zs = zb.next(), zb.next()
                    B.tt("dve", zc[:, :n], z[:, :n], ctab[:, g, :n], ALU.mult, r=[z, ctab], w=[zc])
                    B.tt("pool", zs[:, :n], z[:, :n], stab[:, g, :n], ALU.mult, r=[z, stab], w=[zs])
                    B.cp("act", zend[:, g:g + 1], z[:, n - 1:n], r=[z], w=[zend])
                    B.mm(pY[gh][:, :n], CP[:, 0, g, :], zc[:, :n], g8 == 0, False, r=[CP, zc], w=[pY[gh]], inc=True)
                    B.mm(pY[gh][:, :n], CP[:, 1, g, :], zs[:, :n], False, g8 == 7, r=[CP, zs], w=[pY[gh]], inc=True)
                ps = pG.next()
                B.mm(ps[:, 0:16], cs("swap"), zend[:, :], True, True, r=[C, zend], w=[ps])
                t1, t2 = w1.next(), w1.next()
                B.tt("dve", t1[:, 0:16], zend[:, :], ctab[:, :, n - 1], ALU.mult, r=[zend, ctab], w=[t1])
                B.tt("dve", t2[:, 0:16], ps[:, 0:16], stabS[:, :, n - 1], ALU.mult, r=[ps, stabS], w=[t2])
                B.stt("dve", xst[:, :], t2[:, 0:16], -1.0, t1[:, 0:16], ALU.mult, ALU.add,
                      r=[t1, t2], w=[xst])
                for j in range(2):
                    y_, x2 = yv.next(), yv.next()
                    B.stt("dve", y_[:, :n], u_[:, j, :n], s5d[:, j:j + 1], pY[j][:, :n], ALU.mult, ALU.add,
                          r=[u_, V, pY[j]], w=[y_])
                    B.tt("pool", x2[:, :n], y_[:, :n], y_[:, :n], ALU.mult, r=[y_], w=[x2])
                    B.tsc("pool", x2[:, :n], x2[:, :n], 0.044715, 1.0, ALU.mult, ALU.add, r=[x2], w=[x2])
                    B.tt("pool", x2[:, :n], x2[:, :n], y_[:, :n], ALU.mult, r=[x2, y_], w=[x2])
                    B.act(x2[:, :n], x2[:, :n], AF.Sigmoid, r=[x2], w=[x2], scale=1.5957691216057308)
                    B.tt("pool", zz[:, j, :n], y_[:, :n], x2[:, :n], ALU.mult, r=[y_, x2], w=[zz])
                    B.cp("pool", zzb[:, j, :n], zz[:, j, :n], r=[zz], w=[zzb])
                for jo in range(2):
                    ps = pG.next()
                    for j in range(2):
                        B.mm(ps[:, :n], Wg[:, j, jo * 128:(jo + 1) * 128], zzb[:, j, :n], j == 0, j == 1,
                             r=[Wg, zzb], w=[ps])
                    gt = yv.next()
                    B.act(gt[:, :n], ps[:, :n], AF.Sigmoid, r=[ps, V], w=[gt], bias=bgl[:, jo:jo + 1])
                    o_ = mo.next()
                    B.tt("dve", o_[:, :n], zz[:, jo, :n], gt[:, :n], ALU.mult, r=[zz, gt], w=[o_])
                    P.dma("sp", MT[l][384 + jo * 128:384 + (jo + 1) * 128, r0 + c0:r0 + c0 + n], o_[:, :n],
                          r=[o_], w=[MT[l]])
            P.dma("sp", o_s5[l, si], xst[:, :], r=[xst], w=[o_s5])
        P.barrier()
        B.release(m3)

    C0 = math.exp(-0.5)

    def phase4(l):
        m4 = B.mark()
        ring7 = Ring([B.ps("rB%d" % i) for i in range(7)])
        pSc = pM_ = slots = ring7
        pYb = B.ps("rY")
        V = B.sb("vec4", [128, 64])
        P.dma("sp", V[:, :], vec[l], r=[], w=[V])
        mu, w0, a0, k_k, k_a, r_k, gng, gnb = (V[:, 7:17], V[:, 17:20], V[:, 20:23], V[:, 23:26], V[:, 26:29],
                                               V[:, 29:32], V[:, 32:35], V[:, 35:38])
        omka = B.sb("omka", [128, 3])
        B.tsc("dve", omka[:, :], k_a, -1.0, 1.0, ALU.mult, ALU.add, r=[V], w=[omka])
        lo = B.sb("lo", [128, 384], BF16)
        B.load_bf16(lo[:, :], lo, lora[l], [128, 384])
        cmask = B.sb("cmask", [128, 640], BF16)
        B.cp("dve", cmask[:, 0:512], cs("m4"), r=[C], w=[cmask])
        B.cp("dve", cmask[:, 512:640], cs("maskL"), r=[C], w=[cmask])
        mk = lambda nm, shp, dt=F32, k=2: Ring([B.sb("%s%d" % (nm, i), shp, dt) for i in range(k)])
        cur, prv, dd = mk("cur", [128, 10, 128]), mk("prv", [128, 10, 128]), mk("dd", [128, 10, 128], F32, 1)
        psx = mk("psx", [128, 10, 128])
        sm = mk("sm", [128, 128], BF16, 4)
        lr3 = mk("lr3", [128, 128], BF16, 6)
        f3 = lambda nm, k=2: mk(nm, [128, 3, 128], F32, k)
        sig, aa, gg, kk_, kap, kti, bb_, bon, css, dm, pin, pinv, pex = (f3("sig"), f3("aa"), f3("gg", 3), f3("kk"),
            f3("kap"), f3("kti"), f3("bb"), f3("bon", 3), f3("css", 1), f3("dm", 1), f3("pin", 3), f3("pinv", 1), f3("pex", 1))
        b3 = lambda nm, k=2: mk(nm, [128, 3, 128], BF16, k)
        rh, kph, bhb, khb = b3("rh", 3), b3("kph", 3), b3("bhb"), b3("khb")
        bhf, khf = f3("bhf", 1), f3("khf", 1)
        kTt, bTt, vtt = b3("kTt", 3), b3("bTt", 3), b3("vtt", 3)
        scb = [mk("scb%d" % h, [128, 512], BF16, 3) for h in range(NH)]
        nbN = [mk("nbN%d" % h, [128, 128], BF16, 3) for h in range(NH)]
        nbB = [mk("nbB%d" % h, [128, 128], BF16, 3) for h in range(NH)]
        Mb = [mk("Mb%d" % h, [128, 128], BF16, 9) for h in range(NH)]
        zn = [mk("zn%d" % h, [128, 64], BF16, 2) for h in range(NH)]
        u2 = mk("u2", [128, 128], BF16, 6)
        Hf = B.sb("Hf", [128, 3, 64])
        Hb = mk("Hb", [128, 3, 64], BF16, 2)
        pmid, ppr = mk("pmid", [128, 3], F32, 3), mk("ppr", [128, 3], F32, 3)
        st6 = mk("st6", [128, 6, 6], F32, 2)
        mv6 = mk("mv6", [128, 6, 2], F32, 2)
        yn = mk("yn", [128, 384], F32, 2)
        fo = mk("fo", [128, 128], F32, 3)
        mo = mk("mo4", [128, 128], BF16, 3)
        chunks = [(si, c0) for si, sq in enumerate(B.seqs) for c0 in range(0, sq[1], 128)]

        def gen_prep(si, c0, X):
            r0, nt, k0, nk, p0 = B.seqs[si]
            n = min(128, nt - c0)
            a_, b_ = r0 + c0, r0 + c0 + n
            cu, pv = cur.next(), prv.next()
            P.dma("sp", cu[:, :, :n], PT[l][:, a_:b_].rearrange("(j p) n -> p j n", p=128), r=[PT[l]], w=[cu])
            if c0 == 0:
                if si == 0:
                    B.mset("pool", pv[:, :, 0:1], 0.0, w=[pv])
                else:
                    P.dma("act", pv[:, :, 0], shst[l, si - 1], r=[], w=[pv])
                if n > 1:
                    P.dma("act", pv[:, :, 1:n], PT[l][:, a_:b_ - 1].rearrange("(j p) n -> p j n", p=128),
                          r=[PT[l]], w=[pv])
            else:
                P.dma("act", pv[:, :, :n], PT[l][:, a_ - 1:b_ - 1].rearrange("(j p) n -> p j n", p=128),
                      r=[PT[l]], w=[pv])
            d_ = dd.next()
            B.tt("pool", d_[:, :, :n], pv[:, :, :n], cu[:, :, :n], ALU.subtract, r=[pv, cu], w=[d_])
            yield
            px = psx.next()
            for j in range(10):
                B.stt("dve" if j % 2 else "pool", px[:, j, :n], d_[:, j, :n], mu[:, j:j + 1], cu[:, j, :n],
                      ALU.mult, ALU.add, r=[d_, cu, V], w=[px])
            R_, K_, V_ = (lambda c: px[:, c, :n]), (lambda c: px[:, 3 + c, :n]), (lambda c: px[:, 6 + c, :n])
            th, adb, sgd = lr3.next(), lr3.next(), lr3.next()
            tht = fo.next()
            B.act(tht[0:32, :n], px[0:32, 9, :n], AF.Sigmoid, r=[px], w=[tht], scale=2.0)
            B.tsc("dve", th[0:32, :n], tht[0:32, :n], 2.0, -1.0, ALU.mult, ALU.add, r=[tht], w=[th])
            B.cp("dve", adb[32:64, :n], px[32:64, 9, :n], r=[px], w=[adb])
            B.act(sgd[64:128, :n], px[64:128, 9, :n], AF.Sigmoid, r=[px], w=[sgd])
            yield
            sg, a3, g3, k3, kp3, kt3, b3_, bn3 = (sig.next(), aa.next(), gg.next(), kk_.next(), kap.next(),
                                                  kti.next(), bb_.next(), bon.next())
            for c in range(3):
                cc_ = slice(c * 128, (c + 1) * 128)
                p1 = pM_.next()
                B.mm(p1[:, 0:n], lo[0:32, cc_], th[0:32, :n], True, True, r=[lo, th], w=[p1])
                B.mm(p1[:, 128:128 + n], lo[32:64, cc_], adb[32:64, :n], True, True, r=[lo, adb], w=[p1])
                B.mm(p1[:, 256:256 + n], lo[64:128, cc_], sgd[64:128, :n], True, True, r=[lo, sgd], w=[p1])
                B.act(sg[:, c, :n], p1[:, 0:n], AF.Sigmoid, r=[p1, V], w=[sg], bias=w0[:, c:c + 1])
                B.act(a3[:, c, :n], p1[:, 128:128 + n], AF.Sigmoid, r=[p1, V], w=[a3], bias=a0[:, c:c + 1])
                B.cp("act", g3[:, c, :n], p1[:, 256:256 + n], r=[p1], w=[g3])
            for c in range(3):
                B.tsc("pool", k3[:, c, :n], K_(c), k_k[:, c:c + 1], None, ALU.mult, None, r=[px, V], w=[k3])
                s_ = sm.next()
                B.tt("pool", s_[:, :n], k3[:, c, :n], k3[:, c, :n], ALU.mult, r=[k3], w=[s_])
                p2 = pM_.next()
                B.mm(p2[:, 0:n], bonesb, s_[:, :n], True, True, r=[Cb, s_], w=[p2])
                t_ = fo.next()
                sqrt_pow("dve", t_[:, :n], p2[:, 0:n], 1e-24, -0.5, r=[p2], w=[t_])
                B.tt("pool", kp3[:, c, :n], k3[:, c, :n], t_[:, :n], ALU.mult, r=[k3, t_], w=[kp3])
                t2_ = fo.next()
                B.tsc("dve", t2_[:, :n], a3[:, c, :n], k_a[:, c:c + 1], omka[:, c:c + 1], ALU.mult, ALU.add,
                      r=[a3, V, omka], w=[t2_])
                B.tt("pool", kt3[:, c, :n], K_(c), t2_[:, :n], ALU.mult, r=[px, t2_], w=[kt3])
                B.tt("pool", b3_[:, c, :n], kp3[:, c, :n], a3[:, c, :n], ALU.mult, r=[kp3, a3], w=[b3_])
                s2_ = sm.next()
                B.stt("dve", s2_[:, :n], R_(c), r_k[:, c:c + 1], kt3[:, c, :n], ALU.mult, ALU.mult,
                      r=[px, V, kt3], w=[s2_])
                B.mm(p2[:, 128:128 + n], bonesb, s2_[:, :n], True, True, r=[Cb, s2_], w=[p2])
                B.tt("dve", bn3[:, c, :n], p2[:, 128:128 + n], V_(c), ALU.mult, r=[p2, px], w=[bn3])
            yield
            cs_, dm_, pi_, piv, pe_ = css.next(), dm.next(), pin.next(), pinv.next(), pex.next()
            mid = min(63, n - 1)
            pm_, pr_ = pmid.next(), ppr.next()
            for c in range(3):
                P.op("dve", lambda: nc.vector.tensor_tensor_scan(cs_[:, c, :n], cs("ones")[:, 0:n], sg[:, c, :n],
                                                                 0.0, ALU.mult, ALU.add), r=[C, sg], w=[cs_])
                B.tsc("dve", dm_[:, c, :n], cs_[:, c, :n], cs_[:, c, mid:mid + 1], None, ALU.subtract, None,
                      r=[cs_], w=[dm_])
            B.act(pi_[:, :, :n], dm_[:, :, :n], AF.Exp, r=[dm_], w=[pi_], scale=-C0)
            B.act(piv[:, :, :n], dm_[:, :, :n], AF.Exp, r=[dm_], w=[piv], scale=C0)
            B.tt("pool", dm_[:, :, :n], dm_[:, :, :n], sg[:, :, :n], ALU.subtract, r=[dm_, sg], w=[dm_])
            B.act(pe_[:, :, :n], dm_[:, :, :n], AF.Exp, r=[dm_], w=[pe_], scale=-C0)
            B.act(pm_[:, :], cs_[:, :, mid], AF.Exp, r=[cs_], w=[pm_], scale=-C0)
            B.tt("dve", pr_[:, :], pm_[:, :], pi_[:, :, n - 1], ALU.mult, r=[pm_, pi_], w=[pr_])
            yield
            rh_, kph_, bhb_, khb_, bhf_, khf_ = rh.next(), kph.next(), bhb.next(), khb.next(), bhf.next(), khf.next()
            B.tt("pool", rh_[:, :, :n], px[:, 0:3, :n], pi_[:, :, :n], ALU.mult, r=[px, pi_], w=[rh_])
            B.tt("pool", kph_[:, :, :n], kp3[:, :, :n], pe_[:, :, :n], ALU.mult, r=[kp3, pe_], w=[kph_])
            B.tt("dve", bhf_[:, :, :n], b3_[:, :, :n], piv[:, :, :n], ALU.mult, r=[b3_, piv], w=[bhf_])
            B.tt("dve", khf_[:, :, :n], kt3[:, :, :n], piv[:, :, :n], ALU.mult, r=[kt3, piv], w=[khf_])
            B.cp("pool", bhb_[:, :, :n], bhf_[:, :, :n], r=[bhf_], w=[bhb_])
            B.cp("pool", khb_[:, :, :n], khf_[:, :, :n], r=[khf_], w=[khb_])
            yield
            kT_, bT_, vt_ = kTt.next(), bTt.next(), vtt.next()
            for c in range(3):
                for src, dst in ((khf_[:, c, :n], kT_), (bhf_[:, c, :n], bT_), (V_(c), vt_)):
                    p1 = pM_.next()
                    B.tr(p1[:n, 0:128], src, ident, r=[khf_, bhf_, px, C], w=[p1])
                    B.cp(B.ev(), dst[:n, c, :], p1[:n, 0:128], r=[p1], w=[dst])
            X.update(n=n, a_=a_, b_=b_, rh_=rh_, kph_=kph_, vt_=vt_, kT_=kT_, bT_=bT_, pi_=pi_, pm_=pm_,
                     pr_=pr_, bn3=bn3, g3=g3, bhb_=bhb_, khb_=khb_)
            yield

        def gen_ab(si, c0, X):
            n, rh_, kph_, bhb_, khb_ = (X[k] for k in ('n', 'rh_', 'kph_', 'bhb_', 'khb_'))
            nlev = 6 if n > 64 else (5 if n > 32 else 4)
            heads = [(c, hh) for c in range(3) for hh in range(2)]
            Rs = lambda hh: slice(hh * 64, hh * 64 + 64)
            st = {}
            for (c, hh) in heads:
                R = Rs(hh)
                sc = pSc.next()
                B.mm(sc[:n, 0:n], bhb_[R, c, :n], kph_[R, c, :n], True, True, r=[bhb_, kph_], w=[sc])
                B.mm(sc[:n, n:2 * n], khb_[R, c, :n], kph_[R, c, :n], True, True, r=[khb_, kph_], w=[sc])
                B.mm(sc[:n, 2 * n:3 * n], bhb_[R, c, :n], rh_[R, c, :n], True, True, r=[bhb_, rh_], w=[sc])
                B.mm(sc[:n, 3 * n:4 * n], khb_[R, c, :n], rh_[R, c, :n], True, True, r=[khb_, rh_], w=[sc])
                sb_ = scb[2 * c + hh].next()
                if n == 128:
                    B.tt("dve", sb_[:n, :], sc[:n, :], cmask[:n, 0:512], ALU.mult, r=[sc, cmask], w=[sb_])
                else:
                    for q in range(4):
                        B.tt("dve", sb_[:n, q * n:(q + 1) * n], sc[:n, q * n:(q + 1) * n],
                             cmask[:n, q * 128:q * 128 + n], ALU.mult, r=[sc, cmask], w=[sb_])
                p1 = slots.next()
                B.mm(p1[:n, 0:n], kph_[R, c, :n], bhb_[R, c, :n], True, True, r=[kph_, bhb_], w=[p1])
                Nk_ = nbN[2 * c + hh].next()
                B.tt("dve", Nk_[:n, :n], p1[:n, 0:n], cmask[:n, 512:512 + n], ALU.mult, r=[p1, cmask], w=[Nk_])
                M_ = Mb[2 * c + hh].next()
                B.tt("pool", M_[:n, :n], sb_[:n, 0:n], identb[:n, :n], ALU.add, r=[sb_, Cb], w=[M_])
                st[(c, hh)] = dict(sb=sb_, Bk=sb_[:n, 0:n], BkB=sb_, Nk=Nk_[:n, :n], NkB=Nk_, M=M_)
            yield
            for lev in range(nlev):
                lastl = lev == nlev - 1
                for (c, hh) in heads:
                    S_ = st[(c, hh)]
                    pa = slots.next()
                    B.mm(pa[:n, 0:n], S_["Bk"], S_["Nk"], True, True, r=[S_["BkB"], S_["NkB"]], w=[pa])
                    if not lastl:
                        pb = slots.next()
                        B.mm(pb[:n, 0:n], S_["Nk"], S_["Bk"], True, True, r=[S_["BkB"], S_["NkB"]], w=[pb])
                    nb1 = nbN[2 * c + hh].next()
                    B.cp("act", nb1[:n, :n], pa[:n, 0:n], r=[pa], w=[nb1])
                    if not lastl:
                        nb2 = nbB[2 * c + hh].next()
                        B.cp("act", nb2[:n, :n], pb[:n, 0:n], r=[pb], w=[nb2])
                        S_["Bk"], S_["BkB"] = nb2[:n, :n], nb2
                    S_["Nk"], S_["NkB"] = nb1[:n, :n], nb1
                for (c, hh) in heads:
                    S_ = st[(c, hh)]
                    pm2 = slots.next()
                    B.mm(pm2[:n, 0:n], S_["Nk"], S_["M"][:n, :n], True, True, r=[S_["NkB"], S_["M"]], w=[pm2])
                    M2 = Mb[2 * c + hh].next()
                    B.tt("dve", M2[:n, :n], pm2[:n, 0:n], S_["M"][:n, :n], ALU.add, r=[pm2, S_["M"]], w=[M2])
                    S_["M"] = M2
                yield
            X.update(st=st, heads=heads, Rs=Rs)
            yield

        def gen_c(si, c0, X):
            r0, nt, k0, nk, p0 = B.seqs[si]
            n, a_, b_, rh_, kph_, vt_, kT_, bT_, st, pi_, pm_, pr_, bn3, g3, heads, Rs = (X[k] for k in (
                'n', 'a_', 'b_', 'rh_', 'kph_', 'vt_', 'kT_', 'bT_', 'st', 'pi_', 'pm_', 'pr_', 'bn3', 'g3', 'heads', 'Rs'))
            if c0 == 0:
                if si == 0:
                    B.mset("pool", Hf[:, :, :], 0.0, w=[Hf])
                else:
                    P.dma("sp", Hf[:, :, :], rwst[l, si - 1], r=[], w=[Hf])
            hb = Hb.next()
            for c in range(3):
                B.tsc("dve", hb[:, c, :], Hf[:, c, :], pm_[:, c:c + 1], None, ALU.mult, None, r=[Hf, pm_], w=[hb])
            u_s = [u2.next() for c in range(3)]
            yield
            for (c, hh) in heads:
                S_ = st[(c, hh)]
                R = Rs(hh)
                pz = slots.next()
                AKm = S_["sb"][:n, n:2 * n]
                B.mm(pz[:n, 0:64], kph_[R, c, :n], hb[R, c, :], True, False, r=[kph_, hb], w=[pz], inc=True)
                B.mm(pz[:n, 0:64], AKm, vt_[:n, c, R], False, True, r=[S_["sb"], vt_], w=[pz])
                z_ = zn[2 * c + hh].next()
                B.act(z_[:n, :], pz[:n, 0:64], AF.Copy, r=[pz], w=[z_], scale=-1.0)
                S_["z"] = z_
            yield
            for (c, hh) in heads:
                S_ = st[(c, hh)]
                R = Rs(hh)
                pu = slots.next()
                B.mm(pu[:n, 0:64], S_["M"][:n, :n], S_["z"][:n, :], True, True, r=[S_["M"], S_["z"]], w=[pu])
                B.cp("act", u_s[c][:n, R], pu[:n, 0:64], r=[pu], w=[u_s[c]])
            yield
            for (c, hh) in heads:
                S_ = st[(c, hh)]
                R = Rs(hh)
                h = 2 * c + hh
                RBm, RKm = S_["sb"][:n, 2 * n:3 * n], S_["sb"][:n, 3 * n:4 * n]
                yr = pYb[:n, h * 64:(h + 1) * 64]
                B.mm(yr, rh_[R, c, :n], hb[R, c, :], True, False, r=[rh_, hb], w=[pYb], inc=True)
                B.mm(yr, RBm, u_s[c][:n, R], False, False, r=[S_["sb"], u_s[c]], w=[pYb], inc=True)
                B.mm(yr, RKm, vt_[:n, c, R], False, True, r=[S_["sb"], vt_], w=[pYb])
            yield
            for c in range(3):
                pH_ = ring7.next()
                B.mm(pH_[:, 0:128], kT_[:n, c, :], vt_[:n, c, :], True, False, r=[kT_, vt_], w=[pH_], inc=True)
                B.mm(pH_[:, 0:128], bT_[:n, c, :], u_s[c][:n, :], False, True, r=[bT_, u_s[c]], w=[pH_])
                for hh in range(2):
                    R = Rs(hh)
                    t_ = fo.next()
                    B.tsc("dve", t_[R, 0:64], pH_[R, R], pi_[R, c, n - 1:n], None, ALU.mult, None,
                          r=[pH_, pi_], w=[t_])
                    B.stt("dve", Hf[R, c, :], Hf[R, c, :], pr_[R, c:c + 1], t_[R, 0:64], ALU.mult, ALU.add,
                          r=[Hf, pr_, t_], w=[Hf])
            yield
            s6, m6, y_ = st6.next(), mv6.next(), yn.next()
            for h in range(NH):
                P.op("dve", lambda: nc.vector.bn_stats(s6[:n, h, :], pYb[:n, h * 64:(h + 1) * 64]), r=[pYb], w=[s6])
                P.op("dve", lambda: nc.vector.bn_aggr(m6[:n, h, :], s6[:n, h, :]), r=[s6], w=[m6])
            sqrt_pow("dve", m6[:n, :, 1], m6[:n, :, 1], 64e-5, -0.5, r=[m6], w=[m6])
            for h in range(NH):
                B.tsc("dve", y_[:n, h * 64:(h + 1) * 64], pYb[:n, h * 64:(h + 1) * 64], m6[:n, h, 0:1],
                      m6[:n, h, 1:2], ALU.subtract, ALU.mult, r=[pYb, m6], w=[y_])
            yield
            for c in range(3):
                p1 = pM_.next()
                B.tr(p1[:, 0:n], y_[:n, c * 128:(c + 1) * 128], ident[:n, :n], r=[y_, C], w=[p1])
                t_ = fo.next()
                B.tsc("dve", t_[:, :n], p1[:, 0:n], gng[:, c:c + 1], gnb[:, c:c + 1], ALU.mult, ALU.add,
                      r=[p1, V], w=[t_])
                B.tt("pool", t_[:, :n], t_[:, :n], bn3[:, c, :n], ALU.add, r=[t_, bn3], w=[t_])
                o_ = mo.next()
                B.tt("pool", o_[:, :n], t_[:, :n], g3[:, c, :n], ALU.mult, r=[t_, g3], w=[o_])
                P.dma("sp", MT[l][640 + c * 128:640 + (c + 1) * 128, a_:b_], o_[:, :n], r=[o_], w=[MT[l]])
            if c0 + 128 >= nt:
                P.dma("sp", o_rw[l, si], Hf[:, :, :], r=[Hf], w=[o_rw])
            yield

        nck = len(chunks)
        Xs = {}
        for k in range(nck + 2):
            gens = []
            if k < nck:
                Xs[k] = {}
                gens.append(gen_prep(chunks[k][0], chunks[k][1], Xs[k]))
            if 0 <= k - 1 < nck:
                gens.append(gen_ab(chunks[k - 1][0], chunks[k - 1][1], Xs[k - 1]))
            if 0 <= k - 2 < nck:
                gens.append(gen_c(chunks[k - 2][0], chunks[k - 2][1], Xs[k - 2]))
            alive = list(gens)
            while alive:
                for g_ in list(alive):
                    try:
                        next(g_)
                    except StopIteration:
                        alive.remove(g_)
            Xs.pop(k - 2, None)
        P.barrier()
        B.release(m4)

    def layer_norm(z, ss, gB, bB, outb, st, mv):
        for hf in range(2):
            P.op("dve", lambda: nc.vector.bn_stats(st[:ss, hf, :], z[:ss, hf * 512:(hf + 1) * 512]), r=[z], w=[st])
        P.op("dve", lambda: nc.vector.bn_aggr(mv[:ss, :], st[:ss, :, :].rearrange("p a b -> p (a b)")), r=[st], w=[mv])
        sqrt_pow("dve", mv[:ss, 1:2], mv[:ss, 1:2], 1e-5, -0.5, r=[mv], w=[mv])
        B.tsc("dve", outb[:ss, :], z[:ss, :], mv[:ss, 0:1], mv[:ss, 1:2], ALU.subtract, ALU.mult, r=[z, mv], w=[outb])
        B.tt("pool", outb[:ss, :], outb[:ss, :], gB[:ss, :], ALU.mult, r=[outb, gB], w=[outb])
        B.tt("pool", outb[:ss, :], outb[:ss, :], bB[:ss, :], ALU.add, r=[outb, bB], w=[outb])

    def phase5a(l):
        m5 = B.mark()
        pso = Ring([B.ps("oP%d" % i) for i in range(6)])
        Wo = B.sb("Wo", [128, 8, 1024], BF16)
        for kc in range(8):
            B.load_bf16(Wo[:, kc, :], Wo, wout[l, :, kc, :], [128, 1024], q="sp" if kc % 2 else "act")
        gB, bB = B.sb("g1", [128, 1024]), B.sb("b1", [128, 1024])
        P.dma("sp", gB[:, :], lnp[l, 0], r=[], w=[gB])
        P.dma("sp", bB[:, :], lnp[l, 1], r=[], w=[bB])
        Xsrc = xin if l == 0 else XN[l - 1]
        mts = Ring([B.sb("mt%d" % i, [128, 8, 512], BF16) for i in range(2)])
        xts = Ring([B.sb("x5_%d" % i, [128, 4, D]) for i in range(2)])
        zs = Ring([B.sb("z5_%d" % i, [128, D]) for i in range(2)])
        os_ = Ring([B.sb("o5_%d" % i, [128, D]) for i in range(3)])
        xT = Ring([B.sb("xT5_%d" % i, [128, 8, 512], BF16) for i in range(2)])
        st, mv = B.sb("st5", [128, 2, 6]), B.sb("mv5", [128, 2])
        wcv = Ring([B.sb("wcv%d" % i, [128, 1024], BF16) for i in range(2)])
        for j in range(32):
            stg = B.stage.next()
            P.dma("pool", stg[:, 0:1024], wup[l, j], r=[], w=[stg])
            wb = wcv.next()
            B.cp("act", wb[:, :], stg[:, 0:1024], r=[stg], w=[wb])
            P.dma("pool", WUPB[l, j], wb[:, :], r=[wb], w=[WUPB])
        pre5 = {}

        def prefetch5(ti):
            t0, n = B.tiles[ti]
            ss = min(128, n)
            nsub = n // ss
            mt, xt = mts.next(), xts.next()
            P.dma("sp", mt[:, :, :n], MT[l][:, t0:t0 + n].rearrange("(k p) n -> p k n", p=128), r=[MT[l]], w=[mt])
            P.dma("sp", xt[:ss, :nsub, :], Xsrc[t0:t0 + n, :].rearrange("(s p) d -> p s d", p=ss), r=[Xsrc], w=[xt])
            pre5[ti] = (mt, xt)

        prefetch5(0)
        for ti, (t0, n) in enumerate(B.tiles):
            ss = min(128, n)
            nsub = n // ss
            if ti + 1 < len(B.tiles):
                prefetch5(ti + 1)
            mt, xt = pre5.pop(ti)
            x_T = xT.next()
            def mm_part(s):
                z = zs.next()
                for hf in range(2):
                    ps = pso.next()
                    for kc in range(8):
                        B.mm(ps[:ss, :], mt[:, kc, s * ss:(s + 1) * ss], Wo[:, kc, hf * 512:(hf + 1) * 512],
                             kc == 0, kc == 7, r=[mt, Wo], w=[ps])
                    B.stt("dve", z[:ss, hf * 512:(hf + 1) * 512], xt[:ss, s, hf * 512:(hf + 1) * 512], ALPHA,
                          ps[:ss, :], ALU.mult, ALU.add, r=[xt, ps], w=[z])
                o_ = os_.next()
                layer_norm(z, ss, gB, bB, o_, st, mv)
                P.dma("pool", X1[l][t0 + s * ss:t0 + (s + 1) * ss, :], o_[:ss, :], r=[o_], w=[X1[l]])
                return o_

            def tr_part(s, o_):
                for kc in range(0, 8, 4):
                    ps = pso.next()
                    for k2 in range(4):
                        B.tr(ps[:, k2 * 128:k2 * 128 + ss], o_[:ss, (kc + k2) * 128:(kc + k2 + 1) * 128], ident[:ss, :ss],
                             r=[o_, C], w=[ps], inc=(k2 == 3))
                    B.cp(B.ev(), x_T[:, kc:kc + 4, s * ss:(s + 1) * ss],
                         ps[:, :].rearrange("p (a b) -> p a b", a=4)[:, :, 0:ss], r=[ps], w=[x_T])

            prev = None
            for s in range(nsub):
                o_ = mm_part(s)
                if prev is not None:
                    tr_part(*prev)
                prev = (s, o_)
            tr_part(*prev)
            P.dma("pool", X1T[l][:, t0:t0 + n].rearrange("(k p) n -> p k n", p=128), x_T[:, :, :n], r=[x_T], w=[X1T[l]])
        P.barrier()
        B.release(m5)

    def phase5b(l):
        m5 = B.mark()
        psu = Ring([B.ps("uP%d" % i) for i in range(4)])
        psd = Ring([B.ps("dP%d" % i) for i in range(4)])
        Wd = B.sb("Wd", [128, 32, 1024], BF16)
        for j in range(32):
            B.load_bf16(Wd[:, j, :], Wd, wdn[l, :, j, :], [128, 1024], q="sp" if j % 2 else "act")
        gB, bB = B.sb("g2", [128, 1024]), B.sb("b2", [128, 1024])
        P.dma("sp", gB[:, :], lnp[l, 2], r=[], w=[gB])
        P.dma("sp", bB[:, :], lnp[l, 3], r=[], w=[bB])
        xTs = Ring([B.sb("xT6_%d" % i, [128, 8, 512], BF16) for i in range(2)])
        slab = Ring([B.sb("sl%d" % i, [128, 1024], BF16) for i in range(6)])
        hT = B.sb("hT", [128, 32, 512], BF16)
        rl_ = Ring([B.sb("rl6_%d" % i, [128, 512]) for i in range(4)])
        x1s = Ring([B.sb("x6_%d" % i, [128, D]) for i in range(2)])
        zs = Ring([B.sb("z6_%d" % i, [128, D]) for i in range(2)])
        os_ = Ring([B.sb("o6_%d" % i, [128, D]) for i in range(2)])
        st, mv = B.sb("st6b", [128, 2, 6]), B.sb("mv6b", [128, 2])
        pre6 = {}

        def prefetch6(ti):
            t0, n = B.tiles[ti]
            x_T = xTs.next()
            P.dma("sp", x_T[:, :, :n], X1T[l][:, t0:t0 + n].rearrange("(k p) n -> p k n", p=128), r=[X1T[l]], w=[x_T])
            pre6[ti] = x_T

        prefetch6(0)
        for ti, (t0, n) in enumerate(B.tiles):
            ss = min(128, n)
            nsub = n // ss
            if ti + 1 < len(B.tiles):
                prefetch6(ti + 1)
            x_T = pre6.pop(ti)
            slq = {}

            def slab_load(j):
                sl = slab.next()
                P.dma("sp", sl[:, :], WUPB[l, j], r=[WUPB], w=[sl])
                slq[j] = sl

            for j in range(3):
                slab_load(j)
            for j in range(32):
                if j + 3 < 32:
                    slab_load(j + 3)
                sl = slq.pop(j)
                ps = psu.next()
                for kc in range(8):
                    B.mm(ps[:, :n], sl[:, kc * 128:(kc + 1) * 128], x_T[:, kc, :n], kc == 0, kc == 7, r=[sl, x_T], w=[ps])
                if j % 2:
                    t_ = rl_.next()
                    B.tsc("dve", t_[:, :n], ps[:, :n], 0.0, None, ALU.max, None, r=[ps], w=[t_])
                    B.tt("dve", hT[:, j, :n], t_[:, :n], t_[:, :n], ALU.mult, r=[t_], w=[hT])
                else:
                    t_ = rl_.next()
                    B.act(t_[:, :n], ps[:, :n], AF.Relu, r=[ps], w=[t_])
                    B.tt("pool", hT[:, j, :n], t_[:, :n], t_[:, :n], ALU.mult, r=[t_], w=[hT])
            for s in range(nsub):
                x1 = x1s.next()
                P.dma("sp", x1[:ss, :], X1[l][t0 + s * ss:t0 + (s + 1) * ss, :], r=[X1[l]], w=[x1])
                z = zs.next()
                for hf in range(2):
                    ps = psd.next()
                    for j in range(32):
                        B.mm(ps[:ss, :], hT[:, j, s * ss:(s + 1) * ss], Wd[:, j, hf * 512:(hf + 1) * 512],
                             j == 0, j == 31, r=[hT, Wd], w=[ps])
                    B.stt("dve", z[:ss, hf * 512:(hf + 1) * 512], x1[:ss, hf * 512:(hf + 1) * 512], ALPHA,
                          ps[:ss, :], ALU.mult, ALU.add, r=[x1, ps], w=[z])
                o_ = os_.next()
                layer_norm(z, ss, gB, bB, o_, st, mv)
                P.dma("pool", XN[l][t0 + s * ss:t0 + (s + 1) * ss, :], o_[:ss, :], r=[o_], w=[XN[l]])
        P.barrier()
        B.release(m5)

    phases = dbg if dbg is not None else ["1", "2", "3", "4", "5a", "5b"]
    for l in range(L):
        for ph, fn in (("1", phase1), ("2", phase2), ("3", phase3), ("4", phase4), ("5a", phase5a), ("5b", phase5b)):
            if ph in phases:
                fn(l)
    P.barrier()
    return nc


def _layout_weights(w, PAST):
    f = lambda a: np.ascontiguousarray(a, dtype=np.float32)
    cm = lambda v: np.ascontiguousarray(v.reshape(L, -1, 128).transpose(0, 2, 1))
    w_in = w["w_in"]
    Wp = np.zeros((L, D, NCOL), np.float32)
    Wp[:, :, 0:384] = w_in[:, :, 0:384]
    kr = w_in[:, :, 384:416]
    Wp[:, :, 384 + 64:480] = kr
    Wp[:, :, 480 + 64:480 + 80] = kr[:, :, 16:32]
    Wp[:, :, 480 + 80:576] = kr[:, :, 0:16]
    Wp[:, :, 576:832] = w_in[:, :, 416:672]
    Wp[:, :, 832:2112] = w_in[:, :, 672:1952]
    o = {}
    o["win"] = f(Wp.reshape(L, 8, 128, NCOL).transpose(0, 2, 1, 3))
    wq = w["w_qb"]
    o["wqb"] = f(wq.reshape(L, 2, 128, 576).transpose(0, 2, 1, 3))
    wqs = np.zeros_like(wq)
    for h in range(NH):
        b = h * 96
        wqs[:, :, b + 64:b + 80] = wq[:, :, b + 80:b + 96]
        wqs[:, :, b + 80:b + 96] = wq[:, :, b + 64:b + 80]
    o["wqbs"] = f(wqs.reshape(L, 2, 128, 576).transpose(0, 2, 1, 3))
    wkv = w["w_kvb"].reshape(L, 128, NH, 128)
    o["wkvk"] = f(wkv[:, :, :, 0:64].reshape(L, 128, 384))
    o["wkvv"] = f(wkv[:, :, :, 64:128].reshape(L, 128, 384))
    o["wout"] = f(w["w_out"].reshape(L, 8, 128, 1024).transpose(0, 2, 1, 3))
    o["wup"] = f(w["w_up"].reshape(L, 8, 128, 32, 128).transpose(0, 3, 2, 1, 4).reshape(L, 32, 128, 1024))
    o["wdn"] = f(w["w_down"].reshape(L, 32, 128, 1024).transpose(0, 2, 1, 3))
    vec = np.zeros((L, 128, 64), np.float32)
    vec[:, :, 0:2] = cm(w["q_norm_g"])
    vec[:, :, 2:3] = cm(w["kv_norm_g"])
    vec[:, :, 3:5] = cm(w["s5_d"])
    vec[:, :, 5:7] = cm(w["b_glu"])
    vec[:, :, 7:17] = cm(w["mu_shift"])
    vec[:, :, 17:20] = cm(w["w0"])
    vec[:, :, 20:23] = cm(w["a0"])
    vec[:, :, 23:26] = cm(w["k_k"])
    vec[:, :, 26:29] = cm(w["k_a"])
    vec[:, :, 29:32] = cm(w["r_k"].reshape(L, 384))
    vec[:, :, 32:35] = cm(w["gn_g"])
    vec[:, :, 35:38] = cm(w["gn_b"])
    o["vec"] = vec
    ln = np.stack([w["ln1_g"], w["ln1_b"], w["ln2_g"], w["ln2_b"]], 1)
    o["lnp"] = f(np.broadcast_to(ln[:, :, None, :], (L, 4, 128, 1024)))
    o["lora"] = f(np.concatenate([w["w_w2"], w["w_a2"], w["w_g2"]], 1))
    o["s5v"] = f(np.concatenate([w["lam_re"], w["lam_im"], np.repeat(w["log_dt"][:, :, None], 64, 2)], 2))
    bt = lambda b: b.reshape(L, 2, 8, 64, 16).transpose(0, 2, 4, 1, 3).reshape(L, 128, 2, 64)
    o["s5b"] = f(np.stack([bt(w["b_re"]), bt(w["b_im"])], 1))
    ct = lambda c: c.transpose(0, 3, 1, 2).reshape(L, 64, 256)
    o["s5c"] = f(np.stack([ct(w["c_re"]), ct(w["c_im"])], 1))
    o["wglu"] = f(w["w_glu"].reshape(L, 2, 128, 256).transpose(0, 2, 1, 3))
    return o


_CACHE = {}


def run_cores(inp, T, PAST, n_cores, dbg=None, extra_out=()):
    cstv, offs = make_consts(PAST)
    key = (T, PAST, tuple(dbg) if dbg else None)
    if key not in _CACHE:
        _CACHE[key] = build_program(T, PAST, cstv.shape[1], offs, dbg)
    nc = _CACHE[key]
    wl = _layout_weights(inp, PAST)
    wl["cst"] = cstv
    in_maps = []
    for c in range(n_cores):
        m = dict(wl)
        sb = slice(NSB * c, NSB * c + NSB)
        m["xin"] = np.ascontiguousarray(np.concatenate(
            [inp["x_prompt"][c], inp["x_sample"][sb].reshape(NSB * TS, D)], 0), dtype=np.float32)
        m["ckvc"] = np.ascontiguousarray(inp["cache_mla_ckv"][:, sb])
        m["krc"] = np.ascontiguousarray(inp["cache_mla_krope"][:, sb])
        s5 = inp["state_s5"][:, sb]
        m["s5st"] = np.ascontiguousarray(s5.transpose(0, 1, 4, 3, 2).reshape(L, NSB, 128, 16))
        rw = inp["state_rwkv"][:, sb].reshape(L, NSB, 3, 2, 64, 64)
        m["rwst"] = np.ascontiguousarray(rw.transpose(0, 1, 3, 5, 2, 4).reshape(L, NSB, 128, 3, 64))
        sh = inp["state_rwkv_shift"][:, sb].reshape(L, NSB, 10, 128)
        m["shst"] = np.ascontiguousarray(sh.transpose(0, 1, 3, 2))
        in_maps.append(m)
    res = run_bass_kernel_spmd(nc, in_maps, core_ids=list(range(n_cores)))
    return res.results


def assemble(rs, T):
    nb = len(rs)
    cat = lambda f: np.stack([f(r) for r in rs], 0)
    y = cat(lambda r: r["y"])
    y_p = y[:, :T]
    y_s = y[:, T:].reshape(nb * NSB, TS, D)

    def tok(name, w):
        a = cat(lambda r: r[name])
        p = a[:, :, :T].transpose(1, 0, 2, 3)
        s = a[:, :, T:].reshape(nb, L, NSB, TS, w).transpose(1, 0, 2, 3, 4).reshape(L, nb * NSB, TS, w)
        return np.ascontiguousarray(p), np.ascontiguousarray(s)

    ckv_p, ckv_s = tok("o_ckv", 128)
    kr_p, kr_s = tok("o_kr", 32)

    def st(name, conv):
        a = cat(lambda r: r[name])
        a = conv(a)
        p = a[:, :, 0].transpose(1, 0, *range(2, a.ndim - 1))
        s = a[:, :, 1:].transpose(1, 0, *range(2, a.ndim))
        s = s.reshape((L, nb * NSB) + s.shape[3:])
        return np.ascontiguousarray(p), np.ascontiguousarray(s)

    s5_p, s5_s = st("o_s5", lambda a: a.reshape(nb, L, 3, 2, 64, 16).transpose(0, 1, 2, 5, 4, 3))
    rw_p, rw_s = st("o_rw", lambda a: a.reshape(nb, L, 3, 2, 64, 3, 64).transpose(0, 1, 2, 5, 3, 6, 4)
                    .reshape(nb, L, 3, 6, 64, 64))
    sh_p, sh_s = st("o_sh", lambda a: a.transpose(0, 1, 2, 4, 3).reshape(nb, L, 3, 1, 1280))
    return (np.ascontiguousarray(y_p), np.ascontiguousarray(y_s), ckv_p, kr_p, s5_p, rw_p, sh_p,
            ckv_s, kr_s, s5_s, rw_s, sh_s)


def kernel(**inputs):
    inp = {k: np.asarray(v) for k, v in inputs.items()}
    T = inp["x_prompt"].shape[1]
    PAST = inp["cache_mla_ckv"].shape[2]
    nb = inp["x_prompt"].shape[0]
    rs = run_cores(inp, T, PAST, nb)
    return assemble(rs, T)
```

```python
import math
import numpy as np
import concourse.bass as bass
import concourse.mybir as mybir
from concourse.bass_utils import run_bass_kernel_spmd

F32 = mybir.dt.float32
BF16 = mybir.dt.bfloat16
AF = mybir.ActivationFunctionType
ALU = mybir.AluOpType
AX = mybir.AxisListType

D = 1024
L = 2
NH = 6
SCALE = 96 ** -0.5
ALPHA = (2 * L) ** 0.25
TS = 32
NSB = 2
NCOL = 2112
GROUPS = [(0, 128), (128, 128), (256, 128), (384, 96), (480, 96), (576, 128), (704, 128)] + \
         [(832 + 128 * i, 128) for i in range(10)]
TWO_PI = 2.0 * math.pi


class Trk:
    __slots__ = ("w", "r")

    def __init__(self):
        self.w = None
        self.r = {}


class Buf:
    def __init__(self, ap, trk=None):
        self.ap = ap
        self.k = trk if trk is not None else Trk()

    def __getitem__(self, key):
        return self.ap[key]


class Ring:
    def __init__(self, bufs):
        self.bufs = bufs
        self.i = 0

    def next(self):
        b = self.bufs[self.i]
        self.i = (self.i + 1) % len(self.bufs)
        return b


class Prog:
    def __init__(self, nc, ndma=8):
        self.nc = nc
        self.es = {}
        self.sems = {}
        self._ctx = []
        for nm, eng in (("pe", nc.tensor), ("act", nc.scalar), ("dve", nc.vector), ("pool", nc.gpsimd),
                        ("sp", nc.sync)):
            cm = nc.semaphore("s_" + nm)
            self.sems[nm] = cm.__enter__()
            self._ctx.append(cm)
            self.es[nm] = dict(eng=eng, cnt=0, known={})
        self.snap = {}
        self.rings = {}
        for q in ("sp", "pool", "act"):
            ring = []
            for i in range(ndma):
                key = "d_%s%d" % (q, i)
                cm = nc.semaphore(key)
                self.sems[key] = cm.__enter__()
                self._ctx.append(cm)
                ring.append([key, 0])
            self.rings[q] = dict(ring=ring, nxt=0)

    def close(self):
        for cm in reversed(self._ctx):
            cm.__exit__(None, None, None)

    def _needs(self, r, w):
        needs = {}
        for b in r:
            t = b.k
            if t.w is not None:
                k, v = t.w
                if needs.get(k, 0) < v:
                    needs[k] = v
        for b in w:
            t = b.k
            if t.w is not None:
                k, v = t.w
                if needs.get(k, 0) < v:
                    needs[k] = v
            for k, v in t.r.items():
                if needs.get(k, 0) < v:
                    needs[k] = v
        return needs

    def _waits(self, en, needs):
        E = self.es[en]
        out = []
        for k, v in needs.items():
            if k == en and v > E["cnt"]:
                continue
            if E["known"].get(k, 0) < v:
                E["known"][k] = v
                out.append((k, v))
        for k, v in out:
            sn = self.snap.get(k, {}).get(v)
            if sn:
                for k2, v2 in sn.items():
                    if k2 != en and E["known"].get(k2, 0) < v2:
                        E["known"][k2] = v2
        return out

    def _mark(self, tok, r, w):
        for b in w:
            b.k.w = tok
            b.k.r = {}
        for b in r:
            if b.k.r.get(tok[0], 0) < tok[1]:
                b.k.r[tok[0]] = tok[1]

    def op(self, en, fn, r=(), w=(), inc=True):
        E = self.es[en]
        waits = self._waits(en, self._needs(r, w))
        for k, v in waits[:-1]:
            E["eng"].wait_ge(self.sems[k], v)
        ins = fn()
        if waits:
            k, v = waits[-1]
            ins._wait_ge(self.sems[k], v)
        if inc:
            E["cnt"] += 1
            ins.then_inc(self.sems[en], 1)
            tok = (en, E["cnt"])
            self.snap.setdefault(en, {})[E["cnt"]] = dict(E["known"])
        else:
            tok = (en, E["cnt"] + 1)
        self._mark(tok, r, w)
        return ins

    def dma(self, q, out, in_, r=(), w=(), **kw):
        E = self.es[q]
        R = self.rings[q]
        slot = R["ring"][R["nxt"]]
        R["nxt"] = (R["nxt"] + 1) % len(R["ring"])
        needs = self._needs(r, w)
        if slot[1] > 0:
            needs[slot[0]] = max(needs.get(slot[0], 0), slot[1])
        for k, v in self._waits(q, needs):
            E["eng"].wait_ge(self.sems[k], v)
        slot[1] += 16
        ins = E["eng"].dma_start(out=out, in_=in_, **kw)
        ins.then_inc(self.sems[slot[0]], 16)
        self.snap.setdefault(slot[0], {})[slot[1]] = dict(E["known"])
        self._mark((slot[0], slot[1]), r, w)
        return ins

    def barrier(self):
        targets = {}
        for en, E in self.es.items():
            if E["cnt"] > 0:
                targets[en] = E["cnt"]
        for q, R in self.rings.items():
            for k, v in R["ring"]:
                if v > 0:
                    targets[k] = v
        for en, E in self.es.items():
            for k, v in targets.items():
                if k != en and E["known"].get(k, 0) < v:
                    E["known"][k] = v
                    E["eng"].wait_ge(self.sems[k], v)


class Builder:
    def __init__(self, T, PAST):
        self.T = T
        self.PAST = PAST
        self.TT = T + NSB * TS
        self.KS = PAST + TS
        self.KTOT = T + NSB * self.KS
        self.seqs = [(0, T, 0, T, 0)]
        for s in range(NSB):
            self.seqs.append((T + s * TS, TS, T + s * self.KS, self.KS, PAST))
        self.tiles = [(i * 512, 512) for i in range(T // 512)] + [(T, NSB * TS)]
        nc = bass.Bass("TRN2", target_bir_lowering=False)
        self.nc = nc
        self.P = Prog(nc)
        self._cms = []
        self.evi = 0

    def sb(self, name, shape, dt=F32):
        self.uid = getattr(self, "uid", 0) + 1
        name = "%s_u%d" % (name, self.uid)
        cm = self.nc.sbuf_tensor(name, list(shape), dt)
        t = cm.__enter__()
        self._cms.append(cm)
        return Buf(t)

    def ps(self, name, shape=(128, 512), dt=F32):
        self.uid = getattr(self, "uid", 0) + 1
        name = "%s_u%d" % (name, self.uid)
        cm = self.nc.psum_tensor(name, list(shape), dt)
        t = cm.__enter__()
        self._cms.append(cm)
        return Buf(t)

    def mark(self):
        return len(self._cms)

    def release(self, m):
        while len(self._cms) > m:
            self._cms.pop().__exit__(None, None, None)

    def dram(self, name, shape, dt=F32, kind="Internal"):
        return Buf(self.nc.dram_tensor(name, list(shape), dt, kind=kind).ap())

    def mm(self, out, lhsT, rhs, start, stop, r, w, inc=None):
        nc = self.nc
        if inc is None:
            inc = stop
        return self.P.op("pe", lambda: nc.tensor.matmul(out, lhsT, rhs, start=start, stop=stop), r, w, inc)

    def tr(self, out, in_, ident, r, w, inc=True):
        nc = self.nc
        return self.P.op("pe", lambda: nc.tensor.transpose(out, in_, ident), r, w, inc)

    def act(self, out, in_, func, r, w, bias=None, scale=None, accum_out=None):
        nc = self.nc
        kw = {}
        if bias is not None:
            kw["bias"] = bias
        if scale is not None:
            kw["scale"] = scale
        if accum_out is not None:
            kw["accum_out"] = accum_out
        return self.P.op("act", lambda: nc.scalar.activation(out=out, in_=in_, func=func, **kw), r, w)

    def _ve(self, en):
        return self.nc.vector if en == "dve" else self.nc.gpsimd

    def cp(self, en, out, in_, r, w):
        if en == "act":
            return self.act(out, in_, AF.Copy, r, w)
        e = self._ve(en)
        return self.P.op(en, lambda: e.tensor_copy(out, in_), r, w)

    def tt(self, en, out, a, b, op, r, w):
        e = self._ve(en)
        return self.P.op(en, lambda: e.tensor_tensor(out, a, b, op), r, w)

    def tsc(self, en, out, a, s1, s2, op0, op1, r, w):
        e = self._ve(en)
        if op1 is None:
            return self.P.op(en, lambda: e.tensor_scalar(out, a, s1, None, op0), r, w)
        return self.P.op(en, lambda: e.tensor_scalar(out, a, s1, s2, op0, op1), r, w)

    def stt(self, en, out, a, s, b, op0, op1, r, w):
        en = "dve"
        e = self._ve(en)
        return self.P.op(en, lambda: e.scalar_tensor_tensor(out, a, s, b, op0, op1), r, w)

    def mset(self, en, out, val, w):
        e = self._ve(en)
        return self.P.op(en, lambda: e.memset(out, val), (), w)

    def ev(self):
        self.evi ^= 1
        return "act" if self.evi else "dve"

    def dma(self, q, out, in_, r, w, **kw):
        return self.P.dma(q, out, in_, r, w, **kw)

    def load_bf16(self, dst_ap, dst_buf, src_ap, shape, q="sp"):
        p, f = shape
        st = self.stage.next()
        self.dma(q, st[:p, :f], src_ap, r=[], w=[st])
        self.cvi = getattr(self, "cvi", 0) ^ 1
        self.cp("dve" if self.cvi else "act", dst_ap, st[:p, :f], r=[st], w=[dst_buf])


def make_consts(PAST):
    c = {}
    i = np.arange(128)
    c["ident"] = np.eye(128, dtype=np.float32)
    c["ones"] = np.ones((128, 128), np.float32)
    c["bones"] = (i[:, None] // 64 == i[None, :] // 64).astype(np.float32)
    c["maskS"] = (i[None, :] > i[:, None]).astype(np.float32)
    c["maskI"] = (i[None, :] >= i[:, None]).astype(np.float32)
    c["maskL"] = -(i[None, :] < i[:, None]).astype(np.float32)
    c["m4"] = np.concatenate([-c["maskS"], c["maskS"], c["maskI"], c["maskI"]], 1)
    sel = np.zeros((128, 64), np.float32)
    sel[64 + np.arange(64), np.arange(64)] = 1.0
    c["sel"] = sel
    c["swap"] = (i[None, :] == (i[:, None] + 64) % 128).astype(np.float32)
    c["iota"] = np.tile(np.arange(512, dtype=np.float32)[None, :], (128, 1))
    c["spos"] = np.tile((PAST + np.arange(NSB * TS) % TS).astype(np.float32)[None, :], (128, 1))
    c["gm"] = (i[:, None] // 16 == np.arange(8)[None, :]).astype(np.float32)
    E = np.zeros((128, 2, 128), np.float32)
    for g in range(16):
        E[g, g // 8, (g % 8) * 16:(g % 8) * 16 + 16] = 1.0
    c["E"] = E.reshape(128, 256)
    invf = np.zeros((128, 1), np.float32)
    f = (10000.0 ** (-np.arange(0, 32, 2, dtype=np.float32) / 32)).astype(np.float32)
    invf[64:96, 0] = np.concatenate([f, f])
    c["invf"] = invf
    sgn = np.zeros((128, 1), np.float32)
    sgn[64:80] = -1.0
    sgn[80:96] = 1.0
    c["sgn"] = sgn
    sg2 = np.ones((128, 2), np.float32)
    sg2[64:, 0] = -1.0
    sg2[:, 1] = -sg2[:, 0]
    c["sg2"] = sg2
    offs = {}
    o = 0
    parts = []
    for k, v in c.items():
        offs[k] = (o, v.shape[1])
        o += v.shape[1]
        parts.append(v)
    return np.ascontiguousarray(np.concatenate(parts, 1)), offs


def build_program(T, PAST, cst_w, offs, dbg=None):
    B = Builder(T, PAST)
    nc, P = B.nc, B.P
    TT, KTOT, KS = B.TT, B.KTOT, B.KS
    _ncd = nc.allow_non_contiguous_dma("small strided state / layout transfers")
    _ncd.__enter__()
    ext = lambda name, shape: Buf(nc.dram_tensor(name, list(shape), F32, kind="ExternalInput").ap())
    out_ = lambda name, shape: Buf(nc.dram_tensor(name, list(shape), F32, kind="ExternalOutput").ap())
    xin = ext("xin", [TT, D])
    ckvc = ext("ckvc", [L, NSB, PAST, 128])
    krc = ext("krc", [L, NSB, PAST, 32])
    s5st = ext("s5st", [L, NSB, 128, 16])
    rwst = ext("rwst", [L, NSB, 128, 3, 64])
    shst = ext("shst", [L, NSB, 128, 10])
    cst = ext("cst", [128, cst_w])
    win = ext("win", [L, 128, 8, NCOL])
    wqb = ext("wqb", [L, 128, 2, 576])
    wqbs = ext("wqbs", [L, 128, 2, 576])
    wkvk = ext("wkvk", [L, 128, 384])
    wkvv = ext("wkvv", [L, 128, 384])
    wout = ext("wout", [L, 128, 8, 1024])
    wup = ext("wup", [L, 32, 128, 1024])
    wdn = ext("wdn", [L, 128, 32, 1024])
    vec = ext("vec", [L, 128, 64])
    lnp = ext("lnp", [L, 4, 128, 1024])
    lora = ext("lora", [L, 128, 384])
    s5v = ext("s5v", [L, 16, 192])
    s5b = ext("s5b", [L, 2, 128, 2, 64])
    s5c = ext("s5c", [L, 2, 64, 256])
    wglu = ext("wglu", [L, 128, 2, 256])
    y = out_("y", [TT, D])
    o_ckv = out_("o_ckv", [L, TT, 128])
    o_kr = out_("o_kr", [L, TT, 32])
    o_s5 = out_("o_s5", [L, 3, 128, 16])
    o_rw = out_("o_rw", [L, 3, 128, 3, 64])
    o_sh = out_("o_sh", [L, 3, 128, 10])
    ROPE = B.dram("ROPE", [4, 128, TT])
    QT = [B.dram("QT%d" % l, [97, NH, TT], BF16) for l in range(L)]
    KT = [B.dram("KT%d" % l, [96, NH, KTOT], BF16) for l in range(L)]
    NKT = (KTOT + 127) // 128 + 4
    VA = [B.dram("VA%d" % l, [NKT, 128, 384], BF16) for l in range(L)]
    UT = [B.dram("UT%d" % l, [256, TT]) for l in range(L)]
    PT = [B.dram("PT%d" % l, [1280, TT]) for l in range(L)]
    MT = [B.dram("MT%d" % l, [1024, TT], BF16) for l in range(L)]
    X1 = [B.dram("X1_%d" % l, [TT, D]) for l in range(L)]
    X1T = [B.dram("X1T%d" % l, [1024, TT], BF16) for l in range(L)]
    XN = [B.dram("XN%d" % l, [TT, D]) for l in range(L - 1)] + [y]
    WUPB = B.dram("WUPB", [L, 32, 128, 1024], BF16)
    KMX = B.dram("KMX", [L, 128, NH])

    ktiles = []
    vt = 0
    for (r0, n, k0, nk, p0) in B.seqs:
        lst = []
        c = 0
        while c < nk:
            m = min(128, nk - c)
            lst.append((k0 + c, m, vt))
            vt += 1
            c += m
        ktiles.append(lst)

    C = B.sb("cst", [128, cst_w])
    P.dma("sp", C[:, :], cst[:, :], r=[], w=[C])
    cs = lambda k: C[:, offs[k][0]:offs[k][0] + offs[k][1]]
    Cb = B.sb("cstb", [128, 640], BF16)
    B.cp("dve", Cb[:, 0:128], cs("ident"), r=[C], w=[Cb])
    B.cp("dve", Cb[:, 128:256], cs("ones"), r=[C], w=[Cb])
    B.cp("dve", Cb[:, 256:384], cs("bones"), r=[C], w=[Cb])
    identb, onesb, bonesb = Cb[:, 0:128], Cb[:, 128:256], Cb[:, 256:384]
    ident = cs("ident")
    B.stage = Ring([B.sb("stage%d" % i, [128, 2112]) for i in range(2)])

    EPS = {256 * 1e-6: 0, 128 * 1e-6: 1, 1e-24: 2, 64e-5: 3, 1e-5: 4, 0.0: 5}
    epsb = B.sb("epsb", [128, 8])
    for v_, i_ in EPS.items():
        B.mset("dve", epsb[:, i_:i_ + 1], float(v_), w=[epsb])
    I32 = mybir.dt.int32

    def sqrt_pow(en, out, in_, add, expo, r, w, p0=0):
        i_ = EPS[add]
        B.act(out, in_, AF.Sqrt, r=list(r) + [epsb], w=w, bias=epsb[p0:p0 + out.shape[0], i_:i_ + 1])
        if expo < 0:
            P.op("dve", lambda: nc.vector.reciprocal(out, out), r=w, w=w)

    def _p0(ap):
        return ap.base_partition()

    def sincos(en, out_s, out_c, ang, wk, r, w):
        it_ap, ft_ap, wb = wk
        for o, sh in ((out_s, 0.0), (out_c, 0.25)):
            B.tsc(en, o, ang, 1.0 / TWO_PI, sh, ALU.mult, ALU.add, r=r, w=w)
            B.cp(en, it_ap, o, r=w, w=wb)
            B.cp(en, ft_ap, it_ap, r=wb, w=wb)
            B.tt(en, o, o, ft_ap, ALU.subtract, r=w + wb, w=w)
            B.act(o, o, AF.Sin, r=w, w=w, scale=TWO_PI * (1.0 - 1e-6))

    m0 = B.mark()
    wk = Ring([B.sb("r0_%d" % i, [128, 512]) for i in range(6)])
    r0i = B.sb("r0i", [128, 512], mybir.dt.int32)
    r0f = B.sb("r0f", [128, 512])
    r0w = Buf(None)
    for (t0, n) in B.tiles:
        pos = wk.next()
        if n == 512:
            B.tsc("pool", pos[64:96, :n], cs("iota")[64:96, :n], float(t0), None, ALU.add, None, r=[C], w=[pos])
        else:
            B.cp("pool", pos[64:96, :n], cs("spos")[64:96, :n], r=[C], w=[pos])
        B.tsc("dve", pos[64:96, :n], pos[64:96, :n], cs("invf")[64:96, :], None, ALU.mult, None, r=[pos, C], w=[pos])
        res = {"sin": wk.next(), "cos": wk.next()}
        sincos("dve", res["sin"][64:96, :n], res["cos"][64:96, :n], pos[64:96, :n],
               (r0i[64:96, :n], r0f[64:96, :n], [r0w]), r=[pos], w=[res["sin"], res["cos"]])
        B.tsc("dve", res["sin"][64:96, :n], res["sin"][64:96, :n], cs("sgn")[64:96, :], None, ALU.mult, None,
              r=[res["sin"], C], w=[res["sin"]])
        for i, nm in enumerate(("cos", "sin")):
            a = res[nm]
            P.dma("sp", ROPE[2 + i, 64:96, t0:t0 + n], a[64:96, :n], r=[a], w=[ROPE])
            q = wk.next()
            B.tsc("pool", q[64:96, :n], a[64:96, :n], SCALE, None, ALU.mult, None, r=[a], w=[q])
            P.dma("sp", ROPE[i, 64:96, t0:t0 + n], q[64:96, :n], r=[q], w=[ROPE])
    P.barrier()
    B.release(m0)

    def phase1(l):
        m1 = B.mark()
        psum = Ring([B.ps("ps%d" % i) for i in range(8)])
        Wi = B.sb("Wi", [128, 8, NCOL], BF16)
        for kc in range(8):
            B.load_bf16(Wi[:, kc, :], Wi, win[l, :, kc, :], [128, NCOL], q="sp" if kc % 2 else "act")
        Wq = B.sb("Wq", [128, 2, 576], BF16)
        Wqs = B.sb("Wqs", [128, 2, 576], BF16)
        B.load_bf16(Wq[:, :, :].rearrange("p a b -> p (a b)"), Wq, wqb[l].rearrange("p a b -> p (a b)"), [128, 1152])
        B.load_bf16(Wqs[:, :, :].rearrange("p a b -> p (a b)"), Wqs, wqbs[l].rearrange("p a b -> p (a b)"), [128, 1152])
        Wk = B.sb("Wk", [128, 384], BF16)
        Wv = B.sb("Wv", [128, 384], BF16)
        B.load_bf16(Wk[:, :], Wk, wkvk[l], [128, 384])
        B.load_bf16(Wv[:, :], Wv, wkvv[l], [128, 384])
        V = B.sb("vec", [128, 64])
        P.dma("sp", V[:, :], vec[l], r=[], w=[V])
        g16 = B.sb("g16", [128, 4])
        B.tsc("dve", g16[:, 0:2], V[:, 0:2], 16.0, None, ALU.mult, None, r=[V], w=[g16])
        B.tsc("dve", g16[:, 2:3], V[:, 2:3], math.sqrt(128.0), None, ALU.mult, None, r=[V], w=[g16])
        kmx = B.sb("kmx", [128, NH, 512])
        B.mset("pool", kmx[:, :, :], 0.0, w=[kmx])
        xts = Ring([B.sb("xt%d" % i, [128, 4, D]) for i in range(2)])
        xTs = Ring([B.sb("xT%d" % i, [128, 8, 512], BF16) for i in range(2)])
        ql = B.sb("ql", [128, 2, 512])
        sq = Ring([B.sb("sq%d" % i, [128, 512], BF16) for i in range(3)])
        rin = Ring([B.sb("rin%d" % i, [128, 512]) for i in range(2)])
        qn = B.sb("qn", [128, 2, 512], BF16)
        qT = Ring([B.sb("qT%d" % i, [128, NH, 512], BF16) for i in range(1)])
        kT = Ring([B.sb("kT%d" % i, [128, NH, 512], BF16) for i in range(1)])
        va = Ring([B.sb("va%d" % i, [128, 4, 384], BF16) for i in range(2)])
        rt = Ring([B.sb("rt%d" % i, [128, 4, 512]) for i in range(1)])
        tmp = Ring([B.sb("tmp%d" % i, [128, 512]) for i in range(4)])
        kvl = B.sb("kvl", [128, 512])
        ckvT = B.sb("ckvT", [128, 512])
        ckvTb = Ring([B.sb("ckvTb%d" % i, [128, 512], BF16) for i in range(2)])
        krT = B.sb("krT", [128, 512])
        ot = Ring([B.sb("ot%d" % i, [128, 4, 128]) for i in range(2)])
        ot2 = Ring([B.sb("ot2%d" % i, [128, 4, 32]) for i in range(2)])
        ut = Ring([B.sb("ut%d" % i, [128, 2, 512]) for i in range(1)])
        pt = Ring([B.sb("pt%d" % i, [128, 5, 512]) for i in range(1)])
        Xsrc = xin if l == 0 else XN[l - 1]

        def kv_expand(cb, kr, n, seq_parts):
            k_ = kT.next()
            v_ = va.next()
            for h in range(NH):
                ps = psum.next()
                B.mm(ps[0:64, :n], Wk[:, h * 64:(h + 1) * 64], cb[:, :n], True, True, r=[Wk, cb], w=[ps])
                B.cp(B.ev(), k_[0:64, h, :n], ps[0:64, :n], r=[ps], w=[k_])
                B.cp("pool", k_[64:96, h, :n], kr[64:96, :n], r=[kr, k_], w=[k_])
            for h in range(NH):
                s_ = sq.next()
                B.tt("pool", s_[0:96, :n], k_[0:96, h, :n], k_[0:96, h, :n], ALU.mult, r=[k_], w=[s_])
                ps = psum.next()
                B.mm(ps[0:97, :n], onesb[0:96, 0:97], s_[0:96, :n], True, True, r=[Cb, s_], w=[ps])
                B.tt("dve", kmx[0:97, h, :n], kmx[0:97, h, :n], ps[0:97, :n], ALU.max, r=[ps, kmx], w=[kmx])
            ss = min(128, n)
            for s in range((n + 127) // 128):
                ps = psum.next()
                B.mm(ps[:ss, 0:384], cb[:, s * ss:(s + 1) * ss], Wv[:, :], True, True, r=[cb, Wv], w=[ps])
                B.cp(B.ev(), v_[:ss, s, :], ps[:ss, 0:384], r=[ps], w=[v_])
            for (c0, ncol, kcol, vparts) in seq_parts:
                P.dma("pool", KT[l][:, :, kcol:kcol + ncol], k_[0:96, :, c0:c0 + ncol], r=[k_], w=[KT[l]])
                for (vti, s, row0, rows) in vparts:
                    P.dma("pool", VA[l][vti, 0:rows, :], v_[row0:row0 + rows, s, :], r=[v_], w=[VA[l]])

        pre1 = {}

        def prefetch1(ti):
            t0, n = B.tiles[ti]
            ss = min(128, n)
            nsub = n // ss
            xt = xts.next()
            P.dma("sp", xt[:ss, :nsub, :], Xsrc[t0:t0 + n, :].rearrange("(s p) d -> p s d", p=ss), r=[Xsrc], w=[xt])
            pre1[ti] = xt

        prefetch1(0)
        for ti, (t0, n) in enumerate(B.tiles):
            ss = min(128, n)
            nsub = n // ss
            if ti + 1 < len(B.tiles):
                prefetch1(ti + 1)
            xt = pre1.pop(ti)
            r_ = rt.next()
            P.dma("sp", r_[64:96, :, :n], ROPE[:, 64:96, t0:t0 + n].rearrange("a p n -> p a n"), r=[ROPE], w=[r_])
            xT = xTs.next()
            for kc in range(8):
                ps = psum.next()
                for s in range(nsub):
                    B.tr(ps[:, s * ss:(s + 1) * ss], xt[:ss, s, kc * 128:(kc + 1) * 128], ident[:ss, :ss],
                         r=[xt, C], w=[ps], inc=(s == nsub - 1))
                B.cp(B.ev(), xT[:, kc, :n], ps[:, :n], r=[ps], w=[xT])
            grp = []
            for gi, (c0, M) in enumerate(GROUPS):
                ps = psum.next()
                for kc in range(8):
                    B.mm(ps[:M, :n], Wi[:, kc, c0:c0 + M], xT[:, kc, :n], kc == 0, kc == 7, r=[Wi, xT], w=[ps])
                if gi < 2:
                    B.cp("act", ql[:, gi, :n], ps[:, :n], r=[ps], w=[ql])
                    if gi == 1:
                        ps2 = psum.next()
                        for j in range(2):
                            s_ = sq.next()
                            B.tt("pool", s_[:, :n], ql[:, j, :n], ql[:, j, :n], ALU.mult, r=[ql], w=[s_])
                            B.mm(ps2[:, :n], onesb, s_[:, :n], j == 0, j == 1, r=[Cb, s_], w=[ps2])
                        ri = rin.next()
                        sqrt_pow("dve", ri[:, :n], ps2[:, :n], 256 * 1e-6, -0.5, r=[ps2], w=[ri])
                        for j in range(2):
                            B.stt("dve", qn[:, j, :n], ql[:, j, :n], g16[:, j:j + 1], ri[:, :n], ALU.mult, ALU.mult,
                                  r=[ql, g16, ri], w=[qn])
                        q_ = qT.next()
                        for h in range(NH):
                            pa, pb = psum.next(), psum.next()
                            for j in range(2):
                                B.mm(pa[0:96, :n], Wq[:, j, h * 96:(h + 1) * 96], qn[:, j, :n], j == 0, j == 1,
                                     r=[Wq, qn], w=[pa])
                            for j in range(2):
                                B.mm(pb[0:96, :n], Wqs[:, j, h * 96:(h + 1) * 96], qn[:, j, :n], j == 0, j == 1,
                                     r=[Wqs, qn], w=[pb])
                            B.act(q_[0:64, h, :n], pa[0:64, :n], AF.Copy, r=[pa], w=[q_], scale=SCALE)
                            t1, t2 = tmp.next(), tmp.next()
                            B.tt("dve", t1[64:96, :n], pa[64:96, :n], r_[64:96, 0, :n], ALU.mult, r=[pa, r_], w=[t1])
                            B.tt("dve", t2[64:96, :n], pb[64:96, :n], r_[64:96, 1, :n], ALU.mult, r=[pb, r_], w=[t2])
                            B.tt("pool", q_[64:96, h, :n], t1[64:96, :n], t2[64:96, :n], ALU.add, r=[t1, t2, q_], w=[q_])
                            s_ = sq.next()
                            B.tt("pool", s_[0:96, :n], q_[0:96, h, :n], q_[0:96, h, :n], ALU.mult, r=[q_], w=[s_])
                            ps3 = psum.next()
                            B.mm(ps3[0:97, :n], onesb[0:96, 0:97], s_[0:96, :n], True, True, r=[Cb, s_], w=[ps3])
                            t3 = tmp.next()
                            sqrt_pow("dve", t3[96:97, :n], ps3[96:97, :n], 0.0, 0.5, r=[ps3], w=[t3], p0=96)
                            B.tsc("dve", q_[96:97, h, :n], t3[96:97, :n], -1.0, None, ALU.mult, None, r=[t3, q_], w=[q_])
                        P.dma("pool", QT[l][:, :, t0:t0 + n], q_[0:97, :, :n], r=[q_], w=[QT[l]])
                elif gi == 2:
                    B.cp("act", kvl[:, :n], ps[:, :n], r=[ps], w=[kvl])
                    s_ = sq.next()
                    B.tt("pool", s_[:, :n], kvl[:, :n], kvl[:, :n], ALU.mult, r=[kvl], w=[s_])
                    ps2 = psum.next()
                    B.mm(ps2[:, :n], onesb, s_[:, :n], True, True, r=[Cb, s_], w=[ps2])
                    ri = rin.next()
                    sqrt_pow("dve", ri[:, :n], ps2[:, :n], 128 * 1e-6, -0.5, r=[ps2], w=[ri])
                    B.stt("dve", ckvT[:, :n], kvl[:, :n], g16[:, 2:3], ri[:, :n], ALU.mult, ALU.mult,
                          r=[kvl, g16, ri], w=[ckvT])
                    cb = ckvTb.next()
                    B.cp("pool", cb[:, :n], ckvT[:, :n], r=[ckvT], w=[cb])
                    ps2 = psum.next()
                    for s in range(nsub):
                        B.tr(ps2[:ss, s * 128:(s + 1) * 128], ckvT[:, s * ss:(s + 1) * ss], ident, r=[ckvT, C],
                             w=[ps2], inc=(s == nsub - 1))
                    o_ = ot.next()
                    B.cp("act", o_[:ss, :nsub, :], ps2[:ss, 0:nsub * 128].rearrange("p (s d) -> p s d", d=128),
                         r=[ps2], w=[o_])
                    P.dma("pool", o_ckv[l, t0:t0 + n, :].rearrange("(s p) d -> p s d", p=ss), o_[:ss, :nsub, :],
                          r=[o_], w=[o_ckv])
                elif gi == 3:
                    pkr = ps
                elif gi == 4:
                    t1, t2 = tmp.next(), tmp.next()
                    B.tt("dve", t1[64:96, :n], pkr[64:96, :n], r_[64:96, 2, :n], ALU.mult, r=[pkr, r_], w=[t1])
                    B.tt("dve", t2[64:96, :n], ps[64:96, :n], r_[64:96, 3, :n], ALU.mult, r=[ps, r_], w=[t2])
                    B.tt("pool", krT[64:96, :n], t1[64:96, :n], t2[64:96, :n], ALU.add, r=[t1, t2], w=[krT])
                    ps2 = psum.next()
                    for s in range(nsub):
                        B.tr(ps2[:ss, s * 32:(s + 1) * 32], krT[64:96, s * ss:(s + 1) * ss], ident[64:96, 64:96],
                             r=[krT, C], w=[ps2], inc=(s == nsub - 1))
                    o_ = ot2.next()
                    B.cp("act", o_[:ss, :nsub, :], ps2[:ss, 0:nsub * 32].rearrange("p (s d) -> p s d", d=32),
                         r=[ps2], w=[o_])
                    P.dma("pool", o_kr[l, t0:t0 + n, :].rearrange("(s p) d -> p s d", p=ss), o_[:ss, :nsub, :],
                          r=[o_], w=[o_kr])
                    if n == 512:
                        vparts = [(ktiles[0][t0 // 128 + s][2], s, 0, 128) for s in range(4)]
                        parts = [(0, 512, t0, vparts)]
                    else:
                        parts = []
                        for si in range(NSB):
                            kc_, m_, vti = ktiles[1 + si][-1]
                            parts.append((si * TS, TS, kc_, [(vti, 0, si * TS, TS)]))
                    kv_expand(cb, krT, n, parts)
                elif gi < 7:
                    u_ = ut.next() if gi == 5 else u_
                    B.cp(B.ev(), u_[:, gi - 5, :n], ps[:, :n], r=[ps], w=[u_])
                    if gi == 6:
                        P.dma("pool", UT[l][:, t0:t0 + n].rearrange("(j p) n -> p j n", p=128), u_[:, :, :n],
                              r=[u_], w=[UT[l]])
                else:
                    hf_ = (gi - 7) // 5
                    p_ = pt.next() if (gi - 7) % 5 == 0 else p_
                    B.cp(B.ev(), p_[:, (gi - 7) % 5, :n], ps[:, :n], r=[ps], w=[p_])
                    if (gi - 7) % 5 == 4:
                        P.dma("pool", PT[l][hf_ * 640:(hf_ + 1) * 640, t0:t0 + n].rearrange("(j p) n -> p j n", p=128),
                              p_[:, :, :n], r=[p_], w=[PT[l]])
                        for si, (r0, nn, k0, nk, p0) in enumerate(B.seqs):
                            last = r0 + nn - 1
                            if t0 <= last < t0 + n:
                                P.dma("pool", o_sh[l, si, :, hf_ * 5:(hf_ + 1) * 5], p_[:, :, last - t0], r=[p_], w=[o_sh])
        cin = Ring([B.sb("cin%d" % i, [128, 4, 128]) for i in range(2)])
        kin = Ring([B.sb("kin%d" % i, [128, 4, 96]) for i in range(2)])
        for b in kin.bufs:
            B.mset("pool", b[:, :, :], 0.0, w=[b])
        for si in range(NSB):
            for c0 in range(0, PAST, 512):
                n = min(512, PAST - c0)
                nsub = n // 128
                ci, ki = cin.next(), kin.next()
                P.dma("sp", ci[:, :nsub, :], ckvc[l, si, c0:c0 + n, :].rearrange("(s p) d -> p s d", p=128), r=[], w=[ci])
                P.dma("act", ki[:, :nsub, 64:96], krc[l, si, c0:c0 + n, :].rearrange("(s p) d -> p s d", p=128), r=[], w=[ki])
                ps = psum.next()
                for s in range(nsub):
                    B.tr(ps[:, s * 128:(s + 1) * 128], ci[:, s, :], ident, r=[ci, C], w=[ps], inc=(s == nsub - 1))
                cb = ckvTb.next()
                B.cp(B.ev(), cb[:, :n], ps[:, :n], r=[ps], w=[cb])
                ps = psum.next()
                for s in range(nsub):
                    B.tr(ps[0:96, s * 128:(s + 1) * 128], ki[:, s, :], ident, r=[ki, C], w=[ps], inc=(s == nsub - 1))
                B.cp(B.ev(), krT[64:96, :n], ps[64:96, :n], r=[ps], w=[krT])
                kbase = B.seqs[1 + si][2]
                vparts = [(ktiles[1 + si][c0 // 128 + s][2], s, 0, 128) for s in range(nsub)]
                kv_expand(cb, krT, n, [(0, n, kbase + c0, vparts)])
        km = B.sb("km", [128, NH])
        P.op("dve", lambda: nc.vector.tensor_reduce(km[0:97, :], kmx[0:97, :, :], AX.X, ALU.max), r=[kmx], w=[km])
        sqrt_pow("dve", km[0:97, :], km[0:97, :], 0.0, 0.5, r=[km], w=[km])
        P.dma("sp", KMX[l, 0:97, :], km[0:97, :], r=[km], w=[KMX])
        P.barrier()
        B.release(m1)

    def phase2(l):
        m2 = B.mark()
        psS = Ring([B.ps("aS%d" % i, (128, 1024)) for i in range(3)])
        psO = Ring([B.ps("aO%d" % i) for i in range(1)])
        psL = Ring([B.ps("aL%d" % i) for i in range(1)])
        kmr = B.sb("kmr", [128, NH])
        P.dma("sp", kmr[0:97, :], KMX[l, 0:97, :], r=[KMX], w=[kmr])
        maxk = max(T, KS)
        maxt = (maxk + 127) // 128
        Kb = Ring([B.sb("Kb%d" % i, [128, maxk], BF16) for i in range(2)])
        for b in Kb.bufs:
            B.mset("pool", b[96:97, :], 1.0, w=[b])
        Vb = Ring([B.sb("Vb%d" % i, [128, maxt, 64], BF16) for i in range(2)])
        Qb = Ring([B.sb("Qb%d" % i, [128, T], BF16) for i in range(2)])
        ptr = Ring([B.sb("pt%d" % i, [128, 1024], BF16) for i in range(4)])
        rl = Ring([B.sb("rl%d" % i, [64, 512]) for i in range(2)])
        o32 = Ring([B.sb("o32_%d" % i, [64, 512]) for i in range(2)])
        mo = Ring([B.sb("mo%d" % i, [64, 512], BF16) for i in range(2)])
        its = [(h, si) for h in range(NH) for si in range(len(B.seqs))]
        NWARM = 0
        LOOK = 2
        loaded = {}

        def load(i):
            h, si = its[i]
            r0, nt, k0, nk, p0 = B.seqs[si]
            kb, vb, qb = Kb.next(), Vb.next(), Qb.next()
            kts = ktiles[si]
            P.dma("pool", kb[0:96, :nk], KT[l][:, h, k0:k0 + nk], r=[KT[l]], w=[kb])
            nfull = nk // 128
            P.dma("act", vb[:, :nfull, :],
                  VA[l][kts[0][2]:kts[0][2] + nfull, :, h * 64:(h + 1) * 64].rearrange("t p e -> p t e"),
                  r=[VA[l]], w=[vb])
            if nk % 128:
                P.dma("act", vb[:nk % 128, nfull, :], VA[l][kts[-1][2], 0:nk % 128, h * 64:(h + 1) * 64],
                      r=[VA[l]], w=[vb])
            P.dma("pool", qb[0:97, :nt], QT[l][:, h, r0:r0 + nt], r=[QT[l]], w=[qb])
            B.tsc("dve", qb[96:97, :nt], qb[96:97, :nt], kmr[96:97, h:h + 1], None, ALU.mult, None,
                  r=[qb, kmr], w=[qb])
            loaded[i] = (kb, vb, qb)

        load(0)
        for it, (h, si) in enumerate(its):
            r0, nt, k0, nk, p0 = B.seqs[si]
            kts = ktiles[si]
            if it + 1 < len(its):
                load(it + 1)
            kb, vb, qb = loaded.pop(it)
            pW = psS.next()
            for wi in range(NWARM):
                B.mm(pW[:, (wi % 2) * 512:(wi % 2) * 512 + 512], kb[0:97, 0:128], qb[0:97, 0:512] if nt >= 512 else
                     kb[0:97, 0:512], True, True, r=[kb, qb], w=[pW], inc=(wi == NWARM - 1))
            for q0 in range(0, nt, 512):
                nq = min(512, nt - q0)
                W = 512 if nq == 512 else nq
                G = 2 if nq == 512 else max(1, min(8, 1024 // nq))
                pO, pL = psO.next(), psL.next()
                groups = []
                for kt, (kcol, m, vti) in enumerate(kts):
                    if si == 0:
                        if kt * 128 >= q0 + nq:
                            break
                        off = max(0, kt * 128 - q0)
                        diag = (kt * 128 >= q0)
                    else:
                        off, diag = 0, False
                    plain = (not diag) and m == 128
                    if plain and groups and groups[-1][0] and len(groups[-1][1]) < G:
                        groups[-1][1].append((kt, m, off, diag))
                    else:
                        groups.append((plain, [(kt, m, off, diag)]))
                nblk = sum(len(g[1]) for g in groups)
                pend = {}

                def score(gi):
                    plain, tl = groups[gi]
                    pS, p_ = psS.next(), ptr.next()
                    for j, (kt, m, off, diag) in enumerate(tl):
                        B.mm(pS[:m, j * W + off:j * W + nq], kb[0:97, kt * 128:kt * 128 + m],
                             qb[0:97, q0 + off:q0 + nq], True, True, r=[kb, qb], w=[pS])
                    if plain:
                        B.act(p_[:, 0:len(tl) * W], pS[:, 0:len(tl) * W], AF.Exp, r=[pS], w=[p_])
                    else:
                        kt, m, off, diag = tl[0]
                        B.act(p_[:m, off:nq], pS[:m, off:nq], AF.Exp, r=[pS], w=[p_])
                        if diag:
                            B.mset("pool", p_[64:128, off:off + 64], 0.0, w=[p_])
                    sm_ = None
                    pend[gi] = (p_, sm_)

                for gi in range(min(LOOK, len(groups))):
                    score(gi)
                done = 0
                for gi, (plain, tl) in enumerate(groups):
                    if gi + LOOK < len(groups):
                        score(gi + LOOK)
                    p_, sm_ = pend.pop(gi)
                    for j, (kt, m, off, diag) in enumerate(tl):
                        first, last = done == 0, done == nblk - 1
                        B.mm(pO[0:64, off:nq], vb[:m, kt, :], p_[:m, j * W + off:j * W + nq], first, last,
                             r=[vb, p_], w=[pO])
                        if sm_ is None:
                            B.mm(pL[0:64, off:nq], onesb[:m, 0:64], p_[:m, j * W + off:j * W + nq], first, last,
                                 r=[Cb, p_], w=[pL])
                        elif j == 1:
                            B.mm(pL[0:64, 0:nq], onesb[:, 0:64], sm_[:, 0:nq], done == 1, last, r=[Cb, sm_], w=[pL])
                        done += 1
                r_, o3 = rl.next(), o32.next()
                B.cp("dve", r_[:, :nq], pL[0:64, :nq], r=[pL], w=[r_])
                B.cp("dve", o3[:, :nq], pO[0:64, :nq], r=[pO], w=[o3])
                P.op("dve", lambda: nc.vector.reciprocal(r_[:, :nq], r_[:, :nq]), r=[r_], w=[r_])
                o_ = mo.next()
                B.tt("pool", o_[:, :nq], o3[:, :nq], r_[:, :nq], ALU.mult, r=[o3, r_], w=[o_])
                P.dma("sp", MT[l][h * 64:(h + 1) * 64, r0 + q0:r0 + q0 + nq], o_[:, :nq], r=[o_], w=[MT[l]])
        P.barrier()
        B.release(m2)

    LT = 256

    def phase3(l):
        m3 = B.mark()
        pAB = Ring([B.ps("sA%d" % i) for i in range(4)])
        pY = [B.ps("sY%d" % i) for i in range(2)]
        pG = Ring([B.ps("sG%d" % i) for i in range(2)])
        V = B.sb("vec3", [128, 64])
        P.dma("sp", V[:, :], vec[l], r=[], w=[V])
        s5d, bgl = V[:, 3:5], V[:, 5:7]
        sv = B.sb("sv", [16, 192])
        P.dma("sp", sv[:, :], s5v[l], r=[], w=[sv])
        w16 = B.sb("w16", [16, 16, 64])
        W = lambda i: w16[:, i, :]
        K16 = [w16]
        lre, lim = sv[:, 0:64], sv[:, 64:128]
        B.act(W(0), sv[:, 128:192], AF.Exp, r=[sv], w=K16)
        B.tt("dve", W(1), lre, W(0), ALU.mult, r=[sv] + K16, w=K16)
        B.tt("dve", W(2), lim, W(0), ALU.mult, r=[sv] + K16, w=K16)
        B.act(W(3), W(1), AF.Exp, r=K16, w=K16)
        w16i = B.sb("w16i", [16, 64], mybir.dt.int32)
        sincos("dve", W(4), W(5), W(2), (w16i[:, :], W(13), K16), r=K16, w=K16)
        B.tt("dve", W(6), W(3), W(5), ALU.mult, r=K16, w=K16)
        B.tsc("dve", W(6), W(6), -1.0, None, ALU.add, None, r=K16, w=K16)
        B.tt("dve", W(7), W(3), W(4), ALU.mult, r=K16, w=K16)
        B.tt("dve", W(8), lre, lre, ALU.mult, r=[sv] + K16, w=K16)
        B.tt("dve", W(9), lim, lim, ALU.mult, r=[sv] + K16, w=K16)
        B.tt("dve", W(8), W(8), W(9), ALU.add, r=K16, w=K16)
        P.op("dve", lambda: nc.vector.reciprocal(W(8), W(8)), r=K16, w=K16)
        B.tt("dve", W(9), W(6), lre, ALU.mult, r=[sv] + K16, w=K16)
        B.tt("dve", W(10), W(7), lim, ALU.mult, r=[sv] + K16, w=K16)
        B.tt("dve", W(9), W(9), W(10), ALU.add, r=K16, w=K16)
        B.tt("dve", W(11), W(9), W(8), ALU.mult, r=K16, w=K16)
        B.tt("dve", W(9), W(7), lre, ALU.mult, r=[sv] + K16, w=K16)
        B.tt("dve", W(10), W(6), lim, ALU.mult, r=[sv] + K16, w=K16)
        B.tt("dve", W(9), W(9), W(10), ALU.subtract, r=K16, w=K16)
        B.tt("dve", W(12), W(9), W(8), ALU.mult, r=K16, w=K16)
        cat = B.sb("cat", [16, 2, 128])
        for i, src in enumerate((W(2), W(3))):
            B.cp("dve", cat[:, i, 0:64], src, r=K16, w=[cat])
            B.cp("dve", cat[:, i, 64:128], src, r=K16, w=[cat])
        thr = B.sb("thr", [128, 2, 16])
        for i in range(2):
            ps = pG.next()
            B.tr(ps[:, 0:16], cat[:, i, :], ident[0:16, 0:16], r=[cat, C], w=[ps])
            B.cp("dve", thr[:, i, :], ps[:, 0:16], r=[ps], w=[thr])
        thS, rS = thr[:, 0, :], thr[:, 1, :]
        ctab = B.sb("ctab", [128, 16, LT])
        stab = B.sb("stab", [128, 16, LT])
        rmat = B.sb("rmat", [128, 16, LT])
        stabS = B.sb("stabS", [128, 16, LT])
        io1 = B.sb("io1", [128, LT])
        B.tsc("dve", io1[:, :], cs("iota")[:, 0:LT], 1.0, None, ALU.add, None, r=[C], w=[io1])
        ang = B.sb("ang", [128, LT])
        angi = B.sb("angi", [128, LT], mybir.dt.int32)
        angf = B.sb("angf", [128, LT])
        for g in range(16):
            B.tsc("dve", ang[:, :], io1[:, :], thS[:, g:g + 1], None, ALU.mult, None, r=[io1, thr], w=[ang])
            sincos("dve", stab[:, g, :], ctab[:, g, :], ang[:, :], (angi[:, :], angf[:, :], [angf]), r=[ang],
                   w=[stab, ctab])
            B.tsc("pool", rmat[:, g, :], io1[:, :], 0.0, rS[:, g:g + 1], ALU.mult, ALU.add, r=[io1, thr], w=[rmat])
            B.tsc("pool", stabS[:, g, :], stab[:, g, :], cs("sg2")[:, 0:1], None, ALU.mult, None, r=[stab, C], w=[stabS])
        bb = B.sb("bb", [128, 2, 2, 64])
        P.dma("sp", bb[:, 0, :, :], s5b[l, 0], r=[], w=[bb])
        P.dma("sp", bb[:, 1, :, :], s5b[l, 1], r=[], w=[bb])
        Ff = B.sb("Ff", [128, 2, 2, 64])
        for i, fsrc in enumerate((W(11), W(12))):
            for gh in range(2):
                ps = pG.next()
                B.mm(ps[:, 0:64], cs("E")[0:16, gh * 128:(gh + 1) * 128], fsrc, True, True, r=[C] + K16, w=[ps])
                B.cp("dve", Ff[:, i, gh, :], ps[:, 0:64], r=[ps], w=[Ff])
        bbar = B.sb("bbar", [128, 2, 2, 64])
        t5 = B.sb("t5", [128, 2, 64])
        B.tt("dve", bbar[:, 0, :, :], Ff[:, 0, :, :], bb[:, 0, :, :], ALU.mult, r=[Ff, bb], w=[bbar])
        B.tt("dve", t5[:, :, :], Ff[:, 1, :, :], bb[:, 1, :, :], ALU.mult, r=[Ff, bb], w=[t5])
        B.tt("dve", bbar[:, 0, :, :], bbar[:, 0, :, :], t5[:, :, :], ALU.subtract, r=[bbar, t5], w=[bbar])
        B.tt("dve", bbar[:, 1, :, :], Ff[:, 0, :, :], bb[:, 1, :, :], ALU.mult, r=[Ff, bb], w=[bbar])
        B.tt("dve", t5[:, :, :], Ff[:, 1, :, :], bb[:, 0, :, :], ALU.mult, r=[Ff, bb, bbar], w=[t5])
        B.tt("dve", bbar[:, 1, :, :], bbar[:, 1, :, :], t5[:, :, :], ALU.add, r=[bbar, t5], w=[bbar])
        LB = B.sb("LB", [128, 2, 16, 128], BF16)
        for g in range(16):
            gh, g8 = g // 8, g % 8
            for sw in range(2):
                for half in range(2):
                    src = bbar[:, half ^ sw, gh, :]
                    B.tsc("pool" if half else "dve", LB[:, sw, g, half * 64:(half + 1) * 64], src,
                          cs("gm")[:, g8:g8 + 1], None, ALU.mult, None, r=[bbar, C], w=[LB])
        cc = B.sb("cc", [128, 2, 256])
        P.dma("sp", cc[0:64, 0, :], s5c[l, 0], r=[], w=[cc])
        P.dma("sp", cc[64:128, 0, :], s5c[l, 1], r=[], w=[cc])
        P.dma("sp", cc[0:64, 1, :], s5c[l, 1], r=[], w=[cc])
        P.dma("sp", cc[64:128, 1, :], s5c[l, 0], r=[], w=[cc])
        B.tsc("dve", cc[64:128, 0, :], cc[64:128, 0, :], -1.0, None, ALU.mult, None, r=[cc], w=[cc])
        B.tsc("dve", cc[:, 1, :], cc[:, 1, :], -1.0, None, ALU.mult, None, r=[cc], w=[cc])
        CP = B.sb("CP", [128, 2, 16, 128], BF16)
        B.mset("pool", CP[:, :, :, :], 0.0, w=[CP])
        for g in range(16):
            g8 = g % 8
            for i in range(2):
                B.cp("dve", CP[:, i, g, g8 * 16:(g8 + 1) * 16], cc[:, i, g * 16:(g + 1) * 16], r=[cc], w=[CP])
        Wg = B.sb("Wg", [128, 2, 256], BF16)
        B.load_bf16(Wg[:, :, :].rearrange("p a b -> p (a b)"), Wg, wglu[l].rearrange("p a b -> p (a b)"), [128, 512])
        uts = Ring([B.sb("u3_%d" % i, [128, 2, LT]) for i in range(2)])
        ubs = Ring([B.sb("ub3_%d" % i, [128, 2, LT], BF16) for i in range(2)])
        w1 = Ring([B.sb("w1_%d" % i, [128, LT]) for i in range(12)])
        zr = Ring([B.sb("z_%d" % i, [128, LT]) for i in range(4)])
        zb = Ring([B.sb("zb_%d" % i, [128, LT], BF16) for i in range(6)])
        zend = B.sb("zend", [128, 16])
        xst = B.sb("xst", [128, 16])
        yv = Ring([B.sb("yv_%d" % i, [128, LT]) for i in range(4)])
        zz = B.sb("zz", [128, 2, LT])
        zzb = B.sb("zzb", [128, 2, LT], BF16)
        mo = Ring([B.sb("mo3_%d" % i, [128, LT], BF16) for i in range(2)])
        for si, (r0, nt, k0, nk, p0) in enumerate(B.seqs):
            if si == 0:
                B.mset("pool", xst[:, :], 0.0, w=[xst])
            else:
                P.dma("sp", xst[:, :], s5st[l, si - 1], r=[], w=[xst])
            for c0 in range(0, nt, LT):
                n = min(LT, nt - c0)
                u_, ub = uts.next(), ubs.next()
                P.dma("sp", u_[:, :, :n], UT[l][:, r0 + c0:r0 + c0 + n].rearrange("(j p) n -> p j n", p=128),
                      r=[UT[l]], w=[u_])
                B.cp("pool", ub[:, :, :n], u_[:, :, :n], r=[u_], w=[ub])
                pab = {}

                def stA(g):
                    gh = g // 8
                    pa, pb = pAB.next(), pAB.next()
                    B.mm(pa[:, :n], LB[:, 0, g, :], ub[:, gh, :n], True, True, r=[LB, ub], w=[pa])
                    B.mm(pb[:, :n], LB[:, 1, g, :], ub[:, gh, :n], True, True, r=[LB, ub], w=[pb])
                    pab[g] = (pa, pb)

                wvs = {}

                def stB1(g):
                    pa, pb = pab.pop(g)
                    t1, t2, wv = w1.next(), w1.next(), w1.next()
                    B.tt("dve", t1[:, :n], pa[:, :n], ctab[:, g, :n], ALU.mult, r=[pa, ctab], w=[t1])
                    B.tt("dve", t2[:, :n], pb[:, :n], stabS[:, g, :n], ALU.mult, r=[pb, stabS], w=[t2])
                    B.tt("pool", wv[:, :n], t2[:, :n], t1[:, :n], ALU.add, r=[t1, t2], w=[wv])
                    wvs[g] = wv

                stA(0)
                stA(1)
                stB1(0)
                for g in range(16):
                    gh, g8 = g // 8, g % 8
                    if g + 1 < 16:
                        stB1(g + 1)
                    if g + 2 < 16:
                        stA(g + 2)
                    wv = wvs.pop(g)
                    z = zr.next()
                    P.op("dve", lambda: nc.vector.tensor_tensor_scan(z[:, :n], rmat[:, g, :n], wv[:, :n],
                                                                     xst[:, g:g + 1], ALU.mult, ALU.add),
                         r=[rmat, wv, xst], w=[z])
                    zc, zs = zb.next(), zb.next()
                    B.tt("dve", zc[:, :n], z[:, :n], ctab[:, g, :n], ALU.mult, r=[z, ctab], w=[zc])
                    B.tt("pool", zs[:, :n], z[:, :n], stab[:, g, :n], ALU.mult, r=[z, stab], w=[zs])
                    B.cp("act", zend[:, g:g + 1], z[:, n - 1:n], r=[z], w=[zend])
                    B.mm(pY[gh][:, :n], CP[:, 0, g, :], zc[:, :n], g8 == 0, False, r=[CP, zc], w=[pY[gh]], inc=True)
                    B.mm(pY[gh][:, :n], CP[:, 1, g, :], zs[:, :n], False, g8 == 7, r=[CP, zs], w=[pY[gh]], inc=True)
                ps = pG.next()
                B.mm(ps[:, 0:16], cs("swap"), zend[:, :], True, True, r=[C, zend], w=[ps])
                t1, t2 = w1.next(), w1.next()
                B.tt("dve", t1[:, 0:16], zend[:, :], ctab[:, :, n - 1], ALU.mult, r=[zend, ctab], w=[t1])
                B.tt("dve", t2[:, 0:16], ps[:, 0:16], stabS[:, :, n - 1], ALU.mult, r=[ps, stabS], w=[t2])
                B.stt("dve", xst[:, :], t2[:, 0:16], -1.0, t1[:, 0:16], ALU.mult, ALU.add,
                      r=[t1, t2], w=[xst])
                for j in range(2):
                    y_, x2 = yv.next(), yv.next()
                    B.stt("dve", y_[:, :n], u_[:, j, :n], s5d[:, j:j + 1], pY[j][:, :n], ALU.mult, ALU.add,
                          r=[u_, V, pY[j]], w=[y_])
                    B.tt("pool", x2[:, :n], y_[:, :n], y_[:, :n], ALU.mult, r=[y_], w=[x2])
                    B.tsc("pool", x2[:, :n], x2[:, :n], 0.044715, 1.0, ALU.mult, ALU.add, r=[x2], w=[x2])
                    B.tt("pool", x2[:, :n], x2[:, :n], y_[:, :n], ALU.mult, r=[x2, y_], w=[x2])
                    B.act(x2[:, :n], x2[:, :n], AF.Sigmoid, r=[x2], w=[x2], scale=1.5957691216057308)
                    B.tt("pool", zz[:, j, :n], y_[:, :n], x2[:, :n], ALU.mult, r=[y_, x2], w=[zz])
                    B.cp("pool", zzb[:, j, :n], zz[:, j, :n], r=[zz], w=[zzb])
                for jo in range(2):
                    ps = pG.next()
                    for j in range(2):
                        B.mm(ps[:, :n], Wg[:, j, jo * 128:(jo + 1) * 128], zzb[:, j, :n], j == 0, j == 1,
                             r=[Wg, zzb], w=[ps])
                    gt = yv.next()
                    B.act(gt[:, :n], ps[:, :n], AF.Sigmoid, r=[ps, V], w=[gt], bias=bgl[:, jo:jo + 1])
                    o_ = mo.next()
                    B.tt("dve", o_[:, :n], zz[:, jo, :n], gt[:, :n], ALU.mult, r=[zz, gt], w=[o_])
                    P.dma("sp", MT[l][384 + jo * 128:384 + (jo + 1) * 128, r0 + c0:r0 + c0 + n], o_[:, :n],
                          r=[o_], w=[MT[l]])
            P.dma("sp", o_s5[l, si], xst[:, :], r=[xst], w=[o_s5])
        P.barrier()
        B.release(m3)

    C0 = math.exp(-0.5)

    def phase4(l):
        m4 = B.mark()

        def rsqrt_ln(out, in_, add, r, w):
            i_ = EPS[add]
            B.act(out, in_, AF.Ln, r=list(r) + [epsb], w=w, bias=epsb[0:out.shape[0], i_:i_ + 1])
            B.act(out, out, AF.Exp, r=w, w=w, scale=-0.5)
        ring7 = Ring([B.ps("rB%d" % i) for i in range(7)])
        pSc = pM_ = slots = ring7
        pYb = B.ps("rY")
        V = B.sb("vec4", [128, 64])
        P.dma("sp", V[:, :], vec[l], r=[], w=[V])
        mu, w0, a0, k_k, k_a, r_k, gng, gnb = (V[:, 7:17], V[:, 17:20], V[:, 20:23], V[:, 23:26], V[:, 26:29],
                                               V[:, 29:32], V[:, 32:35], V[:, 35:38])
        omka = B.sb("omka", [128, 3])
        B.tsc("dve", omka[:, :], k_a, -1.0, 1.0, ALU.mult, ALU.add, r=[V], w=[omka])
        lo = B.sb("lo", [128, 384], BF16)
        B.load_bf16(lo[:, :], lo, lora[l], [128, 384])
        cmask = B.sb("cmask", [128, 640], BF16)
        B.cp("dve", cmask[:, 0:512], cs("m4"), r=[C], w=[cmask])
        B.cp("dve", cmask[:, 512:640], cs("maskL"), r=[C], w=[cmask])
        mk = lambda nm, shp, dt=F32, k=2: Ring([B.sb("%s%d" % (nm, i), shp, dt) for i in range(k)])
        cur, prv, dd = mk("cur", [128, 10, 128]), mk("prv", [128, 10, 128]), mk("dd", [128, 10, 128], F32, 1)
        psx = mk("psx", [128, 10, 128])
        sm = mk("sm", [128, 128], BF16, 4)
        lr3 = mk("lr3", [128, 128], BF16, 6)
        f3 = lambda nm, k=2: mk(nm, [128, 3, 128], F32, k)
        sig, aa, gg, kk_, kap, kti, bb_, bon, css, dm, pin, pinv, pex = (f3("sig"), f3("aa"), f3("gg", 3), f3("kk"),
            f3("kap"), f3("kti"), f3("bb"), f3("bon", 3), f3("css", 1), f3("dm", 1), f3("pin", 3), f3("pinv", 1), f3("pex", 1))
        b3 = lambda nm, k=2: mk(nm, [128, 3, 128], BF16, k)
        rh, kph, bhb, khb = b3("rh", 3), b3("kph", 3), b3("bhb"), b3("khb")
        bhf, khf = f3("bhf", 1), f3("khf", 1)
        kTt, bTt, vtt = b3("kTt", 3), b3("bTt", 3), b3("vtt", 3)
        scb = [mk("scb%d" % h, [128, 512], BF16, 3) for h in range(NH)]
        nbN = [mk("nbN%d" % h, [128, 128], BF16, 3) for h in range(NH)]
        nbB = [mk("nbB%d" % h, [128, 128], BF16, 3) for h in range(NH)]
        Mb = [mk("Mb%d" % h, [128, 128], BF16, 9) for h in range(NH)]
        zn = [mk("zn%d" % h, [128, 64], BF16, 2) for h in range(NH)]
        u2 = mk("u2", [128, 128], BF16, 6)
        Hf = B.sb("Hf", [128, 3, 64])
        Hb = mk("Hb", [128, 3, 64], BF16, 2)
        pmid, ppr = mk("pmid", [128, 3], F32, 3), mk("ppr", [128, 3], F32, 3)
        st6 = mk("st6", [128, 6, 6], F32, 2)
        mv6 = mk("mv6", [128, 6, 2], F32, 2)
        yn = mk("yn", [128, 384], F32, 2)
        fo = mk("fo", [128, 128], F32, 3)
        mo = mk("mo4", [128, 128], BF16, 3)
        chunks = [(si, c0) for si, sq in enumerate(B.seqs) for c0 in range(0, sq[1], 128)]

        def gen_prep(si, c0, X):
            r0, nt, k0, nk, p0 = B.seqs[si]
            n = min(128, nt - c0)
            a_, b_ = r0 + c0, r0 + c0 + n
            cu, pv = cur.next(), prv.next()
            P.dma("sp", cu[:, :, :n], PT[l][:, a_:b_].rearrange("(j p) n -> p j n", p=128), r=[PT[l]], w=[cu])
            if c0 == 0:
                if si == 0:
                    B.mset("pool", pv[:, :, 0:1], 0.0, w=[pv])
                else:
                    P.dma("act", pv[:, :, 0], shst[l, si - 1], r=[], w=[pv])
                if n > 1:
                    P.dma("act", pv[:, :, 1:n], PT[l][:, a_:b_ - 1].rearrange("(j p) n -> p j n", p=128),
                          r=[PT[l]], w=[pv])
            else:
                P.dma("act", pv[:, :, :n], PT[l][:, a_ - 1:b_ - 1].rearrange("(j p) n -> p j n", p=128),
                      r=[PT[l]], w=[pv])
            d_ = dd.next()
            B.tt("pool", d_[:, :, :n], pv[:, :, :n], cu[:, :, :n], ALU.subtract, r=[pv, cu], w=[d_])
            yield
            px = psx.next()
            for j in range(10):
                B.stt("dve" if j % 2 else "pool", px[:, j, :n], d_[:, j, :n], mu[:, j:j + 1], cu[:, j, :n],
                      ALU.mult, ALU.add, r=[d_, cu, V], w=[px])
            R_, K_, V_ = (lambda c: px[:, c, :n]), (lambda c: px[:, 3 + c, :n]), (lambda c: px[:, 6 + c, :n])
            th, adb, sgd = lr3.next(), lr3.next(), lr3.next()
            tht = fo.next()
            B.act(tht[0:32, :n], px[0:32, 9, :n], AF.Sigmoid, r=[px], w=[tht], scale=2.0)
            B.tsc("dve", th[0:32, :n], tht[0:32, :n], 2.0, -1.0, ALU.mult, ALU.add, r=[tht], w=[th])
            B.cp("dve", adb[32:64, :n], px[32:64, 9, :n], r=[px], w=[adb])
            B.act(sgd[64:128, :n], px[64:128, 9, :n], AF.Sigmoid, r=[px], w=[sgd])
            yield
            sg, a3, g3, k3, kp3, kt3, b3_, bn3 = (sig.next(), aa.next(), gg.next(), kk_.next(), kap.next(),
                                                  kti.next(), bb_.next(), bon.next())
            for c in range(3):
                cc_ = slice(c * 128, (c + 1) * 128)
                p1 = pM_.next()
                B.mm(p1[:, 0:n], lo[0:32, cc_], th[0:32, :n], True, True, r=[lo, th], w=[p1])
                B.mm(p1[:, 128:128 + n], lo[32:64, cc_], adb[32:64, :n], True, True, r=[lo, adb], w=[p1])
                B.mm(p1[:, 256:256 + n], lo[64:128, cc_], sgd[64:128, :n], True, True, r=[lo, sgd], w=[p1])
                B.act(sg[:, c, :n], p1[:, 0:n], AF.Sigmoid, r=[p1, V], w=[sg], bias=w0[:, c:c + 1])
                B.act(a3[:, c, :n], p1[:, 128:128 + n], AF.Sigmoid, r=[p1, V], w=[a3], bias=a0[:, c:c + 1])
                B.cp("act", g3[:, c, :n], p1[:, 256:256 + n], r=[p1], w=[g3])
            for c in range(3):
                B.tsc("pool", k3[:, c, :n], K_(c), k_k[:, c:c + 1], None, ALU.mult, None, r=[px, V], w=[k3])
                s_ = sm.next()
                B.tt("pool", s_[:, :n], k3[:, c, :n], k3[:, c, :n], ALU.mult, r=[k3], w=[s_])
                p2 = pM_.next()
                B.mm(p2[:, 0:n], bonesb, s_[:, :n], True, True, r=[Cb, s_], w=[p2])
                t_ = fo.next()
                rsqrt_ln(t_[:, :n], p2[:, 0:n], 1e-24, r=[p2], w=[t_])
                B.tt("pool", kp3[:, c, :n], k3[:, c, :n], t_[:, :n], ALU.mult, r=[k3, t_], w=[kp3])
                t2_ = fo.next()
                B.tsc("dve", t2_[:, :n], a3[:, c, :n], k_a[:, c:c + 1], omka[:, c:c + 1], ALU.mult, ALU.add,
                      r=[a3, V, omka], w=[t2_])
                B.tt("pool", kt3[:, c, :n], K_(c), t2_[:, :n], ALU.mult, r=[px, t2_], w=[kt3])
                B.tt("pool", b3_[:, c, :n], kp3[:, c, :n], a3[:, c, :n], ALU.mult, r=[kp3, a3], w=[b3_])
                s2_ = sm.next()
                B.stt("dve", s2_[:, :n], R_(c), r_k[:, c:c + 1], kt3[:, c, :n], ALU.mult, ALU.mult,
                      r=[px, V, kt3], w=[s2_])
                B.mm(p2[:, 128:128 + n], bonesb, s2_[:, :n], True, True, r=[Cb, s2_], w=[p2])
                B.tt("dve", bn3[:, c, :n], p2[:, 128:128 + n], V_(c), ALU.mult, r=[p2, px], w=[bn3])
            yield
            cs_, dm_, pi_, piv, pe_ = css.next(), dm.next(), pin.next(), pinv.next(), pex.next()
            mid = min(63, n - 1)
            pm_, pr_ = pmid.next(), ppr.next()
            for c in range(3):
                P.op("dve", lambda: nc.vector.tensor_tensor_scan(cs_[:, c, :n], cs("ones")[:, 0:n], sg[:, c, :n],
                                                                 0.0, ALU.mult, ALU.add), r=[C, sg], w=[cs_])
                B.tsc("dve", dm_[:, c, :n], cs_[:, c, :n], cs_[:, c, mid:mid + 1], None, ALU.subtract, None,
                      r=[cs_], w=[dm_])
            B.act(pi_[:, :, :n], dm_[:, :, :n], AF.Exp, r=[dm_], w=[pi_], scale=-C0)
            B.act(piv[:, :, :n], dm_[:, :, :n], AF.Exp, r=[dm_], w=[piv], scale=C0)
            B.tt("pool", dm_[:, :, :n], dm_[:, :, :n], sg[:, :, :n], ALU.subtract, r=[dm_, sg], w=[dm_])
            B.act(pe_[:, :, :n], dm_[:, :, :n], AF.Exp, r=[dm_], w=[pe_], scale=-C0)
            B.act(pm_[:, :], cs_[:, :, mid], AF.Exp, r=[cs_], w=[pm_], scale=-C0)
            B.tt("dve", pr_[:, :], pm_[:, :], pi_[:, :, n - 1], ALU.mult, r=[pm_, pi_], w=[pr_])
            yield
            rh_, kph_, bhb_, khb_, bhf_, khf_ = rh.next(), kph.next(), bhb.next(), khb.next(), bhf.next(), khf.next()
            B.tt("pool", rh_[:, :, :n], px[:, 0:3, :n], pi_[:, :, :n], ALU.mult, r=[px, pi_], w=[rh_])
            B.tt("pool", kph_[:, :, :n], kp3[:, :, :n], pe_[:, :, :n], ALU.mult, r=[kp3, pe_], w=[kph_])
            B.tt("dve", bhf_[:, :, :n], b3_[:, :, :n], piv[:, :, :n], ALU.mult, r=[b3_, piv], w=[bhf_])
            B.tt("dve", khf_[:, :, :n], kt3[:, :, :n], piv[:, :, :n], ALU.mult, r=[kt3, piv], w=[khf_])
            B.cp("pool", bhb_[:, :, :n], bhf_[:, :, :n], r=[bhf_], w=[bhb_])
            B.cp("pool", khb_[:, :, :n], khf_[:, :, :n], r=[khf_], w=[khb_])
            yield
            kT_, bT_, vt_ = kTt.next(), bTt.next(), vtt.next()
            for c in range(3):
                for src, dst in ((khf_[:, c, :n], kT_), (bhf_[:, c, :n], bT_), (V_(c), vt_)):
                    p1 = pM_.next()
                    B.tr(p1[:n, 0:128], src, ident, r=[khf_, bhf_, px, C], w=[p1])
                    B.cp(B.ev(), dst[:n, c, :], p1[:n, 0:128], r=[p1], w=[dst])
            X.update(n=n, a_=a_, b_=b_, rh_=rh_, kph_=kph_, vt_=vt_, kT_=kT_, bT_=bT_, pi_=pi_, pm_=pm_,
                     pr_=pr_, bn3=bn3, g3=g3, bhb_=bhb_, khb_=khb_)
            yield

        def gen_ab(si, c0, X):
            n, rh_, kph_, bhb_, khb_ = (X[k] for k in ('n', 'rh_', 'kph_', 'bhb_', 'khb_'))
            nlev = 6 if n > 64 else (5 if n > 32 else 4)
            heads = [(c, hh) for c in range(3) for hh in range(2)]
            Rs = lambda hh: slice(hh * 64, hh * 64 + 64)
            st = {}
            for (c, hh) in heads:
                R = Rs(hh)
                sc = pSc.next()
                B.mm(sc[:n, 0:n], bhb_[R, c, :n], kph_[R, c, :n], True, True, r=[bhb_, kph_], w=[sc])
                B.mm(sc[:n, n:2 * n], khb_[R, c, :n], kph_[R, c, :n], True, True, r=[khb_, kph_], w=[sc])
                B.mm(sc[:n, 2 * n:3 * n], bhb_[R, c, :n], rh_[R, c, :n], True, True, r=[bhb_, rh_], w=[sc])
                B.mm(sc[:n, 3 * n:4 * n], khb_[R, c, :n], rh_[R, c, :n], True, True, r=[khb_, rh_], w=[sc])
                sb_ = scb[2 * c + hh].next()
                if n == 128:
                    B.tt("dve", sb_[:n, :], sc[:n, :], cmask[:n, 0:512], ALU.mult, r=[sc, cmask], w=[sb_])
                else:
                    for q in range(4):
                        B.tt("dve", sb_[:n, q * n:(q + 1) * n], sc[:n, q * n:(q + 1) * n],
                             cmask[:n, q * 128:q * 128 + n], ALU.mult, r=[sc, cmask], w=[sb_])
                p1 = slots.next()
                B.mm(p1[:n, 0:n], kph_[R, c, :n], bhb_[R, c, :n], True, True, r=[kph_, bhb_], w=[p1])
                Nk_ = nbN[2 * c + hh].next()
                B.tt("dve", Nk_[:n, :n], p1[:n, 0:n], cmask[:n, 512:512 + n], ALU.mult, r=[p1, cmask], w=[Nk_])
                M_ = Mb[2 * c + hh].next()
                B.tt("pool", M_[:n, :n], sb_[:n, 0:n], identb[:n, :n], ALU.add, r=[sb_, Cb], w=[M_])
                st[(c, hh)] = dict(sb=sb_, Bk=sb_[:n, 0:n], BkB=sb_, Nk=Nk_[:n, :n], NkB=Nk_, M=M_)
            yield
            for lev in range(nlev):
                lastl = lev == nlev - 1
                for (c, hh) in heads:
                    S_ = st[(c, hh)]
                    pa = slots.next()
                    B.mm(pa[:n, 0:n], S_["Bk"], S_["Nk"], True, True, r=[S_["BkB"], S_["NkB"]], w=[pa])
                    if not lastl:
                        pb = slots.next()
                        B.mm(pb[:n, 0:n], S_["Nk"], S_["Bk"], True, True, r=[S_["BkB"], S_["NkB"]], w=[pb])
                    nb1 = nbN[2 * c + hh].next()
                    B.cp("act", nb1[:n, :n], pa[:n, 0:n], r=[pa], w=[nb1])
                    if not lastl:
                        nb2 = nbB[2 * c + hh].next()
                        B.cp("act", nb2[:n, :n], pb[:n, 0:n], r=[pb], w=[nb2])
                        S_["Bk"], S_["BkB"] = nb2[:n, :n], nb2
                    S_["Nk"], S_["NkB"] = nb1[:n, :n], nb1
                for (c, hh) in heads:
                    S_ = st[(c, hh)]
                    pm2 = slots.next()
                    B.mm(pm2[:n, 0:n], S_["Nk"], S_["M"][:n, :n], True, True, r=[S_["NkB"], S_["M"]], w=[pm2])
                    M2 = Mb[2 * c + hh].next()
                    B.tt("dve", M2[:n, :n], pm2[:n, 0:n], S_["M"][:n, :n], ALU.add, r=[pm2, S_["M"]], w=[M2])
                    S_["M"] = M2
                yield
            X.update(st=st, heads=heads, Rs=Rs)
            yield

        def gen_c(si, c0, X):
            r0, nt, k0, nk, p0 = B.seqs[si]
            n, a_, b_, rh_, kph_, vt_, kT_, bT_, st, pi_, pm_, pr_, bn3, g3, heads, Rs = (X[k] for k in (
                'n', 'a_', 'b_', 'rh_', 'kph_', 'vt_', 'kT_', 'bT_', 'st', 'pi_', 'pm_', 'pr_', 'bn3', 'g3', 'heads', 'Rs'))
            if c0 == 0:
                if si == 0:
                    B.mset("pool", Hf[:, :, :], 0.0, w=[Hf])
                else:
                    P.dma("sp", Hf[:, :, :], rwst[l, si - 1], r=[], w=[Hf])
            hb = Hb.next()
            for c in range(3):
                B.tsc("dve", hb[:, c, :], Hf[:, c, :], pm_[:, c:c + 1], None, ALU.mult, None, r=[Hf, pm_], w=[hb])
            u_s = [u2.next() for c in range(3)]
            yield
            for (c, hh) in heads:
                S_ = st[(c, hh)]
                R = Rs(hh)
                pz = slots.next()
                AKm = S_["sb"][:n, n:2 * n]
                B.mm(pz[:n, 0:64], kph_[R, c, :n], hb[R, c, :], True, False, r=[kph_, hb], w=[pz], inc=True)
                B.mm(pz[:n, 0:64], AKm, vt_[:n, c, R], False, True, r=[S_["sb"], vt_], w=[pz])
                z_ = zn[2 * c + hh].next()
                B.act(z_[:n, :], pz[:n, 0:64], AF.Copy, r=[pz], w=[z_], scale=-1.0)
                S_["z"] = z_
            yield
            for (c, hh) in heads:
                S_ = st[(c, hh)]
                R = Rs(hh)
                pu = slots.next()
                B.mm(pu[:n, 0:64], S_["M"][:n, :n], S_["z"][:n, :], True, True, r=[S_["M"], S_["z"]], w=[pu])
                B.cp("act", u_s[c][:n, R], pu[:n, 0:64], r=[pu], w=[u_s[c]])
            yield
            for (c, hh) in heads:
                S_ = st[(c, hh)]
                R = Rs(hh)
                h = 2 * c + hh
                RBm, RKm = S_["sb"][:n, 2 * n:3 * n], S_["sb"][:n, 3 * n:4 * n]
                yr = pYb[:n, h * 64:(h + 1) * 64]
                B.mm(yr, rh_[R, c, :n], hb[R, c, :], True, False, r=[rh_, hb], w=[pYb], inc=True)
                B.mm(yr, RBm, u_s[c][:n, R], False, False, r=[S_["sb"], u_s[c]], w=[pYb], inc=True)
                B.mm(yr, RKm, vt_[:n, c, R], False, True, r=[S_["sb"], vt_], w=[pYb])
            yield
            for c in range(3):
                pH_ = ring7.next()
                B.mm(pH_[:, 0:128], kT_[:n, c, :], vt_[:n, c, :], True, False, r=[kT_, vt_], w=[pH_], inc=True)
                B.mm(pH_[:, 0:128], bT_[:n, c, :], u_s[c][:n, :], False, True, r=[bT_, u_s[c]], w=[pH_])
                for hh in range(2):
                    R = Rs(hh)
                    t_ = fo.next()
                    B.tsc("dve", t_[R, 0:64], pH_[R, R], pi_[R, c, n - 1:n], None, ALU.mult, None,
                          r=[pH_, pi_], w=[t_])
                    B.stt("dve", Hf[R, c, :], Hf[R, c, :], pr_[R, c:c + 1], t_[R, 0:64], ALU.mult, ALU.add,
                          r=[Hf, pr_, t_], w=[Hf])
            yield
            s6, m6, y_ = st6.next(), mv6.next(), yn.next()
            for h in range(NH):
                P.op("dve", lambda: nc.vector.bn_stats(s6[:n, h, :], pYb[:n, h * 64:(h + 1) * 64]), r=[pYb], w=[s6])
                P.op("dve", lambda: nc.vector.bn_aggr(m6[:n, h, :], s6[:n, h, :]), r=[s6], w=[m6])
            rsqrt_ln(m6[:n, :, 1], m6[:n, :, 1], 64e-5, r=[m6], w=[m6])
            for h in range(NH):
                B.tsc("dve", y_[:n, h * 64:(h + 1) * 64], pYb[:n, h * 64:(h + 1) * 64], m6[:n, h, 0:1],
                      m6[:n, h, 1:2], ALU.subtract, ALU.mult, r=[pYb, m6], w=[y_])
            yield
            for c in range(3):
                p1 = pM_.next()
                B.tr(p1[:, 0:n], y_[:n, c * 128:(c + 1) * 128], ident[:n, :n], r=[y_, C], w=[p1])
                t_ = fo.next()
                B.tsc("dve", t_[:, :n], p1[:, 0:n], gng[:, c:c + 1], gnb[:, c:c + 1], ALU.mult, ALU.add,
                      r=[p1, V], w=[t_])
                B.tt("pool", t_[:, :n], t_[:, :n], bn3[:, c, :n], ALU.add, r=[t_, bn3], w=[t_])
                o_ = mo.next()
                B.tt("pool", o_[:, :n], t_[:, :n], g3[:, c, :n], ALU.mult, r=[t_, g3], w=[o_])
                P.dma("sp", MT[l][640 + c * 128:640 + (c + 1) * 128, a_:b_], o_[:, :n], r=[o_], w=[MT[l]])
            if c0 + 128 >= nt:
                P.dma("sp", o_rw[l, si], Hf[:, :, :], r=[Hf], w=[o_rw])
            yield

        nck = len(chunks)
        Xs = {}
        for k in range(nck + 2):
            gens = []
            if k < nck:
                Xs[k] = {}
                gens.append(gen_prep(chunks[k][0], chunks[k][1], Xs[k]))
            if 0 <= k - 1 < nck:
                gens.append(gen_ab(chunks[k - 1][0], chunks[k - 1][1], Xs[k - 1]))
            if 0 <= k - 2 < nck:
                gens.append(gen_c(chunks[k - 2][0], chunks[k - 2][1], Xs[k - 2]))
            alive = list(gens)
            while alive:
                for g_ in list(alive):
                    try:
                        next(g_)
                    except StopIteration:
                        alive.remove(g_)
            Xs.pop(k - 2, None)
        P.barrier()
        B.release(m4)

    def layer_norm(z, ss, gB, bB, outb, st, mv):
        for hf in range(2):
            P.op("dve", lambda: nc.vector.bn_stats(st[:ss, hf, :], z[:ss, hf * 512:(hf + 1) * 512]), r=[z], w=[st])
        P.op("dve", lambda: nc.vector.bn_aggr(mv[:ss, :], st[:ss, :, :].rearrange("p a b -> p (a b)")), r=[st], w=[mv])
        sqrt_pow("dve", mv[:ss, 1:2], mv[:ss, 1:2], 1e-5, -0.5, r=[mv], w=[mv])
        B.tsc("dve", outb[:ss, :], z[:ss, :], mv[:ss, 0:1], mv[:ss, 1:2], ALU.subtract, ALU.mult, r=[z, mv], w=[outb])
        B.tt("pool", outb[:ss, :], outb[:ss, :], gB[:ss, :], ALU.mult, r=[outb, gB], w=[outb])
        B.tt("pool", outb[:ss, :], outb[:ss, :], bB[:ss, :], ALU.add, r=[outb, bB], w=[outb])

    def phase5a(l):
        m5 = B.mark()
        pso = Ring([B.ps("oP%d" % i) for i in range(6)])
        Wo = B.sb("Wo", [128, 8, 1024], BF16)
        for kc in range(8):
            B.load_bf16(Wo[:, kc, :], Wo, wout[l, :, kc, :], [128, 1024], q="sp" if kc % 2 else "act")
        gB, bB = B.sb("g1", [128, 1024]), B.sb("b1", [128, 1024])
        P.dma("sp", gB[:, :], lnp[l, 0], r=[], w=[gB])
        P.dma("sp", bB[:, :], lnp[l, 1], r=[], w=[bB])
        Xsrc = xin if l == 0 else XN[l - 1]
        mts = Ring([B.sb("mt%d" % i, [128, 8, 512], BF16) for i in range(2)])
        xts = Ring([B.sb("x5_%d" % i, [128, 4, D]) for i in range(2)])
        zs = Ring([B.sb("z5_%d" % i, [128, D]) for i in range(2)])
        os_ = Ring([B.sb("o5_%d" % i, [128, D]) for i in range(3)])
        xT = Ring([B.sb("xT5_%d" % i, [128, 8, 512], BF16) for i in range(2)])
        st, mv = B.sb("st5", [128, 2, 6]), B.sb("mv5", [128, 2])
        wcv = Ring([B.sb("wcv%d" % i, [128, 1024], BF16) for i in range(2)])
        for j in range(32):
            stg = B.stage.next()
            P.dma("pool", stg[:, 0:1024], wup[l, j], r=[], w=[stg])
            wb = wcv.next()
            B.cp("act", wb[:, :], stg[:, 0:1024], r=[stg], w=[wb])
            P.dma("pool", WUPB[l, j], wb[:, :], r=[wb], w=[WUPB])
        pre5 = {}

        def prefetch5(ti):
            t0, n = B.tiles[ti]
            ss = min(128, n)
            nsub = n // ss
            mt, xt = mts.next(), xts.next()
            P.dma("sp", mt[:, :, :n], MT[l][:, t0:t0 + n].rearrange("(k p) n -> p k n", p=128), r=[MT[l]], w=[mt])
            P.dma("sp", xt[:ss, :nsub, :], Xsrc[t0:t0 + n, :].rearrange("(s p) d -> p s d", p=ss), r=[Xsrc], w=[xt])
            pre5[ti] = (mt, xt)

        prefetch5(0)
        for ti, (t0, n) in enumerate(B.tiles):
            ss = min(128, n)
            nsub = n // ss
            if ti + 1 < len(B.tiles):
                prefetch5(ti + 1)
            mt, xt = pre5.pop(ti)
            x_T = xT.next()
            def mm_part(s):
                z = zs.next()
                for hf in range(2):
                    ps = pso.next()
                    for kc in range(8):
                        B.mm(ps[:ss, :], mt[:, kc, s * ss:(s + 1) * ss], Wo[:, kc, hf * 512:(hf + 1) * 512],
                             kc == 0, kc == 7, r=[mt, Wo], w=[ps])
                    B.stt("dve", z[:ss, hf * 512:(hf + 1) * 512], xt[:ss, s, hf * 512:(hf + 1) * 512], ALPHA,
                          ps[:ss, :], ALU.mult, ALU.add, r=[xt, ps], w=[z])
                o_ = os_.next()
                layer_norm(z, ss, gB, bB, o_, st, mv)
                P.dma("pool", X1[l][t0 + s * ss:t0 + (s + 1) * ss, :], o_[:ss, :], r=[o_], w=[X1[l]])
                return o_

            def tr_part(s, o_):
                for kc in range(0, 8, 4):
                    ps = pso.next()
                    for k2 in range(4):
                        B.tr(ps[:, k2 * 128:k2 * 128 + ss], o_[:ss, (kc + k2) * 128:(kc + k2 + 1) * 128], ident[:ss, :ss],
                             r=[o_, C], w=[ps], inc=(k2 == 3))
                    B.cp(B.ev(), x_T[:, kc:kc + 4, s * ss:(s + 1) * ss],
                         ps[:, :].rearrange("p (a b) -> p a b", a=4)[:, :, 0:ss], r=[ps], w=[x_T])

            prev = None
            for s in range(nsub):
                o_ = mm_part(s)
                if prev is not None:
                    tr_part(*prev)
                prev = (s, o_)
            tr_part(*prev)
            P.dma("pool", X1T[l][:, t0:t0 + n].rearrange("(k p) n -> p k n", p=128), x_T[:, :, :n], r=[x_T], w=[X1T[l]])
        P.barrier()
        B.release(m5)

    def phase5b(l):
        m5 = B.mark()
        psu = Ring([B.ps("uP%d" % i) for i in range(4)])
        psd = Ring([B.ps("dP%d" % i) for i in range(4)])
        Wd = B.sb("Wd", [128, 32, 1024], BF16)
        for j in range(32):
            B.load_bf16(Wd[:, j, :], Wd, wdn[l, :, j, :], [128, 1024], q="sp" if j % 2 else "act")
        gB, bB = B.sb("g2", [128, 1024]), B.sb("b2", [128, 1024])
        P.dma("sp", gB[:, :], lnp[l, 2], r=[], w=[gB])
        P.dma("sp", bB[:, :], lnp[l, 3], r=[], w=[bB])
        xTs = Ring([B.sb("xT6_%d" % i, [128, 8, 512], BF16) for i in range(2)])
        slab = Ring([B.sb("sl%d" % i, [128, 1024], BF16) for i in range(6)])
        hT = B.sb("hT", [128, 32, 512], BF16)
        rl_ = Ring([B.sb("rl6_%d" % i, [128, 512]) for i in range(4)])
        x1s = Ring([B.sb("x6_%d" % i, [128, D]) for i in range(2)])
        zs = Ring([B.sb("z6_%d" % i, [128, D]) for i in range(2)])
        os_ = Ring([B.sb("o6_%d" % i, [128, D]) for i in range(2)])
        st, mv = B.sb("st6b", [128, 2, 6]), B.sb("mv6b", [128, 2])
        pre6 = {}

        def prefetch6(ti):
            t0, n = B.tiles[ti]
            x_T = xTs.next()
            P.dma("sp", x_T[:, :, :n], X1T[l][:, t0:t0 + n].rearrange("(k p) n -> p k n", p=128), r=[X1T[l]], w=[x_T])
            pre6[ti] = x_T

        prefetch6(0)
        for ti, (t0, n) in enumerate(B.tiles):
            ss = min(128, n)
            nsub = n // ss
            if ti + 1 < len(B.tiles):
                prefetch6(ti + 1)
            x_T = pre6.pop(ti)
            slq = {}

            def slab_load(j):
                sl = slab.next()
                P.dma("sp", sl[:, :], WUPB[l, j], r=[WUPB], w=[sl])
                slq[j] = sl

            for j in range(3):
                slab_load(j)
            for j in range(32):
                if j + 3 < 32:
                    slab_load(j + 3)
                sl = slq.pop(j)
                ps = psu.next()
                for kc in range(8):
                    B.mm(ps[:, :n], sl[:, kc * 128:(kc + 1) * 128], x_T[:, kc, :n], kc == 0, kc == 7, r=[sl, x_T], w=[ps])
                if j % 2:
                    t_ = rl_.next()
                    B.tsc("dve", t_[:, :n], ps[:, :n], 0.0, None, ALU.max, None, r=[ps], w=[t_])
                    B.tt("dve", hT[:, j, :n], t_[:, :n], t_[:, :n], ALU.mult, r=[t_], w=[hT])
                else:
                    t_ = rl_.next()
                    B.act(t_[:, :n], ps[:, :n], AF.Relu, r=[ps], w=[t_])
                    B.tt("pool", hT[:, j, :n], t_[:, :n], t_[:, :n], ALU.mult, r=[t_], w=[hT])
            for s in range(nsub):
                x1 = x1s.next()
                P.dma("sp", x1[:ss, :], X1[l][t0 + s * ss:t0 + (s + 1) * ss, :], r=[X1[l]], w=[x1])
                z = zs.next()
                for hf in range(2):
                    ps = psd.next()
                    for j in range(32):
                        B.mm(ps[:ss, :], hT[:, j, s * ss:(s + 1) * ss], Wd[:, j, hf * 512:(hf + 1) * 512],
                             j == 0, j == 31, r=[hT, Wd], w=[ps])
                    B.stt("dve", z[:ss, hf * 512:(hf + 1) * 512], x1[:ss, hf * 512:(hf + 1) * 512], ALPHA,
                          ps[:ss, :], ALU.mult, ALU.add, r=[x1, ps], w=[z])
                o_ = os_.next()
                layer_norm(z, ss, gB, bB, o_, st, mv)
                P.dma("pool", XN[l][t0 + s * ss:t0 + (s + 1) * ss, :], o_[:ss, :], r=[o_], w=[XN[l]])
        P.barrier()
        B.release(m5)

    phases = dbg if dbg is not None else ["1", "2", "3", "4", "5a", "5b"]
    for l in range(L):
        for ph, fn in (("1", phase1), ("2", phase2), ("3", phase3), ("4", phase4), ("5a", phase5a), ("5b", phase5b)):
            if ph in phases:
                fn(l)
    P.barrier()
    return nc


def _layout_weights(w, PAST):
    f = lambda a: np.ascontiguousarray(a, dtype=np.float32)
    cm = lambda v: np.ascontiguousarray(v.reshape(L, -1, 128).transpose(0, 2, 1))
    w_in = w["w_in"]
    Wp = np.zeros((L, D, NCOL), np.float32)
    Wp[:, :, 0:384] = w_in[:, :, 0:384]
    kr = w_in[:, :, 384:416]
    Wp[:, :, 384 + 64:480] = kr
    Wp[:, :, 480 + 64:480 + 80] = kr[:, :, 16:32]
    Wp[:, :, 480 + 80:576] = kr[:, :, 0:16]
    Wp[:, :, 576:832] = w_in[:, :, 416:672]
    Wp[:, :, 832:2112] = w_in[:, :, 672:1952]
    o = {}
    o["win"] = f(Wp.reshape(L, 8, 128, NCOL).transpose(0, 2, 1, 3))
    wq = w["w_qb"]
    o["wqb"] = f(wq.reshape(L, 2, 128, 576).transpose(0, 2, 1, 3))
    wqs = np.zeros_like(wq)
    for h in range(NH):
        b = h * 96
        wqs[:, :, b + 64:b + 80] = wq[:, :, b + 80:b + 96]
        wqs[:, :, b + 80:b + 96] = wq[:, :, b + 64:b + 80]
    o["wqbs"] = f(wqs.reshape(L, 2, 128, 576).transpose(0, 2, 1, 3))
    wkv = w["w_kvb"].reshape(L, 128, NH, 128)
    o["wkvk"] = f(wkv[:, :, :, 0:64].reshape(L, 128, 384))
    o["wkvv"] = f(wkv[:, :, :, 64:128].reshape(L, 128, 384))
    o["wout"] = f(w["w_out"].reshape(L, 8, 128, 1024).transpose(0, 2, 1, 3))
    o["wup"] = f(w["w_up"].reshape(L, 8, 128, 32, 128).transpose(0, 3, 2, 1, 4).reshape(L, 32, 128, 1024))
    o["wdn"] = f(w["w_down"].reshape(L, 32, 128, 1024).transpose(0, 2, 1, 3))
    vec = np.zeros((L, 128, 64), np.float32)
    vec[:, :, 0:2] = cm(w["q_norm_g"])
    vec[:, :, 2:3] = cm(w["kv_norm_g"])
    vec[:, :, 3:5] = cm(w["s5_d"])
    vec[:, :, 5:7] = cm(w["b_glu"])
    vec[:, :, 7:17] = cm(w["mu_shift"])
    vec[:, :, 17:20] = cm(w["w0"])
    vec[:, :, 20:23] = cm(w["a0"])
    vec[:, :, 23:26] = cm(w["k_k"])
    vec[:, :, 26:29] = cm(w["k_a"])
    vec[:, :, 29:32] = cm(w["r_k"].reshape(L, 384))
    vec[:, :, 32:35] = cm(w["gn_g"])
    vec[:, :, 35:38] = cm(w["gn_b"])
    o["vec"] = vec
    ln = np.stack([w["ln1_g"], w["ln1_b"], w["ln2_g"], w["ln2_b"]], 1)
    o["lnp"] = f(np.broadcast_to(ln[:, :, None, :], (L, 4, 128, 1024)))
    o["lora"] = f(np.concatenate([w["w_w2"], w["w_a2"], w["w_g2"]], 1))
    o["s5v"] = f(np.concatenate([w["lam_re"], w["lam_im"], np.repeat(w["log_dt"][:, :, None], 64, 2)], 2))
    bt = lambda b: b.reshape(L, 2, 8, 64, 16).transpose(0, 2, 4, 1, 3).reshape(L, 128, 2, 64)
    o["s5b"] = f(np.stack([bt(w["b_re"]), bt(w["b_im"])], 1))
    ct = lambda c: c.transpose(0, 3, 1, 2).reshape(L, 64, 256)
    o["s5c"] = f(np.stack([ct(w["c_re"]), ct(w["c_im"])], 1))
    o["wglu"] = f(w["w_glu"].reshape(L, 2, 128, 256).transpose(0, 2, 1, 3))
    return o


_CACHE = {}


def run_cores(inp, T, PAST, n_cores, dbg=None, extra_out=()):
    cstv, offs = make_consts(PAST)
    key = (T, PAST, tuple(dbg) if dbg else None)
    if key not in _CACHE:
        _CACHE[key] = build_program(T, PAST, cstv.shape[1], offs, dbg)
    nc = _CACHE[key]
    wl = _layout_weights(inp, PAST)
    wl["cst"] = cstv
    in_maps = []
    for c in range(n_cores):
        m = dict(wl)
        sb = slice(NSB * c, NSB * c + NSB)
        m["xin"] = np.ascontiguousarray(np.concatenate(
            [inp["x_prompt"][c], inp["x_sample"][sb].reshape(NSB * TS, D)], 0), dtype=np.float32)
        m["ckvc"] = np.ascontiguousarray(inp["cache_mla_ckv"][:, sb])
        m["krc"] = np.ascontiguousarray(inp["cache_mla_krope"][:, sb])
        s5 = inp["state_s5"][:, sb]
        m["s5st"] = np.ascontiguousarray(s5.transpose(0, 1, 4, 3, 2).reshape(L, NSB, 128, 16))
        rw = inp["state_rwkv"][:, sb].reshape(L, NSB, 3, 2, 64, 64)
        m["rwst"] = np.ascontiguousarray(rw.transpose(0, 1, 3, 5, 2, 4).reshape(L, NSB, 128, 3, 64))
        sh = inp["state_rwkv_shift"][:, sb].reshape(L, NSB, 10, 128)
        m["shst"] = np.ascontiguousarray(sh.transpose(0, 1, 3, 2))
        in_maps.append(m)
    res = run_bass_kernel_spmd(nc, in_maps, core_ids=list(range(n_cores)))
    return res.results


def assemble(rs, T):
    nb = len(rs)
    cat = lambda f: np.stack([f(r) for r in rs], 0)
    y = cat(lambda r: r["y"])
    y_p = y[:, :T]
    y_s = y[:, T:].reshape(nb * NSB, TS, D)

    def tok(name, w):
        a = cat(lambda r: r[name])
        p = a[:, :, :T].transpose(1, 0, 2, 3)
        s = a[:, :, T:].reshape(nb, L, NSB, TS, w).transpose(1, 0, 2, 3, 4).reshape(L, nb * NSB, TS, w)
        return np.ascontiguousarray(p), np.ascontiguousarray(s)

    ckv_p, ckv_s = tok("o_ckv", 128)
    kr_p, kr_s = tok("o_kr", 32)

    def st(name, conv):
        a = cat(lambda r: r[name])
        a = conv(a)
        p = a[:, :, 0].transpose(1, 0, *range(2, a.ndim - 1))
        s = a[:, :, 1:].transpose(1, 0, *range(2, a.ndim))
        s = s.reshape((L, nb * NSB) + s.shape[3:])
        return np.ascontiguousarray(p), np.ascontiguousarray(s)

    s5_p, s5_s = st("o_s5", lambda a: a.reshape(nb, L, 3, 2, 64, 16).transpose(0, 1, 2, 5, 4, 3))
    rw_p, rw_s = st("o_rw", lambda a: a.reshape(nb, L, 3, 2, 64, 3, 64).transpose(0, 1, 2, 5, 3, 6, 4)
                    .reshape(nb, L, 3, 6, 64, 64))
    sh_p, sh_s = st("o_sh", lambda a: a.transpose(0, 1, 2, 4, 3).reshape(nb, L, 3, 1, 1280))
    return (np.ascontiguousarray(y_p), np.ascontiguousarray(y_s), ckv_p, kr_p, s5_p, rw_p, sh_p,
            ckv_s, kr_s, s5_s, rw_s, sh_s)


def kernel(**inputs):
    inp = {k: np.asarray(v) for k, v in inputs.items()}
    T = inp["x_prompt"].shape[1]
    PAST = inp["cache_mla_ckv"].shape[2]
    nb = inp["x_prompt"].shape[0]
    rs = run_cores(inp, T, PAST, nb)
    return assemble(rs, T)
```
